# Optimizing a Trainium2 kernel written in Bass

```python
import functools
import jax, jax.numpy as jnp
from jax import lax
import numpy as np

D_MODEL = 1024
BATCH = 8
SEQ = 4096
DEPTH = 1
DEC_BATCH = 32
DEC_SEQ = 8
PAST_LEN = 16384
PAGE_SIZE = 128

N_HEADS = 8
N_KV_HEADS = 2
HEAD_DIM = 64
GQA_GROUP = N_HEADS // N_KV_HEADS
IDX_HEADS = 8
IDX_DIM = 64
TOPK_MAX = 256
Q_BLOCK = 128
GLA_HEADS = 4
GLA_DK_TOT = D_MODEL // 2
GLA_DV_TOT = D_MODEL
GLA_DK = GLA_DK_TOT // GLA_HEADS
GLA_DV = GLA_DV_TOT // GLA_HEADS
GATE_RANK = 16
GATE_TAU = 16.0
GLA_CHUNK = 64
D_FF = 4 * D_MODEL
NORM_EPS = 1e-5
DN_ALPHA = (2 * DEPTH) ** 0.25
DN_BETA = (8 * DEPTH) ** -0.25

PROJ_SPLITS = (
    ('q_a', N_HEADS * HEAD_DIM),
    ('k_a', N_KV_HEADS * HEAD_DIM),
    ('v_a', N_KV_HEADS * HEAD_DIM),
    ('q_i', IDX_HEADS * IDX_DIM),
    ('k_i', IDX_DIM),
    ('w_i', IDX_HEADS),
    ('q_b', GLA_DK_TOT),
    ('k_b', GLA_DK_TOT),
    ('v_b', GLA_DV_TOT),
    ('g_b', GLA_DV_TOT),
    ('a_b', GATE_RANK),
    ('gate_a', D_MODEL),
    ('gate_b', D_MODEL),
)
D_IN = (N_HEADS * HEAD_DIM + 2 * N_KV_HEADS * HEAD_DIM + IDX_HEADS * IDX_DIM + IDX_DIM + IDX_HEADS
        + 2 * GLA_DK_TOT + 2 * GLA_DV_TOT + GATE_RANK + 2 * D_MODEL)

kernel_name = 'dsa_gla_gated_hybrid_step'


def layer_norm(x, g, b):
    xf = x.astype(jnp.float32)
    xc = xf - jnp.mean(xf, -1, keepdims=True)
    var = jnp.mean(xc * xc, -1, keepdims=True)
    return (xc * lax.rsqrt(var + NORM_EPS) * g + b).astype(x.dtype)


def split_projection(z):
    parts = {}
    off = 0
    for name, width in PROJ_SPLITS:
        parts[name] = z[..., off:off + width]
        off += width
    return parts


def gather_rows(rows, idx):
    return jax.vmap(lambda r, i: r[i])(rows, idx)


def indexer_scores(q_idx, w_idx, k_idx):
    dots = jnp.einsum('bqhd,bsd->bqhs', q_idx, k_idx).astype(jnp.float32) * (IDX_DIM ** -0.5)
    w = w_idx.astype(jnp.float32) * (IDX_HEADS ** -0.5)
    return jnp.einsum('bqhs,bqh->bqs', jax.nn.relu(dots), w)


def select_keys(scores, q_pos, topk):
    k_pos = jnp.arange(scores.shape[-1])
    admissible = k_pos[None, None, :] <= q_pos[None, :, None]
    scores = jnp.where(admissible, scores, -jnp.inf)
    _, idx = lax.top_k(scores, topk)
    valid = idx <= q_pos[None, :, None]
    return idx, valid


def sparse_attend(q, k_sel, v_sel, valid):
    b, t = q.shape[:2]
    qg = q.reshape(b, t, N_KV_HEADS, GQA_GROUP, HEAD_DIM)
    s = jnp.einsum('bqkgd,bqskd->bqkgs', qg, k_sel).astype(jnp.float32) * (HEAD_DIM ** -0.5)
    s = jnp.where(valid[:, :, None, None, :], s, -jnp.inf)
    p = jax.nn.softmax(s, axis=-1).astype(v_sel.dtype)
    o = jnp.einsum('bqkgs,bqskd->bqkgd', p, v_sel)
    return o.reshape(b, t, N_HEADS * HEAD_DIM)


def dsa_prompt(q, k, v, q_idx, k_idx, w_idx):
    b, t = q.shape[:2]
    topk = min(TOPK_MAX, t // 4)
    qb = min(Q_BLOCK, t)

    def block(i):
        start = i * qb
        q_blk = lax.dynamic_slice_in_dim(q, start, qb, axis=1)
        qi_blk = lax.dynamic_slice_in_dim(q_idx, start, qb, axis=1)
        wi_blk = lax.dynamic_slice_in_dim(w_idx, start, qb, axis=1)
        q_pos = start + jnp.arange(qb)
        idx, valid = select_keys(indexer_scores(qi_blk, wi_blk, k_idx), q_pos, topk)
        return sparse_attend(q_blk, gather_rows(k, idx), gather_rows(v, idx), valid)

    o = lax.map(block, jnp.arange(t // qb))
    return o.transpose(1, 0, 2, 3).reshape(b, t, N_HEADS * HEAD_DIM)


def dsa_sample(q, k, v, q_idx, k_idx, w_idx, cache_k, cache_v, cache_kidx, page_table):
    b, t = q.shape[:2]
    past = page_table.shape[1] * PAGE_SIZE
    topk = min(TOPK_MAX, (past + t) // 4)
    k_idx_past = cache_kidx[page_table].reshape(b, past, IDX_DIM).astype(k_idx.dtype)
    k_idx_all = jnp.concatenate([k_idx_past, k_idx], axis=1)
    q_pos = past + jnp.arange(t)
    idx, valid = select_keys(indexer_scores(q_idx, w_idx, k_idx_all), q_pos, topk)
    in_past = (idx < past)[..., None, None]
    p_idx = jnp.minimum(idx, past - 1)
    phys = page_table[jnp.arange(b)[:, None, None], p_idx // PAGE_SIZE]
    off = p_idx % PAGE_SIZE
    n_idx = jnp.clip(idx - past, 0, t - 1)
    k_sel = jnp.where(in_past, cache_k[phys, off].astype(k.dtype), gather_rows(k, n_idx))
    v_sel = jnp.where(in_past, cache_v[phys, off].astype(v.dtype), gather_rows(v, n_idx))
    return sparse_attend(q, k_sel, v_sel, valid)


def gla_chunked(q, k, v, log_a, s0, chunk):
    b, t, h, _ = q.shape
    n = t // chunk

    def to_chunks(u):
        return u.astype(jnp.float32).reshape(b, n, chunk, h, -1).transpose(1, 0, 3, 2, 4)

    causal = jnp.tril(jnp.ones((chunk, chunk), dtype=bool))

    def step(s, inp):
        qx, kx, vx, ax = inp
        cum = jnp.cumsum(ax, axis=-2)
        last = cum[..., -1:, :]
        q_t = qx * jnp.exp(cum)
        k_t = kx * jnp.exp(-cum)
        att = jnp.where(causal, jnp.einsum('bhcd,bhed->bhce', q_t, k_t), 0.0)
        o = jnp.einsum('bhce,bhev->bhcv', att, vx) + jnp.einsum('bhcd,bhdv->bhcv', q_t, s)
        s = jnp.exp(last)[..., 0, :, None] * s + jnp.einsum('bhcd,bhcv->bhdv', kx * jnp.exp(last - cum), vx)
        return s, o

    s_fin, o = lax.scan(step, s0.astype(jnp.float32), (to_chunks(q), to_chunks(k), to_chunks(v), to_chunks(log_a)))
    o = o.transpose(1, 0, 3, 2, 4).reshape(b, t, h, v.shape[-1])
    return o, s_fin


def token_mixer(x, w_in, w_alpha2, b_alpha, gla_norm_g, w_attn_o, w_gla_o, w_out, attend, gla_s0, gla_chunk):
    b, t = x.shape[:2]
    z = split_projection(x @ w_in)
    q_a = z['q_a'].reshape(b, t, N_HEADS, HEAD_DIM)
    k_a = z['k_a'].reshape(b, t, N_KV_HEADS, HEAD_DIM)
    v_a = z['v_a'].reshape(b, t, N_KV_HEADS, HEAD_DIM)
    q_i = z['q_i'].reshape(b, t, IDX_HEADS, IDX_DIM)
    k_i = z['k_i']
    attn = attend(q_a, k_a, v_a, q_i, k_i, z['w_i'])
    log_a = jax.nn.log_sigmoid((z['a_b'] @ w_alpha2 + b_alpha).astype(jnp.float32)) / GATE_TAU
    q_b = z['q_b'].reshape(b, t, GLA_HEADS, GLA_DK) * (GLA_DK ** -0.5)
    k_b = z['k_b'].reshape(b, t, GLA_HEADS, GLA_DK)
    v_b = z['v_b'].reshape(b, t, GLA_HEADS, GLA_DV)
    o_b, s_new = gla_chunked(q_b, k_b, v_b, log_a.reshape(b, t, GLA_HEADS, GLA_DK), gla_s0, gla_chunk)
    o_b = o_b * lax.rsqrt(jnp.mean(o_b * o_b, -1, keepdims=True) + NORM_EPS) * gla_norm_g
    o_b = (o_b.reshape(b, t, GLA_DV_TOT) * jax.nn.silu(z['g_b'].astype(jnp.float32))).astype(x.dtype)
    branch_a = attn @ w_attn_o
    branch_b = o_b @ w_gla_o
    merged = jax.nn.sigmoid(z['gate_a']) * branch_a + jax.nn.sigmoid(z['gate_b']) * branch_b
    return merged @ w_out, k_a, v_a, k_i, s_new.astype(x.dtype)


def residual_block(x, mix_out, ln1_g, ln1_b, w_ff1, w_ff2, ln2_g, ln2_b):
    h = layer_norm(DN_ALPHA * x + mix_out, ln1_g, ln1_b)
    ff = jnp.square(jax.nn.relu(h @ w_ff1)) @ w_ff2
    return layer_norm(DN_ALPHA * h + ff, ln2_g, ln2_b)


def setup_inputs(seed: int = 0) -> dict:
    key = jax.random.key(seed)
    ks = jax.random.split(key, 24)
    f32 = jnp.float32
    n_pages = PAST_LEN // PAGE_SIZE
    used = DEC_BATCH * n_pages
    n_pool = used + max(1, used // 4)

    def nrm(k, shape, scale):
        return jax.random.normal(k, shape, f32) * scale

    page_table = jax.random.permutation(ks[0], n_pool)[:used].reshape(DEC_BATCH, n_pages).astype(jnp.int32)
    return {
        'x_prompt': nrm(ks[1], (BATCH, SEQ, D_MODEL), 1.0),
        'x_sample': nrm(ks[2], (DEC_BATCH, DEC_SEQ, D_MODEL), 1.0),
        'cache_k': nrm(ks[3], (DEPTH, n_pool, PAGE_SIZE, N_KV_HEADS, HEAD_DIM), 1.0),
        'cache_v': nrm(ks[4], (DEPTH, n_pool, PAGE_SIZE, N_KV_HEADS, HEAD_DIM), 1.0),
        'cache_kidx': nrm(ks[5], (DEPTH, n_pool, PAGE_SIZE, IDX_DIM), 1.0),
        'state_gla': nrm(ks[6], (DEPTH, DEC_BATCH, GLA_HEADS, GLA_DK, GLA_DV), 0.5),
        'page_table': page_table,
        'w_in': nrm(ks[7], (DEPTH, D_MODEL, D_IN), D_MODEL ** -0.5),
        'w_alpha2': nrm(ks[8], (DEPTH, GATE_RANK, GLA_DK_TOT), GATE_RANK ** -0.5),
        'b_alpha': nrm(ks[9], (DEPTH, GLA_DK_TOT), 0.1),
        'gla_norm_g': 1.0 + nrm(ks[10], (DEPTH, GLA_DV), 0.02),
        'w_attn_o': nrm(ks[11], (DEPTH, N_HEADS * HEAD_DIM, D_MODEL), (N_HEADS * HEAD_DIM) ** -0.5),
        'w_gla_o': nrm(ks[12], (DEPTH, GLA_DV_TOT, D_MODEL), GLA_DV_TOT ** -0.5),
        'w_out': nrm(ks[13], (DEPTH, D_MODEL, D_MODEL), DN_BETA * D_MODEL ** -0.5),
        'ln1_g': 1.0 + nrm(ks[14], (DEPTH, D_MODEL), 0.02),
        'ln1_b': nrm(ks[15], (DEPTH, D_MODEL), 0.02),
        'w_ff1': nrm(ks[16], (DEPTH, D_MODEL, D_FF), D_MODEL ** -0.5),
        'w_ff2': nrm(ks[17], (DEPTH, D_FF, D_MODEL), DN_BETA * D_FF ** -0.5),
        'ln2_g': 1.0 + nrm(ks[18], (DEPTH, D_MODEL), 0.02),
        'ln2_b': nrm(ks[19], (DEPTH, D_MODEL), 0.02),
    }


def reference(x_prompt, x_sample, cache_k, cache_v, cache_kidx, state_gla, page_table,
              w_in, w_alpha2, b_alpha, gla_norm_g, w_attn_o, w_gla_o, w_out,
              ln1_g, ln1_b, w_ff1, w_ff2, ln2_g, ln2_b):
    h_p, h_s = x_prompt, x_sample
    kp_l, vp_l, kip_l, sp_l = [], [], [], []
    ks_l, vs_l, kis_l, ss_l = [], [], [], []
    for l in range(DEPTH):
        mix_w = (w_in[l], w_alpha2[l], b_alpha[l], gla_norm_g[l], w_attn_o[l], w_gla_o[l], w_out[l])
        s0 = jnp.zeros((h_p.shape[0], GLA_HEADS, GLA_DK, GLA_DV), jnp.float32)
        m_p, k_p, v_p, ki_p, s_p = token_mixer(h_p, *mix_w, dsa_prompt, s0, min(GLA_CHUNK, h_p.shape[1]))
        attend_s = functools.partial(dsa_sample, cache_k=cache_k[l], cache_v=cache_v[l],
                                     cache_kidx=cache_kidx[l], page_table=page_table)
        m_s, k_s, v_s, ki_s, s_s = token_mixer(h_s, *mix_w, attend_s, state_gla[l], h_s.shape[1])
        h_p = residual_block(h_p, m_p, ln1_g[l], ln1_b[l], w_ff1[l], w_ff2[l], ln2_g[l], ln2_b[l])
        h_s = residual_block(h_s, m_s, ln1_g[l], ln1_b[l], w_ff1[l], w_ff2[l], ln2_g[l], ln2_b[l])
        kp_l.append(k_p); vp_l.append(v_p); kip_l.append(ki_p); sp_l.append(s_p)
        ks_l.append(k_s); vs_l.append(v_s); kis_l.append(ki_s); ss_l.append(s_s)
    return (h_p, h_s,
            jnp.stack(kp_l), jnp.stack(vp_l), jnp.stack(kip_l), jnp.stack(sp_l),
            jnp.stack(ks_l), jnp.stack(vs_l), jnp.stack(kis_l), jnp.stack(ss_l))
```

```python
from contextlib import ExitStack
import numpy as np
import concourse.bass as bass
import concourse.mybir as mybir
from concourse.bass_utils import run_bass_kernel_spmd

F32 = mybir.dt.float32
BF16 = mybir.dt.bfloat16
I32 = mybir.dt.int32
AF = mybir.ActivationFunctionType
ALU = mybir.AluOpType
AX = mybir.AxisListType

D = 1024
D_IN = 6488
NS = 32
NB_S = 4
NPAGES = 128
ROUNDS = 20
ALPHA = 2.0 ** 0.25
EPS = 1e-5
NEG = -30000.0
BIG = 1.0e30


class Buf:
    __slots__ = ("name", "writer", "readers", "excl")

    def __init__(self, name, excl=False):
        self.name = name
        self.writer = None
        self.readers = []
        self.excl = excl


class Op:
    __slots__ = ("eng", "fn", "deps", "signal", "sem", "val", "is_dma", "lane")

    def __init__(self, eng, fn, is_dma):
        self.eng = eng
        self.fn = fn
        self.deps = []
        self.signal = False
        self.sem = None
        self.val = 0
        self.is_dma = is_dma
        self.lane = None


ENGS = ("pe", "act", "dve", "pool", "sp")
EPOCH = 12000


class Prog:
    def __init__(self, nc, lanes=8):
        self.nc = nc
        self.ops = []
        self.sems = {}
        self.cnt = {e: 0 for e in ENGS}
        self.waited = {e: {} for e in ENGS}
        self.lanes = {}
        self.lane_rr = {}
        for q in ("sp", "pool", "act"):
            self.lanes[q] = [[nc.alloc_semaphore("ln_%s_%d" % (q, i)), 0] for i in range(lanes)]
            self.lane_rr[q] = 0
        self.last_sig = {e: None for e in ENGS}
        self.all_dma = []
        self.fence_deps = []
        self.n_ops = 0

    def _add(self, eng, fn, reads, writes, is_dma=False):
        op = Op(eng, fn, is_dma)
        deps = set()
        for b in reads:
            if b.writer is not None:
                deps.add(b.writer)
            if b.excl:
                for r in b.readers:
                    if r.eng != eng:
                        deps.add(r)
        for b in writes:
            if b.writer is not None:
                deps.add(b.writer)
            for r in b.readers:
                deps.add(r)
        op.deps = list(deps)
        for b in reads:
            b.readers.append(op)
        for b in writes:
            b.writer = op
            b.readers = []
        self.ops.append(op)
        return op

    def pe(self, fn, reads=(), writes=()):
        return self._add("pe", fn, reads, writes)

    def act(self, fn, reads=(), writes=()):
        return self._add("act", fn, reads, writes)

    def dve(self, fn, reads=(), writes=()):
        return self._add("dve", fn, reads, writes)

    def pool(self, fn, reads=(), writes=()):
        return self._add("pool", fn, reads, writes)

    def dma(self, q, fn, reads=(), writes=()):
        import os
        if q in os.environ.get("SKIPDMA", "").split(","):
            return None
        return self._add(q, fn, reads, writes, is_dma=True)

    def _sem_for(self, eng):
        ep = self.cnt[eng] // EPOCH
        key = (eng, ep)
        if key not in self.sems:
            self.sems[key] = self.nc.alloc_semaphore("s_%s_%d" % (eng, ep))
        return self.sems[key], ep

    def flush(self, final=False):
        nc = self.nc
        ops = self.ops
        self.ops = []
        if not ops and not final:
            return
        self.n_ops += len(ops)
        needed = set()
        for op in ops:
            for d in op.deps:
                if d.is_dma:
                    continue
                if d.eng == "pe" and op.eng == "pe" and not op.is_dma:
                    continue
                needed.add(d)
        last = {}
        for op in ops:
            if not op.is_dma:
                last[op.eng] = op
        for op in last.values():
            needed.add(op)
        for op in ops:
            if op.is_dma:
                lanes = self.lanes[op.eng]
                li = self.lane_rr[op.eng]
                self.lane_rr[op.eng] = (li + 1) % len(lanes)
                lane = lanes[li]
                op.lane = (lane[0], lane[1])
                lane[1] += 16
                op.sem = lane[0]
                op.val = lane[1]
                self.all_dma.append(op)
            elif op in needed and op.sem is None:
                sem, ep = self._sem_for(op.eng)
                self.cnt[op.eng] += 1
                op.sem = sem
                op.val = self.cnt[op.eng] - ep * EPOCH
                op.signal = True
                self.last_sig[op.eng] = op
        streams = {e: [] for e in ENGS}
        for op in ops:
            streams[op.eng].append(op)
        fence = self.fence_deps

        def emit_stream(eng_name, e):
            waited = self.waited[eng_name]

            def wait(sem, val):
                if waited.get(sem.name, 0) >= val:
                    return
                waited[sem.name] = val
                e.wait_ge(sem, val)

            first = True
            for op in streams[eng_name]:
                if first:
                    for d in fence:
                        if d.sem is not None:
                            wait(d.sem, d.val)
                    first = False
                for d in op.deps:
                    if d.sem is None:
                        continue
                    if (not d.is_dma) and d.eng == "pe" and eng_name == "pe" and not op.is_dma:
                        continue
                    wait(d.sem, d.val)
                if op.is_dma:
                    if op.lane[1] > 0:
                        wait(op.lane[0], op.lane[1])
                    ins = op.fn(e)
                    ins.then_inc(op.sem, 16)
                else:
                    ins = op.fn(e)
                    if op.signal:
                        ins.then_inc(op.sem, 1)
            if final and eng_name == "sp":
                for q in self.lanes:
                    for sem, tot in self.lanes[q]:
                        if tot > 0:
                            wait(sem, tot)

        with nc.Block() as block:
            @block.tensor
            def _(e):
                emit_stream("pe", e)

            @block.scalar
            def _(e):
                emit_stream("act", e)

            @block.vector
            def _(e):
                emit_stream("dve", e)

            @block.gpsimd
            def _(e):
                emit_stream("pool", e)

            @block.sync
            def _(e):
                emit_stream("sp", e)
        fd = [op for op in self.last_sig.values() if op is not None]
        latest = {}
        for op in self.all_dma:
            latest[op.sem.name] = op
        fd += list(latest.values())
        self.fence_deps = fd
        self.all_dma = list(latest.values())


def CALL(name, *a, **k):
    return lambda e: getattr(e, name)(*a, **k)


class TT:
    __slots__ = ("t", "b")

    def __init__(self, t, name, excl=False):
        self.t = t
        self.b = Buf(name, excl)


C_ID, C_UT, C_LT, C_LE, C_CN, C_G, C_PW, C_EPS, C_ONE = 0, 128, 256, 384, 512, 640, 768, 800, 801
NCST = 832


def make_consts():
    c = np.zeros((128, NCST), np.float32)
    p = np.arange(128)[:, None]
    j = np.arange(128)[None, :]
    c[:, C_ID:C_ID + 128] = (p == j)
    c[:, C_UT:C_UT + 128] = np.where(p <= j, -1.0 / 16, 0.0)
    c[:, C_LT:C_LT + 128] = np.where(p > j, -1.0 / 16, 0.0)
    c[:, C_LE:C_LE + 128] = (p <= j)
    c[:, C_CN:C_CN + 128] = np.where(j > p, -BIG, 0.0)
    c[:, C_G:C_G + 128] = ((p // 32 == j // 32) & (p % 8 == j % 8))
    c[:, C_PW:C_PW + 32] = 2.0 ** -(np.arange(32)[None, :] + 1.0)
    c[:, C_EPS] = EPS
    c[:, C_ONE] = 1.0
    return c


O_QA, O_KA, O_VA, O_QI, O_KI, O_WI, O_QB, O_KB, O_VB, O_GB, O_AB, O_GA = (
    0, 512, 640, 768, 1280, 1344, 1352, 1864, 2376, 3400, 4424, 4440)


def w_layout():
    fm, tm = [], []
    off = 0

    def add(lst, name, pieces):
        nonlocal off
        w = sum(b - a for a, b in pieces)
        lst.append((name, off, w, pieces))
        off += w

    for j in range(4):
        add(fm, "qa%d" % j, [(O_QA + 64 * j, O_QA + 64 * j + 64), (O_QA + 64 * (4 + j), O_QA + 64 * (4 + j) + 64)])
    add(fm, "ka", [(O_KA, O_KA + 128)])
    for j in range(4):
        add(fm, "qi%d" % j, [(O_QI + 128 * j, O_QI + 128 * j + 128)])
    add(fm, "ki", [(O_KI, O_KI + 64), (O_KI, O_KI + 64)])
    for j in range(4):
        add(fm, "qb%d" % j, [(O_QB + 128 * j, O_QB + 128 * j + 128)])
    for j in range(4):
        add(fm, "kb%d" % j, [(O_KB + 128 * j, O_KB + 128 * j + 128)])
    add(fm, "ab", [(O_AB, O_AB + 16)])
    for j in range(16):
        add(fm, "gt%d" % j, [(O_GA + 128 * j, O_GA + 128 * j + 128)])
    add(tm, "ta", [(O_KA, O_KA + 256), (O_KI, O_KI + 72)])
    add(tm, "tkb", [(O_KB, O_KB + 512)])
    add(tm, "tvb0", [(O_VB, O_VB + 512)])
    add(tm, "tvb1", [(O_VB + 512, O_VB + 1024)])
    add(tm, "tgb0", [(O_GB, O_GB + 512)])
    add(tm, "tgb1", [(O_GB + 512, O_GB + 1024)])
    return fm, tm, off


def build(NTP=4096, NPOOL=5120, phases="12345", dbg=False):
    nc = bass.Bass("TRN2", target_bir_lowering=False)
    P = Prog(nc)
    NT = NTP + NS
    NBLK = NTP // 128
    TOPK = min(256, NTP // 4)
    TOPK_S = min(256, (NPAGES * 128 + 8) // 4)

    def din(name, shape, dt=F32):
        return nc.dram_tensor(name, list(shape), dt, kind="ExternalInput")

    def dout(name, shape, dt=F32):
        return nc.dram_tensor(name, list(shape), dt, kind="ExternalOutput")

    def dscr(name, shape, dt):
        return nc.dram_tensor(name, list(shape), dt, kind="Internal")

    xT_d = din("xT", [D, NT])
    x_d = din("x", [NT, D])
    ck_d = din("cache_k", [NPOOL * 8, 2048])
    cv_d = din("cache_v", [NPOOL * 8, 2048])
    cki_d = din("cache_kidx", [NPOOL * 4, 2048])
    st_d = din("state_gla", [NB_S, 4, 128, 256])
    pt_d = din("page_table", [NB_S, NPAGES], I32)
    w_in_d = din("w_in", [D, D_IN])
    wal_d = din("w_alpha2", [16, 512])
    bal_d = din("b_alpha", [1, 512])
    gng_d = din("gla_norm_g", [1, 256])
    wao_d = din("w_attn_o", [512, D])
    wgo_d = din("w_gla_o", [D, D])
    wo_d = din("w_out", [D, D])
    ln_d = {n: din(n, [1, D]) for n in ("ln1_g", "ln1_b", "ln2_g", "ln2_b")}
    wf1_d = din("w_ff1", [D, 4096])
    wf2_d = din("w_ff2", [4096, D])
    cst_d = din("cst", [128, NCST])

    y_d = dout("y", [NT, D])
    ko_d = dout("ko", [NT, 128])
    vo_d = dout("vo", [NT, 128])
    kio_d = dout("kio", [NT, 64])
    glap_d = dout("gla_p", [4, 128, 256])
    glas_d = dout("gla_s", [NB_S, 4, 128, 256])

    qaT_s = dscr("qaT_s", [4, 128, NT], BF16)
    qiT_s = dscr("qiT_s", [4, 128, NT], BF16)
    wi_s = dscr("wi_s", [NT, 8], F32)
    qbT_s = dscr("qbT_s", [4, 128, NT], BF16)
    kbT_s = dscr("kbT_s", [4, 128, NT], BF16)
    abT_s = dscr("abT_s", [16, NT], BF16)
    gtT_s = dscr("gtT_s", [16, 128, NT], F32)
    kb_s = dscr("kb_s", [NT, 512], BF16)
    vb_s = dscr("vb_s", [NT, 1024], BF16)
    gb_s = dscr("gb_s", [NT, 1024], F32)
    attnT_s = dscr("attnT_s", [4, 128, NT], BF16)
    obT_s = dscr("obT_s", [8, 128, NT], BF16)
    hT_s = dscr("hT_s", [8, 128, NT], BF16)
    h_s = dscr("h_s", [NT, D], F32)

    DBD = {}
    NJ = {"qaT": 4, "qiT": 4, "wi": 1, "qbT": 4, "kbT": 4, "abT": 1, "gtT": 16, "kb": 1, "vb": 2, "gb": 2, "attnT": 4,
          "obT": 1, "hT": 1, "h": 1, "ko": 1, "vo": 1, "kio": 1, "y": 1}
    B_gla_out = Buf("gla_out")

    def dbs(name, tok0, n, j=None):
        js = range(NJ[name]) if j is None else [j]
        out = []
        for jj in js:
            for tl in range(tok0 // 128, (tok0 + n - 1) // 128 + 1):
                key = (name, jj, tl)
                if key not in DBD:
                    DBD[key] = Buf("%s_%d_%d" % key)
                out.append(DBD[key])
        return out

    g = ExitStack()

    def SB(stack, name, shape, dt):
        return TT(stack.enter_context(nc.sbuf_tensor("sb_" + name, list(shape), dt)), name)

    def PSB(stack, name, shape, dt=F32):
        return TT(stack.enter_context(nc.psum_tensor("pp_" + name, list(shape), dt)), name, True)

    with g:
        cst = SB(g, "cst", [128, NCST], F32)
        ident_bf = SB(g, "ident_bf", [128, 128], BF16)
        i4_bf = SB(g, "i4_bf", [128, 4, 128], BF16)
        mle_bf = SB(g, "mle_bf", [128, 128], BF16)
        ones_bf = SB(g, "ones_bf", [128, 128], BF16)
        P.dma("sp", CALL("dma_start", out=cst.t[:], in_=cst_d[:, :]), [], [cst.b])
        P.dve(CALL("tensor_copy", out=ident_bf.t[:], in_=cst.t[:, C_ID:C_ID + 128]), [cst.b], [ident_bf.b])
        for k in range(4):
            P.dve(CALL("tensor_copy", out=i4_bf.t[:, k, :], in_=cst.t[:, C_ID:C_ID + 128]), [cst.b], [i4_bf.b])
        P.dve(CALL("tensor_copy", out=mle_bf.t[:], in_=cst.t[:, C_LE:C_LE + 128]), [cst.b], [mle_bf.b])
        P.pool(CALL("memset", ones_bf.t[:], 1.0), [], [ones_bf.b])
        ident_f = cst.t[:, C_ID:C_ID + 128]
        utri_f = cst.t[:, C_UT:C_UT + 128]
        ltri_f = cst.t[:, C_LT:C_LT + 128]
        cneg_f = cst.t[:, C_CN:C_CN + 128]
        eps_c = cst.t[:, C_EPS:C_EPS + 1]
        one_c = cst.t[:, C_ONE:C_ONE + 1]

        lnb = {n: SB(g, "lnb_" + n, [128, D], F32) for n in ln_d}
        s12 = ExitStack()
        with s12:
            kA = SB(s12, "kA", [128, NT], BF16)
            kB = SB(s12, "kB", [128, NT], BF16)
            kiA = SB(s12, "kiA", [128, NT], BF16)
            kiB = SB(s12, "kiB", [128, NT], BF16)
            vaug = SB(s12, "vaug", [128, NBLK, 2, 65], BF16)
            for tt_ in (kA, kB, kiA, kiB):
                P.pool(CALL("memset", tt_.t[:], 0.0), [], [tt_.b])
            P.pool(CALL("memset", vaug.t[:], 1.0), [], [vaug.b])

            if "1" in phases:
                s1 = ExitStack()
                with s1:
                    fm, tm, ncol = w_layout()
                    w_sb = SB(s1, "w_sb", [128, 8, ncol], BF16)
                    w_src = w_in_d.rearrange("(kc p) n -> p kc n", p=128)
                    wb = {}
                    for lst in (fm, tm):
                        for (name, off, wd, pieces) in lst:
                            b = Buf("w_" + name)
                            wb[name] = b
                            o = off
                            for (a0, a1) in pieces:
                                P.dma("pool", CALL("dma_start",
                                    out=w_sb.t[:, :, o:o + (a1 - a0)], in_=w_src[:, :, a0:a1]), [], [b])
                                o += a1 - a0
                    import os as _os
                    NXS = int(_os.environ.get("XSLOTS", "2"))
                    xts = [SB(s1, "xT%d" % i, [128, 8, 512], BF16) for i in range(NXS)]
                    stg_b = [SB(s1, "stgb%d" % i, [128, 512], BF16) for i in range(4)]
                    stg_f = [SB(s1, "stgf%d" % i, [128, 512], F32) for i in range(4)]
                    pss = [PSB(s1, "ps1_%d" % i, [128, 512]) for i in range(6)]
                    rr = {"ps": 0, "sb": 0, "sf": 0, "ev": 0}
                    xT_src = xT_d.rearrange("(kc p) t -> p kc t", p=128)

                    def nxt(key, n):
                        v = rr[key]
                        rr[key] = (v + 1) % n
                        return v

                    def evac(out_ap, in_ap, reads, writes):
                        if nxt("ev", 2) == 0:
                            P.act(CALL("activation", out=out_ap, in_=in_ap, func=AF.Copy), reads, writes)
                        else:
                            P.dve(CALL("tensor_copy", out=out_ap, in_=in_ap), reads, writes)

                    STW = int(_os.environ.get("STW", "512"))
                    sts = [(t0, min(STW, NTP - t0)) for t0 in range(0, NTP, STW)] + [(NTP, NS)]

                    def load_x(si):
                        t0, W = sts[si]
                        xt = xts[si % NXS]
                        P.dma("pool", CALL("dma_start", out=xt.t[:, :, 0:W], in_=xT_src[:, :, t0:t0 + W]), [], [xt.b])

                    load_x(0)
                    for si, (t0, W) in enumerate(sts):
                        if si + 1 < len(sts):
                            load_x(si + 1)
                        xt = xts[si % NXS]
                        for (name, off, m, pieces) in fm:
                            ps = pss[nxt("ps", 6)]
                            for kc in range(8):
                                P.pe(CALL("matmul",
                                    ps.t[0:m, 0:W], lhsT=w_sb.t[:, kc, off:off + m], rhs=xt.t[:, kc, 0:W],
                                    start=(kc == 0), stop=(kc == 7)), [xt.b, wb[name]], [ps.b])
                            if name == "ka":
                                evac(kA.t[0:64, t0:t0 + W], ps.t[0:64, 0:W], [ps.b], [kA.b])
                                evac(kB.t[64:128, t0:t0 + W], ps.t[64:128, 0:W], [ps.b], [kB.b])
                            elif name == "ki":
                                evac(kiA.t[0:64, t0:t0 + W], ps.t[0:64, 0:W], [ps.b], [kiA.b])
                                evac(kiB.t[64:128, t0:t0 + W], ps.t[64:128, 0:W], [ps.b], [kiB.b])
                            else:
                                kind = name[:2]
                                j = int(name[2:]) if len(name) > 2 else 0
                                if kind == "gt":
                                    sg = stg_f[nxt("sf", 4)]
                                    dst = gtT_s[j, :, t0:t0 + W]
                                    dbn = "gtT"
                                else:
                                    sg = stg_b[nxt("sb", 4)]
                                    dst = {"qa": qaT_s, "qi": qiT_s, "qb": qbT_s, "kb": kbT_s}[kind][j, :, t0:t0 + W] \
                                        if kind != "ab" else abT_s[:, t0:t0 + W]
                                    dbn = {"qa": "qaT", "qi": "qiT", "qb": "qbT", "kb": "kbT", "ab": "abT"}[kind]
                                evac(sg.t[0:m, 0:W], ps.t[0:m, 0:W], [ps.b], [sg.b])
                                P.dma("sp", CALL("dma_start", out=dst, in_=sg.t[0:m, 0:W]),
                                      [sg.b], dbs(dbn, t0, W, j if NJ[dbn] > 1 else 0))
                        ntile = (W + 127) // 128
                        for tt_i in range(ntile):
                            T = min(128, W - tt_i * 128)
                            tk0 = t0 + tt_i * 128
                            for (name, off, n, pieces) in tm:
                                ps = pss[nxt("ps", 6)]
                                for kc in range(8):
                                    P.pe(CALL("matmul",
                                        ps.t[0:T, 0:n], lhsT=xt.t[:, kc, tt_i * 128:tt_i * 128 + T],
                                        rhs=w_sb.t[:, kc, off:off + n], start=(kc == 0), stop=(kc == 7)),
                                        [xt.b, wb[name]], [ps.b])
                                if name == "ta":
                                    sg = stg_f[nxt("sf", 4)]
                                    evac(sg.t[0:T, 0:n], ps.t[0:T, 0:n], [ps.b], [sg.b])
                                    for (dst, c0, c1, dbn) in ((ko_d, 0, 128, "ko"), (vo_d, 128, 256, "vo"),
                                                               (kio_d, 256, 320, "kio"), (wi_s, 320, 328, "wi")):
                                        P.dma("sp", CALL("dma_start",
                                            out=dst[tk0:tk0 + T, :], in_=sg.t[0:T, c0:c1]), [sg.b], dbs(dbn, tk0, T))
                                    if tk0 < NTP:
                                        blk = tk0 // 128
                                        P.dve(CALL("tensor_copy",
                                            out=vaug.t[:, blk, :, 1:65],
                                            in_=ps.t[:, 128:256].rearrange("p (h d) -> p h d", h=2)), [ps.b], [vaug.b])
                                else:
                                    if name.startswith("tgb"):
                                        sg = stg_f[nxt("sf", 4)]
                                        dst, dbn = gb_s, "gb"
                                    else:
                                        sg = stg_b[nxt("sb", 4)]
                                        dst, dbn = (kb_s, "kb") if name == "tkb" else (vb_s, "vb")
                                    c0 = 512 if name.endswith("1") else 0
                                    evac(sg.t[0:T, 0:n], ps.t[0:T, 0:n], [ps.b], [sg.b])
                                    P.dma("sp", CALL("dma_start",
                                        out=dst[tk0:tk0 + T, c0:c0 + n], in_=sg.t[0:T, 0:n]), [sg.b],
                                        dbs(dbn, tk0, T, (c0 // 512) if NJ[dbn] > 1 else 0))
                    P.flush()

            if "2" in phases:
                s2 = ExitStack()
                with s2:
                    NK = NTP
                    isc = [SB(s2, "isc%d" % i, [128, NK], F32) for i in range(2)]
                    mbs = [SB(s2, "mb%d" % i, [128, NK], BF16) for i in range(2)]
                    junk = SB(s2, "junk2", [128, NK], BF16)
                    rsl = [SB(s2, "rsl%d" % i, [128, 512], BF16) for i in range(4)]
                    ptl = [SB(s2, "ptl%d" % i, [128, 512], BF16) for i in range(4)]
                    qib = [SB(s2, "qib%d" % i, [128, 4, 128], BF16) for i in range(2)]
                    qab = [SB(s2, "qab%d" % i, [128, 4, 128], BF16) for i in range(2)]
                    wib = [SB(s2, "wib%d" % i, [128, 8], F32) for i in range(2)]
                    dgs = [SB(s2, "dg%d" % i, [128, 8, 128], BF16) for i in range(2)]
                    stt = [SB(s2, "st%d" % i, [128, 64], F32) for i in range(2)]
                    ast = [SB(s2, "ast%d" % i, [65, 4, 128], BF16) for i in range(2)]
                    bcs = SB(s2, "bcs", [65, 512], F32)
                    rcp = SB(s2, "rcp", [1, 512], F32)
                    tmpd = SB(s2, "tmpd", [128, 128], F32)
                    psd = [PSB(s2, "psd%d" % i, [128, 512]) for i in range(2)]
                    psi = [PSB(s2, "psi%d" % i, [128, 512]) for i in range(2)]
                    pss_ = [PSB(s2, "pss%d" % i, [128, 512]) for i in range(2)]
                    pso = PSB(s2, "pso", [128, 512])
                    psb = PSB(s2, "psb", [128, 512])
                    rr2 = {"d": 0, "i": 0, "s": 0, "r": 0, "p": 0}

                    def nx2(k, n):
                        v = rr2[k]
                        rr2[k] = (v + 1) % n
                        return v

                    S_MIN1, S_MIN2, S_MAX, S_W0, S_MID, S_CNT, S_U, S_THR, S_WALL = 0, 1, 2, 3, 4, 5, 6, 7, 8
                    WSCALE = (8.0 ** -0.5) / 8.0

                    def stage_a(i):
                        sl = i % 2
                        q0 = i * 128
                        nk = (i + 1) * 128
                        qi_, qa_, wi_, dg_, I_, st_ = qib[sl], qab[sl], wib[sl], dgs[sl], isc[sl], stt[sl]
                        P.dma("sp", CALL("dma_start", out=qi_.t[:], in_=qiT_s[:, :, q0:q0 + 128].rearrange("j p t -> p j t")),
                              dbs("qiT", q0, 128), [qi_.b])
                        P.dma("sp", CALL("dma_start", out=qa_.t[:], in_=qaT_s[:, :, q0:q0 + 128].rearrange("j p t -> p j t")),
                              dbs("qaT", q0, 128), [qa_.b])
                        P.dma("sp", CALL("dma_start", out=wi_.t[:], in_=wi_s[q0:q0 + 128, :]), dbs("wi", q0, 128), [wi_.b])
                        for h in range(8):
                            P.dve(CALL("tensor_scalar", out=dg_.t[:, h, :], in0=ident_f, scalar1=wi_.t[:, h:h + 1],
                                                                 scalar2=WSCALE, op0=ALU.mult, op1=ALU.mult),
                                  [wi_.b, cst.b], [dg_.b])
                        nch = (nk + 511) // 512
                        for c in range(nch):
                            k0 = c * 512
                            Wc = min(512, nk - k0)
                            pI = psi[nx2("i", 2)]
                            for h in range(8):
                                pd = psd[nx2("d", 2)]
                                kis = kiA if h % 2 == 0 else kiB
                                P.pe(CALL("matmul",
                                    pd.t[:, 0:Wc], lhsT=qi_.t[:, h // 2, :], rhs=kis.t[:, k0:k0 + Wc], start=True, stop=True),
                                    [qi_.b, kis.b], [pd.b])
                                r_ = rsl[nx2("r", 4)]
                                P.act(CALL("activation", out=r_.t[:, 0:Wc], in_=pd.t[:, 0:Wc], func=AF.Relu),
                                      [pd.b], [r_.b])
                                P.pe(CALL("matmul",
                                    pI.t[:, 0:Wc], lhsT=dg_.t[:, h, :], rhs=r_.t[:, 0:Wc], start=(h == 0), stop=(h == 7)),
                                    [dg_.b, r_.b], [pI.b])
                            P.dve(CALL("tensor_copy", out=I_.t[:, k0:k0 + Wc], in_=pI.t[:, 0:Wc]), [pI.b], [I_.b])
                        d0 = i * 128
                        P.dve(CALL("tensor_tensor", out=tmpd.t[:], in0=I_.t[:, d0:d0 + 128], in1=cneg_f, op=ALU.subtract),
                              [I_.b, cst.b], [tmpd.b])
                        P.dve(CALL("tensor_reduce", out=st_.t[:, S_MIN2:S_MIN2 + 1], in_=tmpd.t[:], axis=AX.X, op=ALU.min),
                              [tmpd.b], [st_.b])
                        if i > 0:
                            P.dve(CALL("tensor_reduce", out=st_.t[:, S_MIN1:S_MIN1 + 1], in_=I_.t[:, 0:d0], axis=AX.X,
                                                            op=ALU.min), [I_.b], [st_.b])
                            P.dve(CALL("tensor_tensor", out=st_.t[:, S_MIN2:S_MIN2 + 1], in0=st_.t[:, S_MIN2:S_MIN2 + 1],
                                                            in1=st_.t[:, S_MIN1:S_MIN1 + 1], op=ALU.min), [st_.b], [st_.b])
                        P.dve(CALL("tensor_tensor", out=I_.t[:, d0:d0 + 128], in0=I_.t[:, d0:d0 + 128], in1=cneg_f, op=ALU.add),
                              [I_.b, cst.b], [I_.b])
                        P.dve(CALL("tensor_reduce", out=st_.t[:, S_MAX:S_MAX + 1], in_=I_.t[:, 0:nk], axis=AX.X, op=ALU.max),
                              [I_.b], [st_.b])

                    def stage_b(i):
                        sl = i % 2
                        nk = (i + 1) * 128
                        I_, st_, mb_ = isc[sl], stt[sl], mbs[sl]
                        c_ = lambda k: st_.t[:, k:k + 1]
                        P.dve(CALL("tensor_tensor", out=c_(S_W0), in0=c_(S_MAX), in1=c_(S_MIN2), op=ALU.subtract), [st_.b], [st_.b])
                        P.dve(CALL("tensor_scalar", out=c_(S_U), in0=c_(S_W0), scalar1=-1e-3, scalar2=-1e-4, op0=ALU.mult,
                                                        op1=ALU.add), [st_.b], [st_.b])
                        P.dve(CALL("tensor_tensor", out=c_(S_MIN2), in0=c_(S_MIN2), in1=c_(S_U), op=ALU.add), [st_.b], [st_.b])
                        P.dve(CALL("tensor_tensor", out=c_(S_W0), in0=c_(S_MAX), in1=c_(S_MIN2), op=ALU.subtract), [st_.b], [st_.b])
                        P.dve(CALL("tensor_scalar", out=c_(S_W0), in0=c_(S_W0), scalar1=1.0001, scalar2=1e-6, op0=ALU.mult,
                                                        op1=ALU.add), [st_.b], [st_.b])
                        P.dve(CALL("tensor_scalar", out=st_.t[:, S_WALL:S_WALL + 32], in0=cst.t[:, C_PW:C_PW + 32],
                                                        scalar1=c_(S_W0), scalar2=None, op0=ALU.mult), [st_.b, cst.b], [st_.b])
                        P.dve(CALL("tensor_tensor", out=c_(S_MID), in0=c_(S_MIN2), in1=c_(S_WALL), op=ALU.add), [st_.b], [st_.b])
                        for r in range(ROUNDS):
                            P.dve(CALL("tensor_scalar", out=junk.t[:, 0:nk], in0=I_.t[:, 0:nk], scalar1=c_(S_MID), scalar2=0.0,
                                                            op0=ALU.is_ge, op1=ALU.add, accum_out=c_(S_CNT)),
                                  [I_.b, st_.b], [junk.b, st_.b])
                            P.dve(CALL("tensor_scalar", out=c_(S_U), in0=c_(S_CNT), scalar1=TOPK - 0.5,
                                                                 scalar2=c_(S_WALL + r), op0=ALU.is_ge, op1=ALU.mult),
                                  [st_.b], [st_.b])
                            nxt_w = S_WALL + r + 1 if r + 1 < ROUNDS else S_WALL + r
                            dst = S_MID if r + 1 < ROUNDS else S_THR
                            P.dve(CALL("scalar_tensor_tensor",
                                out=c_(dst), in0=c_(S_U), scalar=c_(nxt_w), in1=c_(S_MID), op0=ALU.subtract, op1=ALU.add),
                                [st_.b], [st_.b])
                        P.dve(CALL("tensor_scalar", out=mb_.t[:, 0:nk], in0=I_.t[:, 0:nk], scalar1=c_(S_THR), scalar2=NEG,
                                                        op0=ALU.is_lt, op1=ALU.mult), [I_.b, st_.b], [mb_.b])

                    def stage_c(i):
                        sl = i % 2
                        q0 = i * 128
                        qa_, mb_ = qab[sl], mbs[sl]
                        for kvh in range(2):
                            kk = kA if kvh == 0 else kB
                            for c in range(i + 1):
                                k0 = c * 128
                                ps_ = pss_[nx2("s", 2)]
                                P.pe(CALL("matmul",
                                    ps_.t[:, :], lhsT=kk.t[:, k0:k0 + 128], rhs=qa_.t[:, :, :], start=True, stop=False),
                                    [kk.b, qa_.b], [ps_.b])
                                P.pe(CALL("matmul",
                                    ps_.t[:, :], lhsT=mb_.t[:, k0:k0 + 128], rhs=i4_bf.t[:, :, :], start=False, stop=True),
                                    [mb_.b, i4_bf.b], [ps_.b])
                                pt_ = ptl[nx2("p", 4)]
                                P.act(CALL("activation", out=pt_.t[:, :], in_=ps_.t[:, :], func=AF.Exp,
                                                                                scale=0.125), [ps_.b], [pt_.b])
                                P.pe(CALL("matmul",
                                    pso.t[0:65, :], lhsT=vaug.t[:, c, kvh, :], rhs=pt_.t[:, :], start=(c == 0), stop=(c == i)),
                                    [vaug.b, pt_.b], [pso.b])
                            P.dve(CALL("reciprocal", out=rcp.t[:, :], in_=pso.t[0:1, :]), [pso.b], [rcp.b])
                            P.pe(CALL("matmul", psb.t[0:65, :], lhsT=cst.t[0:1, C_LE:C_LE + 65], rhs=rcp.t[:, :], start=True,
                                                    stop=True), [cst.b, rcp.b], [psb.b])
                            P.act(CALL("activation", out=bcs.t[:, :], in_=psb.t[0:65, :], func=AF.Copy), [psb.b], [bcs.b])
                            as_ = ast[kvh]
                            P.dve(CALL("tensor_tensor",
                                out=as_.t[:, :, :], in0=pso.t[0:65, :].rearrange("p (j q) -> p j q", j=4),
                                in1=bcs.t[:, :].rearrange("p (j q) -> p j q", j=4), op=ALU.mult), [pso.b, bcs.b], [as_.b])
                            for j in range(4):
                                kc = 2 * kvh + j // 2
                                r0 = (j % 2) * 64
                                P.dma("sp", CALL("dma_start",
                                    out=attnT_s[kc, r0:r0 + 64, q0:q0 + 128], in_=as_.t[1:65, j, :]),
                                    [as_.b], dbs("attnT", q0, 128, kc))

                    order = []
                    for i in range(NBLK):
                        stage_a(i)
                        if i >= 1:
                            stage_c(i - 1)
                        stage_b(i)
                    stage_c(NBLK - 1)
                    P.flush()
            if "5" in phases:
                s2b = ExitStack()
                with s2b:
                    ROUNDS_S = 24
                    NKP = NPAGES * 128
                    SEGW = NKP // 4
                    IW = SEGW + 8
                    R = SB(s2b, "R2b", [128, 3 * NKP], BF16)
                    BR0, BR1, BR2 = Buf("BR0"), Buf("BR1"), Buf("BR2")
                    O1, O2 = NKP, 2 * NKP
                    Is = SB(s2b, "Is", [128, IW], F32)
                    MBs = SB(s2b, "MBs", [128, IW], BF16)
                    pti = SB(s2b, "pti", [128, 1], I32)
                    ptf = SB(s2b, "ptf", [128, 1], F32)
                    idf = SB(s2b, "idf", [128, 16], F32)
                    idx = SB(s2b, "idx", [128, 16], I32)
                    qis = SB(s2b, "qis", [64, 8, NS], BF16)
                    qs0 = SB(s2b, "qs0", [128, 4, NS], BF16)
                    qs1 = SB(s2b, "qs1", [128, 4, NS], BF16)
                    wis = SB(s2b, "wis", [8, NB_S, 8], F32)
                    dss = SB(s2b, "dss", [8, NB_S, 8, 8], BF16)
                    sel = SB(s2b, "sel", [128, 16, 4, 8], BF16)
                    rs2 = [SB(s2b, "rs2_%d" % i, [8, 512], BF16) for i in range(4)]
                    sg2 = [SB(s2b, "sg2_%d" % i, [8, 512], F32) for i in range(4)]
                    pt2 = [SB(s2b, "pt2_%d" % i, [128, 32], BF16) for i in range(4)]
                    sst = SB(s2b, "sst", [128, 64], F32)
                    vnew = SB(s2b, "vnew", [8, 128], BF16)
                    rcp2 = SB(s2b, "rcp2", [1, 32], F32)
                    bcs2 = SB(s2b, "bcs2", [64, 32], F32)
                    as2 = SB(s2b, "as2", [64, 4, 8], BF16)
                    pst = [PSB(s2b, "pst%d" % i, [128, 1024], BF16) for i in range(2)]
                    psd2 = [PSB(s2b, "psd2_%d" % i, [128, 512]) for i in range(2)]
                    psi2 = PSB(s2b, "psi2", [128, 512])
                    pso2 = PSB(s2b, "pso2", [128, 512])
                    psm2 = PSB(s2b, "psm2", [128, 512])
                    rr3 = {"t": 0, "d": 0, "r": 0, "g": 0, "p": 0, "e": 0}

                    def nx3(k, n):
                        v = rr3[k]
                        rr3[k] = (v + 1) % n
                        return v

                    def evac3(out_ap, in_ap, reads, writes):
                        if nx3("e", 2) == 0:
                            P.act(CALL("activation", out=out_ap, in_=in_ap, func=AF.Copy), reads, writes)
                        else:
                            P.dve(CALL("tensor_copy", out=out_ap, in_=in_ap), reads, writes)

                    S0 = NTP
                    WSC = (8.0 ** -0.5) / 8.0
                    G_f = cst.t[:, C_G:C_G + 128]
                    P.pool(CALL("memset", qs0.t[:, :, :], 0.0), [], [qs0.b])
                    P.pool(CALL("memset", qs1.t[:, :, :], 0.0), [], [qs1.b])
                    P.pool(CALL("memset", Is.t[:, SEGW:IW], -BIG), [], [Is.b])
                    P.dma("sp", CALL("dma_start", out=qs0.t[0:64, :, :], in_=qaT_s[:, 0:64, S0:S0 + NS].rearrange("j p t -> p j t")),
                          dbs("qaT", S0, NS), [qs0.b])
                    P.dma("sp", CALL("dma_start", out=qs1.t[64:128, :, :], in_=qaT_s[:, 64:128, S0:S0 + NS].rearrange("j p t -> p j t")),
                          dbs("qaT", S0, NS), [qs1.b])
                    for j in range(4):
                        for par in range(2):
                            P.dma("sp", CALL("dma_start", out=qis.t[:, 2 * j + par, :], in_=qiT_s[j, 64 * par:64 * par + 64, S0:S0 + NS]),
                                  dbs("qiT", S0, NS), [qis.b])
                    P.dma("sp", CALL("dma_start", out=wis.t[:, :, :], in_=wi_s[S0:S0 + NS, :].rearrange("(b q) h -> q b h", q=8)),
                          dbs("wi", S0, NS), [wis.b])
                    for b in range(NB_S):
                        for h in range(8):
                            P.dve(CALL("tensor_scalar", out=dss.t[:, b, h, :], in0=cst.t[0:8, C_ID:C_ID + 8], scalar1=wis.t[:, b, h:h + 1],
                                       scalar2=WSC, op0=ALU.mult, op1=ALU.mult), [wis.b, cst.b], [dss.b])
                    for bs in range(16):
                        P.dve(CALL("tensor_copy", out=sel.t[:, bs, :, :],
                                   in_=cst.t[:, C_ID + bs * 8:C_ID + bs * 8 + 8].unsqueeze(1).broadcast_to([128, 4, 8])), [cst.b], [sel.b])

                    def page_idx(b):
                        P.dma("sp", CALL("dma_start", out=pti.t[:, :], in_=pt_d[b:b + 1, :].rearrange("o p -> p o")), [], [pti.b])
                        P.dve(CALL("tensor_copy", out=ptf.t[:, :], in_=pti.t[:, :]), [pti.b], [ptf.b])
                        for o in range(4):
                            P.dve(CALL("tensor_scalar", out=idf.t[:, o:o + 1], in0=ptf.t[:, :], scalar1=4.0, scalar2=float(o),
                                       op0=ALU.mult, op1=ALU.add), [ptf.b], [idf.b])
                        for o in range(8):
                            P.dve(CALL("tensor_scalar", out=idf.t[:, 4 + o:5 + o], in0=ptf.t[:, :], scalar1=8.0, scalar2=float(o),
                                       op0=ALU.mult, op1=ALU.add), [ptf.b], [idf.b])
                        P.dve(CALL("tensor_copy", out=idx.t[:, 0:12], in_=idf.t[:, 0:12]), [idf.b], [idx.b])

                    def gather(src_d, n_o, icol0, dst0, bufR):
                        for o in range(n_o):
                            P.dma("pool", CALL("indirect_dma_start", out=R.t[:, dst0 + o * 2048:dst0 + (o + 1) * 2048], out_offset=None,
                                               in_=src_d[:, :],
                                               in_offset=bass.IndirectOffsetOnAxis(ap=idx.t[:, icol0 + o:icol0 + o + 1], axis=0)),
                                  [idx.b], [bufR])

                    for b in range(NB_S):
                        page_idx(b)
                        gather(cki_d, 4, 0, 0, BR0)
                        for g8 in range(16):
                            pt_ = pst[nx3("t", 2)]
                            for k in range(8):
                                t = g8 * 8 + k
                                P.pe(CALL("transpose", pt_.t[0:64, k * 128:(k + 1) * 128], R.t[:, t * 64:(t + 1) * 64], ident_bf.t[:, :]),
                                     [BR0, ident_bf.b], [pt_.b])
                            evac3(R.t[0:64, O1 + g8 * 1024:O1 + (g8 + 1) * 1024], pt_.t[0:64, :], [pt_.b], [BR1])
                        for c in range(NKP // 512 + 1):
                            new = (c == NKP // 512)
                            Wc = 8 if new else 512
                            for h in range(8):
                                pd = psd2[nx3("d", 2)]
                                rhs = kiA.t[0:64, S0 + 8 * b:S0 + 8 * b + 8] if new else R.t[0:64, O1 + c * 512:O1 + (c + 1) * 512]
                                P.pe(CALL("matmul", pd.t[0:8, 0:Wc], lhsT=qis.t[:, h, 8 * b:8 * b + 8], rhs=rhs, start=True, stop=True),
                                     [qis.b, kiA.b if new else BR1], [pd.b])
                                r_ = rs2[nx3("r", 4)]
                                P.act(CALL("activation", out=r_.t[:, 0:Wc], in_=pd.t[0:8, 0:Wc], func=AF.Relu), [pd.b], [r_.b])
                                P.pe(CALL("matmul", psi2.t[0:8, 0:Wc], lhsT=dss.t[:, b, h, :], rhs=r_.t[:, 0:Wc], start=(h == 0), stop=(h == 7)),
                                     [dss.b, r_.b], [psi2.b])
                            sg_ = sg2[nx3("g", 4)]
                            if new:
                                P.dve(CALL("tensor_tensor", out=sg_.t[:, 0:8], in0=psi2.t[0:8, 0:8], in1=cst.t[0:8, C_CN:C_CN + 8], op=ALU.add),
                                      [psi2.b, cst.b], [sg_.b])
                                P.dma("sp", CALL("dma_start", out=Is.t[b * 32:b * 32 + 8, SEGW:IW], in_=sg_.t[:, 0:8]), [sg_.b], [Is.b])
                            else:
                                P.dve(CALL("tensor_copy", out=sg_.t[:, :], in_=psi2.t[0:8, :]), [psi2.b], [sg_.b])
                                seg, cc = c // (SEGW // 512), c % (SEGW // 512)
                                r0 = b * 32 + seg * 8
                                P.dma("sp", CALL("dma_start", out=Is.t[r0:r0 + 8, cc * 512:(cc + 1) * 512], in_=sg_.t[:, :]), [sg_.b], [Is.b])

                    c2 = lambda k: sst.t[:, k:k + 1]
                    Q_MAX, Q_MIN, Q_A, Q_HI, Q_LO, Q_W0, Q_MID, Q_CNT, Q_U, Q_THR, Q_WALL = 0, 1, 2, 3, 4, 5, 6, 7, 8, 9, 16
                    P.dve(CALL("tensor_reduce", out=c2(Q_MAX), in_=Is.t[:, 0:IW], axis=AX.X, op=ALU.max), [Is.b], [sst.b])
                    P.dve(CALL("tensor_reduce", out=c2(Q_MIN), in_=Is.t[:, 0:SEGW], axis=AX.X, op=ALU.min), [Is.b], [sst.b])
                    P.dve(CALL("tensor_scalar", out=c2(Q_MIN), in0=c2(Q_MIN), scalar1=-1.0, scalar2=None, op0=ALU.mult), [sst.b], [sst.b])
                    P.dve(CALL("tensor_tensor", out=c2(Q_A), in0=c2(Q_MAX), in1=c2(Q_MIN), op=ALU.max), [sst.b], [sst.b])
                    P.pe(CALL("matmul", psi2.t[:, 0:1], lhsT=G_f, rhs=c2(Q_A), start=True, stop=True), [sst.b, cst.b], [psi2.b])
                    P.dve(CALL("tensor_scalar", out=c2(Q_HI), in0=psi2.t[:, 0:1], scalar1=1.001, scalar2=1e-3, op0=ALU.mult, op1=ALU.add),
                          [psi2.b], [sst.b])
                    P.dve(CALL("tensor_scalar", out=c2(Q_LO), in0=c2(Q_HI), scalar1=-1.0, scalar2=None, op0=ALU.mult), [sst.b], [sst.b])
                    P.dve(CALL("tensor_scalar", out=c2(Q_W0), in0=c2(Q_HI), scalar1=2.0, scalar2=None, op0=ALU.mult), [sst.b], [sst.b])
                    P.dve(CALL("tensor_scalar", out=sst.t[:, Q_WALL:Q_WALL + 32], in0=cst.t[:, C_PW:C_PW + 32], scalar1=c2(Q_W0), scalar2=None,
                               op0=ALU.mult), [sst.b, cst.b], [sst.b])
                    P.dve(CALL("tensor_tensor", out=c2(Q_MID), in0=c2(Q_LO), in1=c2(Q_WALL), op=ALU.add), [sst.b], [sst.b])
                    for r in range(ROUNDS_S):
                        P.dve(CALL("tensor_scalar", out=MBs.t[:, :], in0=Is.t[:, :], scalar1=c2(Q_MID), scalar2=0.0, op0=ALU.is_ge,
                                   op1=ALU.add, accum_out=c2(Q_CNT)), [Is.b, sst.b], [MBs.b, sst.b])
                        P.pe(CALL("matmul", psi2.t[:, 0:1], lhsT=G_f, rhs=c2(Q_CNT), start=True, stop=True), [sst.b, cst.b], [psi2.b])
                        P.dve(CALL("tensor_scalar", out=c2(Q_U), in0=psi2.t[:, 0:1], scalar1=TOPK_S - 0.5, scalar2=c2(Q_WALL + r),
                                   op0=ALU.is_ge, op1=ALU.mult), [psi2.b, sst.b], [sst.b])
                        nxt_w = Q_WALL + r + 1 if r + 1 < ROUNDS_S else Q_WALL + r
                        dst = Q_MID if r + 1 < ROUNDS_S else Q_THR
                        P.dve(CALL("scalar_tensor_tensor", out=c2(dst), in0=c2(Q_U), scalar=c2(nxt_w), in1=c2(Q_MID), op0=ALU.subtract,
                                   op1=ALU.add), [sst.b], [sst.b])
                    P.dve(CALL("tensor_scalar", out=MBs.t[:, :], in0=Is.t[:, :], scalar1=c2(Q_THR), scalar2=NEG, op0=ALU.is_lt, op1=ALU.mult),
                          [Is.b, sst.b], [MBs.b])

                    for b in range(NB_S):
                        page_idx(b)
                        gather(ck_d, 8, 4, 0, BR0)
                        gather(cv_d, 8, 4, O2, BR2)
                        P.dma("pool", CALL("dma_start", out=vnew.t[:, :], in_=vo_d[S0 + 8 * b:S0 + 8 * b + 8, :]), dbs("vo", S0, NS), [vnew.b])
                        for g8 in range(16):
                            pt_ = pst[nx3("t", 2)]
                            for k in range(8):
                                t = g8 * 8 + k
                                P.pe(CALL("transpose", pt_.t[:, k * 128:(k + 1) * 128], R.t[:, t * 128:(t + 1) * 128], ident_bf.t[:, :]),
                                     [BR0, ident_bf.b], [pt_.b])
                            evac3(R.t[:, O1 + g8 * 1024:O1 + (g8 + 1) * 1024], pt_.t[:, :], [pt_.b], [BR1])
                        for kvh in range(2):
                            qs = qs0 if kvh == 0 else qs1
                            kk = kA if kvh == 0 else kB
                            for t in range(129):
                                new = (t == 128)
                                nkk = 8 if new else 128
                                ps_ = psd2[nx3("d", 2)]
                                if new:
                                    l1 = kk.t[:, S0 + 8 * b:S0 + 8 * b + 8]
                                    l2 = MBs.t[:, SEGW:IW]
                                    sl_ = sel.t[:, b * 4, :, :]
                                    rd1 = kk.b
                                else:
                                    l1 = R.t[:, O1 + t * 128:O1 + (t + 1) * 128]
                                    seg, cc = t // 32, t % 32
                                    l2 = MBs.t[:, cc * 128:(cc + 1) * 128]
                                    sl_ = sel.t[:, b * 4 + seg, :, :]
                                    rd1 = BR1
                                P.pe(CALL("matmul", ps_.t[0:nkk, 0:32], lhsT=l1, rhs=qs.t[:, :, 8 * b:8 * b + 8], start=True, stop=False),
                                     [rd1, qs.b], [ps_.b])
                                P.pe(CALL("matmul", ps_.t[0:nkk, 0:32], lhsT=l2, rhs=sl_, start=False, stop=True), [MBs.b, sel.b], [ps_.b])
                                p_ = pt2[nx3("p", 4)]
                                P.act(CALL("activation", out=p_.t[0:nkk, :], in_=ps_.t[0:nkk, 0:32], func=AF.Exp, scale=0.125), [ps_.b], [p_.b])
                                if new:
                                    lv = vnew.t[0:8, kvh * 64:(kvh + 1) * 64]
                                    rdv = vnew.b
                                else:
                                    lv = R.t[:, O2 + t * 128 + kvh * 64:O2 + t * 128 + kvh * 64 + 64]
                                    rdv = BR2
                                P.pe(CALL("matmul", pso2.t[0:64, 0:32], lhsT=lv, rhs=p_.t[0:nkk, :], start=(t == 0), stop=new),
                                     [rdv, p_.b], [pso2.b])
                                P.pe(CALL("matmul", psm2.t[0:1, 0:32], lhsT=ones_bf.t[0:nkk, 0:1], rhs=p_.t[0:nkk, :], start=(t == 0), stop=new),
                                     [ones_bf.b, p_.b], [psm2.b])
                            P.dve(CALL("reciprocal", out=rcp2.t[:, :], in_=psm2.t[0:1, 0:32]), [psm2.b], [rcp2.b])
                            pb_ = psd2[nx3("d", 2)]
                            P.pe(CALL("matmul", pb_.t[0:64, 0:32], lhsT=cst.t[0:1, C_LE:C_LE + 64], rhs=rcp2.t[:, :], start=True, stop=True),
                                 [cst.b, rcp2.b], [pb_.b])
                            P.act(CALL("activation", out=bcs2.t[:, :], in_=pb_.t[0:64, 0:32], func=AF.Copy), [pb_.b], [bcs2.b])
                            P.dve(CALL("tensor_tensor", out=as2.t[:, :, :], in0=pso2.t[0:64, 0:32].rearrange("p (j q) -> p j q", j=4),
                                       in1=bcs2.t[:, :].rearrange("p (j q) -> p j q", j=4), op=ALU.mult), [pso2.b, bcs2.b], [as2.b])
                            for j in range(4):
                                kc = 2 * kvh + j // 2
                                r0 = (j % 2) * 64
                                P.dma("sp", CALL("dma_start", out=attnT_s[kc, r0:r0 + 64, S0 + 8 * b:S0 + 8 * b + 8], in_=as2.t[:, j, :]),
                                      [as2.b], dbs("attnT", S0, NS, kc))
                    P.flush()
        s3w = ExitStack()
        with s3w:
            wao = SB(s3w, "wao", [128, 4, D], BF16)
            wgo = SB(s3w, "wgo", [128, 8, D], BF16)
            wo = SB(s3w, "wo", [128, 8, D], BF16)
            if "4" in phases:
                for (wt, wd) in ((wao, wao_d), (wgo, wgo_d), (wo, wo_d)):
                    P.dma("pool", CALL("dma_start",
                        out=wt.t[:, :, :], in_=wd.rearrange("(kc p) n -> p kc n", p=128)), [], [wt.b])
                for n in ln_d:
                    P.dma("sp", CALL("dma_start", out=lnb[n].t[:, :], in_=ln_d[n][0:1, :].broadcast_to([128, D])),
                          [], [lnb[n].b])
            if "3" in phases:
                s3 = ExitStack()
                with s3:
                    aug = SB(s3, "aug", [17, 128], BF16)
                    wal = SB(s3, "wal", [17, 512], BF16)
                    gnb = SB(s3, "gnb", [128, 256], F32)
                    qbb = SB(s3, "qbb", [128, 4, 128], BF16)
                    kbb = SB(s3, "kbb", [128, 4, 128], BF16)
                    kbt = SB(s3, "kbt", [128, 512], BF16)
                    vbt = SB(s3, "vbt", [128, 1024], BF16)
                    gbt = SB(s3, "gbt", [128, 1024], F32)
                    ee = SB(s3, "ee", [128, 512], F32)
                    la = SB(s3, "la", [128, 512], F32)
                    Eq = SB(s3, "Eq", [128, 4, 128], F32)
                    Ek = SB(s3, "Ek", [128, 4, 128], F32)
                    Er = SB(s3, "Er", [128, 512], F32)
                    qt = SB(s3, "qt", [128, 4, 128], BF16)
                    kt = SB(s3, "kt", [128, 4, 128], BF16)
                    kp = SB(s3, "kp", [128, 512], BF16)
                    attm = SB(s3, "attm", [128, 4, 128], BF16)
                    Sf = SB(s3, "Sf", [128, 4, 256], F32)
                    Sb = SB(s3, "Sb", [128, 4, 256], BF16)
                    onr = SB(s3, "onr", [128, 1024], F32)
                    sgt = SB(s3, "sgt", [128, 1024], F32)
                    obb = SB(s3, "obb", [128, 1024], BF16)
                    obT = SB(s3, "obT", [128, 8, 128], BF16)
                    gst = SB(s3, "gst", [128, 16], F32)
                    jk3 = SB(s3, "jk3", [128, 256], BF16)
                    psA = PSB(s3, "psA", [128, 512])
                    psC = PSB(s3, "psC", [128, 512])
                    psT_ = PSB(s3, "psT", [128, 512])
                    psO = PSB(s3, "psO", [128, 1024])
                    psS = PSB(s3, "psS", [128, 1024])
                    psX = PSB(s3, "psX", [128, 1024], BF16)

                    P.pool(CALL("memset", aug.t[:, :], 1.0), [], [aug.b])
                    P.dma("pool", CALL("dma_start", out=wal.t[1:17, :], in_=wal_d[:, :]), [], [wal.b])
                    P.dma("pool", CALL("dma_start", out=wal.t[0:1, :], in_=bal_d[:, :]), [], [wal.b])
                    P.dma("sp", CALL("dma_start", out=gnb.t[:, :], in_=gng_d[0:1, :].broadcast_to([128, 256])), [], [gnb.b])

                    def gla_chunk(tok0, C):
                        P.dma("sp", CALL("dma_start", out=aug.t[1:17, 0:C], in_=abT_s[:, tok0:tok0 + C]), dbs("abT", tok0, C), [aug.b])
                        P.dma("sp", CALL("dma_start", out=qbb.t[:, :, 0:C], in_=qbT_s[:, :, tok0:tok0 + C].rearrange("j p t -> p j t")),
                              dbs("qbT", tok0, C), [qbb.b])
                        P.dma("sp", CALL("dma_start", out=kbb.t[:, :, 0:C], in_=kbT_s[:, :, tok0:tok0 + C].rearrange("j p t -> p j t")),
                              dbs("kbT", tok0, C), [kbb.b])
                        P.dma("sp", CALL("dma_start", out=kbt.t[0:C, :], in_=kb_s[tok0:tok0 + C, :]), dbs("kb", tok0, C), [kbt.b])
                        P.dma("sp", CALL("dma_start", out=vbt.t[0:C, :], in_=vb_s[tok0:tok0 + C, :]), dbs("vb", tok0, C), [vbt.b])
                        P.dma("sp", CALL("dma_start", out=gbt.t[0:C, :], in_=gb_s[tok0:tok0 + C, :]), dbs("gb", tok0, C), [gbt.b])
                        P.pe(CALL("matmul", psA.t[0:C, :], lhsT=aug.t[:, 0:C], rhs=wal.t[:, :], start=True, stop=True),
                             [aug.b, wal.b], [psA.b])
                        P.act(CALL("activation", out=ee.t[0:C, :], in_=psA.t[0:C, :], func=AF.Exp, scale=-1.0), [psA.b], [ee.b])
                        P.act(CALL("activation", out=la.t[0:C, :], in_=ee.t[0:C, :], func=AF.Ln, bias=one_c[0:C, :], scale=1.0),
                              [ee.b, cst.b], [la.b])
                        P.pe(CALL("matmul", psA.t[0:C, :], lhsT=ltri_f[0:C, 0:C], rhs=la.t[0:C, :], start=True, stop=True),
                             [la.b, cst.b], [psA.b])
                        for h in range(4):
                            P.pe(CALL("matmul", psC.t[:, h * 128:h * 128 + C], lhsT=la.t[0:C, h * 128:(h + 1) * 128],
                                                         rhs=utri_f[0:C, 0:C], start=True, stop=True), [la.b, cst.b], [psC.b])
                        psC3 = psC.t[:, :].rearrange("p (h t) -> p h t", h=4)
                        P.act(CALL("activation", out=Er.t[0:C, :], in_=psA.t[0:C, :], func=AF.Exp), [psA.b], [Er.b])
                        P.act(CALL("activation", out=Eq.t[:, :, 0:C], in_=psC3[:, :, 0:C], func=AF.Exp), [psC.b], [Eq.b])
                        P.act(CALL("activation", out=Ek.t[:, :, 0:C], in_=psC3[:, :, 0:C], func=AF.Exp, scale=-1.0), [psC.b], [Ek.b])
                        P.dve(CALL("scalar_tensor_tensor", out=qt.t[:, :, 0:C], in0=qbb.t[:, :, 0:C], scalar=128.0 ** -0.5,
                                                               in1=Eq.t[:, :, 0:C], op0=ALU.mult, op1=ALU.mult), [qbb.b, Eq.b], [qt.b])
                        P.dve(CALL("tensor_tensor", out=kt.t[:, :, 0:C], in0=kbb.t[:, :, 0:C], in1=Ek.t[:, :, 0:C], op=ALU.mult),
                              [kbb.b, Ek.b], [kt.b])
                        P.dve(CALL("tensor_tensor", out=kp.t[0:C, :], in0=kbt.t[0:C, :], in1=Er.t[0:C, :], op=ALU.mult),
                              [kbt.b, Er.b], [kp.b])
                        for h in range(4):
                            P.pe(CALL("matmul", psT_.t[0:C, h * 128:h * 128 + C], lhsT=kt.t[:, h, 0:C], rhs=qt.t[:, h, 0:C],
                                                         start=True, stop=True), [kt.b, qt.b], [psT_.b])
                        psT3 = psT_.t[:, :].rearrange("p (h t) -> p h t", h=4)
                        P.dve(CALL("tensor_tensor", out=attm.t[0:C, :, 0:C], in0=psT3[0:C, :, 0:C],
                                                        in1=cst.t[0:C, C_LE:C_LE + C].unsqueeze(1).broadcast_to([C, 4, C]), op=ALU.mult),
                              [psT_.b, cst.b], [attm.b])
                        for h in range(4):
                            P.pe(CALL("matmul", psO.t[0:C, h * 256:(h + 1) * 256], lhsT=attm.t[0:C, h, 0:C],
                                                         rhs=vbt.t[0:C, h * 256:(h + 1) * 256], start=True, stop=False),
                                 [attm.b, vbt.b], [psO.b])
                            P.pe(CALL("matmul", psO.t[0:C, h * 256:(h + 1) * 256], lhsT=qt.t[:, h, 0:C], rhs=Sb.t[:, h, :],
                                                         start=False, stop=True), [qt.b, Sb.b], [psO.b])
                        for h in range(4):
                            P.pe(CALL("matmul", psS.t[:, h * 256:(h + 1) * 256], lhsT=kp.t[0:C, h * 128:(h + 1) * 128],
                                                         rhs=vbt.t[0:C, h * 256:(h + 1) * 256], start=True, stop=True),
                                 [kp.b, vbt.b], [psS.b])
                        for h in range(4):
                            P.dve(CALL("scalar_tensor_tensor", out=Sf.t[:, h, :], in0=Sf.t[:, h, :], scalar=Eq.t[:, h, C - 1:C],
                                                                        in1=psS.t[:, h * 256:(h + 1) * 256], op0=ALU.mult, op1=ALU.add),
                                  [Sf.b, Eq.b, psS.b, Sb.b], [Sf.b])
                        P.act(CALL("activation", out=Sb.t[:, :, :], in_=Sf.t[:, :, :], func=AF.Copy), [Sf.b, psO.b], [Sb.b])
                        for h in range(4):
                            P.act(CALL("activation", out=jk3.t[0:C, :], in_=psO.t[0:C, h * 256:(h + 1) * 256], func=AF.Square,
                                                              accum_out=gst.t[0:C, h:h + 1]), [psO.b], [jk3.b, gst.b])
                        P.act(CALL("activation", out=gst.t[0:C, 4:8], in_=gst.t[0:C, 0:4], func=AF.Ln, bias=eps_c[0:C, :],
                                                     scale=1.0 / 256), [gst.b, cst.b], [gst.b])
                        P.act(CALL("activation", out=gst.t[0:C, 8:12], in_=gst.t[0:C, 4:8], func=AF.Exp, scale=-0.5), [gst.b], [gst.b])
                        for h in range(4):
                            P.dve(CALL("scalar_tensor_tensor", out=onr.t[0:C, h * 256:(h + 1) * 256],
                                                                        in0=psO.t[0:C, h * 256:(h + 1) * 256],
                                                                        scalar=gst.t[0:C, 8 + h:9 + h], in1=gnb.t[0:C, :],
                                                                        op0=ALU.mult, op1=ALU.mult), [psO.b, gst.b, gnb.b], [onr.b])
                        P.act(CALL("activation", out=sgt.t[0:C, :], in_=gbt.t[0:C, :], func=AF.Silu), [gbt.b], [sgt.b])
                        P.dve(CALL("tensor_tensor", out=obb.t[0:C, :], in0=onr.t[0:C, :], in1=sgt.t[0:C, :], op=ALU.mult),
                              [onr.b, sgt.b], [obb.b])
                        for kc in range(8):
                            P.pe(CALL("transpose", psX.t[:, kc * 128:kc * 128 + C], obb.t[0:C, kc * 128:(kc + 1) * 128],
                                                              ident_bf.t[0:C, 0:C]), [obb.b, ident_bf.b], [psX.b])
                        P.dve(CALL("tensor_copy", out=obT.t[:, :, 0:C],
                                                      in_=psX.t[:, :].rearrange("p (k t) -> p k t", k=8)[:, :, 0:C]), [psX.b], [obT.b])
                        P.dma("sp", CALL("dma_start", out=obT_s[:, :, tok0:tok0 + C].rearrange("k p t -> p k t"),
                                                          in_=obT.t[:, :, 0:C]), [obT.b], dbs("obT", tok0, C))

                    P.pool(CALL("memset", Sf.t[:, :, :], 0.0), [], [Sf.b])
                    P.pool(CALL("memset", Sb.t[:, :, :], 0.0), [], [Sb.b])
                    for i in range(NBLK):
                        gla_chunk(i * 128, 128)
                    P.dma("sp", CALL("dma_start", out=glap_d.rearrange("h d v -> d h v"), in_=Sf.t[:, :, :]), [Sf.b], [B_gla_out])
                    for b in range(NB_S):
                        P.dma("sp", CALL("dma_start", out=Sf.t[:, :, :], in_=st_d[b].rearrange("h d v -> d h v")),
                              [B_gla_out], [Sf.b])
                        P.act(CALL("activation", out=Sb.t[:, :, :], in_=Sf.t[:, :, :], func=AF.Copy), [Sf.b], [Sb.b])
                        gla_chunk(NTP + 8 * b, 8)
                        P.dma("sp", CALL("dma_start", out=glas_d[b].rearrange("h d v -> d h v"), in_=Sf.t[:, :, :]),
                              [Sf.b], [B_gla_out])
                    P.flush()

            def layer_norm(stack_tiles, y1, T, gname, bname, out_t, stt_):
                jk, = stack_tiles
                P.act(CALL("activation", out=jk.t[0:T, :], in_=y1.t[0:T, :], func=AF.Copy, accum_out=stt_.t[0:T, 0:1]),
                      [y1.b], [jk.b, stt_.b])
                P.act(CALL("activation", out=jk.t[0:T, :], in_=y1.t[0:T, :], func=AF.Square, accum_out=stt_.t[0:T, 1:2]),
                      [y1.b, jk.b], [jk.b, stt_.b])
                P.dve(CALL("tensor_scalar", out=stt_.t[0:T, 2:3], in0=stt_.t[0:T, 0:1], scalar1=1.0 / D, scalar2=None,
                                                op0=ALU.mult), [stt_.b], [stt_.b])
                P.dve(CALL("tensor_tensor", out=stt_.t[0:T, 3:4], in0=stt_.t[0:T, 2:3], in1=stt_.t[0:T, 2:3], op=ALU.mult),
                      [stt_.b], [stt_.b])
                P.dve(CALL("scalar_tensor_tensor", out=stt_.t[0:T, 4:5], in0=stt_.t[0:T, 1:2], scalar=1.0 / D,
                                                       in1=stt_.t[0:T, 3:4], op0=ALU.mult, op1=ALU.subtract), [stt_.b], [stt_.b])
                P.act(CALL("activation", out=stt_.t[0:T, 5:6], in_=stt_.t[0:T, 4:5], func=AF.Ln, bias=eps_c[0:T, :], scale=1.0),
                      [stt_.b, cst.b], [stt_.b])
                P.act(CALL("activation", out=stt_.t[0:T, 6:7], in_=stt_.t[0:T, 5:6], func=AF.Exp, scale=-0.5), [stt_.b], [stt_.b])
                P.dve(CALL("tensor_scalar", out=out_t.t[0:T, :], in0=y1.t[0:T, :], scalar1=stt_.t[0:T, 2:3],
                                                scalar2=stt_.t[0:T, 6:7], op0=ALU.subtract, op1=ALU.mult), [y1.b, stt_.b], [out_t.b])
                P.pool(CALL("tensor_tensor", out=out_t.t[0:T, :], in0=out_t.t[0:T, :], in1=lnb[gname].t[0:T, :], op=ALU.mult),
                       [out_t.b, lnb[gname].b], [out_t.b])
                P.pool(CALL("tensor_tensor", out=out_t.t[0:T, :], in0=out_t.t[0:T, :], in1=lnb[bname].t[0:T, :], op=ALU.add),
                       [out_t.b, lnb[bname].b], [out_t.b])

            tiles4 = [(t0, 128) for t0 in range(0, NTP, 128)] + ([(NTP, NS)] if "x" not in phases else [])
            if "4" in phases:
                s4 = ExitStack()
                with s4:
                    atb = [SB(s4, "atb%d" % i, [128, 4, 128], BF16) for i in range(2)]
                    obl = [SB(s4, "obl%d" % i, [128, 8, 128], BF16) for i in range(2)]
                    gtl = [SB(s4, "gtl%d" % i, [128, 16, 128], F32) for i in range(2)]
                    xbl = [SB(s4, "xbl%d" % i, [128, D], F32) for i in range(2)]
                    sig = SB(s4, "sig", [128, 16, 128], F32)
                    t1 = SB(s4, "t1", [128, 8, 128], F32)
                    mrg = SB(s4, "mrg", [128, 8, 128], BF16)
                    y1 = SB(s4, "y1", [128, D], F32)
                    hh = SB(s4, "hh", [128, D], F32)
                    hb = SB(s4, "hb", [128, D], BF16)
                    hTt = SB(s4, "hTt", [128, 8, 128], BF16)
                    jk4 = SB(s4, "jk4", [128, D], BF16)
                    st4 = SB(s4, "st4", [128, 8], F32)
                    psa = PSB(s4, "psa", [128, 1024])
                    psb4 = PSB(s4, "psb4", [128, 1024])
                    psm = PSB(s4, "psm", [128, 1024])
                    psx = PSB(s4, "psx4", [128, 1024], BF16)

                    def load4(ti):
                        t0, T = tiles4[ti]
                        sl = ti % 2
                        P.dma("sp", CALL("dma_start", out=atb[sl].t[:, :, 0:T], in_=attnT_s[:, :, t0:t0 + T].rearrange("k p t -> p k t")),
                              dbs("attnT", t0, T), [atb[sl].b])
                        P.dma("sp", CALL("dma_start", out=obl[sl].t[:, :, 0:T], in_=obT_s[:, :, t0:t0 + T].rearrange("k p t -> p k t")),
                              dbs("obT", t0, T), [obl[sl].b])
                        P.dma("sp", CALL("dma_start", out=gtl[sl].t[:, :, 0:T], in_=gtT_s[:, :, t0:t0 + T].rearrange("k p t -> p k t")),
                              dbs("gtT", t0, T), [gtl[sl].b])
                        P.dma("sp", CALL("dma_start", out=xbl[sl].t[0:T, :], in_=x_d[t0:t0 + T, :]), [], [xbl[sl].b])

                    load4(0)
                    for ti, (t0, T) in enumerate(tiles4):
                        if ti + 1 < len(tiles4):
                            load4(ti + 1)
                        sl = ti % 2
                        at_, ob_, gt_, xb_ = atb[sl], obl[sl], gtl[sl], xbl[sl]
                        P.act(CALL("activation", out=sig.t[:, :, 0:T], in_=gt_.t[:, :, 0:T], func=AF.Sigmoid), [gt_.b], [sig.b])
                        psa3 = psa.t[:, :].rearrange("p (c t) -> p c t", c=8)
                        psb3 = psb4.t[:, :].rearrange("p (c t) -> p c t", c=8)
                        for c in range(8):
                            for kc in range(4):
                                P.pe(CALL("matmul", psa.t[:, c * 128:c * 128 + T], lhsT=wao.t[:, kc, c * 128:(c + 1) * 128],
                                                                    rhs=at_.t[:, kc, 0:T], start=(kc == 0), stop=(kc == 3)),
                                     [wao.b, at_.b], [psa.b])
                        for c in range(8):
                            for kc in range(8):
                                P.pe(CALL("matmul", psb4.t[:, c * 128:c * 128 + T], lhsT=wgo.t[:, kc, c * 128:(c + 1) * 128],
                                                                    rhs=ob_.t[:, kc, 0:T], start=(kc == 0), stop=(kc == 7)),
                                     [wgo.b, ob_.b], [psb4.b])
                        P.dve(CALL("tensor_tensor", out=t1.t[:, :, 0:T], in0=psa3[:, :, 0:T], in1=sig.t[:, 0:8, 0:T], op=ALU.mult),
                              [psa.b, sig.b], [t1.b])
                        P.dve(CALL("tensor_tensor", out=sig.t[:, 8:16, 0:T], in0=psb3[:, :, 0:T], in1=sig.t[:, 8:16, 0:T], op=ALU.mult),
                              [psb4.b, sig.b], [sig.b])
                        P.pool(CALL("tensor_tensor", out=mrg.t[:, :, 0:T], in0=t1.t[:, :, 0:T], in1=sig.t[:, 8:16, 0:T], op=ALU.add),
                               [t1.b, sig.b], [mrg.b])
                        for n in range(2):
                            for kc in range(8):
                                P.pe(CALL("matmul", psm.t[0:T, n * 512:(n + 1) * 512], lhsT=mrg.t[:, kc, 0:T],
                                                                    rhs=wo.t[:, kc, n * 512:(n + 1) * 512], start=(kc == 0), stop=(kc == 7)),
                                     [mrg.b, wo.b], [psm.b])
                        for n in range(2):
                            P.dve(CALL("scalar_tensor_tensor", out=y1.t[0:T, n * 512:(n + 1) * 512], in0=xb_.t[0:T, n * 512:(n + 1) * 512],
                                                                        scalar=ALPHA, in1=psm.t[0:T, n * 512:(n + 1) * 512], op0=ALU.mult,
                                                                        op1=ALU.add), [xb_.b, psm.b], [y1.b])
                        layer_norm((jk4,), y1, T, "ln1_g", "ln1_b", hh, st4)
                        P.dma("sp", CALL("dma_start", out=h_s[t0:t0 + T, :], in_=hh.t[0:T, :]), [hh.b], dbs("h", t0, T))
                        P.act(CALL("activation", out=hb.t[0:T, :], in_=hh.t[0:T, :], func=AF.Copy), [hh.b], [hb.b])
                        for kc in range(8):
                            P.pe(CALL("transpose", psx.t[:, kc * 128:kc * 128 + T], hb.t[0:T, kc * 128:(kc + 1) * 128],
                                                              ident_bf.t[0:T, 0:T]), [hb.b, ident_bf.b], [psx.b])
                        P.dve(CALL("tensor_copy", out=hTt.t[:, :, 0:T], in_=psx.t[:, :].rearrange("p (k t) -> p k t", k=8)[:, :, 0:T]),
                              [psx.b], [hTt.b])
                        P.dma("sp", CALL("dma_start", out=hT_s[:, :, t0:t0 + T].rearrange("k p t -> p k t"),
                                                                      in_=hTt.t[:, :, 0:T]), [hTt.b], dbs("hT", t0, T))
                    P.flush()

        if "4" in phases:
            s5 = ExitStack()
            with s5:
                wf1 = SB(s5, "wf1", [128, 8, 4096], BF16)
                wf2 = SB(s5, "wf2", [128, 32, D], BF16)
                wf1b = [Buf("wf1_%d" % i) for i in range(8)]
                wf2b = [Buf("wf2_%d" % i) for i in range(8)]
                for kc in range(8):
                    for c0 in (0, 2048):
                        P.dma("pool", CALL("dma_start", out=wf1.t[:, kc, c0:c0 + 2048],
                                                                          in_=wf1_d[kc * 128:(kc + 1) * 128, c0:c0 + 2048]),
                              [], [wf1b[kc]])
                for q in range(8):
                    P.dma("pool", CALL("dma_start", out=wf2.t[:, 4 * q:4 * q + 4, :],
                                                             in_=wf2_d[q * 512:(q + 1) * 512, :].rearrange("(kc p) n -> p kc n", p=128)),
                          [], [wf2b[q]])
                hTl = [SB(s5, "hTl%d" % i, [128, 8, 128], BF16) for i in range(2)]
                hl = [SB(s5, "hl%d" % i, [128, D], F32) for i in range(2)]
                rl = [SB(s5, "rl%d" % i, [128, 4, 128], BF16) for i in range(2)]
                hid = SB(s5, "hid", [128, 32, 128], BF16)
                y2 = SB(s5, "y2", [128, D], F32)
                yo = SB(s5, "yo", [128, D], F32)
                jk5 = SB(s5, "jk5", [128, D], BF16)
                st5 = SB(s5, "st5", [128, 8], F32)
                psf = [PSB(s5, "psf%d" % i, [128, 512]) for i in range(3)]
                psy = PSB(s5, "psy", [128, 1024])

                def load5(ti):
                    t0, T = tiles4[ti]
                    sl = ti % 2
                    P.dma("sp", CALL("dma_start", out=hTl[sl].t[:, :, 0:T], in_=hT_s[:, :, t0:t0 + T].rearrange("k p t -> p k t")),
                          dbs("hT", t0, T), [hTl[sl].b])
                    P.dma("sp", CALL("dma_start", out=hl[sl].t[0:T, :], in_=h_s[t0:t0 + T, :]), dbs("h", t0, T), [hl[sl].b])

                load5(0)
                fi = 0
                for ti, (t0, T) in enumerate(tiles4):
                    if ti + 1 < len(tiles4):
                        load5(ti + 1)
                    sl = ti % 2
                    hT_, h_ = hTl[sl], hl[sl]
                    for f4 in range(8):
                        pf = psf[fi % 3]
                        r_ = rl[fi % 2]
                        fi += 1
                        for cc in range(4):
                            f = f4 * 4 + cc
                            for kc in range(8):
                                P.pe(CALL("matmul",
                                    pf.t[:, cc * 128:cc * 128 + T], lhsT=wf1.t[:, kc, f * 128:(f + 1) * 128], rhs=hT_.t[:, kc, 0:T],
                                    start=(kc == 0), stop=(kc == 7)), [wf1b[kc], hT_.b], [pf.b])
                        pf3 = pf.t[:, :].rearrange("p (c t) -> p c t", c=4)
                        P.act(CALL("activation", out=r_.t[:, :, 0:T], in_=pf3[:, :, 0:T], func=AF.Relu),
                              [pf.b], [r_.b])
                        P.dve(CALL("tensor_tensor", out=hid.t[:, f4 * 4:f4 * 4 + 4, 0:T], in0=r_.t[:, :, 0:T],
                                                                      in1=r_.t[:, :, 0:T], op=ALU.mult), [r_.b], [hid.b])
                    for n in range(2):
                        for kc in range(32):
                            P.pe(CALL("matmul", psy.t[0:T, n * 512:(n + 1) * 512], lhsT=hid.t[:, kc, 0:T],
                                                                rhs=wf2.t[:, kc, n * 512:(n + 1) * 512], start=(kc == 0), stop=(kc == 31)),
                                 [hid.b, wf2b[kc // 4]], [psy.b])
                    for n in range(2):
                        P.dve(CALL("scalar_tensor_tensor", out=y2.t[0:T, n * 512:(n + 1) * 512], in0=h_.t[0:T, n * 512:(n + 1) * 512],
                                                                    scalar=ALPHA, in1=psy.t[0:T, n * 512:(n + 1) * 512], op0=ALU.mult,
                                                                    op1=ALU.add), [h_.b, psy.b], [y2.b])
                    layer_norm((jk5,), y2, T, "ln2_g", "ln2_b", yo, st5)
                    P.dma("sp", CALL("dma_start", out=y_d[t0:t0 + T, :], in_=yo.t[0:T, :]), [yo.b], dbs("y", t0, T))
                P.flush()
        P.flush(final=True)
    return nc


_NC_CACHE = {}


def make_in_maps(inp, n_cores, NTP, NPOOL):
    cst = make_consts()
    ck = np.ascontiguousarray(inp["cache_k"]).reshape(NPOOL * 8, 2048)
    cv = np.ascontiguousarray(inp["cache_v"]).reshape(NPOOL * 8, 2048)
    cki = np.ascontiguousarray(inp["cache_kidx"]).reshape(NPOOL * 4, 2048)
    maps = []
    for c in range(n_cores):
        xs = np.asarray(inp["x_sample"][NB_S * c:NB_S * (c + 1)]).reshape(NS, D)
        x = np.concatenate([np.asarray(inp["x_prompt"][c]), xs], axis=0).astype(np.float32)
        m = {
            "x": np.ascontiguousarray(x),
            "xT": np.ascontiguousarray(x.T),
            "cache_k": ck, "cache_v": cv, "cache_kidx": cki,
            "state_gla": np.ascontiguousarray(inp["state_gla"][0, NB_S * c:NB_S * (c + 1)]),
            "page_table": np.ascontiguousarray(inp["page_table"][NB_S * c:NB_S * (c + 1)]).astype(np.int32),
            "w_in": np.ascontiguousarray(inp["w_in"][0]),
            "w_alpha2": np.ascontiguousarray(inp["w_alpha2"][0]),
            "b_alpha": np.ascontiguousarray(inp["b_alpha"][0]).reshape(1, 512),
            "gla_norm_g": np.ascontiguousarray(inp["gla_norm_g"][0]).reshape(1, 256),
            "w_attn_o": np.ascontiguousarray(inp["w_attn_o"][0]),
            "w_gla_o": np.ascontiguousarray(inp["w_gla_o"][0]),
            "w_out": np.ascontiguousarray(inp["w_out"][0]),
            "ln1_g": np.ascontiguousarray(inp["ln1_g"][0]).reshape(1, D),
            "ln1_b": np.ascontiguousarray(inp["ln1_b"][0]).reshape(1, D),
            "ln2_g": np.ascontiguousarray(inp["ln2_g"][0]).reshape(1, D),
            "ln2_b": np.ascontiguousarray(inp["ln2_b"][0]).reshape(1, D),
            "w_ff1": np.ascontiguousarray(inp["w_ff1"][0]),
            "w_ff2": np.ascontiguousarray(inp["w_ff2"][0]),
            "cst": cst,
        }
        maps.append(m)
    return maps


def assemble(res, n_cores, NTP):
    f = np.float32
    y = np.stack([r["y"][:NTP] for r in res]).astype(f)
    ys = np.concatenate([r["y"][NTP:].reshape(NB_S, 8, D) for r in res]).astype(f)
    kp = np.stack([r["ko"][:NTP].reshape(NTP, 2, 64) for r in res])[None].astype(f)
    vp = np.stack([r["vo"][:NTP].reshape(NTP, 2, 64) for r in res])[None].astype(f)
    kip = np.stack([r["kio"][:NTP] for r in res])[None].astype(f)
    gp = np.stack([r["gla_p"] for r in res])[None].astype(f)
    ks = np.concatenate([r["ko"][NTP:].reshape(NB_S, 8, 2, 64) for r in res])[None].astype(f)
    vs = np.concatenate([r["vo"][NTP:].reshape(NB_S, 8, 2, 64) for r in res])[None].astype(f)
    kis = np.concatenate([r["kio"][NTP:].reshape(NB_S, 8, 64) for r in res])[None].astype(f)
    gs = np.concatenate([r["gla_s"] for r in res])[None].astype(f)
    return (y, ys, kp, vp, kip, gp, ks, vs, kis, gs)


def kernel(**inputs):
    n_cores = 8
    NTP = inputs["x_prompt"].shape[1]
    NPOOL = inputs["cache_k"].shape[1]
    nc = build(NTP=NTP, NPOOL=NPOOL)
    maps = make_in_maps(inputs, n_cores, NTP, NPOOL)
    out = run_bass_kernel_spmd(nc, maps, core_ids=list(range(n_cores)))
    return assemble(out.results, n_cores, NTP)
```

```python
from contextlib import ExitStack
import numpy as np
import concourse.bass as bass
import concourse.mybir as mybir
from concourse.bass_utils import run_bass_kernel_spmd

F32 = mybir.dt.float32
BF16 = mybir.dt.bfloat16
I32 = mybir.dt.int32
AF = mybir.ActivationFunctionType
ALU = mybir.AluOpType
AX = mybir.AxisListType

D = 1024
D_IN = 6488
NS = 32
NB_S = 4
NPAGES = 128
ROUNDS = 18
ALPHA = 2.0 ** 0.25
EPS = 1e-5
NEG = -30000.0
BIG = 1.0e30


class Buf:
    __slots__ = ("name", "writer", "readers", "excl")

    def __init__(self, name, excl=False):
        self.name = name
        self.writer = None
        self.readers = []
        self.excl = excl


class Op:
    __slots__ = ("eng", "fn", "deps", "signal", "sem", "val", "is_dma", "lane")

    def __init__(self, eng, fn, is_dma):
        self.eng = eng
        self.fn = fn
        self.deps = []
        self.signal = False
        self.sem = None
        self.val = 0
        self.is_dma = is_dma
        self.lane = None


ENGS = ("pe", "act", "dve", "pool", "sp")
EPOCH = 12000


class Prog:
    def __init__(self, nc, lanes=8):
        self.nc = nc
        self.ops = []
        self.sems = {}
        self.cnt = {e: 0 for e in ENGS}
        self.waited = {e: {} for e in ENGS}
        self.lanes = {}
        self.lane_rr = {}
        for q in ("sp", "pool", "act"):
            self.lanes[q] = [[nc.alloc_semaphore("ln_%s_%d" % (q, i)), 0] for i in range(lanes)]
            self.lane_rr[q] = 0
        self.last_sig = {e: None for e in ENGS}
        self.all_dma = []
        self.fence_deps = []
        self.n_ops = 0

    def _add(self, eng, fn, reads, writes, is_dma=False):
        op = Op(eng, fn, is_dma)
        deps = set()
        for b in reads:
            if b.writer is not None:
                deps.add(b.writer)
            if b.excl:
                for r in b.readers:
                    if r.eng != eng:
                        deps.add(r)
        for b in writes:
            if b.writer is not None:
                deps.add(b.writer)
            for r in b.readers:
                deps.add(r)
        op.deps = list(deps)
        for b in reads:
            b.readers.append(op)
        for b in writes:
            b.writer = op
            b.readers = []
        self.ops.append(op)
        return op

    def pe(self, fn, reads=(), writes=()):
        return self._add("pe", fn, reads, writes)

    def act(self, fn, reads=(), writes=()):
        return self._add("act", fn, reads, writes)

    def dve(self, fn, reads=(), writes=()):
        return self._add("dve", fn, reads, writes)

    def pool(self, fn, reads=(), writes=()):
        return self._add("pool", fn, reads, writes)

    def dma(self, q, fn, reads=(), writes=()):
        import os
        if q in os.environ.get("SKIPDMA", "").split(","):
            return None
        return self._add(q, fn, reads, writes, is_dma=True)

    def _sem_for(self, eng):
        ep = self.cnt[eng] // EPOCH
        key = (eng, ep)
        if key not in self.sems:
            self.sems[key] = self.nc.alloc_semaphore("s_%s_%d" % (eng, ep))
        return self.sems[key], ep

    def flush(self, final=False):
        nc = self.nc
        ops = self.ops
        self.ops = []
        if not ops and not final:
            return
        self.n_ops += len(ops)
        needed = set()
        for op in ops:
            for d in op.deps:
                if d.is_dma:
                    continue
                if d.eng == "pe" and op.eng == "pe" and not op.is_dma:
                    continue
                needed.add(d)
        last = {}
        for op in ops:
            if not op.is_dma:
                last[op.eng] = op
        for op in last.values():
            needed.add(op)
        for op in ops:
            if op.is_dma:
                lanes = self.lanes[op.eng]
                li = self.lane_rr[op.eng]
                self.lane_rr[op.eng] = (li + 1) % len(lanes)
                lane = lanes[li]
                op.lane = (lane[0], lane[1])
                lane[1] += 16
                op.sem = lane[0]
                op.val = lane[1]
                self.all_dma.append(op)
            elif op in needed and op.sem is None:
                sem, ep = self._sem_for(op.eng)
                self.cnt[op.eng] += 1
                op.sem = sem
                op.val = self.cnt[op.eng] - ep * EPOCH
                op.signal = True
                self.last_sig[op.eng] = op
        streams = {e: [] for e in ENGS}
        for op in ops:
            streams[op.eng].append(op)
        fence = self.fence_deps

        def emit_stream(eng_name, e):
            waited = self.waited[eng_name]

            def wait(sem, val):
                if waited.get(sem.name, 0) >= val:
                    return
                waited[sem.name] = val
                e.wait_ge(sem, val)

            first = True
            for op in streams[eng_name]:
                if first:
                    for d in fence:
                        if d.sem is not None:
                            wait(d.sem, d.val)
                    first = False
                for d in op.deps:
                    if d.sem is None:
                        continue
                    if (not d.is_dma) and d.eng == "pe" and eng_name == "pe" and not op.is_dma:
                        continue
                    wait(d.sem, d.val)
                if op.is_dma:
                    if op.lane[1] > 0:
                        wait(op.lane[0], op.lane[1])
                    ins = op.fn(e)
                    ins.then_inc(op.sem, 16)
                else:
                    ins = op.fn(e)
                    if op.signal:
                        ins.then_inc(op.sem, 1)
            if final and eng_name == "sp":
                for q in self.lanes:
                    for sem, tot in self.lanes[q]:
                        if tot > 0:
                            wait(sem, tot)

        with nc.Block() as block:
            @block.tensor
            def _(e):
                emit_stream("pe", e)

            @block.scalar
            def _(e):
                emit_stream("act", e)

            @block.vector
            def _(e):
                emit_stream("dve", e)

            @block.gpsimd
            def _(e):
                emit_stream("pool", e)

            @block.sync
            def _(e):
                emit_stream("sp", e)
        fd = [op for op in self.last_sig.values() if op is not None]
        latest = {}
        for op in self.all_dma:
            latest[op.sem.name] = op
        fd += list(latest.values())
        self.fence_deps = fd
        self.all_dma = list(latest.values())


def CALL(name, *a, **k):
    return lambda e: getattr(e, name)(*a, **k)


class TT:
    __slots__ = ("t", "b")

    def __init__(self, t, name, excl=False):
        self.t = t
        self.b = Buf(name, excl)


C_ID, C_UT, C_LT, C_LE, C_CN, C_G, C_PW, C_EPS, C_ONE = 0, 128, 256, 384, 512, 640, 768, 800, 801
NCST = 832


def make_consts():
    c = np.zeros((128, NCST), np.float32)
    p = np.arange(128)[:, None]
    j = np.arange(128)[None, :]
    c[:, C_ID:C_ID + 128] = (p == j)
    c[:, C_UT:C_UT + 128] = np.where(p <= j, -1.0 / 16, 0.0)
    c[:, C_LT:C_LT + 128] = np.where(p > j, -1.0 / 16, 0.0)
    c[:, C_LE:C_LE + 128] = (p <= j)
    c[:, C_CN:C_CN + 128] = np.where(j > p, -BIG, 0.0)
    c[:, C_G:C_G + 128] = ((p // 32 == j // 32) & (p % 8 == j % 8))
    c[:, C_PW:C_PW + 32] = 2.0 ** -(np.arange(32)[None, :] + 1.0)
    c[:, C_EPS] = EPS
    c[:, C_ONE] = 1.0
    return c


O_QA, O_KA, O_VA, O_QI, O_KI, O_WI, O_QB, O_KB, O_VB, O_GB, O_AB, O_GA = (
    0, 512, 640, 768, 1280, 1344, 1352, 1864, 2376, 3400, 4424, 4440)


def w_layout():
    fm, tm = [], []
    off = 0

    def add(lst, name, pieces):
        nonlocal off
        w = sum(b - a for a, b in pieces)
        lst.append((name, off, w, pieces))
        off += w

    for j in range(4):
        add(fm, "qa%d" % j, [(O_QA + 64 * j, O_QA + 64 * j + 64), (O_QA + 64 * (4 + j), O_QA + 64 * (4 + j) + 64)])
    add(fm, "ka", [(O_KA, O_KA + 128)])
    for j in range(4):
        add(fm, "qi%d" % j, [(O_QI + 128 * j, O_QI + 128 * j + 128)])
    add(fm, "ki", [(O_KI, O_KI + 64), (O_KI, O_KI + 64)])
    for j in range(4):
        add(fm, "qb%d" % j, [(O_QB + 128 * j, O_QB + 128 * j + 128)])
    for j in range(4):
        add(fm, "kb%d" % j, [(O_KB + 128 * j, O_KB + 128 * j + 128)])
    add(fm, "ab", [(O_AB, O_AB + 16)])
    for j in range(16):
        add(fm, "gt%d" % j, [(O_GA + 128 * j, O_GA + 128 * j + 128)])
    add(tm, "ta", [(O_KA, O_KA + 256), (O_KI, O_KI + 72)])
    add(tm, "tkb", [(O_KB, O_KB + 512)])
    add(tm, "tvb0", [(O_VB, O_VB + 512)])
    add(tm, "tvb1", [(O_VB + 512, O_VB + 1024)])
    add(tm, "tgb0", [(O_GB, O_GB + 512)])
    add(tm, "tgb1", [(O_GB + 512, O_GB + 1024)])
    return fm, tm, off


def build(NTP=4096, NPOOL=5120, phases="12345", dbg=False):
    nc = bass.Bass("TRN2", target_bir_lowering=False)
    P = Prog(nc)
    NT = NTP + NS
    NBLK = NTP // 128
    TOPK = min(256, NTP // 4)
    TOPK_S = min(256, (NPAGES * 128 + 8) // 4)

    def din(name, shape, dt=F32):
        return nc.dram_tensor(name, list(shape), dt, kind="ExternalInput")

    def dout(name, shape, dt=F32):
        return nc.dram_tensor(name, list(shape), dt, kind="ExternalOutput")

    def dscr(name, shape, dt):
        return nc.dram_tensor(name, list(shape), dt, kind="Internal")

    xT_d = din("xT", [D, NT])
    x_d = din("x", [NT, D])
    ck_d = din("cache_k", [NPOOL * 8, 2048])
    cv_d = din("cache_v", [NPOOL * 8, 2048])
    cki_d = din("cache_kidx", [NPOOL * 4, 2048])
    st_d = din("state_gla", [NB_S, 4, 128, 256])
    pt_d = din("page_table", [NB_S, NPAGES], I32)
    w_in_d = din("w_in", [D, D_IN])
    wal_d = din("w_alpha2", [16, 512])
    bal_d = din("b_alpha", [1, 512])
    gng_d = din("gla_norm_g", [1, 256])
    wao_d = din("w_attn_o", [512, D])
    wgo_d = din("w_gla_o", [D, D])
    wo_d = din("w_out", [D, D])
    ln_d = {n: din(n, [1, D]) for n in ("ln1_g", "ln1_b", "ln2_g", "ln2_b")}
    wf1_d = din("w_ff1", [D, 4096])
    wf2_d = din("w_ff2", [4096, D])
    cst_d = din("cst", [128, NCST])

    y_d = dout("y", [NT, D])
    ko_d = dout("ko", [NT, 128])
    vo_d = dout("vo", [NT, 128])
    kio_d = dout("kio", [NT, 64])
    glap_d = dout("gla_p", [4, 128, 256])
    glas_d = dout("gla_s", [NB_S, 4, 128, 256])

    qaT_s = dscr("qaT_s", [4, 128, NT], BF16)
    qiT_s = dscr("qiT_s", [4, 128, NT], BF16)
    wi_s = dscr("wi_s", [NT, 8], F32)
    qbT_s = dscr("qbT_s", [4, 128, NT], BF16)
    kbT_s = dscr("kbT_s", [4, 128, NT], BF16)
    abT_s = dscr("abT_s", [16, NT], BF16)
    gtT_s = dscr("gtT_s", [16, 128, NT], F32)
    kb_s = dscr("kb_s", [NT, 512], BF16)
    vb_s = dscr("vb_s", [NT, 1024], BF16)
    gb_s = dscr("gb_s", [NT, 1024], F32)
    attnT_s = dscr("attnT_s", [4, 128, NT], BF16)
    obT_s = dscr("obT_s", [8, 128, NT], BF16)
    hT_s = dscr("hT_s", [8, 128, NT], BF16)
    h_s = dscr("h_s", [NT, D], F32)

    DBD = {}
    NJ = {"qaT": 4, "qiT": 4, "wi": 1, "qbT": 4, "kbT": 4, "abT": 1, "gtT": 16, "kb": 1, "vb": 2, "gb": 2, "attnT": 4,
          "obT": 1, "hT": 1, "h": 1, "ko": 1, "vo": 1, "kio": 1, "y": 1}
    B_gla_out = Buf("gla_out")

    def dbs(name, tok0, n, j=None):
        js = range(NJ[name]) if j is None else [j]
        out = []
        for jj in js:
            for tl in range(tok0 // 128, (tok0 + n - 1) // 128 + 1):
                key = (name, jj, tl)
                if key not in DBD:
                    DBD[key] = Buf("%s_%d_%d" % key)
                out.append(DBD[key])
        return out

    g = ExitStack()

    def SB(stack, name, shape, dt):
        return TT(stack.enter_context(nc.sbuf_tensor("sb_" + name, list(shape), dt)), name)

    def PSB(stack, name, shape, dt=F32):
        return TT(stack.enter_context(nc.psum_tensor("pp_" + name, list(shape), dt)), name, True)

    with g:
        cst = SB(g, "cst", [128, NCST], F32)
        ident_bf = SB(g, "ident_bf", [128, 128], BF16)
        i4_bf = SB(g, "i4_bf", [128, 4, 128], BF16)
        mle_bf = SB(g, "mle_bf", [128, 128], BF16)
        ones_bf = SB(g, "ones_bf", [128, 128], BF16)
        P.dma("sp", CALL("dma_start", out=cst.t[:], in_=cst_d[:, :]), [], [cst.b])
        P.dve(CALL("tensor_copy", out=ident_bf.t[:], in_=cst.t[:, C_ID:C_ID + 128]), [cst.b], [ident_bf.b])
        for k in range(4):
            P.dve(CALL("tensor_copy", out=i4_bf.t[:, k, :], in_=cst.t[:, C_ID:C_ID + 128]), [cst.b], [i4_bf.b])
        P.dve(CALL("tensor_copy", out=mle_bf.t[:], in_=cst.t[:, C_LE:C_LE + 128]), [cst.b], [mle_bf.b])
        P.pool(CALL("memset", ones_bf.t[:], 1.0), [], [ones_bf.b])
        ident_f = cst.t[:, C_ID:C_ID + 128]
        utri_f = cst.t[:, C_UT:C_UT + 128]
        ltri_f = cst.t[:, C_LT:C_LT + 128]
        cneg_f = cst.t[:, C_CN:C_CN + 128]
        eps_c = cst.t[:, C_EPS:C_EPS + 1]
        one_c = cst.t[:, C_ONE:C_ONE + 1]

        lnb = {n: SB(g, "lnb_" + n, [128, D], F32) for n in ln_d}
        s12 = ExitStack()
        with s12:
            kA = SB(s12, "kA", [128, NT], BF16)
            kB = SB(s12, "kB", [128, NT], BF16)
            kiA = SB(s12, "kiA", [128, NT], BF16)
            kiB = SB(s12, "kiB", [128, NT], BF16)
            vaug = SB(s12, "vaug", [128, NBLK, 2, 65], BF16)
            for tt_ in (kA, kB, kiA, kiB):
                P.pool(CALL("memset", tt_.t[:], 0.0), [], [tt_.b])
            P.pool(CALL("memset", vaug.t[:], 1.0), [], [vaug.b])

            if "1" in phases:
                s1 = ExitStack()
                with s1:
                    fm, tm, ncol = w_layout()
                    w_sb = SB(s1, "w_sb", [128, 8, ncol], BF16)
                    w_src = w_in_d.rearrange("(kc p) n -> p kc n", p=128)
                    wb = {}
                    for lst in (fm, tm):
                        for (name, off, wd, pieces) in lst:
                            b = Buf("w_" + name)
                            wb[name] = b
                            o = off
                            for (a0, a1) in pieces:
                                P.dma("pool", CALL("dma_start",
                                    out=w_sb.t[:, :, o:o + (a1 - a0)], in_=w_src[:, :, a0:a1]), [], [b])
                                o += a1 - a0
                    import os as _os
                    NXS = int(_os.environ.get("XSLOTS", "2"))
                    xts = [SB(s1, "xT%d" % i, [128, 8, 512], BF16) for i in range(NXS)]
                    stg_b = [SB(s1, "stgb%d" % i, [128, 512], BF16) for i in range(4)]
                    stg_f = [SB(s1, "stgf%d" % i, [128, 512], F32) for i in range(4)]
                    pss = [PSB(s1, "ps1_%d" % i, [128, 512]) for i in range(6)]
                    rr = {"ps": 0, "sb": 0, "sf": 0, "ev": 0}
                    xT_src = xT_d.rearrange("(kc p) t -> p kc t", p=128)

                    def nxt(key, n):
                        v = rr[key]
                        rr[key] = (v + 1) % n
                        return v

                    def evac(out_ap, in_ap, reads, writes):
                        if nxt("ev", 2) == 0:
                            P.act(CALL("activation", out=out_ap, in_=in_ap, func=AF.Copy), reads, writes)
                        else:
                            P.dve(CALL("tensor_copy", out=out_ap, in_=in_ap), reads, writes)

                    STW = int(_os.environ.get("STW", "512"))
                    sts = [(t0, min(STW, NTP - t0)) for t0 in range(0, NTP, STW)] + [(NTP, NS)]

                    def load_x(si):
                        t0, W = sts[si]
                        xt = xts[si % NXS]
                        P.dma("pool", CALL("dma_start", out=xt.t[:, :, 0:W], in_=xT_src[:, :, t0:t0 + W]), [], [xt.b])

                    load_x(0)
                    for si, (t0, W) in enumerate(sts):
                        if si + 1 < len(sts):
                            load_x(si + 1)
                        xt = xts[si % NXS]
                        for (name, off, m, pieces) in fm:
                            ps = pss[nxt("ps", 6)]
                            for kc in range(8):
                                P.pe(CALL("matmul",
                                    ps.t[0:m, 0:W], lhsT=w_sb.t[:, kc, off:off + m], rhs=xt.t[:, kc, 0:W],
                                    start=(kc == 0), stop=(kc == 7)), [xt.b, wb[name]], [ps.b])
                            if name == "ka":
                                evac(kA.t[0:64, t0:t0 + W], ps.t[0:64, 0:W], [ps.b], [kA.b])
                                evac(kB.t[64:128, t0:t0 + W], ps.t[64:128, 0:W], [ps.b], [kB.b])
                            elif name == "ki":
                                evac(kiA.t[0:64, t0:t0 + W], ps.t[0:64, 0:W], [ps.b], [kiA.b])
                                evac(kiB.t[64:128, t0:t0 + W], ps.t[64:128, 0:W], [ps.b], [kiB.b])
                            else:
                                kind = name[:2]
                                j = int(name[2:]) if len(name) > 2 else 0
                                if kind == "gt":
                                    sg = stg_f[nxt("sf", 4)]
                                    dst = gtT_s[j, :, t0:t0 + W]
                                    dbn = "gtT"
                                else:
                                    sg = stg_b[nxt("sb", 4)]
                                    dst = {"qa": qaT_s, "qi": qiT_s, "qb": qbT_s, "kb": kbT_s}[kind][j, :, t0:t0 + W] \
                                        if kind != "ab" else abT_s[:, t0:t0 + W]
                                    dbn = {"qa": "qaT", "qi": "qiT", "qb": "qbT", "kb": "kbT", "ab": "abT"}[kind]
                                evac(sg.t[0:m, 0:W], ps.t[0:m, 0:W], [ps.b], [sg.b])
                                P.dma("sp", CALL("dma_start", out=dst, in_=sg.t[0:m, 0:W]),
                                      [sg.b], dbs(dbn, t0, W, j if NJ[dbn] > 1 else 0))
                        ntile = (W + 127) // 128
                        for tt_i in range(ntile):
                            T = min(128, W - tt_i * 128)
                            tk0 = t0 + tt_i * 128
                            for (name, off, n, pieces) in tm:
                                ps = pss[nxt("ps", 6)]
                                for kc in range(8):
                                    P.pe(CALL("matmul",
                                        ps.t[0:T, 0:n], lhsT=xt.t[:, kc, tt_i * 128:tt_i * 128 + T],
                                        rhs=w_sb.t[:, kc, off:off + n], start=(kc == 0), stop=(kc == 7)),
                                        [xt.b, wb[name]], [ps.b])
                                if name == "ta":
                                    sg = stg_f[nxt("sf", 4)]
                                    evac(sg.t[0:T, 0:n], ps.t[0:T, 0:n], [ps.b], [sg.b])
                                    for (dst, c0, c1, dbn) in ((ko_d, 0, 128, "ko"), (vo_d, 128, 256, "vo"),
                                                               (kio_d, 256, 320, "kio"), (wi_s, 320, 328, "wi")):
                                        P.dma("sp", CALL("dma_start",
                                            out=dst[tk0:tk0 + T, :], in_=sg.t[0:T, c0:c1]), [sg.b], dbs(dbn, tk0, T))
                                    if tk0 < NTP:
                                        blk = tk0 // 128
                                        P.dve(CALL("tensor_copy",
                                            out=vaug.t[:, blk, :, 1:65],
                                            in_=ps.t[:, 128:256].rearrange("p (h d) -> p h d", h=2)), [ps.b], [vaug.b])
                                else:
                                    if name.startswith("tgb"):
                                        sg = stg_f[nxt("sf", 4)]
                                        dst, dbn = gb_s, "gb"
                                    else:
                                        sg = stg_b[nxt("sb", 4)]
                                        dst, dbn = (kb_s, "kb") if name == "tkb" else (vb_s, "vb")
                                    c0 = 512 if name.endswith("1") else 0
                                    evac(sg.t[0:T, 0:n], ps.t[0:T, 0:n], [ps.b], [sg.b])
                                    P.dma("sp", CALL("dma_start",
                                        out=dst[tk0:tk0 + T, c0:c0 + n], in_=sg.t[0:T, 0:n]), [sg.b],
                                        dbs(dbn, tk0, T, (c0 // 512) if NJ[dbn] > 1 else 0))
                    P.flush()

            if "2" in phases:
                s2 = ExitStack()
                with s2:
                    NK = NTP
                    isc = [SB(s2, "isc%d" % i, [128, NK], F32) for i in range(2)]
                    mbs = [SB(s2, "mb%d" % i, [128, NK], BF16) for i in range(2)]
                    junk = SB(s2, "junk2", [128, NK], BF16)
                    rsl = [SB(s2, "rsl%d" % i, [128, 512], BF16) for i in range(4)]
                    ptl = [SB(s2, "ptl%d" % i, [128, 512], BF16) for i in range(4)]
                    qib = [SB(s2, "qib%d" % i, [128, 4, 128], BF16) for i in range(2)]
                    qab = [SB(s2, "qab%d" % i, [128, 4, 128], BF16) for i in range(2)]
                    wib = [SB(s2, "wib%d" % i, [128, 8], F32) for i in range(2)]
                    dgs = [SB(s2, "dg%d" % i, [128, 8, 128], BF16) for i in range(2)]
                    stt = [SB(s2, "st%d" % i, [128, 64], F32) for i in range(2)]
                    ost = [[SB(s2, "ost%d_%d" % (i, k), [128, 260], F32) for k in range(2)] for i in range(2)]
                    atk = [SB(s2, "atk%d" % i, [128, 512], BF16) for i in range(2)]
                    atT = SB(s2, "atT", [128, 4, 128], BF16)
                    rct = SB(s2, "rct", [128, 8], F32)
                    tmpd = SB(s2, "tmpd", [128, 128], F32)
                    psd = [PSB(s2, "psd%d" % i, [128, 512]) for i in range(2)]
                    psi = [PSB(s2, "psi%d" % i, [128, 512]) for i in range(2)]
                    pss_ = [PSB(s2, "pss%d" % i, [128, 512]) for i in range(2)]
                    pos_ = [PSB(s2, "pos%d" % i, [128, 512]) for i in range(2)]
                    rr2 = {"d": 0, "i": 0, "s": 0, "r": 0, "p": 0}

                    def nx2(k, n):
                        v = rr2[k]
                        rr2[k] = (v + 1) % n
                        return v

                    S_MIN1, S_MIN2, S_MAX, S_W0, S_MID, S_CNT, S_U, S_THR, S_WALL = 0, 1, 2, 3, 4, 5, 6, 7, 8
                    WSCALE = (8.0 ** -0.5) / 8.0

                    def stage_a(i):
                        sl = i % 2
                        q0 = i * 128
                        nk = (i + 1) * 128
                        qi_, qa_, wi_, dg_, I_, st_ = qib[sl], qab[sl], wib[sl], dgs[sl], isc[sl], stt[sl]
                        P.dma("sp", CALL("dma_start", out=qi_.t[:], in_=qiT_s[:, :, q0:q0 + 128].rearrange("j p t -> p j t")),
                              dbs("qiT", q0, 128), [qi_.b])
                        P.dma("sp", CALL("dma_start", out=qa_.t[:], in_=qaT_s[:, :, q0:q0 + 128].rearrange("j p t -> p j t")),
                              dbs("qaT", q0, 128), [qa_.b])
                        P.dma("sp", CALL("dma_start", out=wi_.t[:], in_=wi_s[q0:q0 + 128, :]), dbs("wi", q0, 128), [wi_.b])
                        for h in range(8):
                            P.pool(CALL("tensor_scalar", out=dg_.t[:, h, :], in0=ident_f, scalar1=wi_.t[:, h:h + 1],
                                                                 scalar2=WSCALE, op0=ALU.mult, op1=ALU.mult),
                                   [wi_.b, cst.b], [dg_.b])
                        nch = (nk + 511) // 512
                        steps = [(c, h) for c in range(nch) for h in range(8)]
                        pds = {}
                        pIs = {}

                        def dots(k):
                            c, h = steps[k]
                            k0 = c * 512
                            Wc = min(512, nk - k0)
                            pd = psd[nx2("d", 2)]
                            pds[k] = pd
                            kis = kiA if h % 2 == 0 else kiB
                            P.pe(CALL("matmul", pd.t[:, 0:Wc], lhsT=qi_.t[:, h // 2, :], rhs=kis.t[:, k0:k0 + Wc], start=True, stop=True),
                                 [qi_.b, kis.b], [pd.b])

                        dots(0)
                        for k, (c, h) in enumerate(steps):
                            k0 = c * 512
                            Wc = min(512, nk - k0)
                            if h == 0:
                                pIs[c] = psi[nx2("i", 2)]
                            pI = pIs[c]
                            pd = pds[k]
                            r_ = rsl[nx2("r", 4)]
                            P.act(CALL("activation", out=r_.t[:, 0:Wc], in_=pd.t[:, 0:Wc], func=AF.Relu), [pd.b], [r_.b])
                            if k + 1 < len(steps):
                                dots(k + 1)
                            P.pe(CALL("matmul", pI.t[:, 0:Wc], lhsT=dg_.t[:, h, :], rhs=r_.t[:, 0:Wc], start=(h == 0), stop=(h == 7)),
                                 [dg_.b, r_.b], [pI.b])
                            if h == 7:
                                P.act(CALL("activation", out=I_.t[:, k0:k0 + Wc], in_=pI.t[:, 0:Wc], func=AF.Copy), [pI.b], [I_.b])
                        d0 = i * 128
                        P.dve(CALL("tensor_tensor", out=tmpd.t[:], in0=I_.t[:, d0:d0 + 128], in1=cneg_f, op=ALU.subtract),
                              [I_.b, cst.b], [tmpd.b])
                        P.dve(CALL("tensor_reduce", out=st_.t[:, S_MIN2:S_MIN2 + 1], in_=tmpd.t[:], axis=AX.X, op=ALU.min),
                              [tmpd.b], [st_.b])
                        if i > 0:
                            P.dve(CALL("tensor_reduce", out=st_.t[:, S_MIN1:S_MIN1 + 1], in_=I_.t[:, 0:d0], axis=AX.X,
                                                            op=ALU.min), [I_.b], [st_.b])
                            P.dve(CALL("tensor_tensor", out=st_.t[:, S_MIN2:S_MIN2 + 1], in0=st_.t[:, S_MIN2:S_MIN2 + 1],
                                                            in1=st_.t[:, S_MIN1:S_MIN1 + 1], op=ALU.min), [st_.b], [st_.b])
                        P.dve(CALL("tensor_tensor", out=I_.t[:, d0:d0 + 128], in0=I_.t[:, d0:d0 + 128], in1=cneg_f, op=ALU.add),
                              [I_.b, cst.b], [I_.b])
                        P.dve(CALL("tensor_reduce", out=st_.t[:, S_MAX:S_MAX + 1], in_=I_.t[:, 0:nk], axis=AX.X, op=ALU.max),
                              [I_.b], [st_.b])

                    def stage_b(i):
                        sl = i % 2
                        nk = (i + 1) * 128
                        I_, st_, mb_ = isc[sl], stt[sl], mbs[sl]
                        c_ = lambda k: st_.t[:, k:k + 1]
                        P.dve(CALL("tensor_tensor", out=c_(S_W0), in0=c_(S_MAX), in1=c_(S_MIN2), op=ALU.subtract), [st_.b], [st_.b])
                        P.dve(CALL("tensor_scalar", out=c_(S_U), in0=c_(S_W0), scalar1=-1e-3, scalar2=-1e-4, op0=ALU.mult,
                                                        op1=ALU.add), [st_.b], [st_.b])
                        P.dve(CALL("tensor_tensor", out=c_(S_MIN2), in0=c_(S_MIN2), in1=c_(S_U), op=ALU.add), [st_.b], [st_.b])
                        P.dve(CALL("tensor_tensor", out=c_(S_W0), in0=c_(S_MAX), in1=c_(S_MIN2), op=ALU.subtract), [st_.b], [st_.b])
                        P.dve(CALL("tensor_scalar", out=c_(S_W0), in0=c_(S_W0), scalar1=1.0001, scalar2=1e-6, op0=ALU.mult,
                                                        op1=ALU.add), [st_.b], [st_.b])
                        P.dve(CALL("tensor_scalar", out=st_.t[:, S_WALL:S_WALL + 32], in0=cst.t[:, C_PW:C_PW + 32],
                                                        scalar1=c_(S_W0), scalar2=None, op0=ALU.mult), [st_.b, cst.b], [st_.b])
                        P.dve(CALL("tensor_tensor", out=c_(S_MID), in0=c_(S_MIN2), in1=c_(S_WALL), op=ALU.add), [st_.b], [st_.b])
                        for r in range(ROUNDS):
                            P.dve(CALL("tensor_scalar", out=junk.t[:, 0:nk], in0=I_.t[:, 0:nk], scalar1=c_(S_MID), scalar2=0.0,
                                                            op0=ALU.is_ge, op1=ALU.add, accum_out=c_(S_CNT)),
                                  [I_.b, st_.b], [junk.b, st_.b])
                            P.dve(CALL("tensor_scalar", out=c_(S_U), in0=c_(S_CNT), scalar1=TOPK - 0.5,
                                                                 scalar2=c_(S_WALL + r), op0=ALU.is_ge, op1=ALU.mult),
                                  [st_.b], [st_.b])
                            nxt_w = S_WALL + r + 1 if r + 1 < ROUNDS else S_WALL + r
                            dst = S_MID if r + 1 < ROUNDS else S_THR
                            P.dve(CALL("scalar_tensor_tensor",
                                out=c_(dst), in0=c_(S_U), scalar=c_(nxt_w), in1=c_(S_MID), op0=ALU.subtract, op1=ALU.add),
                                [st_.b], [st_.b])
                        P.dve(CALL("tensor_scalar", out=mb_.t[:, 0:nk], in0=I_.t[:, 0:nk], scalar1=c_(S_THR), scalar2=NEG,
                                                        op0=ALU.is_lt, op1=ALU.mult), [I_.b, st_.b], [mb_.b])

                    def stage_c(i):
                        sl = i % 2
                        qa_, mb_ = qab[sl], mbs[sl]
                        steps = [(kvh, c) for kvh in range(2) for c in range(i + 1)]
                        pls = {}

                        def logits(k):
                            kvh, c = steps[k]
                            kk = kA if kvh == 0 else kB
                            k0 = c * 128
                            ps_ = pss_[nx2("s", 2)]
                            pls[k] = ps_
                            P.pe(CALL("matmul", ps_.t[:, :], lhsT=kk.t[:, k0:k0 + 128], rhs=qa_.t[:, :, :], start=True, stop=False),
                                 [kk.b, qa_.b], [ps_.b])
                            P.pe(CALL("matmul", ps_.t[:, :], lhsT=mb_.t[:, k0:k0 + 128], rhs=i4_bf.t[:, :, :], start=False, stop=True),
                                 [mb_.b, i4_bf.b], [ps_.b])

                        logits(0)
                        for k, (kvh, c) in enumerate(steps):
                            po = pos_[kvh]
                            ps_ = pls[k]
                            pt_ = ptl[nx2("p", 4)]
                            P.act(CALL("activation", out=pt_.t[:, :], in_=ps_.t[:, :], func=AF.Exp, scale=0.125), [ps_.b], [pt_.b])
                            if k + 1 < len(steps):
                                logits(k + 1)
                            for j in range(4):
                                P.pe(CALL("matmul", po.t[:, j * 65:(j + 1) * 65], lhsT=pt_.t[:, j * 128:(j + 1) * 128],
                                          rhs=vaug.t[:, c, kvh, :], start=(c == 0 and j == 0), stop=(c == i), skip_group_check=True),
                                     [vaug.b, pt_.b], [po.b])
                            if c == i:
                                o_ = ost[sl][kvh]
                                P.act(CALL("activation", out=o_.t[:, :], in_=po.t[:, 0:260], func=AF.Copy), [po.b], [o_.b])

                    def stage_c_norm(i):
                        sl = i % 2
                        at_ = atk[sl]
                        for kvh in range(2):
                            o3 = ost[sl][kvh].t[:, :].rearrange("p (j e) -> p j e", e=65)
                            P.dve(CALL("reciprocal", out=rct.t[:, kvh * 4:kvh * 4 + 4].unsqueeze(2), in_=o3[:, :, 0:1]), [ost[sl][kvh].b], [rct.b])
                            P.dve(CALL("tensor_tensor", out=at_.t[:, kvh * 256:(kvh + 1) * 256].rearrange("p (j d) -> p j d", j=4),
                                       in0=o3[:, :, 1:65], in1=rct.t[:, kvh * 4:kvh * 4 + 4].unsqueeze(2).broadcast_to([128, 4, 64]),
                                       op=ALU.mult), [ost[sl][kvh].b, rct.b], [at_.b])

                    def stage_c_T(i):
                        sl = i % 2
                        q0 = i * 128
                        at_ = atk[sl]
                        po = pos_[0]
                        psx_ = po.t.bitcast(BF16)
                        for kc in range(4):
                            P.pe(CALL("transpose", psx_[:, kc * 128:(kc + 1) * 128], at_.t[:, kc * 128:(kc + 1) * 128], ident_bf.t[:, :]),
                                 [at_.b, ident_bf.b], [po.b])
                        P.act(CALL("activation", out=atT.t[:, :, :], in_=psx_[:, 0:512].rearrange("p (k t) -> p k t", k=4), func=AF.Copy),
                              [po.b], [atT.b])
                        P.dma("sp", CALL("dma_start", out=attnT_s[:, :, q0:q0 + 128].rearrange("k p t -> p k t"), in_=atT.t[:, :, :]),
                              [atT.b], dbs("attnT", q0, 128))

                    for i in range(NBLK):
                        stage_a(i)
                        if i >= 2:
                            stage_c_T(i - 2)
                        if i >= 1:
                            stage_c(i - 1)
                        stage_b(i)
                        if i >= 1:
                            stage_c_norm(i - 1)
                    if NBLK >= 2:
                        stage_c_T(NBLK - 2)
                    stage_c(NBLK - 1)
                    stage_c_norm(NBLK - 1)
                    stage_c_T(NBLK - 1)
                    P.flush()
            if "5" in phases:
                s2b = ExitStack()
                with s2b:
                    ROUNDS_S = 24
                    NKP = NPAGES * 128
                    SEGW = NKP // 4
                    IW = SEGW + 8
                    R = SB(s2b, "R2b", [128, 3 * NKP], BF16)
                    BR0, BR1, BR2 = Buf("BR0"), Buf("BR1"), Buf("BR2")
                    O1, O2 = NKP, 2 * NKP
                    Is = SB(s2b, "Is", [128, IW], F32)
                    MBs = SB(s2b, "MBs", [128, IW], BF16)
                    pti = SB(s2b, "pti", [128, 1], I32)
                    ptf = SB(s2b, "ptf", [128, 1], F32)
                    idf = SB(s2b, "idf", [128, 16], F32)
                    idx = SB(s2b, "idx", [128, 16], I32)
                    qis = SB(s2b, "qis", [64, 8, NS], BF16)
                    qs0 = SB(s2b, "qs0", [128, 4, NS], BF16)
                    qs1 = SB(s2b, "qs1", [128, 4, NS], BF16)
                    wis = SB(s2b, "wis", [8, NB_S, 8], F32)
                    dss = SB(s2b, "dss", [8, NB_S, 8, 8], BF16)
                    rs2 = [SB(s2b, "rs2_%d" % i, [64, 512], BF16) for i in range(4)]
                    qisb = SB(s2b, "qisb", [64, NB_S, 64], BF16)
                    wsel = SB(s2b, "wsel", [64, NB_S, 8], BF16)
                    qsb = SB(s2b, "qsb", [128, NB_S, 64], BF16)
                    sel2 = SB(s2b, "sel2", [128, 16, 64], BF16)
                    sg2 = [SB(s2b, "sg2_%d" % i, [8, 512], F32) for i in range(4)]
                    pt2 = [SB(s2b, "pt2_%d" % i, [128, 256], BF16) for i in range(4)]
                    sst = SB(s2b, "sst", [128, 64], F32)
                    vnew = SB(s2b, "vnew", [8, 128], BF16)
                    rcp2 = SB(s2b, "rcp2", [1, 64], F32)
                    bcs2 = SB(s2b, "bcs2", [64, 64], F32)
                    as2 = SB(s2b, "as2", [64, 2, 4, 8], BF16)
                    pst = [PSB(s2b, "pst%d" % i, [128, 1024], BF16) for i in range(2)]
                    psd2 = [PSB(s2b, "psd2_%d" % i, [128, 512]) for i in range(2)]
                    psi2s = [PSB(s2b, "psi2_%d" % i, [128, 512]) for i in range(2)]
                    psi2 = psi2s[0]
                    pso2s = [psi2s[1], PSB(s2b, "pso2b", [128, 512])]
                    psm2 = PSB(s2b, "psm2", [128, 512])
                    rr3 = {"t": 0, "d": 0, "r": 0, "g": 0, "p": 0, "e": 0, "i": 0}

                    def nx3(k, n):
                        v = rr3[k]
                        rr3[k] = (v + 1) % n
                        return v

                    def evac3(out_ap, in_ap, reads, writes):
                        if nx3("e", 2) == 0:
                            P.act(CALL("activation", out=out_ap, in_=in_ap, func=AF.Copy), reads, writes)
                        else:
                            P.dve(CALL("tensor_copy", out=out_ap, in_=in_ap), reads, writes)

                    S0 = NTP
                    WSC = (8.0 ** -0.5) / 8.0
                    G_f = cst.t[:, C_G:C_G + 128]
                    P.pool(CALL("memset", qs0.t[:, :, :], 0.0), [], [qs0.b])
                    P.pool(CALL("memset", qs1.t[:, :, :], 0.0), [], [qs1.b])
                    P.pool(CALL("memset", Is.t[:, SEGW:IW], -BIG), [], [Is.b])
                    P.dma("sp", CALL("dma_start", out=qs0.t[0:64, :, :], in_=qaT_s[:, 0:64, S0:S0 + NS].rearrange("j p t -> p j t")),
                          dbs("qaT", S0, NS), [qs0.b])
                    P.dma("sp", CALL("dma_start", out=qs1.t[64:128, :, :], in_=qaT_s[:, 64:128, S0:S0 + NS].rearrange("j p t -> p j t")),
                          dbs("qaT", S0, NS), [qs1.b])
                    for j in range(4):
                        for par in range(2):
                            P.dma("sp", CALL("dma_start", out=qis.t[:, 2 * j + par, :], in_=qiT_s[j, 64 * par:64 * par + 64, S0:S0 + NS]),
                                  dbs("qiT", S0, NS), [qis.b])
                    P.dma("sp", CALL("dma_start", out=wis.t[:, :, :], in_=wi_s[S0:S0 + NS, :].rearrange("(b q) h -> q b h", q=8)),
                          dbs("wi", S0, NS), [wis.b])
                    for b in range(NB_S):
                        for h in range(8):
                            P.dve(CALL("tensor_scalar", out=dss.t[:, b, h, :], in0=cst.t[0:8, C_ID:C_ID + 8], scalar1=wis.t[:, b, h:h + 1],
                                       scalar2=WSC, op0=ALU.mult, op1=ALU.mult), [wis.b, cst.b], [dss.b])
                    for bs in range(16):
                        P.dve(CALL("tensor_copy", out=sel2.t[:, bs, :].rearrange("p (a q) -> p a q", a=8),
                                   in_=cst.t[:, C_ID + bs * 8:C_ID + bs * 8 + 8].unsqueeze(1).broadcast_to([128, 8, 8])), [cst.b], [sel2.b])

                    def page_idx(b):
                        P.dma("sp", CALL("dma_start", out=pti.t[:, :], in_=pt_d[b:b + 1, :].rearrange("o p -> p o")), [], [pti.b])
                        P.dve(CALL("tensor_copy", out=ptf.t[:, :], in_=pti.t[:, :]), [pti.b], [ptf.b])
                        for o in range(4):
                            P.dve(CALL("tensor_scalar", out=idf.t[:, o:o + 1], in0=ptf.t[:, :], scalar1=4.0, scalar2=float(o),
                                       op0=ALU.mult, op1=ALU.add), [ptf.b], [idf.b])
                        for o in range(8):
                            P.dve(CALL("tensor_scalar", out=idf.t[:, 4 + o:5 + o], in0=ptf.t[:, :], scalar1=8.0, scalar2=float(o),
                                       op0=ALU.mult, op1=ALU.add), [ptf.b], [idf.b])
                        P.dve(CALL("tensor_copy", out=idx.t[:, 0:12], in_=idf.t[:, 0:12]), [idf.b], [idx.b])

                    def gather(src_d, n_o, icol0, dst0, bufR):
                        for o in range(n_o):
                            P.dma("pool", CALL("indirect_dma_start", out=R.t[:, dst0 + o * 2048:dst0 + (o + 1) * 2048], out_offset=None,
                                               in_=src_d[:, :],
                                               in_offset=bass.IndirectOffsetOnAxis(ap=idx.t[:, icol0 + o:icol0 + o + 1], axis=0)),
                                  [idx.b], [bufR])

                    for b in range(NB_S):
                        P.dve(CALL("tensor_copy", out=qisb.t[:, b, :].rearrange("p (h q) -> p h q", h=8), in_=qis.t[:, :, 8 * b:8 * b + 8]),
                              [qis.b], [qisb.b])
                        ptw = pst[nx3("t", 2)]
                        P.pe(CALL("transpose", ptw.t[0:64, 0:8], dss.t[:, b, :, :].rearrange("p h q -> p (h q)"), ident_bf.t[0:8, 0:8]),
                             [dss.b, ident_bf.b], [ptw.b])
                        P.dve(CALL("tensor_copy", out=wsel.t[:, b, :], in_=ptw.t[0:64, 0:8]), [ptw.b], [wsel.b])
                    for b in range(NB_S):
                        page_idx(b)
                        gather(cki_d, 4, 0, 0, BR0)
                        for g8 in range(16):
                            pt_ = pst[nx3("t", 2)]
                            for k in range(8):
                                t = g8 * 8 + k
                                P.pe(CALL("transpose", pt_.t[0:64, k * 128:(k + 1) * 128], R.t[:, t * 64:(t + 1) * 64], ident_bf.t[:, :]),
                                     [BR0, ident_bf.b], [pt_.b])
                            evac3(R.t[0:64, O1 + g8 * 1024:O1 + (g8 + 1) * 1024], pt_.t[0:64, :], [pt_.b], [BR1])
                        for c in range(NKP // 512 + 1):
                            new = (c == NKP // 512)
                            Wc = 8 if new else 512
                            pd = psd2[nx3("d", 2)]
                            rhs = kiA.t[0:64, S0 + 8 * b:S0 + 8 * b + 8] if new else R.t[0:64, O1 + c * 512:O1 + (c + 1) * 512]
                            P.pe(CALL("matmul", pd.t[0:64, 0:Wc], lhsT=qisb.t[:, b, :], rhs=rhs, start=True, stop=True),
                                 [qisb.b, kiA.b if new else BR1], [pd.b])
                            r_ = rs2[nx3("r", 4)]
                            P.act(CALL("activation", out=r_.t[:, 0:Wc], in_=pd.t[0:64, 0:Wc], func=AF.Relu), [pd.b], [r_.b])
                            pi_ = psi2s[nx3("i", 2)]
                            P.pe(CALL("matmul", pi_.t[0:8, 0:Wc], lhsT=wsel.t[:, b, :], rhs=r_.t[:, 0:Wc], start=True, stop=True),
                                 [wsel.b, r_.b], [pi_.b])
                            sg_ = sg2[nx3("g", 4)]
                            if new:
                                P.dve(CALL("tensor_tensor", out=sg_.t[:, 0:8], in0=pi_.t[0:8, 0:8], in1=cst.t[0:8, C_CN:C_CN + 8], op=ALU.add),
                                      [pi_.b, cst.b], [sg_.b])
                                P.dma("sp", CALL("dma_start", out=Is.t[b * 32:b * 32 + 8, SEGW:IW], in_=sg_.t[:, 0:8]), [sg_.b], [Is.b])
                            else:
                                P.dve(CALL("tensor_copy", out=sg_.t[:, :], in_=pi_.t[0:8, :]), [pi_.b], [sg_.b])
                                seg, cc = c // (SEGW // 512), c % (SEGW // 512)
                                r0 = b * 32 + seg * 8
                                P.dma("sp", CALL("dma_start", out=Is.t[r0:r0 + 8, cc * 512:(cc + 1) * 512], in_=sg_.t[:, :]), [sg_.b], [Is.b])

                    c2 = lambda k: sst.t[:, k:k + 1]
                    Q_MAX, Q_MIN, Q_A, Q_HI, Q_LO, Q_W0, Q_MID, Q_CNT, Q_U, Q_THR, Q_WALL = 0, 1, 2, 3, 4, 5, 6, 7, 8, 9, 16
                    P.dve(CALL("tensor_reduce", out=c2(Q_MAX), in_=Is.t[:, 0:IW], axis=AX.X, op=ALU.max), [Is.b], [sst.b])
                    P.dve(CALL("tensor_reduce", out=c2(Q_MIN), in_=Is.t[:, 0:SEGW], axis=AX.X, op=ALU.min), [Is.b], [sst.b])
                    P.dve(CALL("tensor_scalar", out=c2(Q_MIN), in0=c2(Q_MIN), scalar1=-1.0, scalar2=None, op0=ALU.mult), [sst.b], [sst.b])
                    P.dve(CALL("tensor_tensor", out=c2(Q_A), in0=c2(Q_MAX), in1=c2(Q_MIN), op=ALU.max), [sst.b], [sst.b])
                    P.pe(CALL("matmul", psi2.t[:, 0:1], lhsT=G_f, rhs=c2(Q_A), start=True, stop=True), [sst.b, cst.b], [psi2.b])
                    P.dve(CALL("tensor_scalar", out=c2(Q_HI), in0=psi2.t[:, 0:1], scalar1=1.001, scalar2=1e-3, op0=ALU.mult, op1=ALU.add),
                          [psi2.b], [sst.b])
                    P.dve(CALL("tensor_scalar", out=c2(Q_LO), in0=c2(Q_HI), scalar1=-1.0, scalar2=None, op0=ALU.mult), [sst.b], [sst.b])
                    P.dve(CALL("tensor_scalar", out=c2(Q_W0), in0=c2(Q_HI), scalar1=2.0, scalar2=None, op0=ALU.mult), [sst.b], [sst.b])
                    P.dve(CALL("tensor_scalar", out=sst.t[:, Q_WALL:Q_WALL + 32], in0=cst.t[:, C_PW:C_PW + 32], scalar1=c2(Q_W0), scalar2=None,
                               op0=ALU.mult), [sst.b, cst.b], [sst.b])
                    P.dve(CALL("tensor_tensor", out=c2(Q_MID), in0=c2(Q_LO), in1=c2(Q_WALL), op=ALU.add), [sst.b], [sst.b])
                    for r in range(ROUNDS_S):
                        P.dve(CALL("tensor_scalar", out=MBs.t[:, :], in0=Is.t[:, :], scalar1=c2(Q_MID), scalar2=0.0, op0=ALU.is_ge,
                                   op1=ALU.add, accum_out=c2(Q_CNT)), [Is.b, sst.b], [MBs.b, sst.b])
                        P.pe(CALL("matmul", psi2.t[:, 0:1], lhsT=G_f, rhs=c2(Q_CNT), start=True, stop=True), [sst.b, cst.b], [psi2.b])
                        P.dve(CALL("tensor_scalar", out=c2(Q_U), in0=psi2.t[:, 0:1], scalar1=TOPK_S - 0.5, scalar2=c2(Q_WALL + r),
                                   op0=ALU.is_ge, op1=ALU.mult), [psi2.b, sst.b], [sst.b])
                        nxt_w = Q_WALL + r + 1 if r + 1 < ROUNDS_S else Q_WALL + r
                        dst = Q_MID if r + 1 < ROUNDS_S else Q_THR
                        P.dve(CALL("scalar_tensor_tensor", out=c2(dst), in0=c2(Q_U), scalar=c2(nxt_w), in1=c2(Q_MID), op0=ALU.subtract,
                                   op1=ALU.add), [sst.b], [sst.b])
                    P.dve(CALL("tensor_scalar", out=MBs.t[:, :], in0=Is.t[:, :], scalar1=c2(Q_THR), scalar2=NEG, op0=ALU.is_lt, op1=ALU.mult),
                          [Is.b, sst.b], [MBs.b])

                    for b in range(NB_S):
                        P.dve(CALL("tensor_copy", out=qsb.t[:, b, 0:32].rearrange("p (j q) -> p j q", j=4), in_=qs0.t[:, :, 8 * b:8 * b + 8]),
                              [qs0.b], [qsb.b])
                        P.dve(CALL("tensor_copy", out=qsb.t[:, b, 32:64].rearrange("p (j q) -> p j q", j=4), in_=qs1.t[:, :, 8 * b:8 * b + 8]),
                              [qs1.b], [qsb.b])
                    for b in range(NB_S):
                        page_idx(b)
                        gather(ck_d, 8, 4, 0, BR0)
                        gather(cv_d, 8, 4, O2, BR2)
                        P.dma("pool", CALL("dma_start", out=vnew.t[:, :], in_=vo_d[S0 + 8 * b:S0 + 8 * b + 8, :]), dbs("vo", S0, NS), [vnew.b])
                        for g8 in range(16):
                            pt_ = pst[nx3("t", 2)]
                            for k in range(8):
                                t = g8 * 8 + k
                                P.pe(CALL("transpose", pt_.t[:, k * 128:(k + 1) * 128], R.t[:, t * 128:(t + 1) * 128], ident_bf.t[:, :]),
                                     [BR0, ident_bf.b], [pt_.b])
                            evac3(R.t[:, O1 + g8 * 1024:O1 + (g8 + 1) * 1024], pt_.t[:, :], [pt_.b], [BR1])
                        ngrp = 128 // 4
                        for g4 in range(ngrp + 1):
                            new = (g4 == ngrp)
                            nkk = 8 if new else 128
                            ncol = 64 if new else 256
                            ps_ = psd2[nx3("d", 2)]
                            ts_ = [128] if new else [g4 * 4 + k for k in range(4)]
                            for k, t in enumerate(ts_):
                                if new:
                                    l1a, l1b = kA.t[:, S0 + 8 * b:S0 + 8 * b + 8], kB.t[:, S0 + 8 * b:S0 + 8 * b + 8]
                                    l2 = MBs.t[:, SEGW:IW]
                                    sl_ = sel2.t[:, b * 4, :]
                                    P.pe(CALL("matmul", ps_.t[0:nkk, 0:64], lhsT=l1a, rhs=qsb.t[:, b, :], start=True, stop=False),
                                         [kA.b, qsb.b], [ps_.b])
                                    P.pe(CALL("matmul", ps_.t[0:nkk, 0:64], lhsT=l1b, rhs=qsb.t[:, b, :], start=False, stop=False),
                                         [kB.b, qsb.b], [ps_.b])
                                else:
                                    l1 = R.t[:, O1 + t * 128:O1 + (t + 1) * 128]
                                    seg, cc = t // 32, t % 32
                                    l2 = MBs.t[:, cc * 128:(cc + 1) * 128]
                                    sl_ = sel2.t[:, b * 4 + seg, :]
                                    P.pe(CALL("matmul", ps_.t[0:nkk, k * 64:(k + 1) * 64], lhsT=l1, rhs=qsb.t[:, b, :], start=True, stop=False),
                                         [BR1, qsb.b], [ps_.b])
                                P.pe(CALL("matmul", ps_.t[0:nkk, k * 64:(k + 1) * 64], lhsT=l2, rhs=sl_, start=False, stop=True),
                                     [MBs.b, sel2.b], [ps_.b])
                            p_ = pt2[nx3("p", 4)]
                            P.act(CALL("activation", out=p_.t[0:nkk, 0:ncol], in_=ps_.t[0:nkk, 0:ncol], func=AF.Exp, scale=0.125), [ps_.b], [p_.b])
                            for k, t in enumerate(ts_):
                                first = (g4 == 0 and k == 0)
                                for kvh in range(2):
                                    po_ = pso2s[kvh]
                                    if new:
                                        lv = vnew.t[0:8, kvh * 64:(kvh + 1) * 64]
                                        rdv = vnew.b
                                    else:
                                        lv = R.t[:, O2 + t * 128 + kvh * 64:O2 + t * 128 + kvh * 64 + 64]
                                        rdv = BR2
                                    P.pe(CALL("matmul", po_.t[0:64, 0:32], lhsT=lv, rhs=p_.t[0:nkk, k * 64 + kvh * 32:k * 64 + kvh * 32 + 32],
                                              start=first, stop=new), [rdv, p_.b], [po_.b])
                                P.pe(CALL("matmul", psm2.t[0:1, 0:64], lhsT=ones_bf.t[0:nkk, 0:1], rhs=p_.t[0:nkk, k * 64:(k + 1) * 64],
                                          start=first, stop=new), [ones_bf.b, p_.b], [psm2.b])
                        P.dve(CALL("reciprocal", out=rcp2.t[:, :], in_=psm2.t[0:1, 0:64]), [psm2.b], [rcp2.b])
                        pb_ = psd2[nx3("d", 2)]
                        P.pe(CALL("matmul", pb_.t[0:64, 0:64], lhsT=cst.t[0:1, C_LE:C_LE + 64], rhs=rcp2.t[:, :], start=True, stop=True),
                             [cst.b, rcp2.b], [pb_.b])
                        P.act(CALL("activation", out=bcs2.t[:, :], in_=pb_.t[0:64, 0:64], func=AF.Copy), [pb_.b], [bcs2.b])
                        for kvh in range(2):
                            P.dve(CALL("tensor_tensor", out=as2.t[:, kvh, :, :], in0=pso2s[kvh].t[0:64, 0:32].rearrange("p (j q) -> p j q", j=4),
                                       in1=bcs2.t[:, kvh * 32:(kvh + 1) * 32].rearrange("p (j q) -> p j q", j=4), op=ALU.mult),
                                  [pso2s[kvh].b, bcs2.b], [as2.b])
                        for kvh in range(2):
                            for j in range(4):
                                kc = 2 * kvh + j // 2
                                r0 = (j % 2) * 64
                                P.dma("sp", CALL("dma_start", out=attnT_s[kc, r0:r0 + 64, S0 + 8 * b:S0 + 8 * b + 8], in_=as2.t[:, kvh, j, :]),
                                      [as2.b], dbs("attnT", S0, NS, kc))
                    P.flush()
        s3w = ExitStack()
        with s3w:
            wao = SB(s3w, "wao", [128, 4, D], BF16)
            wgo = SB(s3w, "wgo", [128, 8, D], BF16)
            wo = SB(s3w, "wo", [128, 8, D], BF16)
            if "4" in phases:
                for (wt, wd) in ((wao, wao_d), (wgo, wgo_d), (wo, wo_d)):
                    P.dma("pool", CALL("dma_start",
                        out=wt.t[:, :, :], in_=wd.rearrange("(kc p) n -> p kc n", p=128)), [], [wt.b])
                for n in ln_d:
                    P.dma("sp", CALL("dma_start", out=lnb[n].t[:, :], in_=ln_d[n][0:1, :].broadcast_to([128, D])),
                          [], [lnb[n].b])
            if "3" in phases:
                s3 = ExitStack()
                with s3:
                    aug = SB(s3, "aug", [17, 128], BF16)
                    wal = SB(s3, "wal", [17, 512], BF16)
                    gnb = SB(s3, "gnb", [128, 256], F32)
                    qbb = SB(s3, "qbb", [128, 4, 128], BF16)
                    kbb = SB(s3, "kbb", [128, 4, 128], BF16)
                    kbt = SB(s3, "kbt", [128, 512], BF16)
                    vbt = SB(s3, "vbt", [128, 1024], BF16)
                    gbt = SB(s3, "gbt", [128, 1024], F32)
                    ee = SB(s3, "ee", [128, 512], F32)
                    la = SB(s3, "la", [128, 512], F32)
                    Eq = SB(s3, "Eq", [128, 4, 128], F32)
                    Ek = SB(s3, "Ek", [128, 4, 128], F32)
                    Er = SB(s3, "Er", [128, 512], F32)
                    qt = SB(s3, "qt", [128, 4, 128], BF16)
                    kt = SB(s3, "kt", [128, 4, 128], BF16)
                    kp = SB(s3, "kp", [128, 512], BF16)
                    attm = SB(s3, "attm", [128, 4, 128], BF16)
                    Sf = SB(s3, "Sf", [128, 4, 256], F32)
                    Sb = SB(s3, "Sb", [128, 4, 256], BF16)
                    onr = SB(s3, "onr", [128, 1024], F32)
                    sgt = SB(s3, "sgt", [128, 1024], F32)
                    obb = SB(s3, "obb", [128, 1024], BF16)
                    obT = SB(s3, "obT", [128, 8, 128], BF16)
                    gst = SB(s3, "gst", [128, 16], F32)
                    jk3 = SB(s3, "jk3", [128, 256], BF16)
                    psA = PSB(s3, "psA", [128, 512])
                    psC = PSB(s3, "psC", [128, 512])
                    psT_ = PSB(s3, "psT", [128, 512])
                    psO = PSB(s3, "psO", [128, 1024])
                    psS = PSB(s3, "psS", [128, 1024])
                    psX = PSB(s3, "psX", [128, 1024], BF16)

                    P.pool(CALL("memset", aug.t[:, :], 1.0), [], [aug.b])
                    P.dma("pool", CALL("dma_start", out=wal.t[1:17, :], in_=wal_d[:, :]), [], [wal.b])
                    P.dma("pool", CALL("dma_start", out=wal.t[0:1, :], in_=bal_d[:, :]), [], [wal.b])
                    P.dma("sp", CALL("dma_start", out=gnb.t[:, :], in_=gng_d[0:1, :].broadcast_to([128, 256])), [], [gnb.b])

                    def gla_chunk(tok0, C):
                        P.dma("sp", CALL("dma_start", out=aug.t[1:17, 0:C], in_=abT_s[:, tok0:tok0 + C]), dbs("abT", tok0, C), [aug.b])
                        P.dma("sp", CALL("dma_start", out=qbb.t[:, :, 0:C], in_=qbT_s[:, :, tok0:tok0 + C].rearrange("j p t -> p j t")),
                              dbs("qbT", tok0, C), [qbb.b])
                        P.dma("sp", CALL("dma_start", out=kbb.t[:, :, 0:C], in_=kbT_s[:, :, tok0:tok0 + C].rearrange("j p t -> p j t")),
                              dbs("kbT", tok0, C), [kbb.b])
                        P.dma("sp", CALL("dma_start", out=kbt.t[0:C, :], in_=kb_s[tok0:tok0 + C, :]), dbs("kb", tok0, C), [kbt.b])
                        P.dma("sp", CALL("dma_start", out=vbt.t[0:C, :], in_=vb_s[tok0:tok0 + C, :]), dbs("vb", tok0, C), [vbt.b])
                        P.dma("sp", CALL("dma_start", out=gbt.t[0:C, :], in_=gb_s[tok0:tok0 + C, :]), dbs("gb", tok0, C), [gbt.b])
                        P.pe(CALL("matmul", psA.t[0:C, :], lhsT=aug.t[:, 0:C], rhs=wal.t[:, :], start=True, stop=True),
                             [aug.b, wal.b], [psA.b])
                        P.act(CALL("activation", out=ee.t[0:C, :], in_=psA.t[0:C, :], func=AF.Exp, scale=-1.0), [psA.b], [ee.b])
                        P.act(CALL("activation", out=la.t[0:C, :], in_=ee.t[0:C, :], func=AF.Ln, bias=one_c[0:C, :], scale=1.0),
                              [ee.b, cst.b], [la.b])
                        P.pe(CALL("matmul", psA.t[0:C, :], lhsT=ltri_f[0:C, 0:C], rhs=la.t[0:C, :], start=True, stop=True),
                             [la.b, cst.b], [psA.b])
                        for h in range(4):
                            P.pe(CALL("matmul", psC.t[:, h * 128:h * 128 + C], lhsT=la.t[0:C, h * 128:(h + 1) * 128],
                                                         rhs=utri_f[0:C, 0:C], start=True, stop=True), [la.b, cst.b], [psC.b])
                        psC3 = psC.t[:, :].rearrange("p (h t) -> p h t", h=4)
                        P.act(CALL("activation", out=Er.t[0:C, :], in_=psA.t[0:C, :], func=AF.Exp), [psA.b], [Er.b])
                        P.act(CALL("activation", out=Eq.t[:, :, 0:C], in_=psC3[:, :, 0:C], func=AF.Exp), [psC.b], [Eq.b])
                        P.act(CALL("activation", out=Ek.t[:, :, 0:C], in_=psC3[:, :, 0:C], func=AF.Exp, scale=-1.0), [psC.b], [Ek.b])
                        P.dve(CALL("scalar_tensor_tensor", out=qt.t[:, :, 0:C], in0=qbb.t[:, :, 0:C], scalar=128.0 ** -0.5,
                                                               in1=Eq.t[:, :, 0:C], op0=ALU.mult, op1=ALU.mult), [qbb.b, Eq.b], [qt.b])
                        P.dve(CALL("tensor_tensor", out=kt.t[:, :, 0:C], in0=kbb.t[:, :, 0:C], in1=Ek.t[:, :, 0:C], op=ALU.mult),
                              [kbb.b, Ek.b], [kt.b])
                        P.dve(CALL("tensor_tensor", out=kp.t[0:C, :], in0=kbt.t[0:C, :], in1=Er.t[0:C, :], op=ALU.mult),
                              [kbt.b, Er.b], [kp.b])
                        for h in range(4):
                            P.pe(CALL("matmul", psT_.t[0:C, h * 128:h * 128 + C], lhsT=kt.t[:, h, 0:C], rhs=qt.t[:, h, 0:C],
                                                         start=True, stop=True), [kt.b, qt.b], [psT_.b])
                        psT3 = psT_.t[:, :].rearrange("p (h t) -> p h t", h=4)
                        P.dve(CALL("tensor_tensor", out=attm.t[0:C, :, 0:C], in0=psT3[0:C, :, 0:C],
                                                        in1=cst.t[0:C, C_LE:C_LE + C].unsqueeze(1).broadcast_to([C, 4, C]), op=ALU.mult),
                              [psT_.b, cst.b], [attm.b])
                        for h in range(4):
                            P.pe(CALL("matmul", psO.t[0:C, h * 256:(h + 1) * 256], lhsT=attm.t[0:C, h, 0:C],
                                                         rhs=vbt.t[0:C, h * 256:(h + 1) * 256], start=True, stop=False),
                                 [attm.b, vbt.b], [psO.b])
                            P.pe(CALL("matmul", psO.t[0:C, h * 256:(h + 1) * 256], lhsT=qt.t[:, h, 0:C], rhs=Sb.t[:, h, :],
                                                         start=False, stop=True), [qt.b, Sb.b], [psO.b])
                        for h in range(4):
                            P.pe(CALL("matmul", psS.t[:, h * 256:(h + 1) * 256], lhsT=kp.t[0:C, h * 128:(h + 1) * 128],
                                                         rhs=vbt.t[0:C, h * 256:(h + 1) * 256], start=True, stop=True),
                                 [kp.b, vbt.b], [psS.b])
                        for h in range(4):
                            P.dve(CALL("scalar_tensor_tensor", out=Sf.t[:, h, :], in0=Sf.t[:, h, :], scalar=Eq.t[:, h, C - 1:C],
                                                                        in1=psS.t[:, h * 256:(h + 1) * 256], op0=ALU.mult, op1=ALU.add),
                                  [Sf.b, Eq.b, psS.b, Sb.b], [Sf.b])
                        P.act(CALL("activation", out=Sb.t[:, :, :], in_=Sf.t[:, :, :], func=AF.Copy), [Sf.b, psO.b], [Sb.b])
                        for h in range(4):
                            P.act(CALL("activation", out=jk3.t[0:C, :], in_=psO.t[0:C, h * 256:(h + 1) * 256], func=AF.Square,
                                                              accum_out=gst.t[0:C, h:h + 1]), [psO.b], [jk3.b, gst.b])
                        P.act(CALL("activation", out=gst.t[0:C, 4:8], in_=gst.t[0:C, 0:4], func=AF.Ln, bias=eps_c[0:C, :],
                                                     scale=1.0 / 256), [gst.b, cst.b], [gst.b])
                        P.act(CALL("activation", out=gst.t[0:C, 8:12], in_=gst.t[0:C, 4:8], func=AF.Exp, scale=-0.5), [gst.b], [gst.b])
                        for h in range(4):
                            P.dve(CALL("scalar_tensor_tensor", out=onr.t[0:C, h * 256:(h + 1) * 256],
                                                                        in0=psO.t[0:C, h * 256:(h + 1) * 256],
                                                                        scalar=gst.t[0:C, 8 + h:9 + h], in1=gnb.t[0:C, :],
                                                                        op0=ALU.mult, op1=ALU.mult), [psO.b, gst.b, gnb.b], [onr.b])
                        P.act(CALL("activation", out=sgt.t[0:C, :], in_=gbt.t[0:C, :], func=AF.Silu), [gbt.b], [sgt.b])
                        P.dve(CALL("tensor_tensor", out=obb.t[0:C, :], in0=onr.t[0:C, :], in1=sgt.t[0:C, :], op=ALU.mult),
                              [onr.b, sgt.b], [obb.b])
                        for kc in range(8):
                            P.pe(CALL("transpose", psX.t[:, kc * 128:kc * 128 + C], obb.t[0:C, kc * 128:(kc + 1) * 128],
                                                              ident_bf.t[0:C, 0:C]), [obb.b, ident_bf.b], [psX.b])
                        P.dve(CALL("tensor_copy", out=obT.t[:, :, 0:C],
                                                      in_=psX.t[:, :].rearrange("p (k t) -> p k t", k=8)[:, :, 0:C]), [psX.b], [obT.b])
                        P.dma("sp", CALL("dma_start", out=obT_s[:, :, tok0:tok0 + C].rearrange("k p t -> p k t"),
                                                          in_=obT.t[:, :, 0:C]), [obT.b], dbs("obT", tok0, C))

                    P.pool(CALL("memset", Sf.t[:, :, :], 0.0), [], [Sf.b])
                    P.pool(CALL("memset", Sb.t[:, :, :], 0.0), [], [Sb.b])
                    for i in range(NBLK):
                        gla_chunk(i * 128, 128)
                    P.dma("sp", CALL("dma_start", out=glap_d.rearrange("h d v -> d h v"), in_=Sf.t[:, :, :]), [Sf.b], [B_gla_out])
                    for b in range(NB_S):
                        P.dma("sp", CALL("dma_start", out=Sf.t[:, :, :], in_=st_d[b].rearrange("h d v -> d h v")),
                              [B_gla_out], [Sf.b])
                        P.act(CALL("activation", out=Sb.t[:, :, :], in_=Sf.t[:, :, :], func=AF.Copy), [Sf.b], [Sb.b])
                        gla_chunk(NTP + 8 * b, 8)
                        P.dma("sp", CALL("dma_start", out=glas_d[b].rearrange("h d v -> d h v"), in_=Sf.t[:, :, :]),
                              [Sf.b], [B_gla_out])
                    P.flush()

            def layer_norm(stack_tiles, y1, T, gname, bname, out_t, stt_):
                jk, = stack_tiles
                P.act(CALL("activation", out=jk.t[0:T, :], in_=y1.t[0:T, :], func=AF.Copy, accum_out=stt_.t[0:T, 0:1]),
                      [y1.b], [jk.b, stt_.b])
                P.act(CALL("activation", out=jk.t[0:T, :], in_=y1.t[0:T, :], func=AF.Square, accum_out=stt_.t[0:T, 1:2]),
                      [y1.b, jk.b], [jk.b, stt_.b])
                P.dve(CALL("tensor_scalar", out=stt_.t[0:T, 2:3], in0=stt_.t[0:T, 0:1], scalar1=1.0 / D, scalar2=None,
                                                op0=ALU.mult), [stt_.b], [stt_.b])
                P.dve(CALL("tensor_tensor", out=stt_.t[0:T, 3:4], in0=stt_.t[0:T, 2:3], in1=stt_.t[0:T, 2:3], op=ALU.mult),
                      [stt_.b], [stt_.b])
                P.dve(CALL("scalar_tensor_tensor", out=stt_.t[0:T, 4:5], in0=stt_.t[0:T, 1:2], scalar=1.0 / D,
                                                       in1=stt_.t[0:T, 3:4], op0=ALU.mult, op1=ALU.subtract), [stt_.b], [stt_.b])
                P.act(CALL("activation", out=stt_.t[0:T, 5:6], in_=stt_.t[0:T, 4:5], func=AF.Ln, bias=eps_c[0:T, :], scale=1.0),
                      [stt_.b, cst.b], [stt_.b])
                P.act(CALL("activation", out=stt_.t[0:T, 6:7], in_=stt_.t[0:T, 5:6], func=AF.Exp, scale=-0.5), [stt_.b], [stt_.b])
                P.dve(CALL("tensor_scalar", out=out_t.t[0:T, :], in0=y1.t[0:T, :], scalar1=stt_.t[0:T, 2:3],
                                                scalar2=stt_.t[0:T, 6:7], op0=ALU.subtract, op1=ALU.mult), [y1.b, stt_.b], [out_t.b])
                P.pool(CALL("tensor_tensor", out=out_t.t[0:T, :], in0=out_t.t[0:T, :], in1=lnb[gname].t[0:T, :], op=ALU.mult),
                       [out_t.b, lnb[gname].b], [out_t.b])
                P.pool(CALL("tensor_tensor", out=out_t.t[0:T, :], in0=out_t.t[0:T, :], in1=lnb[bname].t[0:T, :], op=ALU.add),
                       [out_t.b, lnb[bname].b], [out_t.b])

            tiles4 = [(t0, 128) for t0 in range(0, NTP, 128)] + ([(NTP, NS)] if "x" not in phases else [])
            if "4" in phases:
                s4 = ExitStack()
                with s4:
                    atb = [SB(s4, "atb%d" % i, [128, 4, 128], BF16) for i in range(2)]
                    obl = [SB(s4, "obl%d" % i, [128, 8, 128], BF16) for i in range(2)]
                    gtl = [SB(s4, "gtl%d" % i, [128, 16, 128], F32) for i in range(2)]
                    xbl = [SB(s4, "xbl%d" % i, [128, D], F32) for i in range(2)]
                    sig = SB(s4, "sig", [128, 16, 128], F32)
                    t1 = SB(s4, "t1", [128, 8, 128], F32)
                    mrg = SB(s4, "mrg", [128, 8, 128], BF16)
                    y1 = SB(s4, "y1", [128, D], F32)
                    hh = SB(s4, "hh", [128, D], F32)
                    hb = SB(s4, "hb", [128, D], BF16)
                    hTt = SB(s4, "hTt", [128, 8, 128], BF16)
                    jk4 = SB(s4, "jk4", [128, D], BF16)
                    st4 = SB(s4, "st4", [128, 8], F32)
                    psa = PSB(s4, "psa", [128, 1024])
                    psb4 = PSB(s4, "psb4", [128, 1024])
                    psm = PSB(s4, "psm", [128, 1024])
                    psx = PSB(s4, "psx4", [128, 1024], BF16)

                    def load4(ti):
                        t0, T = tiles4[ti]
                        sl = ti % 2
                        P.dma("sp", CALL("dma_start", out=atb[sl].t[:, :, 0:T], in_=attnT_s[:, :, t0:t0 + T].rearrange("k p t -> p k t")),
                              dbs("attnT", t0, T), [atb[sl].b])
                        P.dma("sp", CALL("dma_start", out=obl[sl].t[:, :, 0:T], in_=obT_s[:, :, t0:t0 + T].rearrange("k p t -> p k t")),
                              dbs("obT", t0, T), [obl[sl].b])
                        P.dma("sp", CALL("dma_start", out=gtl[sl].t[:, :, 0:T], in_=gtT_s[:, :, t0:t0 + T].rearrange("k p t -> p k t")),
                              dbs("gtT", t0, T), [gtl[sl].b])
                        P.dma("sp", CALL("dma_start", out=xbl[sl].t[0:T, :], in_=x_d[t0:t0 + T, :]), [], [xbl[sl].b])

                    load4(0)
                    for ti, (t0, T) in enumerate(tiles4):
                        if ti + 1 < len(tiles4):
                            load4(ti + 1)
                        sl = ti % 2
                        at_, ob_, gt_, xb_ = atb[sl], obl[sl], gtl[sl], xbl[sl]
                        P.act(CALL("activation", out=sig.t[:, :, 0:T], in_=gt_.t[:, :, 0:T], func=AF.Sigmoid), [gt_.b], [sig.b])
                        psa3 = psa.t[:, :].rearrange("p (c t) -> p c t", c=8)
                        psb3 = psb4.t[:, :].rearrange("p (c t) -> p c t", c=8)
                        for c in range(8):
                            for kc in range(4):
                                P.pe(CALL("matmul", psa.t[:, c * 128:c * 128 + T], lhsT=wao.t[:, kc, c * 128:(c + 1) * 128],
                                                                    rhs=at_.t[:, kc, 0:T], start=(kc == 0), stop=(kc == 3)),
                                     [wao.b, at_.b], [psa.b])
                        for c in range(8):
                            for kc in range(8):
                                P.pe(CALL("matmul", psb4.t[:, c * 128:c * 128 + T], lhsT=wgo.t[:, kc, c * 128:(c + 1) * 128],
                                                                    rhs=ob_.t[:, kc, 0:T], start=(kc == 0), stop=(kc == 7)),
                                     [wgo.b, ob_.b], [psb4.b])
                        P.dve(CALL("tensor_tensor", out=t1.t[:, :, 0:T], in0=psa3[:, :, 0:T], in1=sig.t[:, 0:8, 0:T], op=ALU.mult),
                              [psa.b, sig.b], [t1.b])
                        P.dve(CALL("tensor_tensor", out=sig.t[:, 8:16, 0:T], in0=psb3[:, :, 0:T], in1=sig.t[:, 8:16, 0:T], op=ALU.mult),
                              [psb4.b, sig.b], [sig.b])
                        P.pool(CALL("tensor_tensor", out=mrg.t[:, :, 0:T], in0=t1.t[:, :, 0:T], in1=sig.t[:, 8:16, 0:T], op=ALU.add),
                               [t1.b, sig.b], [mrg.b])
                        for n in range(2):
                            for kc in range(8):
                                P.pe(CALL("matmul", psm.t[0:T, n * 512:(n + 1) * 512], lhsT=mrg.t[:, kc, 0:T],
                                                                    rhs=wo.t[:, kc, n * 512:(n + 1) * 512], start=(kc == 0), stop=(kc == 7)),
                                     [mrg.b, wo.b], [psm.b])
                        for n in range(2):
                            P.dve(CALL("scalar_tensor_tensor", out=y1.t[0:T, n * 512:(n + 1) * 512], in0=xb_.t[0:T, n * 512:(n + 1) * 512],
                                                                        scalar=ALPHA, in1=psm.t[0:T, n * 512:(n + 1) * 512], op0=ALU.mult,
                                                                        op1=ALU.add), [xb_.b, psm.b], [y1.b])
                        layer_norm((jk4,), y1, T, "ln1_g", "ln1_b", hh, st4)
                        P.dma("sp", CALL("dma_start", out=h_s[t0:t0 + T, :], in_=hh.t[0:T, :]), [hh.b], dbs("h", t0, T))
                        P.act(CALL("activation", out=hb.t[0:T, :], in_=hh.t[0:T, :], func=AF.Copy), [hh.b], [hb.b])
                        for kc in range(8):
                            P.pe(CALL("transpose", psx.t[:, kc * 128:kc * 128 + T], hb.t[0:T, kc * 128:(kc + 1) * 128],
                                                              ident_bf.t[0:T, 0:T]), [hb.b, ident_bf.b], [psx.b])
                        P.dve(CALL("tensor_copy", out=hTt.t[:, :, 0:T], in_=psx.t[:, :].rearrange("p (k t) -> p k t", k=8)[:, :, 0:T]),
                              [psx.b], [hTt.b])
                        P.dma("sp", CALL("dma_start", out=hT_s[:, :, t0:t0 + T].rearrange("k p t -> p k t"),
                                                                      in_=hTt.t[:, :, 0:T]), [hTt.b], dbs("hT", t0, T))
                    P.flush()

        if "4" in phases:
            s5 = ExitStack()
            with s5:
                wf1 = SB(s5, "wf1", [128, 8, 4096], BF16)
                wf2 = SB(s5, "wf2", [128, 32, D], BF16)
                wf1b = [Buf("wf1_%d" % i) for i in range(8)]
                wf2b = [Buf("wf2_%d" % i) for i in range(8)]
                for kc in range(8):
                    for c0 in (0, 2048):
                        P.dma("pool", CALL("dma_start", out=wf1.t[:, kc, c0:c0 + 2048],
                                                                          in_=wf1_d[kc * 128:(kc + 1) * 128, c0:c0 + 2048]),
                              [], [wf1b[kc]])
                for q in range(8):
                    P.dma("pool", CALL("dma_start", out=wf2.t[:, 4 * q:4 * q + 4, :],
                                                             in_=wf2_d[q * 512:(q + 1) * 512, :].rearrange("(kc p) n -> p kc n", p=128)),
                          [], [wf2b[q]])
                hTl = [SB(s5, "hTl%d" % i, [128, 8, 128], BF16) for i in range(2)]
                hl = [SB(s5, "hl%d" % i, [128, D], F32) for i in range(2)]
                rl = [SB(s5, "rl%d" % i, [128, 4, 128], BF16) for i in range(2)]
                hid = SB(s5, "hid", [128, 32, 128], BF16)
                y2 = SB(s5, "y2", [128, D], F32)
                yo = SB(s5, "yo", [128, D], F32)
                jk5 = SB(s5, "jk5", [128, D], BF16)
                st5 = SB(s5, "st5", [128, 8], F32)
                psf = [PSB(s5, "psf%d" % i, [128, 512]) for i in range(3)]
                psy = PSB(s5, "psy", [128, 1024])

                def load5(ti):
                    t0, T = tiles4[ti]
                    sl = ti % 2
                    P.dma("sp", CALL("dma_start", out=hTl[sl].t[:, :, 0:T], in_=hT_s[:, :, t0:t0 + T].rearrange("k p t -> p k t")),
                          dbs("hT", t0, T), [hTl[sl].b])
                    P.dma("sp", CALL("dma_start", out=hl[sl].t[0:T, :], in_=h_s[t0:t0 + T, :]), dbs("h", t0, T), [hl[sl].b])

                load5(0)
                fi = 0
                for ti, (t0, T) in enumerate(tiles4):
                    if ti + 1 < len(tiles4):
                        load5(ti + 1)
                    sl = ti % 2
                    hT_, h_ = hTl[sl], hl[sl]
                    for f4 in range(8):
                        pf = psf[fi % 3]
                        r_ = rl[fi % 2]
                        fi += 1
                        for cc in range(4):
                            f = f4 * 4 + cc
                            for kc in range(8):
                                P.pe(CALL("matmul",
                                    pf.t[:, cc * 128:cc * 128 + T], lhsT=wf1.t[:, kc, f * 128:(f + 1) * 128], rhs=hT_.t[:, kc, 0:T],
                                    start=(kc == 0), stop=(kc == 7)), [wf1b[kc], hT_.b], [pf.b])
                        pf3 = pf.t[:, :].rearrange("p (c t) -> p c t", c=4)
                        P.act(CALL("activation", out=r_.t[:, :, 0:T], in_=pf3[:, :, 0:T], func=AF.Relu),
                              [pf.b], [r_.b])
                        P.dve(CALL("tensor_tensor", out=hid.t[:, f4 * 4:f4 * 4 + 4, 0:T], in0=r_.t[:, :, 0:T],
                                                                      in1=r_.t[:, :, 0:T], op=ALU.mult), [r_.b], [hid.b])
                    for n in range(2):
                        for kc in range(32):
                            P.pe(CALL("matmul", psy.t[0:T, n * 512:(n + 1) * 512], lhsT=hid.t[:, kc, 0:T],
                                                                rhs=wf2.t[:, kc, n * 512:(n + 1) * 512], start=(kc == 0), stop=(kc == 31)),
                                 [hid.b, wf2b[kc // 4]], [psy.b])
                    for n in range(2):
                        P.dve(CALL("scalar_tensor_tensor", out=y2.t[0:T, n * 512:(n + 1) * 512], in0=h_.t[0:T, n * 512:(n + 1) * 512],
                                                                    scalar=ALPHA, in1=psy.t[0:T, n * 512:(n + 1) * 512], op0=ALU.mult,
                                                                    op1=ALU.add), [h_.b, psy.b], [y2.b])
                    layer_norm((jk5,), y2, T, "ln2_g", "ln2_b", yo, st5)
                    P.dma("sp", CALL("dma_start", out=y_d[t0:t0 + T, :], in_=yo.t[0:T, :]), [yo.b], dbs("y", t0, T))
                P.flush()
        P.flush(final=True)
    return nc


_NC_CACHE = {}


def make_in_maps(inp, n_cores, NTP, NPOOL):
    cst = make_consts()
    ck = np.ascontiguousarray(inp["cache_k"]).reshape(NPOOL * 8, 2048)
    cv = np.ascontiguousarray(inp["cache_v"]).reshape(NPOOL * 8, 2048)
    cki = np.ascontiguousarray(inp["cache_kidx"]).reshape(NPOOL * 4, 2048)
    maps = []
    for c in range(n_cores):
        xs = np.asarray(inp["x_sample"][NB_S * c:NB_S * (c + 1)]).reshape(NS, D)
        x = np.concatenate([np.asarray(inp["x_prompt"][c]), xs], axis=0).astype(np.float32)
        m = {
            "x": np.ascontiguousarray(x),
            "xT": np.ascontiguousarray(x.T),
            "cache_k": ck, "cache_v": cv, "cache_kidx": cki,
            "state_gla": np.ascontiguousarray(inp["state_gla"][0, NB_S * c:NB_S * (c + 1)]),
            "page_table": np.ascontiguousarray(inp["page_table"][NB_S * c:NB_S * (c + 1)]).astype(np.int32),
            "w_in": np.ascontiguousarray(inp["w_in"][0]),
            "w_alpha2": np.ascontiguousarray(inp["w_alpha2"][0]),
            "b_alpha": np.ascontiguousarray(inp["b_alpha"][0]).reshape(1, 512),
            "gla_norm_g": np.ascontiguousarray(inp["gla_norm_g"][0]).reshape(1, 256),
            "w_attn_o": np.ascontiguousarray(inp["w_attn_o"][0]),
            "w_gla_o": np.ascontiguousarray(inp["w_gla_o"][0]),
            "w_out": np.ascontiguousarray(inp["w_out"][0]),
            "ln1_g": np.ascontiguousarray(inp["ln1_g"][0]).reshape(1, D),
            "ln1_b": np.ascontiguousarray(inp["ln1_b"][0]).reshape(1, D),
            "ln2_g": np.ascontiguousarray(inp["ln2_g"][0]).reshape(1, D),
            "ln2_b": np.ascontiguousarray(inp["ln2_b"][0]).reshape(1, D),
            "w_ff1": np.ascontiguousarray(inp["w_ff1"][0]),
            "w_ff2": np.ascontiguousarray(inp["w_ff2"][0]),
            "cst": cst,
        }
        maps.append(m)
    return maps


def assemble(res, n_cores, NTP):
    f = np.float32
    y = np.stack([r["y"][:NTP] for r in res]).astype(f)
    ys = np.concatenate([r["y"][NTP:].reshape(NB_S, 8, D) for r in res]).astype(f)
    kp = np.stack([r["ko"][:NTP].reshape(NTP, 2, 64) for r in res])[None].astype(f)
    vp = np.stack([r["vo"][:NTP].reshape(NTP, 2, 64) for r in res])[None].astype(f)
    kip = np.stack([r["kio"][:NTP] for r in res])[None].astype(f)
    gp = np.stack([r["gla_p"] for r in res])[None].astype(f)
    ks = np.concatenate([r["ko"][NTP:].reshape(NB_S, 8, 2, 64) for r in res])[None].astype(f)
    vs = np.concatenate([r["vo"][NTP:].reshape(NB_S, 8, 2, 64) for r in res])[None].astype(f)
    kis = np.concatenate([r["kio"][NTP:].reshape(NB_S, 8, 64) for r in res])[None].astype(f)
    gs = np.concatenate([r["gla_s"] for r in res])[None].astype(f)
    return (y, ys, kp, vp, kip, gp, ks, vs, kis, gs)


def kernel(**inputs):
    n_cores = 8
    NTP = inputs["x_prompt"].shape[1]
    NPOOL = inputs["cache_k"].shape[1]
    nc = build(NTP=NTP, NPOOL=NPOOL)
    maps = make_in_maps(inputs, n_cores, NTP, NPOOL)
    out = run_bass_kernel_spmd(nc, maps, core_ids=list(range(n_cores)))
    return assemble(out.results, n_cores, NTP)
```

```python
from contextlib import ExitStack
import numpy as np
import concourse.bass as bass
import concourse.mybir as mybir
from concourse.bass_utils import run_bass_kernel_spmd

F32 = mybir.dt.float32
BF16 = mybir.dt.bfloat16
I32 = mybir.dt.int32
AF = mybir.ActivationFunctionType
ALU = mybir.AluOpType
AX = mybir.AxisListType

D = 1024
D_IN = 6488
NS = 32
NB_S = 4
NPAGES = 128
ROUNDS = 17
ALPHA = 2.0 ** 0.25
EPS = 1e-5
NEG = -30000.0
BIG = 1.0e30


class Buf:
    __slots__ = ("name", "writer", "readers", "excl")

    def __init__(self, name, excl=False):
        self.name = name
        self.writer = None
        self.readers = []
        self.excl = excl


class Op:
    __slots__ = ("eng", "fn", "deps", "signal", "sem", "val", "is_dma", "lane")

    def __init__(self, eng, fn, is_dma):
        self.eng = eng
        self.fn = fn
        self.deps = []
        self.signal = False
        self.sem = None
        self.val = 0
        self.is_dma = is_dma
        self.lane = None


ENGS = ("pe", "act", "dve", "pool", "sp")
EPOCH = 12000


class Prog:
    def __init__(self, nc, lanes=8):
        self.nc = nc
        self.ops = []
        self.sems = {}
        self.cnt = {e: 0 for e in ENGS}
        self.waited = {e: {} for e in ENGS}
        self.lanes = {}
        self.lane_rr = {}
        for q in ("sp", "pool", "act"):
            self.lanes[q] = [[nc.alloc_semaphore("ln_%s_%d" % (q, i)), 0] for i in range(lanes)]
            self.lane_rr[q] = 0
        self.last_sig = {e: None for e in ENGS}
        self.all_dma = []
        self.fence_deps = []
        self.n_ops = 0

    def _add(self, eng, fn, reads, writes, is_dma=False):
        op = Op(eng, fn, is_dma)
        deps = set()
        for b in reads:
            if b.writer is not None:
                deps.add(b.writer)
            if b.excl:
                for r in b.readers:
                    if r.eng != eng:
                        deps.add(r)
        for b in writes:
            if b.writer is not None:
                deps.add(b.writer)
            for r in b.readers:
                deps.add(r)
        op.deps = list(deps)
        for b in reads:
            b.readers.append(op)
        for b in writes:
            b.writer = op
            b.readers = []
        self.ops.append(op)
        return op

    def pe(self, fn, reads=(), writes=()):
        return self._add("pe", fn, reads, writes)

    def act(self, fn, reads=(), writes=()):
        return self._add("act", fn, reads, writes)

    def dve(self, fn, reads=(), writes=()):
        return self._add("dve", fn, reads, writes)

    def pool(self, fn, reads=(), writes=()):
        return self._add("pool", fn, reads, writes)

    def dma(self, q, fn, reads=(), writes=()):
        import os
        if q in os.environ.get("SKIPDMA", "").split(","):
            return None
        return self._add(q, fn, reads, writes, is_dma=True)

    def _sem_for(self, eng):
        ep = self.cnt[eng] // EPOCH
        key = (eng, ep)
        if key not in self.sems:
            self.sems[key] = self.nc.alloc_semaphore("s_%s_%d" % (eng, ep))
        return self.sems[key], ep

    def flush(self, final=False):
        nc = self.nc
        ops = self.ops
        self.ops = []
        if not ops and not final:
            return
        self.n_ops += len(ops)
        needed = set()
        for op in ops:
            for d in op.deps:
                if d.is_dma:
                    continue
                if d.eng == "pe" and op.eng == "pe" and not op.is_dma:
                    continue
                needed.add(d)
        last = {}
        for op in ops:
            if not op.is_dma:
                last[op.eng] = op
        for op in last.values():
            needed.add(op)
        for op in ops:
            if op.is_dma:
                lanes = self.lanes[op.eng]
                li = self.lane_rr[op.eng]
                self.lane_rr[op.eng] = (li + 1) % len(lanes)
                lane = lanes[li]
                op.lane = (lane[0], lane[1])
                lane[1] += 16
                op.sem = lane[0]
                op.val = lane[1]
                self.all_dma.append(op)
            elif op in needed and op.sem is None:
                sem, ep = self._sem_for(op.eng)
                self.cnt[op.eng] += 1
                op.sem = sem
                op.val = self.cnt[op.eng] - ep * EPOCH
                op.signal = True
                self.last_sig[op.eng] = op
        streams = {e: [] for e in ENGS}
        for op in ops:
            streams[op.eng].append(op)
        fence = self.fence_deps

        def emit_stream(eng_name, e):
            waited = self.waited[eng_name]

            def wait(sem, val):
                if waited.get(sem.name, 0) >= val:
                    return
                waited[sem.name] = val
                e.wait_ge(sem, val)

            first = True
            for op in streams[eng_name]:
                if first:
                    for d in fence:
                        if d.sem is not None:
                            wait(d.sem, d.val)
                    first = False
                for d in op.deps:
                    if d.sem is None:
                        continue
                    if (not d.is_dma) and d.eng == "pe" and eng_name == "pe" and not op.is_dma:
                        continue
                    wait(d.sem, d.val)
                if op.is_dma:
                    if op.lane[1] > 0:
                        wait(op.lane[0], op.lane[1])
                    ins = op.fn(e)
                    ins.then_inc(op.sem, 16)
                else:
                    ins = op.fn(e)
                    if op.signal:
                        ins.then_inc(op.sem, 1)
            if final and eng_name == "sp":
                for q in self.lanes:
                    for sem, tot in self.lanes[q]:
                        if tot > 0:
                            wait(sem, tot)

        with nc.Block() as block:
            @block.tensor
            def _(e):
                emit_stream("pe", e)

            @block.scalar
            def _(e):
                emit_stream("act", e)

            @block.vector
            def _(e):
                emit_stream("dve", e)

            @block.gpsimd
            def _(e):
                emit_stream("pool", e)

            @block.sync
            def _(e):
                emit_stream("sp", e)
        fd = [op for op in self.last_sig.values() if op is not None]
        latest = {}
        for op in self.all_dma:
            latest[op.sem.name] = op
        fd += list(latest.values())
        self.fence_deps = fd
        self.all_dma = list(latest.values())


def CALL(name, *a, **k):
    return lambda e: getattr(e, name)(*a, **k)


class TT:
    __slots__ = ("t", "b")

    def __init__(self, t, name, excl=False):
        self.t = t
        self.b = Buf(name, excl)


C_ID, C_UT, C_LT, C_LE, C_CN, C_G, C_PW, C_EPS, C_ONE = 0, 128, 256, 384, 512, 640, 768, 800, 801
NCST = 832


def make_consts():
    c = np.zeros((128, NCST), np.float32)
    p = np.arange(128)[:, None]
    j = np.arange(128)[None, :]
    c[:, C_ID:C_ID + 128] = (p == j)
    c[:, C_UT:C_UT + 128] = np.where(p <= j, -1.0 / 16, 0.0)
    c[:, C_LT:C_LT + 128] = np.where(p > j, -1.0 / 16, 0.0)
    c[:, C_LE:C_LE + 128] = (p <= j)
    c[:, C_CN:C_CN + 128] = np.where(j > p, -BIG, 0.0)
    c[:, C_G:C_G + 128] = ((p // 32 == j // 32) & (p % 8 == j % 8))
    c[:, C_PW:C_PW + 32] = 2.0 ** -(np.arange(32)[None, :] + 1.0)
    c[:, C_EPS] = EPS
    c[:, C_ONE] = 1.0
    return c


O_QA, O_KA, O_VA, O_QI, O_KI, O_WI, O_QB, O_KB, O_VB, O_GB, O_AB, O_GA = (
    0, 512, 640, 768, 1280, 1344, 1352, 1864, 2376, 3400, 4424, 4440)


def w_layout():
    fm, tm = [], []
    off = 0

    def add(lst, name, pieces):
        nonlocal off
        w = sum(b - a for a, b in pieces)
        lst.append((name, off, w, pieces))
        off += w

    for j in range(4):
        add(fm, "qa%d" % j, [(O_QA + 64 * j, O_QA + 64 * j + 64), (O_QA + 64 * (4 + j), O_QA + 64 * (4 + j) + 64)])
    add(fm, "ka", [(O_KA, O_KA + 128)])
    for j in range(4):
        add(fm, "qi%d" % j, [(O_QI + 128 * j, O_QI + 128 * j + 128)])
    add(fm, "ki", [(O_KI, O_KI + 64), (O_KI, O_KI + 64)])
    for j in range(4):
        add(fm, "qb%d" % j, [(O_QB + 128 * j, O_QB + 128 * j + 128)])
    for j in range(4):
        add(fm, "kb%d" % j, [(O_KB + 128 * j, O_KB + 128 * j + 128)])
    add(fm, "ab", [(O_AB, O_AB + 16)])
    for j in range(16):
        add(fm, "gt%d" % j, [(O_GA + 128 * j, O_GA + 128 * j + 128)])
    add(tm, "ta", [(O_KA, O_KA + 256), (O_KI, O_KI + 72)])
    add(tm, "tkb", [(O_KB, O_KB + 512)])
    add(tm, "tvb0", [(O_VB, O_VB + 512)])
    add(tm, "tvb1", [(O_VB + 512, O_VB + 1024)])
    add(tm, "tgb0", [(O_GB, O_GB + 512)])
    add(tm, "tgb1", [(O_GB + 512, O_GB + 1024)])
    return fm, tm, off


def build(NTP=4096, NPOOL=5120, phases="12345", dbg=False):
    nc = bass.Bass("TRN2", target_bir_lowering=False)
    P = Prog(nc)
    NT = NTP + NS
    NBLK = NTP // 128
    TOPK = min(256, NTP // 4)
    TOPK_S = min(256, (NPAGES * 128 + 8) // 4)

    def din(name, shape, dt=F32):
        return nc.dram_tensor(name, list(shape), dt, kind="ExternalInput")

    def dout(name, shape, dt=F32):
        return nc.dram_tensor(name, list(shape), dt, kind="ExternalOutput")

    def dscr(name, shape, dt):
        return nc.dram_tensor(name, list(shape), dt, kind="Internal")

    xT_d = din("xT", [D, NT])
    x_d = din("x", [NT, D])
    ck_d = din("cache_k", [NPOOL * 8, 2048])
    cv_d = din("cache_v", [NPOOL * 8, 2048])
    cki_d = din("cache_kidx", [NPOOL * 4, 2048])
    st_d = din("state_gla", [NB_S, 4, 128, 256])
    pt_d = din("page_table", [NB_S, NPAGES], I32)
    w_in_d = din("w_in", [D, D_IN])
    wal_d = din("w_alpha2", [16, 512])
    bal_d = din("b_alpha", [1, 512])
    gng_d = din("gla_norm_g", [1, 256])
    wao_d = din("w_attn_o", [512, D])
    wgo_d = din("w_gla_o", [D, D])
    wo_d = din("w_out", [D, D])
    ln_d = {n: din(n, [1, D]) for n in ("ln1_g", "ln1_b", "ln2_g", "ln2_b")}
    wf1_d = din("w_ff1", [D, 4096])
    wf2_d = din("w_ff2", [4096, D])
    cst_d = din("cst", [128, NCST])

    y_d = dout("y", [NT, D])
    ko_d = dout("ko", [NT, 128])
    vo_d = dout("vo", [NT, 128])
    kio_d = dout("kio", [NT, 64])
    glap_d = dout("gla_p", [4, 128, 256])
    glas_d = dout("gla_s", [NB_S, 4, 128, 256])

    qaT_s = dscr("qaT_s", [4, 128, NT], BF16)
    qiT_s = dscr("qiT_s", [4, 128, NT], BF16)
    wi_s = dscr("wi_s", [NT, 8], F32)
    qbT_s = dscr("qbT_s", [4, 128, NT], BF16)
    kbT_s = dscr("kbT_s", [4, 128, NT], BF16)
    abT_s = dscr("abT_s", [16, NT], BF16)
    gtT_s = dscr("gtT_s", [16, 128, NT], F32)
    kb_s = dscr("kb_s", [NT, 512], BF16)
    vb_s = dscr("vb_s", [NT, 1024], BF16)
    gb_s = dscr("gb_s", [NT, 1024], F32)
    attnT_s = dscr("attnT_s", [4, 128, NT], BF16)
    obT_s = dscr("obT_s", [8, 128, NT], BF16)
    hT_s = dscr("hT_s", [8, 128, NT], BF16)
    h_s = dscr("h_s", [NT, D], F32)

    DBD = {}
    NJ = {"qaT": 4, "qiT": 4, "wi": 1, "qbT": 4, "kbT": 4, "abT": 1, "gtT": 16, "kb": 1, "vb": 2, "gb": 2, "attnT": 4,
          "obT": 1, "hT": 1, "h": 1, "ko": 1, "vo": 1, "kio": 1, "y": 1}
    B_gla_out = Buf("gla_out")

    def dbs(name, tok0, n, j=None):
        js = range(NJ[name]) if j is None else [j]
        out = []
        for jj in js:
            for tl in range(tok0 // 128, (tok0 + n - 1) // 128 + 1):
                key = (name, jj, tl)
                if key not in DBD:
                    DBD[key] = Buf("%s_%d_%d" % key)
                out.append(DBD[key])
        return out

    g = ExitStack()

    def SB(stack, name, shape, dt):
        return TT(stack.enter_context(nc.sbuf_tensor("sb_" + name, list(shape), dt)), name)

    def PSB(stack, name, shape, dt=F32):
        return TT(stack.enter_context(nc.psum_tensor("pp_" + name, list(shape), dt)), name, True)

    with g:
        cst = SB(g, "cst", [128, NCST], F32)
        ident_bf = SB(g, "ident_bf", [128, 128], BF16)
        i4_bf = SB(g, "i4_bf", [128, 4, 128], BF16)
        mle_bf = SB(g, "mle_bf", [128, 128], BF16)
        ones_bf = SB(g, "ones_bf", [128, 128], BF16)
        P.dma("sp", CALL("dma_start", out=cst.t[:], in_=cst_d[:, :]), [], [cst.b])
        P.dve(CALL("tensor_copy", out=ident_bf.t[:], in_=cst.t[:, C_ID:C_ID + 128]), [cst.b], [ident_bf.b])
        for k in range(4):
            P.dve(CALL("tensor_copy", out=i4_bf.t[:, k, :], in_=cst.t[:, C_ID:C_ID + 128]), [cst.b], [i4_bf.b])
        P.dve(CALL("tensor_copy", out=mle_bf.t[:], in_=cst.t[:, C_LE:C_LE + 128]), [cst.b], [mle_bf.b])
        P.pool(CALL("memset", ones_bf.t[:], 1.0), [], [ones_bf.b])
        ident_f = cst.t[:, C_ID:C_ID + 128]
        utri_f = cst.t[:, C_UT:C_UT + 128]
        ltri_f = cst.t[:, C_LT:C_LT + 128]
        cneg_f = cst.t[:, C_CN:C_CN + 128]
        eps_c = cst.t[:, C_EPS:C_EPS + 1]
        one_c = cst.t[:, C_ONE:C_ONE + 1]

        lnb = {n: SB(g, "lnb_" + n, [128, D], F32) for n in ln_d}
        s12 = ExitStack()
        with s12:
            kA = SB(s12, "kA", [128, NT], BF16)
            kB = SB(s12, "kB", [128, NT], BF16)
            kiA = SB(s12, "kiA", [128, NT], BF16)
            kiB = SB(s12, "kiB", [128, NT], BF16)
            vaug = SB(s12, "vaug", [128, NBLK, 2, 65], BF16)
            for tt_ in (kA, kB, kiA, kiB):
                P.pool(CALL("memset", tt_.t[:], 0.0), [], [tt_.b])
            P.pool(CALL("memset", vaug.t[:], 1.0), [], [vaug.b])

            if "1" in phases:
                s1 = ExitStack()
                with s1:
                    fm, tm, ncol = w_layout()
                    w_sb = SB(s1, "w_sb", [128, 8, ncol], BF16)
                    w_src = w_in_d.rearrange("(kc p) n -> p kc n", p=128)
                    wb = {}
                    for lst in (fm, tm):
                        for (name, off, wd, pieces) in lst:
                            b = Buf("w_" + name)
                            wb[name] = b
                            o = off
                            for (a0, a1) in pieces:
                                P.dma("pool", CALL("dma_start",
                                    out=w_sb.t[:, :, o:o + (a1 - a0)], in_=w_src[:, :, a0:a1]), [], [b])
                                o += a1 - a0
                    import os as _os
                    NXS = int(_os.environ.get("XSLOTS", "2"))
                    xts = [SB(s1, "xT%d" % i, [128, 8, 512], BF16) for i in range(NXS)]
                    stg_b = [SB(s1, "stgb%d" % i, [128, 512], BF16) for i in range(4)]
                    stg_f = [SB(s1, "stgf%d" % i, [128, 512], F32) for i in range(4)]
                    pss = [PSB(s1, "ps1_%d" % i, [128, 512]) for i in range(6)]
                    rr = {"ps": 0, "sb": 0, "sf": 0, "ev": 0}
                    xT_src = xT_d.rearrange("(kc p) t -> p kc t", p=128)

                    def nxt(key, n):
                        v = rr[key]
                        rr[key] = (v + 1) % n
                        return v

                    def evac(out_ap, in_ap, reads, writes):
                        if nxt("ev", 2) == 0:
                            P.act(CALL("activation", out=out_ap, in_=in_ap, func=AF.Copy), reads, writes)
                        else:
                            P.dve(CALL("tensor_copy", out=out_ap, in_=in_ap), reads, writes)

                    STW = int(_os.environ.get("STW", "512"))
                    sts = [(t0, min(STW, NTP - t0)) for t0 in range(0, NTP, STW)] + [(NTP, NS)]

                    def load_x(si):
                        t0, W = sts[si]
                        xt = xts[si % NXS]
                        P.dma("pool", CALL("dma_start", out=xt.t[:, :, 0:W], in_=xT_src[:, :, t0:t0 + W]), [], [xt.b])

                    load_x(0)
                    for si, (t0, W) in enumerate(sts):
                        if si + 1 < len(sts):
                            load_x(si + 1)
                        xt = xts[si % NXS]
                        for (name, off, m, pieces) in fm:
                            ps = pss[nxt("ps", 6)]
                            for kc in range(8):
                                P.pe(CALL("matmul",
                                    ps.t[0:m, 0:W], lhsT=w_sb.t[:, kc, off:off + m], rhs=xt.t[:, kc, 0:W],
                                    start=(kc == 0), stop=(kc == 7)), [xt.b, wb[name]], [ps.b])
                            if name == "ka":
                                evac(kA.t[0:64, t0:t0 + W], ps.t[0:64, 0:W], [ps.b], [kA.b])
                                evac(kB.t[64:128, t0:t0 + W], ps.t[64:128, 0:W], [ps.b], [kB.b])
                            elif name == "ki":
                                evac(kiA.t[0:64, t0:t0 + W], ps.t[0:64, 0:W], [ps.b], [kiA.b])
                                evac(kiB.t[64:128, t0:t0 + W], ps.t[64:128, 0:W], [ps.b], [kiB.b])
                            else:
                                kind = name[:2]
                                j = int(name[2:]) if len(name) > 2 else 0
                                if kind == "gt":
                                    sg = stg_f[nxt("sf", 4)]
                                    dst = gtT_s[j, :, t0:t0 + W]
                                    dbn = "gtT"
                                else:
                                    sg = stg_b[nxt("sb", 4)]
                                    dst = {"qa": qaT_s, "qi": qiT_s, "qb": qbT_s, "kb": kbT_s}[kind][j, :, t0:t0 + W] \
                                        if kind != "ab" else abT_s[:, t0:t0 + W]
                                    dbn = {"qa": "qaT", "qi": "qiT", "qb": "qbT", "kb": "kbT", "ab": "abT"}[kind]
                                evac(sg.t[0:m, 0:W], ps.t[0:m, 0:W], [ps.b], [sg.b])
                                P.dma("sp", CALL("dma_start", out=dst, in_=sg.t[0:m, 0:W]),
                                      [sg.b], dbs(dbn, t0, W, j if NJ[dbn] > 1 else 0))
                        ntile = (W + 127) // 128
                        for tt_i in range(ntile):
                            T = min(128, W - tt_i * 128)
                            tk0 = t0 + tt_i * 128
                            for (name, off, n, pieces) in tm:
                                ps = pss[nxt("ps", 6)]
                                for kc in range(8):
                                    P.pe(CALL("matmul",
                                        ps.t[0:T, 0:n], lhsT=xt.t[:, kc, tt_i * 128:tt_i * 128 + T],
                                        rhs=w_sb.t[:, kc, off:off + n], start=(kc == 0), stop=(kc == 7)),
                                        [xt.b, wb[name]], [ps.b])
                                if name == "ta":
                                    sg = stg_f[nxt("sf", 4)]
                                    evac(sg.t[0:T, 0:n], ps.t[0:T, 0:n], [ps.b], [sg.b])
                                    for (dst, c0, c1, dbn) in ((ko_d, 0, 128, "ko"), (vo_d, 128, 256, "vo"),
                                                               (kio_d, 256, 320, "kio"), (wi_s, 320, 328, "wi")):
                                        P.dma("sp", CALL("dma_start",
                                            out=dst[tk0:tk0 + T, :], in_=sg.t[0:T, c0:c1]), [sg.b], dbs(dbn, tk0, T))
                                    if tk0 < NTP:
                                        blk = tk0 // 128
                                        P.dve(CALL("tensor_copy",
                                            out=vaug.t[:, blk, :, 1:65],
                                            in_=ps.t[:, 128:256].rearrange("p (h d) -> p h d", h=2)), [ps.b], [vaug.b])
                                else:
                                    if name.startswith("tgb"):
                                        sg = stg_f[nxt("sf", 4)]
                                        dst, dbn = gb_s, "gb"
                                    else:
                                        sg = stg_b[nxt("sb", 4)]
                                        dst, dbn = (kb_s, "kb") if name == "tkb" else (vb_s, "vb")
                                    c0 = 512 if name.endswith("1") else 0
                                    evac(sg.t[0:T, 0:n], ps.t[0:T, 0:n], [ps.b], [sg.b])
                                    P.dma("sp", CALL("dma_start",
                                        out=dst[tk0:tk0 + T, c0:c0 + n], in_=sg.t[0:T, 0:n]), [sg.b],
                                        dbs(dbn, tk0, T, (c0 // 512) if NJ[dbn] > 1 else 0))
                    P.flush()

            if "2" in phases:
                s2 = ExitStack()
                with s2:
                    NK = NTP
                    isc = [SB(s2, "isc%d" % i, [128, NK], F32) for i in range(2)]
                    mbs = [SB(s2, "mb%d" % i, [128, NK], BF16) for i in range(2)]
                    junk = SB(s2, "junk2", [128, NK], BF16)
                    rsl = [SB(s2, "rsl%d" % i, [128, 512], BF16) for i in range(4)]
                    ptl = [SB(s2, "ptl%d" % i, [128, 512], BF16) for i in range(4)]
                    qib = [SB(s2, "qib%d" % i, [128, 4, 128], BF16) for i in range(2)]
                    qab = [SB(s2, "qab%d" % i, [128, 4, 128], BF16) for i in range(2)]
                    wib = [SB(s2, "wib%d" % i, [128, 8], F32) for i in range(2)]
                    dgs = [SB(s2, "dg%d" % i, [128, 8, 128], BF16) for i in range(2)]
                    stt = [SB(s2, "st%d" % i, [128, 64], F32) for i in range(2)]
                    ost = [[SB(s2, "ost%d_%d" % (i, k), [128, 260], F32) for k in range(2)] for i in range(2)]
                    atk = [SB(s2, "atk%d" % i, [128, 512], BF16) for i in range(2)]
                    atT = SB(s2, "atT", [128, 4, 128], BF16)
                    rct = SB(s2, "rct", [128, 8], F32)
                    tmpd = SB(s2, "tmpd", [128, 128], F32)
                    psd = [PSB(s2, "psd%d" % i, [128, 512]) for i in range(2)]
                    psi = [PSB(s2, "psi%d" % i, [128, 512]) for i in range(2)]
                    pss_ = [PSB(s2, "pss%d" % i, [128, 512]) for i in range(2)]
                    pos_ = [PSB(s2, "pos%d" % i, [128, 512]) for i in range(2)]
                    rr2 = {"d": 0, "i": 0, "s": 0, "r": 0, "p": 0}

                    def nx2(k, n):
                        v = rr2[k]
                        rr2[k] = (v + 1) % n
                        return v

                    S_MIN1, S_MIN2, S_MAX, S_W0, S_MID, S_CNT, S_U, S_THR, S_WALL = 0, 1, 2, 3, 4, 5, 6, 7, 8
                    WSCALE = (8.0 ** -0.5) / 8.0

                    def stage_a(i):
                        sl = i % 2
                        q0 = i * 128
                        nk = (i + 1) * 128
                        qi_, qa_, wi_, dg_, I_, st_ = qib[sl], qab[sl], wib[sl], dgs[sl], isc[sl], stt[sl]
                        P.dma("sp", CALL("dma_start", out=qi_.t[:], in_=qiT_s[:, :, q0:q0 + 128].rearrange("j p t -> p j t")),
                              dbs("qiT", q0, 128), [qi_.b])
                        P.dma("sp", CALL("dma_start", out=qa_.t[:], in_=qaT_s[:, :, q0:q0 + 128].rearrange("j p t -> p j t")),
                              dbs("qaT", q0, 128), [qa_.b])
                        P.dma("sp", CALL("dma_start", out=wi_.t[:], in_=wi_s[q0:q0 + 128, :]), dbs("wi", q0, 128), [wi_.b])
                        for h in range(8):
                            P.pool(CALL("tensor_scalar", out=dg_.t[:, h, :], in0=ident_f, scalar1=wi_.t[:, h:h + 1],
                                                                 scalar2=WSCALE, op0=ALU.mult, op1=ALU.mult),
                                   [wi_.b, cst.b], [dg_.b])
                        nch = (nk + 511) // 512
                        steps = [(c, h) for c in range(nch) for h in range(8)]
                        pds = {}
                        pIs = {}

                        def dots(k):
                            c, h = steps[k]
                            k0 = c * 512
                            Wc = min(512, nk - k0)
                            pd = psd[nx2("d", 2)]
                            pds[k] = pd
                            kis = kiA if h % 2 == 0 else kiB
                            P.pe(CALL("matmul", pd.t[:, 0:Wc], lhsT=qi_.t[:, h // 2, :], rhs=kis.t[:, k0:k0 + Wc], start=True, stop=True),
                                 [qi_.b, kis.b], [pd.b])

                        dots(0)
                        for k, (c, h) in enumerate(steps):
                            k0 = c * 512
                            Wc = min(512, nk - k0)
                            if h == 0:
                                pIs[c] = psi[nx2("i", 2)]
                            pI = pIs[c]
                            pd = pds[k]
                            r_ = rsl[nx2("r", 4)]
                            P.act(CALL("activation", out=r_.t[:, 0:Wc], in_=pd.t[:, 0:Wc], func=AF.Relu), [pd.b], [r_.b])
                            if k + 1 < len(steps):
                                dots(k + 1)
                            P.pe(CALL("matmul", pI.t[:, 0:Wc], lhsT=dg_.t[:, h, :], rhs=r_.t[:, 0:Wc], start=(h == 0), stop=(h == 7)),
                                 [dg_.b, r_.b], [pI.b])
                            if h == 7:
                                P.act(CALL("activation", out=I_.t[:, k0:k0 + Wc], in_=pI.t[:, 0:Wc], func=AF.Copy), [pI.b], [I_.b])
                        d0 = i * 128
                        P.dve(CALL("tensor_tensor", out=tmpd.t[:], in0=I_.t[:, d0:d0 + 128], in1=cneg_f, op=ALU.subtract),
                              [I_.b, cst.b], [tmpd.b])
                        P.dve(CALL("tensor_reduce", out=st_.t[:, S_MIN2:S_MIN2 + 1], in_=tmpd.t[:], axis=AX.X, op=ALU.min),
                              [tmpd.b], [st_.b])
                        if i > 0:
                            P.dve(CALL("tensor_reduce", out=st_.t[:, S_MIN1:S_MIN1 + 1], in_=I_.t[:, 0:d0], axis=AX.X,
                                                            op=ALU.min), [I_.b], [st_.b])
                            P.dve(CALL("tensor_tensor", out=st_.t[:, S_MIN2:S_MIN2 + 1], in0=st_.t[:, S_MIN2:S_MIN2 + 1],
                                                            in1=st_.t[:, S_MIN1:S_MIN1 + 1], op=ALU.min), [st_.b], [st_.b])
                        P.dve(CALL("tensor_tensor", out=I_.t[:, d0:d0 + 128], in0=I_.t[:, d0:d0 + 128], in1=cneg_f, op=ALU.add),
                              [I_.b, cst.b], [I_.b])
                        P.dve(CALL("tensor_reduce", out=st_.t[:, S_MAX:S_MAX + 1], in_=I_.t[:, 0:nk], axis=AX.X, op=ALU.max),
                              [I_.b], [st_.b])

                    def stage_b(i):
                        sl = i % 2
                        nk = (i + 1) * 128
                        I_, st_, mb_ = isc[sl], stt[sl], mbs[sl]
                        c_ = lambda k: st_.t[:, k:k + 1]
                        P.dve(CALL("tensor_tensor", out=c_(S_W0), in0=c_(S_MAX), in1=c_(S_MIN2), op=ALU.subtract), [st_.b], [st_.b])
                        P.dve(CALL("tensor_scalar", out=c_(S_U), in0=c_(S_W0), scalar1=-1e-3, scalar2=-1e-4, op0=ALU.mult,
                                                        op1=ALU.add), [st_.b], [st_.b])
                        P.dve(CALL("tensor_tensor", out=c_(S_MIN2), in0=c_(S_MIN2), in1=c_(S_U), op=ALU.add), [st_.b], [st_.b])
                        P.dve(CALL("tensor_tensor", out=c_(S_W0), in0=c_(S_MAX), in1=c_(S_MIN2), op=ALU.subtract), [st_.b], [st_.b])
                        P.dve(CALL("tensor_scalar", out=c_(S_W0), in0=c_(S_W0), scalar1=1.0001, scalar2=1e-6, op0=ALU.mult,
                                                        op1=ALU.add), [st_.b], [st_.b])
                        P.dve(CALL("tensor_scalar", out=st_.t[:, S_WALL:S_WALL + 32], in0=cst.t[:, C_PW:C_PW + 32],
                                                        scalar1=c_(S_W0), scalar2=None, op0=ALU.mult), [st_.b, cst.b], [st_.b])
                        P.dve(CALL("tensor_tensor", out=c_(S_MID), in0=c_(S_MIN2), in1=c_(S_WALL), op=ALU.add), [st_.b], [st_.b])
                        for r in range(ROUNDS):
                            P.dve(CALL("tensor_scalar", out=junk.t[:, 0:nk], in0=I_.t[:, 0:nk], scalar1=c_(S_MID), scalar2=0.0,
                                                            op0=ALU.is_ge, op1=ALU.add, accum_out=c_(S_CNT)),
                                  [I_.b, st_.b], [junk.b, st_.b])
                            P.dve(CALL("tensor_scalar", out=c_(S_U), in0=c_(S_CNT), scalar1=TOPK - 0.5,
                                                                 scalar2=c_(S_WALL + r), op0=ALU.is_ge, op1=ALU.mult),
                                  [st_.b], [st_.b])
                            nxt_w = S_WALL + r + 1 if r + 1 < ROUNDS else S_WALL + r
                            dst = S_MID if r + 1 < ROUNDS else S_THR
                            P.dve(CALL("scalar_tensor_tensor",
                                out=c_(dst), in0=c_(S_U), scalar=c_(nxt_w), in1=c_(S_MID), op0=ALU.subtract, op1=ALU.add),
                                [st_.b], [st_.b])
                        P.dve(CALL("tensor_scalar", out=mb_.t[:, 0:nk], in0=I_.t[:, 0:nk], scalar1=c_(S_THR), scalar2=NEG,
                                                        op0=ALU.is_lt, op1=ALU.mult), [I_.b, st_.b], [mb_.b])

                    def stage_c(i):
                        sl = i % 2
                        qa_, mb_ = qab[sl], mbs[sl]
                        steps = [(kvh, c) for kvh in range(2) for c in range(i + 1)]
                        pls = {}

                        def logits(k):
                            kvh, c = steps[k]
                            kk = kA if kvh == 0 else kB
                            k0 = c * 128
                            ps_ = pss_[nx2("s", 2)]
                            pls[k] = ps_
                            P.pe(CALL("matmul", ps_.t[:, :], lhsT=kk.t[:, k0:k0 + 128], rhs=qa_.t[:, :, :], start=True, stop=False),
                                 [kk.b, qa_.b], [ps_.b])
                            P.pe(CALL("matmul", ps_.t[:, :], lhsT=mb_.t[:, k0:k0 + 128], rhs=i4_bf.t[:, :, :], start=False, stop=True),
                                 [mb_.b, i4_bf.b], [ps_.b])

                        logits(0)
                        for k, (kvh, c) in enumerate(steps):
                            po = pos_[kvh]
                            ps_ = pls[k]
                            pt_ = ptl[nx2("p", 4)]
                            P.act(CALL("activation", out=pt_.t[:, :], in_=ps_.t[:, :], func=AF.Exp, scale=0.125), [ps_.b], [pt_.b])
                            if k + 1 < len(steps):
                                logits(k + 1)
                            for j in range(4):
                                P.pe(CALL("matmul", po.t[:, j * 65:(j + 1) * 65], lhsT=pt_.t[:, j * 128:(j + 1) * 128],
                                          rhs=vaug.t[:, c, kvh, :], start=(c == 0 and j == 0), stop=(c == i), skip_group_check=True),
                                     [vaug.b, pt_.b], [po.b])
                            if c == i:
                                o_ = ost[sl][kvh]
                                P.act(CALL("activation", out=o_.t[:, :], in_=po.t[:, 0:260], func=AF.Copy), [po.b], [o_.b])

                    def stage_c_norm(i):
                        sl = i % 2
                        at_ = atk[sl]
                        for kvh in range(2):
                            o3 = ost[sl][kvh].t[:, :].rearrange("p (j e) -> p j e", e=65)
                            P.dve(CALL("reciprocal", out=rct.t[:, kvh * 4:kvh * 4 + 4].unsqueeze(2), in_=o3[:, :, 0:1]), [ost[sl][kvh].b], [rct.b])
                            P.dve(CALL("tensor_tensor", out=at_.t[:, kvh * 256:(kvh + 1) * 256].rearrange("p (j d) -> p j d", j=4),
                                       in0=o3[:, :, 1:65], in1=rct.t[:, kvh * 4:kvh * 4 + 4].unsqueeze(2).broadcast_to([128, 4, 64]),
                                       op=ALU.mult), [ost[sl][kvh].b, rct.b], [at_.b])

                    def stage_c_T(i):
                        sl = i % 2
                        q0 = i * 128
                        at_ = atk[sl]
                        po = pos_[0]
                        psx_ = po.t.bitcast(BF16)
                        for kc in range(4):
                            P.pe(CALL("transpose", psx_[:, kc * 128:(kc + 1) * 128], at_.t[:, kc * 128:(kc + 1) * 128], ident_bf.t[:, :]),
                                 [at_.b, ident_bf.b], [po.b])
                        P.act(CALL("activation", out=atT.t[:, :, :], in_=psx_[:, 0:512].rearrange("p (k t) -> p k t", k=4), func=AF.Copy),
                              [po.b], [atT.b])
                        P.dma("sp", CALL("dma_start", out=attnT_s[:, :, q0:q0 + 128].rearrange("k p t -> p k t"), in_=atT.t[:, :, :]),
                              [atT.b], dbs("attnT", q0, 128))

                    for i in range(NBLK):
                        stage_a(i)
                        if i >= 2:
                            stage_c_T(i - 2)
                        if i >= 1:
                            stage_c(i - 1)
                        stage_b(i)
                        if i >= 1:
                            stage_c_norm(i - 1)
                    if NBLK >= 2:
                        stage_c_T(NBLK - 2)
                    stage_c(NBLK - 1)
                    stage_c_norm(NBLK - 1)
                    stage_c_T(NBLK - 1)
                    P.flush()
            if "5" in phases:
                s2b = ExitStack()
                with s2b:
                    ROUNDS_S = 24
                    NKP = NPAGES * 128
                    SEGW = NKP // 4
                    IW = SEGW + 8
                    R = SB(s2b, "R2b", [128, 3 * NKP], BF16)
                    BR0, BR1, BR2 = Buf("BR0"), Buf("BR1"), Buf("BR2")
                    O1, O2 = NKP, 2 * NKP
                    Is = SB(s2b, "Is", [128, IW], F32)
                    MBs = SB(s2b, "MBs", [128, IW], BF16)
                    pti = SB(s2b, "pti", [128, 1], I32)
                    ptf = SB(s2b, "ptf", [128, 1], F32)
                    idf = SB(s2b, "idf", [128, 16], F32)
                    idxs = [SB(s2b, "idx%d" % i, [128, 16], I32) for i in range(NB_S)]
                    qis = SB(s2b, "qis", [64, 8, NS], BF16)
                    qs0 = SB(s2b, "qs0", [128, 4, NS], BF16)
                    qs1 = SB(s2b, "qs1", [128, 4, NS], BF16)
                    wis = SB(s2b, "wis", [8, NB_S, 8], F32)
                    dss = SB(s2b, "dss", [8, NB_S, 8, 8], BF16)
                    rs2 = [SB(s2b, "rs2_%d" % i, [64, 512], BF16) for i in range(4)]
                    qisb = SB(s2b, "qisb", [64, NB_S, 64], BF16)
                    wsel = SB(s2b, "wsel", [64, NB_S, 8], BF16)
                    qsb = SB(s2b, "qsb", [128, NB_S, 64], BF16)
                    sel2 = SB(s2b, "sel2", [128, 16, 64], BF16)
                    sg2 = [SB(s2b, "sg2_%d" % i, [8, 512], F32) for i in range(4)]
                    pt2 = [SB(s2b, "pt2_%d" % i, [128, 256], BF16) for i in range(4)]
                    sst = SB(s2b, "sst", [128, 64], F32)
                    vnew = SB(s2b, "vnew", [8, 128], BF16)
                    rcp2 = SB(s2b, "rcp2", [1, 64], F32)
                    bcs2 = SB(s2b, "bcs2", [64, 64], F32)
                    as2 = SB(s2b, "as2", [64, 2, 4, 8], BF16)
                    pst = [PSB(s2b, "pst%d" % i, [128, 1024], BF16) for i in range(2)]
                    psd2 = [PSB(s2b, "psd2_%d" % i, [128, 512]) for i in range(2)]
                    psi2s = [PSB(s2b, "psi2_%d" % i, [128, 512]) for i in range(2)]
                    psi2 = psi2s[0]
                    pso2s = [psi2s[1], PSB(s2b, "pso2b", [128, 512])]
                    psm2 = PSB(s2b, "psm2", [128, 512])
                    rr3 = {"t": 0, "d": 0, "r": 0, "g": 0, "p": 0, "e": 0, "i": 0}

                    def nx3(k, n):
                        v = rr3[k]
                        rr3[k] = (v + 1) % n
                        return v

                    def evac3(out_ap, in_ap, reads, writes):
                        if nx3("e", 2) == 0:
                            P.act(CALL("activation", out=out_ap, in_=in_ap, func=AF.Copy), reads, writes)
                        else:
                            P.dve(CALL("tensor_copy", out=out_ap, in_=in_ap), reads, writes)

                    S0 = NTP
                    WSC = (8.0 ** -0.5) / 8.0
                    G_f = cst.t[:, C_G:C_G + 128]
                    P.pool(CALL("memset", qs0.t[:, :, :], 0.0), [], [qs0.b])
                    P.pool(CALL("memset", qs1.t[:, :, :], 0.0), [], [qs1.b])
                    P.pool(CALL("memset", Is.t[:, SEGW:IW], -BIG), [], [Is.b])
                    P.dma("sp", CALL("dma_start", out=qs0.t[0:64, :, :], in_=qaT_s[:, 0:64, S0:S0 + NS].rearrange("j p t -> p j t")),
                          dbs("qaT", S0, NS), [qs0.b])
                    P.dma("sp", CALL("dma_start", out=qs1.t[64:128, :, :], in_=qaT_s[:, 64:128, S0:S0 + NS].rearrange("j p t -> p j t")),
                          dbs("qaT", S0, NS), [qs1.b])
                    for j in range(4):
                        for par in range(2):
                            P.dma("sp", CALL("dma_start", out=qis.t[:, 2 * j + par, :], in_=qiT_s[j, 64 * par:64 * par + 64, S0:S0 + NS]),
                                  dbs("qiT", S0, NS), [qis.b])
                    P.dma("sp", CALL("dma_start", out=wis.t[:, :, :], in_=wi_s[S0:S0 + NS, :].rearrange("(b q) h -> q b h", q=8)),
                          dbs("wi", S0, NS), [wis.b])
                    for b in range(NB_S):
                        for h in range(8):
                            P.dve(CALL("tensor_scalar", out=dss.t[:, b, h, :], in0=cst.t[0:8, C_ID:C_ID + 8], scalar1=wis.t[:, b, h:h + 1],
                                       scalar2=WSC, op0=ALU.mult, op1=ALU.mult), [wis.b, cst.b], [dss.b])
                    for bs in range(16):
                        P.dve(CALL("tensor_copy", out=sel2.t[:, bs, :].rearrange("p (a q) -> p a q", a=8),
                                   in_=cst.t[:, C_ID + bs * 8:C_ID + bs * 8 + 8].unsqueeze(1).broadcast_to([128, 8, 8])), [cst.b], [sel2.b])

                    def page_idx(b):
                        P.dma("sp", CALL("dma_start", out=pti.t[:, :], in_=pt_d[b:b + 1, :].rearrange("o p -> p o")), [], [pti.b])
                        P.dve(CALL("tensor_copy", out=ptf.t[:, :], in_=pti.t[:, :]), [pti.b], [ptf.b])
                        for o in range(4):
                            P.dve(CALL("tensor_scalar", out=idf.t[:, o:o + 1], in0=ptf.t[:, :], scalar1=4.0, scalar2=float(o),
                                       op0=ALU.mult, op1=ALU.add), [ptf.b], [idf.b])
                        for o in range(8):
                            P.dve(CALL("tensor_scalar", out=idf.t[:, 4 + o:5 + o], in0=ptf.t[:, :], scalar1=8.0, scalar2=float(o),
                                       op0=ALU.mult, op1=ALU.add), [ptf.b], [idf.b])
                        P.dve(CALL("tensor_copy", out=idxs[b].t[:, 0:12], in_=idf.t[:, 0:12]), [idf.b], [idxs[b].b])

                    def gather(b, src_d, n_o, icol0, dst0, bufR):
                        for o in range(n_o):
                            P.dma("pool", CALL("indirect_dma_start", out=R.t[:, dst0 + o * 2048:dst0 + (o + 1) * 2048], out_offset=None,
                                               in_=src_d[:, :],
                                               in_offset=bass.IndirectOffsetOnAxis(ap=idxs[b].t[:, icol0 + o:icol0 + o + 1], axis=0)),
                                  [idxs[b].b], [bufR])

                    for b in range(NB_S):
                        page_idx(b)

                    for b in range(NB_S):
                        P.dve(CALL("tensor_copy", out=qisb.t[:, b, :].rearrange("p (h q) -> p h q", h=8), in_=qis.t[:, :, 8 * b:8 * b + 8]),
                              [qis.b], [qisb.b])
                        ptw = pst[nx3("t", 2)]
                        P.pe(CALL("transpose", ptw.t[0:64, 0:8], dss.t[:, b, :, :].rearrange("p h q -> p (h q)"), ident_bf.t[0:8, 0:8]),
                             [dss.b, ident_bf.b], [ptw.b])
                        P.dve(CALL("tensor_copy", out=wsel.t[:, b, :], in_=ptw.t[0:64, 0:8]), [ptw.b], [wsel.b])
                    for b in range(NB_S):
                        gather(b, cki_d, 4, 0, 0, BR0)
                        if b == 0:
                            gather(0, cv_d, 8, 4, O2, BR2)
                        for g8 in range(16):
                            pt_ = pst[nx3("t", 2)]
                            for k in range(8):
                                t = g8 * 8 + k
                                P.pe(CALL("transpose", pt_.t[0:64, k * 128:(k + 1) * 128], R.t[:, t * 64:(t + 1) * 64], ident_bf.t[:, :]),
                                     [BR0, ident_bf.b], [pt_.b])
                            evac3(R.t[0:64, O1 + g8 * 1024:O1 + (g8 + 1) * 1024], pt_.t[0:64, :], [pt_.b], [BR1])
                        NCH = NKP // 512 + 1
                        pdd = {}

                        def idots(c, b=b):
                            new = (c == NKP // 512)
                            Wc = 8 if new else 512
                            pd = psd2[nx3("d", 2)]
                            pdd[c] = pd
                            rhs = kiA.t[0:64, S0 + 8 * b:S0 + 8 * b + 8] if new else R.t[0:64, O1 + c * 512:O1 + (c + 1) * 512]
                            P.pe(CALL("matmul", pd.t[0:64, 0:Wc], lhsT=qisb.t[:, b, :], rhs=rhs, start=True, stop=True),
                                 [qisb.b, kiA.b if new else BR1], [pd.b])

                        idots(0)
                        for c in range(NCH):
                            new = (c == NKP // 512)
                            Wc = 8 if new else 512
                            pd = pdd[c]
                            r_ = rs2[nx3("r", 4)]
                            P.act(CALL("activation", out=r_.t[:, 0:Wc], in_=pd.t[0:64, 0:Wc], func=AF.Relu), [pd.b], [r_.b])
                            if c + 1 < NCH:
                                idots(c + 1)
                            pi_ = psi2s[nx3("i", 2)]
                            P.pe(CALL("matmul", pi_.t[0:8, 0:Wc], lhsT=wsel.t[:, b, :], rhs=r_.t[:, 0:Wc], start=True, stop=True),
                                 [wsel.b, r_.b], [pi_.b])
                            sg_ = sg2[nx3("g", 4)]
                            if new:
                                P.dve(CALL("tensor_tensor", out=sg_.t[:, 0:8], in0=pi_.t[0:8, 0:8], in1=cst.t[0:8, C_CN:C_CN + 8], op=ALU.add),
                                      [pi_.b, cst.b], [sg_.b])
                                P.dma("sp", CALL("dma_start", out=Is.t[b * 32:b * 32 + 8, SEGW:IW], in_=sg_.t[:, 0:8]), [sg_.b], [Is.b])
                            else:
                                P.dve(CALL("tensor_copy", out=sg_.t[:, :], in_=pi_.t[0:8, :]), [pi_.b], [sg_.b])
                                seg, cc = c // (SEGW // 512), c % (SEGW // 512)
                                r0 = b * 32 + seg * 8
                                P.dma("sp", CALL("dma_start", out=Is.t[r0:r0 + 8, cc * 512:(cc + 1) * 512], in_=sg_.t[:, :]), [sg_.b], [Is.b])

                    gather(0, ck_d, 8, 4, 0, BR0)
                    c2 = lambda k: sst.t[:, k:k + 1]
                    Q_MAX, Q_MIN, Q_A, Q_HI, Q_LO, Q_W0, Q_MID, Q_CNT, Q_U, Q_THR, Q_WALL = 0, 1, 2, 3, 4, 5, 6, 7, 8, 9, 16
                    P.dve(CALL("tensor_reduce", out=c2(Q_MAX), in_=Is.t[:, 0:IW], axis=AX.X, op=ALU.max), [Is.b], [sst.b])
                    P.dve(CALL("tensor_reduce", out=c2(Q_MIN), in_=Is.t[:, 0:SEGW], axis=AX.X, op=ALU.min), [Is.b], [sst.b])
                    P.dve(CALL("tensor_scalar", out=c2(Q_MIN), in0=c2(Q_MIN), scalar1=-1.0, scalar2=None, op0=ALU.mult), [sst.b], [sst.b])
                    P.dve(CALL("tensor_tensor", out=c2(Q_A), in0=c2(Q_MAX), in1=c2(Q_MIN), op=ALU.max), [sst.b], [sst.b])
                    P.pe(CALL("matmul", psi2.t[:, 0:1], lhsT=G_f, rhs=c2(Q_A), start=True, stop=True), [sst.b, cst.b], [psi2.b])
                    P.dve(CALL("tensor_scalar", out=c2(Q_HI), in0=psi2.t[:, 0:1], scalar1=1.001, scalar2=1e-3, op0=ALU.mult, op1=ALU.add),
                          [psi2.b], [sst.b])
                    P.dve(CALL("tensor_scalar", out=c2(Q_LO), in0=c2(Q_HI), scalar1=-1.0, scalar2=None, op0=ALU.mult), [sst.b], [sst.b])
                    P.dve(CALL("tensor_scalar", out=c2(Q_W0), in0=c2(Q_HI), scalar1=2.0, scalar2=None, op0=ALU.mult), [sst.b], [sst.b])
                    P.dve(CALL("tensor_scalar", out=sst.t[:, Q_WALL:Q_WALL + 32], in0=cst.t[:, C_PW:C_PW + 32], scalar1=c2(Q_W0), scalar2=None,
                               op0=ALU.mult), [sst.b, cst.b], [sst.b])
                    P.dve(CALL("tensor_tensor", out=c2(Q_MID), in0=c2(Q_LO), in1=c2(Q_WALL), op=ALU.add), [sst.b], [sst.b])
                    for r in range(ROUNDS_S):
                        P.dve(CALL("tensor_scalar", out=MBs.t[:, :], in0=Is.t[:, :], scalar1=c2(Q_MID), scalar2=0.0, op0=ALU.is_ge,
                                   op1=ALU.add, accum_out=c2(Q_CNT)), [Is.b, sst.b], [MBs.b, sst.b])
                        P.pe(CALL("matmul", psi2.t[:, 0:1], lhsT=G_f, rhs=c2(Q_CNT), start=True, stop=True), [sst.b, cst.b], [psi2.b])
                        P.dve(CALL("tensor_scalar", out=c2(Q_U), in0=psi2.t[:, 0:1], scalar1=TOPK_S - 0.5, scalar2=c2(Q_WALL + r),
                                   op0=ALU.is_ge, op1=ALU.mult), [psi2.b, sst.b], [sst.b])
                        nxt_w = Q_WALL + r + 1 if r + 1 < ROUNDS_S else Q_WALL + r
                        dst = Q_MID if r + 1 < ROUNDS_S else Q_THR
                        P.dve(CALL("scalar_tensor_tensor", out=c2(dst), in0=c2(Q_U), scalar=c2(nxt_w), in1=c2(Q_MID), op0=ALU.subtract,
                                   op1=ALU.add), [sst.b], [sst.b])
                    P.dve(CALL("tensor_scalar", out=MBs.t[:, :], in0=Is.t[:, :], scalar1=c2(Q_THR), scalar2=NEG, op0=ALU.is_lt, op1=ALU.mult),
                          [Is.b, sst.b], [MBs.b])

                    for b in range(NB_S):
                        P.dve(CALL("tensor_copy", out=qsb.t[:, b, 0:32].rearrange("p (j q) -> p j q", j=4), in_=qs0.t[:, :, 8 * b:8 * b + 8]),
                              [qs0.b], [qsb.b])
                        P.dve(CALL("tensor_copy", out=qsb.t[:, b, 32:64].rearrange("p (j q) -> p j q", j=4), in_=qs1.t[:, :, 8 * b:8 * b + 8]),
                              [qs1.b], [qsb.b])
                    for b in range(NB_S):
                        if b >= 1:
                            gather(b, cv_d, 8, 4, O2, BR2)
                        P.dma("pool", CALL("dma_start", out=vnew.t[:, :], in_=vo_d[S0 + 8 * b:S0 + 8 * b + 8, :]), dbs("vo", S0, NS), [vnew.b])
                        for g8 in range(16):
                            pt_ = pst[nx3("t", 2)]
                            for k in range(8):
                                t = g8 * 8 + k
                                P.pe(CALL("transpose", pt_.t[:, k * 128:(k + 1) * 128], R.t[:, t * 128:(t + 1) * 128], ident_bf.t[:, :]),
                                     [BR0, ident_bf.b], [pt_.b])
                            evac3(R.t[:, O1 + g8 * 1024:O1 + (g8 + 1) * 1024], pt_.t[:, :], [pt_.b], [BR1])
                        if b + 1 < NB_S:
                            gather(b + 1, ck_d, 8, 4, 0, BR0)
                        ngrp = 128 // 4
                        pll = {}

                        def alogits(g4, b=b):
                            new = (g4 == ngrp)
                            nkk = 8 if new else 128
                            ps_ = psd2[nx3("d", 2)]
                            pll[g4] = ps_
                            ts_ = [128] if new else [g4 * 4 + k for k in range(4)]
                            for k, t in enumerate(ts_):
                                if new:
                                    l1a, l1b = kA.t[:, S0 + 8 * b:S0 + 8 * b + 8], kB.t[:, S0 + 8 * b:S0 + 8 * b + 8]
                                    l2 = MBs.t[:, SEGW:IW]
                                    sl_ = sel2.t[:, b * 4, :]
                                    P.pe(CALL("matmul", ps_.t[0:nkk, 0:64], lhsT=l1a, rhs=qsb.t[:, b, :], start=True, stop=False),
                                         [kA.b, qsb.b], [ps_.b])
                                    P.pe(CALL("matmul", ps_.t[0:nkk, 0:64], lhsT=l1b, rhs=qsb.t[:, b, :], start=False, stop=False),
                                         [kB.b, qsb.b], [ps_.b])
                                else:
                                    l1 = R.t[:, O1 + t * 128:O1 + (t + 1) * 128]
                                    seg, cc = t // 32, t % 32
                                    l2 = MBs.t[:, cc * 128:(cc + 1) * 128]
                                    sl_ = sel2.t[:, b * 4 + seg, :]
                                    P.pe(CALL("matmul", ps_.t[0:nkk, k * 64:(k + 1) * 64], lhsT=l1, rhs=qsb.t[:, b, :], start=True, stop=False),
                                         [BR1, qsb.b], [ps_.b])
                                P.pe(CALL("matmul", ps_.t[0:nkk, k * 64:(k + 1) * 64], lhsT=l2, rhs=sl_, start=False, stop=True),
                                     [MBs.b, sel2.b], [ps_.b])

                        alogits(0)
                        for g4 in range(ngrp + 1):
                            new = (g4 == ngrp)
                            nkk = 8 if new else 128
                            ncol = 64 if new else 256
                            ps_ = pll[g4]
                            ts_ = [128] if new else [g4 * 4 + k for k in range(4)]
                            p_ = pt2[nx3("p", 4)]
                            P.act(CALL("activation", out=p_.t[0:nkk, 0:ncol], in_=ps_.t[0:nkk, 0:ncol], func=AF.Exp, scale=0.125), [ps_.b], [p_.b])
                            if g4 + 1 <= ngrp:
                                alogits(g4 + 1)
                            for k, t in enumerate(ts_):
                                first = (g4 == 0 and k == 0)
                                for kvh in range(2):
                                    po_ = pso2s[kvh]
                                    if new:
                                        lv = vnew.t[0:8, kvh * 64:(kvh + 1) * 64]
                                        rdv = vnew.b
                                    else:
                                        lv = R.t[:, O2 + t * 128 + kvh * 64:O2 + t * 128 + kvh * 64 + 64]
                                        rdv = BR2
                                    P.pe(CALL("matmul", po_.t[0:64, 0:32], lhsT=lv, rhs=p_.t[0:nkk, k * 64 + kvh * 32:k * 64 + kvh * 32 + 32],
                                              start=first, stop=new), [rdv, p_.b], [po_.b])
                                P.pe(CALL("matmul", psm2.t[0:1, 0:64], lhsT=ones_bf.t[0:nkk, 0:1], rhs=p_.t[0:nkk, k * 64:(k + 1) * 64],
                                          start=first, stop=new), [ones_bf.b, p_.b], [psm2.b])
                        P.dve(CALL("reciprocal", out=rcp2.t[:, :], in_=psm2.t[0:1, 0:64]), [psm2.b], [rcp2.b])
                        pb_ = psd2[nx3("d", 2)]
                        P.pe(CALL("matmul", pb_.t[0:64, 0:64], lhsT=cst.t[0:1, C_LE:C_LE + 64], rhs=rcp2.t[:, :], start=True, stop=True),
                             [cst.b, rcp2.b], [pb_.b])
                        P.act(CALL("activation", out=bcs2.t[:, :], in_=pb_.t[0:64, 0:64], func=AF.Copy), [pb_.b], [bcs2.b])
                        for kvh in range(2):
                            P.dve(CALL("tensor_tensor", out=as2.t[:, kvh, :, :], in0=pso2s[kvh].t[0:64, 0:32].rearrange("p (j q) -> p j q", j=4),
                                       in1=bcs2.t[:, kvh * 32:(kvh + 1) * 32].rearrange("p (j q) -> p j q", j=4), op=ALU.mult),
                                  [pso2s[kvh].b, bcs2.b], [as2.b])
                        for kvh in range(2):
                            for j in range(4):
                                kc = 2 * kvh + j // 2
                                r0 = (j % 2) * 64
                                P.dma("sp", CALL("dma_start", out=attnT_s[kc, r0:r0 + 64, S0 + 8 * b:S0 + 8 * b + 8], in_=as2.t[:, kvh, j, :]),
                                      [as2.b], dbs("attnT", S0, NS, kc))
                    P.flush()
        s3w = ExitStack()
        with s3w:
            wao = SB(s3w, "wao", [128, 4, D], BF16)
            wgo = SB(s3w, "wgo", [128, 8, D], BF16)
            wo = SB(s3w, "wo", [128, 8, D], BF16)
            if "4" in phases:
                for (wt, wd) in ((wao, wao_d), (wgo, wgo_d), (wo, wo_d)):
                    P.dma("pool", CALL("dma_start",
                        out=wt.t[:, :, :], in_=wd.rearrange("(kc p) n -> p kc n", p=128)), [], [wt.b])
                for n in ln_d:
                    P.dma("sp", CALL("dma_start", out=lnb[n].t[:, :], in_=ln_d[n][0:1, :].broadcast_to([128, D])),
                          [], [lnb[n].b])
            if "3" in phases:
                s3 = ExitStack()
                with s3:
                    aug = SB(s3, "aug", [17, 128], BF16)
                    wal = SB(s3, "wal", [17, 512], BF16)
                    gnb = SB(s3, "gnb", [128, 256], F32)
                    qbb = SB(s3, "qbb", [128, 4, 128], BF16)
                    kbb = SB(s3, "kbb", [128, 4, 128], BF16)
                    kbt = SB(s3, "kbt", [128, 512], BF16)
                    vbts = [SB(s3, "vbt%d" % i, [128, 1024], BF16) for i in range(2)]
                    gbt = SB(s3, "gbt", [128, 1024], F32)
                    ee = SB(s3, "ee", [128, 512], F32)
                    la = SB(s3, "la", [128, 512], F32)
                    Eqs = [SB(s3, "Eq%d" % i, [128, 4, 128], F32) for i in range(2)]
                    Ek = SB(s3, "Ek", [128, 4, 128], F32)
                    Er = SB(s3, "Er", [128, 512], F32)
                    qts = [SB(s3, "qt%d" % i, [128, 4, 128], BF16) for i in range(2)]
                    kts = [SB(s3, "kt%d" % i, [128, 4, 128], BF16) for i in range(2)]
                    kps = [SB(s3, "kp%d" % i, [128, 512], BF16) for i in range(2)]
                    attm = SB(s3, "attm", [128, 4, 128], BF16)
                    Sf = SB(s3, "Sf", [128, 4, 256], F32)
                    Sb = SB(s3, "Sb", [128, 4, 256], BF16)
                    onr = SB(s3, "onr", [128, 1024], F32)
                    osbs = [SB(s3, "osb%d" % i, [128, 1024], F32) for i in range(2)]
                    sgts = [SB(s3, "sgt%d" % i, [128, 1024], F32) for i in range(2)]
                    obb = SB(s3, "obb", [128, 1024], BF16)
                    obT = SB(s3, "obT", [128, 8, 128], BF16)
                    gst = SB(s3, "gst", [128, 16], F32)
                    jk3 = SB(s3, "jk3", [128, 256], BF16)
                    psA = PSB(s3, "psA", [128, 512])
                    psC = PSB(s3, "psC", [128, 512])
                    psT_ = PSB(s3, "psT", [128, 512])
                    psO = PSB(s3, "psO", [128, 1024])
                    psS = PSB(s3, "psS", [128, 1024])
                    psX = PSB(s3, "psX", [128, 1024], BF16)

                    P.pool(CALL("memset", aug.t[:, :], 1.0), [], [aug.b])
                    P.dma("pool", CALL("dma_start", out=wal.t[1:17, :], in_=wal_d[:, :]), [], [wal.b])
                    P.dma("pool", CALL("dma_start", out=wal.t[0:1, :], in_=bal_d[:, :]), [], [wal.b])
                    P.dma("sp", CALL("dma_start", out=gnb.t[:, :], in_=gng_d[0:1, :].broadcast_to([128, 256])), [], [gnb.b])

                    def gla_prep(ci, tok0, C):
                        sl = ci % 2
                        qt, kt, kp, vbt, sgt, Eq = qts[sl], kts[sl], kps[sl], vbts[sl], sgts[sl], Eqs[sl]
                        P.dma("sp", CALL("dma_start", out=aug.t[1:17, 0:C], in_=abT_s[:, tok0:tok0 + C]), dbs("abT", tok0, C), [aug.b])
                        P.dma("sp", CALL("dma_start", out=qbb.t[:, :, 0:C], in_=qbT_s[:, :, tok0:tok0 + C].rearrange("j p t -> p j t")),
                              dbs("qbT", tok0, C), [qbb.b])
                        P.dma("sp", CALL("dma_start", out=kbb.t[:, :, 0:C], in_=kbT_s[:, :, tok0:tok0 + C].rearrange("j p t -> p j t")),
                              dbs("kbT", tok0, C), [kbb.b])
                        P.dma("sp", CALL("dma_start", out=kbt.t[0:C, :], in_=kb_s[tok0:tok0 + C, :]), dbs("kb", tok0, C), [kbt.b])
                        P.dma("sp", CALL("dma_start", out=vbt.t[0:C, :], in_=vb_s[tok0:tok0 + C, :]), dbs("vb", tok0, C), [vbt.b])
                        P.pe(CALL("matmul", psA.t[0:C, :], lhsT=aug.t[:, 0:C], rhs=wal.t[:, :], start=True, stop=True), [aug.b, wal.b], [psA.b])
                        P.act(CALL("activation", out=ee.t[0:C, :], in_=psA.t[0:C, :], func=AF.Exp, scale=-1.0), [psA.b], [ee.b])
                        P.act(CALL("activation", out=la.t[0:C, :], in_=ee.t[0:C, :], func=AF.Ln, bias=one_c[0:C, :], scale=1.0),
                              [ee.b, cst.b], [la.b])
                        P.pe(CALL("matmul", psA.t[0:C, :], lhsT=ltri_f[0:C, 0:C], rhs=la.t[0:C, :], start=True, stop=True), [la.b, cst.b], [psA.b])
                        for h in range(4):
                            P.pe(CALL("matmul", psC.t[:, h * 128:h * 128 + C], lhsT=la.t[0:C, h * 128:(h + 1) * 128], rhs=utri_f[0:C, 0:C],
                                      start=True, stop=True), [la.b, cst.b], [psC.b])
                        psC3 = psC.t[:, :].rearrange("p (h t) -> p h t", h=4)
                        P.act(CALL("activation", out=Er.t[0:C, :], in_=psA.t[0:C, :], func=AF.Exp), [psA.b], [Er.b])
                        P.act(CALL("activation", out=Eq.t[:, :, 0:C], in_=psC3[:, :, 0:C], func=AF.Exp), [psC.b], [Eq.b])
                        P.act(CALL("activation", out=Ek.t[:, :, 0:C], in_=psC3[:, :, 0:C], func=AF.Exp, scale=-1.0), [psC.b], [Ek.b])
                        P.dve(CALL("scalar_tensor_tensor", out=qt.t[:, :, 0:C], in0=qbb.t[:, :, 0:C], scalar=128.0 ** -0.5, in1=Eq.t[:, :, 0:C],
                                   op0=ALU.mult, op1=ALU.mult), [qbb.b, Eq.b], [qt.b])
                        P.dve(CALL("tensor_tensor", out=kt.t[:, :, 0:C], in0=kbb.t[:, :, 0:C], in1=Ek.t[:, :, 0:C], op=ALU.mult),
                              [kbb.b, Ek.b], [kt.b])
                        P.dve(CALL("tensor_tensor", out=kp.t[0:C, :], in0=kbt.t[0:C, :], in1=Er.t[0:C, :], op=ALU.mult), [kbt.b, Er.b], [kp.b])

                    def gla_state(ci, tok0, C):
                        sl = ci % 2
                        qt, kt, kp, vbt, sgt, Eq = qts[sl], kts[sl], kps[sl], vbts[sl], sgts[sl], Eqs[sl]
                        for h in range(4):
                            P.pe(CALL("matmul", psT_.t[0:C, h * 128:h * 128 + C], lhsT=kt.t[:, h, 0:C], rhs=qt.t[:, h, 0:C], start=True, stop=True),
                                 [kt.b, qt.b], [psT_.b])
                        psT3 = psT_.t[:, :].rearrange("p (h t) -> p h t", h=4)
                        P.dve(CALL("tensor_tensor", out=attm.t[0:C, :, 0:C], in0=psT3[0:C, :, 0:C],
                                   in1=cst.t[0:C, C_LE:C_LE + C].unsqueeze(1).broadcast_to([C, 4, C]), op=ALU.mult), [psT_.b, cst.b], [attm.b])
                        for h in range(4):
                            P.pe(CALL("matmul", psO.t[0:C, h * 256:(h + 1) * 256], lhsT=attm.t[0:C, h, 0:C], rhs=vbt.t[0:C, h * 256:(h + 1) * 256],
                                      start=True, stop=False), [attm.b, vbt.b], [psO.b])
                            P.pe(CALL("matmul", psO.t[0:C, h * 256:(h + 1) * 256], lhsT=qt.t[:, h, 0:C], rhs=Sb.t[:, h, :], start=False, stop=True),
                                 [qt.b, Sb.b], [psO.b])
                        for h in range(4):
                            P.pe(CALL("matmul", psS.t[:, h * 256:(h + 1) * 256], lhsT=kp.t[0:C, h * 128:(h + 1) * 128],
                                      rhs=vbt.t[0:C, h * 256:(h + 1) * 256], start=True, stop=True), [kp.b, vbt.b], [psS.b])
                        for h in range(4):
                            P.dve(CALL("scalar_tensor_tensor", out=Sf.t[:, h, :], in0=Sf.t[:, h, :], scalar=Eq.t[:, h, C - 1:C],
                                       in1=psS.t[:, h * 256:(h + 1) * 256], op0=ALU.mult, op1=ALU.add), [Sf.b, Eq.b, psS.b], [Sf.b])
                        P.act(CALL("activation", out=Sb.t[:, :, :], in_=Sf.t[:, :, :], func=AF.Copy), [Sf.b], [Sb.b])
                        osb = osbs[sl]
                        P.act(CALL("activation", out=osb.t[0:C, :], in_=psO.t[0:C, :], func=AF.Copy), [psO.b], [osb.b])

                    def gla_out(ci, tok0, C):
                        sl = ci % 2
                        sgt = sgts[0]
                        osb = osbs[sl]
                        P.dma("sp", CALL("dma_start", out=gbt.t[0:C, :], in_=gb_s[tok0:tok0 + C, :]), dbs("gb", tok0, C), [gbt.b])
                        P.act(CALL("activation", out=sgt.t[0:C, :], in_=gbt.t[0:C, :], func=AF.Silu), [gbt.b], [sgt.b])
                        for h in range(4):
                            P.act(CALL("activation", out=jk3.t[0:C, :], in_=osb.t[0:C, h * 256:(h + 1) * 256], func=AF.Square,
                                       accum_out=gst.t[0:C, h:h + 1]), [osb.b], [jk3.b, gst.b])
                        P.act(CALL("activation", out=gst.t[0:C, 4:8], in_=gst.t[0:C, 0:4], func=AF.Ln, bias=eps_c[0:C, :], scale=1.0 / 256),
                              [gst.b, cst.b], [gst.b])
                        P.act(CALL("activation", out=gst.t[0:C, 8:12], in_=gst.t[0:C, 4:8], func=AF.Exp, scale=-0.5), [gst.b], [gst.b])
                        for h in range(4):
                            P.dve(CALL("scalar_tensor_tensor", out=onr.t[0:C, h * 256:(h + 1) * 256], in0=osb.t[0:C, h * 256:(h + 1) * 256],
                                       scalar=gst.t[0:C, 8 + h:9 + h], in1=gnb.t[0:C, :], op0=ALU.mult, op1=ALU.mult),
                                  [osb.b, gst.b, gnb.b], [onr.b])
                        P.dve(CALL("tensor_tensor", out=obb.t[0:C, :], in0=onr.t[0:C, :], in1=sgt.t[0:C, :], op=ALU.mult), [onr.b, sgt.b], [obb.b])
                        for kc in range(8):
                            P.pe(CALL("transpose", psX.t[:, kc * 128:kc * 128 + C], obb.t[0:C, kc * 128:(kc + 1) * 128], ident_bf.t[0:C, 0:C]),
                                 [obb.b, ident_bf.b], [psX.b])
                        P.dve(CALL("tensor_copy", out=obT.t[:, :, 0:C], in_=psX.t[:, :].rearrange("p (k t) -> p k t", k=8)[:, :, 0:C]),
                              [psX.b], [obT.b])
                        P.dma("sp", CALL("dma_start", out=obT_s[:, :, tok0:tok0 + C].rearrange("k p t -> p k t"), in_=obT.t[:, :, 0:C]),
                              [obT.b], dbs("obT", tok0, C))

                    chunks = [(i * 128, 128, None) for i in range(NBLK)] + [(NTP + 8 * b, 8, b) for b in range(NB_S)]
                    P.pool(CALL("memset", Sf.t[:, :, :], 0.0), [], [Sf.b])
                    P.pool(CALL("memset", Sb.t[:, :, :], 0.0), [], [Sb.b])
                    gla_prep(0, chunks[0][0], chunks[0][1])
                    for ci, (tok0, C, sb_) in enumerate(chunks):
                        if ci + 1 < len(chunks):
                            gla_prep(ci + 1, chunks[ci + 1][0], chunks[ci + 1][1])
                        if sb_ is not None:
                            if sb_ == 0:
                                P.dma("sp", CALL("dma_start", out=glap_d.rearrange("h d v -> d h v"), in_=Sf.t[:, :, :]), [Sf.b], [B_gla_out])
                            P.dma("sp", CALL("dma_start", out=Sf.t[:, :, :], in_=st_d[sb_].rearrange("h d v -> d h v")), [B_gla_out], [Sf.b])
                            P.act(CALL("activation", out=Sb.t[:, :, :], in_=Sf.t[:, :, :], func=AF.Copy), [Sf.b], [Sb.b])
                        gla_state(ci, tok0, C)
                        if sb_ is not None:
                            P.dma("sp", CALL("dma_start", out=glas_d[sb_].rearrange("h d v -> d h v"), in_=Sf.t[:, :, :]), [Sf.b], [B_gla_out])
                        if ci >= 1:
                            gla_out(ci - 1, chunks[ci - 1][0], chunks[ci - 1][1])
                    gla_out(len(chunks) - 1, chunks[-1][0], chunks[-1][1])
                    P.flush()

            def layer_norm(stack_tiles, y1, T, gname, bname, out_t, stt_):
                jk, = stack_tiles
                P.act(CALL("activation", out=jk.t[0:T, :], in_=y1.t[0:T, :], func=AF.Copy, accum_out=stt_.t[0:T, 0:1]),
                      [y1.b], [jk.b, stt_.b])
                P.act(CALL("activation", out=jk.t[0:T, :], in_=y1.t[0:T, :], func=AF.Square, accum_out=stt_.t[0:T, 1:2]),
                      [y1.b, jk.b], [jk.b, stt_.b])
                P.dve(CALL("tensor_scalar", out=stt_.t[0:T, 2:3], in0=stt_.t[0:T, 0:1], scalar1=1.0 / D, scalar2=None,
                                                op0=ALU.mult), [stt_.b], [stt_.b])
                P.dve(CALL("tensor_tensor", out=stt_.t[0:T, 3:4], in0=stt_.t[0:T, 2:3], in1=stt_.t[0:T, 2:3], op=ALU.mult),
                      [stt_.b], [stt_.b])
                P.dve(CALL("scalar_tensor_tensor", out=stt_.t[0:T, 4:5], in0=stt_.t[0:T, 1:2], scalar=1.0 / D,
                                                       in1=stt_.t[0:T, 3:4], op0=ALU.mult, op1=ALU.subtract), [stt_.b], [stt_.b])
                P.act(CALL("activation", out=stt_.t[0:T, 5:6], in_=stt_.t[0:T, 4:5], func=AF.Ln, bias=eps_c[0:T, :], scale=1.0),
                      [stt_.b, cst.b], [stt_.b])
                P.act(CALL("activation", out=stt_.t[0:T, 6:7], in_=stt_.t[0:T, 5:6], func=AF.Exp, scale=-0.5), [stt_.b], [stt_.b])
                P.dve(CALL("tensor_scalar", out=out_t.t[0:T, :], in0=y1.t[0:T, :], scalar1=stt_.t[0:T, 2:3],
                                                scalar2=stt_.t[0:T, 6:7], op0=ALU.subtract, op1=ALU.mult), [y1.b, stt_.b], [out_t.b])
                P.dve(CALL("tensor_tensor", out=out_t.t[0:T, :], in0=out_t.t[0:T, :], in1=lnb[gname].t[0:T, :], op=ALU.mult),
                      [out_t.b, lnb[gname].b], [out_t.b])
                P.dve(CALL("tensor_tensor", out=out_t.t[0:T, :], in0=out_t.t[0:T, :], in1=lnb[bname].t[0:T, :], op=ALU.add),
                      [out_t.b, lnb[bname].b], [out_t.b])

            tiles4 = [(t0, 128) for t0 in range(0, NTP, 128)] + ([(NTP, NS)] if "x" not in phases else [])
            if "4" in phases:
                s4 = ExitStack()
                with s4:
                    atb = [SB(s4, "atb%d" % i, [128, 4, 128], BF16) for i in range(2)]
                    obl = [SB(s4, "obl%d" % i, [128, 8, 128], BF16) for i in range(2)]
                    gtl = [SB(s4, "gtl%d" % i, [128, 16, 128], F32) for i in range(2)]
                    xbl = [SB(s4, "xbl%d" % i, [128, D], F32) for i in range(2)]
                    sig = SB(s4, "sig", [128, 16, 128], F32)
                    t1 = SB(s4, "t1", [128, 8, 128], F32)
                    mrgs = [SB(s4, "mrg%d" % i, [128, 8, 128], BF16) for i in range(2)]
                    y1 = SB(s4, "y1", [128, D], F32)
                    hh = SB(s4, "hh", [128, D], F32)
                    hb = SB(s4, "hb", [128, D], BF16)
                    hTt = SB(s4, "hTt", [128, 8, 128], BF16)
                    jk4 = SB(s4, "jk4", [128, D], BF16)
                    st4 = SB(s4, "st4", [128, 8], F32)
                    psa = PSB(s4, "psa", [128, 1024])
                    psb4 = PSB(s4, "psb4", [128, 1024])
                    psm = PSB(s4, "psm", [128, 1024])
                    psx = PSB(s4, "psx4", [128, 1024], BF16)

                    def load4(ti):
                        t0, T = tiles4[ti]
                        sl = ti % 2
                        P.dma("sp", CALL("dma_start", out=atb[sl].t[:, :, 0:T], in_=attnT_s[:, :, t0:t0 + T].rearrange("k p t -> p k t")),
                              dbs("attnT", t0, T), [atb[sl].b])
                        P.dma("sp", CALL("dma_start", out=obl[sl].t[:, :, 0:T], in_=obT_s[:, :, t0:t0 + T].rearrange("k p t -> p k t")),
                              dbs("obT", t0, T), [obl[sl].b])
                        P.dma("sp", CALL("dma_start", out=gtl[sl].t[:, :, 0:T], in_=gtT_s[:, :, t0:t0 + T].rearrange("k p t -> p k t")),
                              dbs("gtT", t0, T), [gtl[sl].b])
                        P.dma("sp", CALL("dma_start", out=xbl[sl].t[0:T, :], in_=x_d[t0:t0 + T, :]), [], [xbl[sl].b])

                    def s1_4a(ti):
                        t0, T = tiles4[ti]
                        sl = ti % 2
                        at_, ob_, gt_ = atb[sl], obl[sl], gtl[sl]
                        mrg = mrgs[sl]
                        P.act(CALL("activation", out=sig.t[:, :, 0:T], in_=gt_.t[:, :, 0:T], func=AF.Sigmoid), [gt_.b], [sig.b])
                        psa3 = psa.t[:, :].rearrange("p (c t) -> p c t", c=8)
                        psb3 = psb4.t[:, :].rearrange("p (c t) -> p c t", c=8)
                        for c in range(8):
                            for kc in range(4):
                                P.pe(CALL("matmul", psa.t[:, c * 128:c * 128 + T], lhsT=wao.t[:, kc, c * 128:(c + 1) * 128],
                                          rhs=at_.t[:, kc, 0:T], start=(kc == 0), stop=(kc == 3)), [wao.b, at_.b], [psa.b])
                        for c in range(8):
                            for kc in range(8):
                                P.pe(CALL("matmul", psb4.t[:, c * 128:c * 128 + T], lhsT=wgo.t[:, kc, c * 128:(c + 1) * 128],
                                          rhs=ob_.t[:, kc, 0:T], start=(kc == 0), stop=(kc == 7)), [wgo.b, ob_.b], [psb4.b])
                        P.dve(CALL("tensor_tensor", out=t1.t[:, :, 0:T], in0=psa3[:, :, 0:T], in1=sig.t[:, 0:8, 0:T], op=ALU.mult),
                              [psa.b, sig.b], [t1.b])
                        P.dve(CALL("tensor_tensor", out=sig.t[:, 8:16, 0:T], in0=psb3[:, :, 0:T], in1=sig.t[:, 8:16, 0:T], op=ALU.mult),
                              [psb4.b, sig.b], [sig.b])
                        P.dve(CALL("tensor_tensor", out=mrg.t[:, :, 0:T], in0=t1.t[:, :, 0:T], in1=sig.t[:, 8:16, 0:T], op=ALU.add),
                              [t1.b, sig.b], [mrg.b])

                    def s2_4a(ti):
                        t0, T = tiles4[ti]
                        sl = ti % 2
                        xb_ = xbl[sl]
                        mrg = mrgs[sl]
                        for n in range(2):
                            for kc in range(8):
                                P.pe(CALL("matmul", psm.t[0:T, n * 512:(n + 1) * 512], lhsT=mrg.t[:, kc, 0:T],
                                          rhs=wo.t[:, kc, n * 512:(n + 1) * 512], start=(kc == 0), stop=(kc == 7)), [mrg.b, wo.b], [psm.b])
                        for n in range(2):
                            P.dve(CALL("scalar_tensor_tensor", out=y1.t[0:T, n * 512:(n + 1) * 512], in0=xb_.t[0:T, n * 512:(n + 1) * 512],
                                       scalar=ALPHA, in1=psm.t[0:T, n * 512:(n + 1) * 512], op0=ALU.mult, op1=ALU.add),
                                  [xb_.b, psm.b], [y1.b])
                        layer_norm((jk4,), y1, T, "ln1_g", "ln1_b", hh, st4)
                        P.dma("sp", CALL("dma_start", out=h_s[t0:t0 + T, :], in_=hh.t[0:T, :]), [hh.b], dbs("h", t0, T))
                        P.act(CALL("activation", out=hb.t[0:T, :], in_=hh.t[0:T, :], func=AF.Copy), [hh.b], [hb.b])
                        for kc in range(8):
                            P.pe(CALL("transpose", psx.t[:, kc * 128:kc * 128 + T], hb.t[0:T, kc * 128:(kc + 1) * 128],
                                      ident_bf.t[0:T, 0:T]), [hb.b, ident_bf.b], [psx.b])
                        P.dve(CALL("tensor_copy", out=hTt.t[:, :, 0:T], in_=psx.t[:, :].rearrange("p (k t) -> p k t", k=8)[:, :, 0:T]),
                              [psx.b], [hTt.b])
                        P.dma("sp", CALL("dma_start", out=hT_s[:, :, t0:t0 + T].rearrange("k p t -> p k t"), in_=hTt.t[:, :, 0:T]),
                              [hTt.b], dbs("hT", t0, T))

                    load4(0)
                    s1_4a(0)
                    for ti in range(len(tiles4)):
                        if ti + 1 < len(tiles4):
                            load4(ti + 1)
                            s1_4a(ti + 1)
                        s2_4a(ti)
                    P.flush()

        if "4" in phases:
            s5 = ExitStack()
            with s5:
                wf1 = SB(s5, "wf1", [128, 8, 4096], BF16)
                wf2 = SB(s5, "wf2", [128, 32, D], BF16)
                wf1b = [Buf("wf1_%d" % i) for i in range(8)]
                wf2b = [Buf("wf2_%d" % i) for i in range(8)]
                for kc in range(8):
                    for c0 in (0, 2048):
                        P.dma("pool", CALL("dma_start", out=wf1.t[:, kc, c0:c0 + 2048],
                                                                          in_=wf1_d[kc * 128:(kc + 1) * 128, c0:c0 + 2048]),
                              [], [wf1b[kc]])
                for q in range(8):
                    P.dma("pool", CALL("dma_start", out=wf2.t[:, 4 * q:4 * q + 4, :],
                                                             in_=wf2_d[q * 512:(q + 1) * 512, :].rearrange("(kc p) n -> p kc n", p=128)),
                          [], [wf2b[q]])
                SW5 = 256
                hTl = [SB(s5, "hTl%d" % i, [128, 8, SW5], BF16) for i in range(2)]
                hl = [SB(s5, "hl%d" % i, [128, D], F32) for i in range(2)]
                rl = [SB(s5, "rl%d" % i, [128, 2, SW5], BF16) for i in range(2)]
                hid = SB(s5, "hid", [128, 32, SW5], BF16)
                y2 = SB(s5, "y2", [128, D], F32)
                yo = SB(s5, "yo", [128, D], F32)
                jk5 = SB(s5, "jk5", [128, D], BF16)
                st5 = SB(s5, "st5", [128, 8], F32)
                psf = [PSB(s5, "psf%d" % i, [128, 512]) for i in range(3)]
                psy = PSB(s5, "psy", [128, 1024])
                sup5 = [(t0, min(SW5, NTP - t0)) for t0 in range(0, NTP, SW5)] + (
                    [(NTP, NS)] if "x" not in phases else [])

                def load5T(ui):
                    t0, W = sup5[ui]
                    sl = ui % 2
                    P.dma("sp", CALL("dma_start", out=hTl[sl].t[:, :, 0:W], in_=hT_s[:, :, t0:t0 + W].rearrange("k p t -> p k t")),
                          dbs("hT", t0, W), [hTl[sl].b])

                fi_ = [0]
                hi_ = [0]

                def s1_4b(ui):
                    t0, W = sup5[ui]
                    hT_ = hTl[ui % 2]
                    for f2 in range(16):
                        pf = psf[fi_[0] % 3]
                        r_ = rl[fi_[0] % 2]
                        fi_[0] += 1
                        for cc in range(2):
                            f = f2 * 2 + cc
                            for kc in range(8):
                                P.pe(CALL("matmul", pf.t[:, cc * SW5:cc * SW5 + W], lhsT=wf1.t[:, kc, f * 128:(f + 1) * 128],
                                          rhs=hT_.t[:, kc, 0:W], start=(kc == 0), stop=(kc == 7)), [wf1b[kc], hT_.b], [pf.b])
                        pf3 = pf.t[:, :].rearrange("p (c t) -> p c t", c=2)
                        P.act(CALL("activation", out=r_.t[:, :, 0:W], in_=pf3[:, :, 0:W], func=AF.Relu), [pf.b], [r_.b])
                        P.dve(CALL("tensor_tensor", out=hid.t[:, f2 * 2:f2 * 2 + 2, 0:W], in0=r_.t[:, :, 0:W], in1=r_.t[:, :, 0:W],
                                   op=ALU.mult), [r_.b], [hid.b])

                def s2_4b(t0, T, c0):
                    h_ = hl[hi_[0] % 2]
                    hi_[0] += 1
                    P.dma("sp", CALL("dma_start", out=h_.t[0:T, :], in_=h_s[t0:t0 + T, :]), dbs("h", t0, T), [h_.b])
                    for n in range(2):
                        for kc in range(32):
                            P.pe(CALL("matmul", psy.t[0:T, n * 512:(n + 1) * 512], lhsT=hid.t[:, kc, c0:c0 + T],
                                      rhs=wf2.t[:, kc, n * 512:(n + 1) * 512], start=(kc == 0), stop=(kc == 31)),
                                 [hid.b, wf2b[kc // 4]], [psy.b])
                    for n in range(2):
                        P.dve(CALL("scalar_tensor_tensor", out=y2.t[0:T, n * 512:(n + 1) * 512], in0=h_.t[0:T, n * 512:(n + 1) * 512],
                                   scalar=ALPHA, in1=psy.t[0:T, n * 512:(n + 1) * 512], op0=ALU.mult, op1=ALU.add),
                              [h_.b, psy.b], [y2.b])
                    layer_norm((jk5,), y2, T, "ln2_g", "ln2_b", yo, st5)
                    P.dma("sp", CALL("dma_start", out=y_d[t0:t0 + T, :], in_=yo.t[0:T, :]), [yo.b], dbs("y", t0, T))

                load5T(0)
                for ui, (u0, W) in enumerate(sup5):
                    if ui + 1 < len(sup5):
                        load5T(ui + 1)
                    s1_4b(ui)
                    for c0 in range(0, W, 128):
                        s2_4b(u0 + c0, min(128, W - c0), c0)
                P.flush()
        P.flush(final=True)
    return nc


_NC_CACHE = {}


def make_in_maps(inp, n_cores, NTP, NPOOL):
    cst = make_consts()
    ck = np.ascontiguousarray(inp["cache_k"]).reshape(NPOOL * 8, 2048)
    cv = np.ascontiguousarray(inp["cache_v"]).reshape(NPOOL * 8, 2048)
    cki = np.ascontiguousarray(inp["cache_kidx"]).reshape(NPOOL * 4, 2048)
    maps = []
    for c in range(n_cores):
        xs = np.asarray(inp["x_sample"][NB_S * c:NB_S * (c + 1)]).reshape(NS, D)
        x = np.concatenate([np.asarray(inp["x_prompt"][c]), xs], axis=0).astype(np.float32)
        m = {
            "x": np.ascontiguousarray(x),
            "xT": np.ascontiguousarray(x.T),
            "cache_k": ck, "cache_v": cv, "cache_kidx": cki,
            "state_gla": np.ascontiguousarray(inp["state_gla"][0, NB_S * c:NB_S * (c + 1)]),
            "page_table": np.ascontiguousarray(inp["page_table"][NB_S * c:NB_S * (c + 1)]).astype(np.int32),
            "w_in": np.ascontiguousarray(inp["w_in"][0]),
            "w_alpha2": np.ascontiguousarray(inp["w_alpha2"][0]),
            "b_alpha": np.ascontiguousarray(inp["b_alpha"][0]).reshape(1, 512),
            "gla_norm_g": np.ascontiguousarray(inp["gla_norm_g"][0]).reshape(1, 256),
            "w_attn_o": np.ascontiguousarray(inp["w_attn_o"][0]),
            "w_gla_o": np.ascontiguousarray(inp["w_gla_o"][0]),
            "w_out": np.ascontiguousarray(inp["w_out"][0]),
            "ln1_g": np.ascontiguousarray(inp["ln1_g"][0]).reshape(1, D),
            "ln1_b": np.ascontiguousarray(inp["ln1_b"][0]).reshape(1, D),
            "ln2_g": np.ascontiguousarray(inp["ln2_g"][0]).reshape(1, D),
            "ln2_b": np.ascontiguousarray(inp["ln2_b"][0]).reshape(1, D),
            "w_ff1": np.ascontiguousarray(inp["w_ff1"][0]),
            "w_ff2": np.ascontiguousarray(inp["w_ff2"][0]),
            "cst": cst,
        }
        maps.append(m)
    return maps


def assemble(res, n_cores, NTP):
    f = np.float32
    y = np.stack([r["y"][:NTP] for r in res]).astype(f)
    ys = np.concatenate([r["y"][NTP:].reshape(NB_S, 8, D) for r in res]).astype(f)
    kp = np.stack([r["ko"][:NTP].reshape(NTP, 2, 64) for r in res])[None].astype(f)
    vp = np.stack([r["vo"][:NTP].reshape(NTP, 2, 64) for r in res])[None].astype(f)
    kip = np.stack([r["kio"][:NTP] for r in res])[None].astype(f)
    gp = np.stack([r["gla_p"] for r in res])[None].astype(f)
    ks = np.concatenate([r["ko"][NTP:].reshape(NB_S, 8, 2, 64) for r in res])[None].astype(f)
    vs = np.concatenate([r["vo"][NTP:].reshape(NB_S, 8, 2, 64) for r in res])[None].astype(f)
    kis = np.concatenate([r["kio"][NTP:].reshape(NB_S, 8, 64) for r in res])[None].astype(f)
    gs = np.concatenate([r["gla_s"] for r in res])[None].astype(f)
    return (y, ys, kp, vp, kip, gp, ks, vs, kis, gs)


def kernel(**inputs):
    n_cores = 8
    NTP = inputs["x_prompt"].shape[1]
    NPOOL = inputs["cache_k"].shape[1]
    nc = build(NTP=NTP, NPOOL=NPOOL)
    maps = make_in_maps(inputs, n_cores, NTP, NPOOL)
    out = run_bass_kernel_spmd(nc, maps, core_ids=list(range(n_cores)))
    return assemble(out.results, n_cores, NTP)
```

```python
from contextlib import ExitStack
import numpy as np
import concourse.bass as bass
import concourse.mybir as mybir
from concourse.bass_utils import run_bass_kernel_spmd

F32 = mybir.dt.float32
BF16 = mybir.dt.bfloat16
I32 = mybir.dt.int32
AF = mybir.ActivationFunctionType
ALU = mybir.AluOpType
AX = mybir.AxisListType

D = 1024
D_IN = 6488
NS = 32
NB_S = 4
NPAGES = 128
ROUNDS = 15
ALPHA = 2.0 ** 0.25
EPS = 1e-5
NEG = -30000.0
BIG = 1.0e30


class Buf:
    __slots__ = ("name", "writer", "readers", "excl")

    def __init__(self, name, excl=False):
        self.name = name
        self.writer = None
        self.readers = []
        self.excl = excl


class Op:
    __slots__ = ("eng", "fn", "deps", "signal", "sem", "val", "is_dma", "lane")

    def __init__(self, eng, fn, is_dma):
        self.eng = eng
        self.fn = fn
        self.deps = []
        self.signal = False
        self.sem = None
        self.val = 0
        self.is_dma = is_dma
        self.lane = None


ENGS = ("pe", "act", "dve", "pool", "sp")
EPOCH = 12000


class Prog:
    def __init__(self, nc, lanes=8):
        self.nc = nc
        self.ops = []
        self.sems = {}
        self.cnt = {e: 0 for e in ENGS}
        self.waited = {e: {} for e in ENGS}
        self.lanes = {}
        self.lane_rr = {}
        for q in ("sp", "pool", "act"):
            self.lanes[q] = [[nc.alloc_semaphore("ln_%s_%d" % (q, i)), 0] for i in range(lanes)]
            self.lane_rr[q] = 0
        self.last_sig = {e: None for e in ENGS}
        self.all_dma = []
        self.fence_deps = []
        self.n_ops = 0

    def _add(self, eng, fn, reads, writes, is_dma=False):
        op = Op(eng, fn, is_dma)
        deps = set()
        for b in reads:
            if b.writer is not None:
                deps.add(b.writer)
            if b.excl:
                for r in b.readers:
                    if r.eng != eng:
                        deps.add(r)
        for b in writes:
            if b.writer is not None:
                deps.add(b.writer)
            for r in b.readers:
                deps.add(r)
        op.deps = list(deps)
        for b in reads:
            b.readers.append(op)
        for b in writes:
            b.writer = op
            b.readers = []
        self.ops.append(op)
        return op

    def pe(self, fn, reads=(), writes=()):
        return self._add("pe", fn, reads, writes)

    def act(self, fn, reads=(), writes=()):
        return self._add("act", fn, reads, writes)

    def dve(self, fn, reads=(), writes=()):
        return self._add("dve", fn, reads, writes)

    def pool(self, fn, reads=(), writes=()):
        return self._add("pool", fn, reads, writes)

    def dma(self, q, fn, reads=(), writes=()):
        import os
        if q in os.environ.get("SKIPDMA", "").split(","):
            return None
        return self._add(q, fn, reads, writes, is_dma=True)

    def _sem_for(self, eng):
        ep = self.cnt[eng] // EPOCH
        key = (eng, ep)
        if key not in self.sems:
            self.sems[key] = self.nc.alloc_semaphore("s_%s_%d" % (eng, ep))
        return self.sems[key], ep

    def flush(self, final=False):
        nc = self.nc
        ops = self.ops
        self.ops = []
        if not ops and not final:
            return
        self.n_ops += len(ops)
        needed = set()
        for op in ops:
            for d in op.deps:
                if d.is_dma:
                    continue
                if d.eng == "pe" and op.eng == "pe" and not op.is_dma:
                    continue
                needed.add(d)
        last = {}
        for op in ops:
            if not op.is_dma:
                last[op.eng] = op
        for op in last.values():
            needed.add(op)
        for op in ops:
            if op.is_dma:
                lanes = self.lanes[op.eng]
                li = self.lane_rr[op.eng]
                self.lane_rr[op.eng] = (li + 1) % len(lanes)
                lane = lanes[li]
                op.lane = (lane[0], lane[1])
                lane[1] += 16
                op.sem = lane[0]
                op.val = lane[1]
                self.all_dma.append(op)
            elif op in needed and op.sem is None:
                sem, ep = self._sem_for(op.eng)
                self.cnt[op.eng] += 1
                op.sem = sem
                op.val = self.cnt[op.eng] - ep * EPOCH
                op.signal = True
                self.last_sig[op.eng] = op
        streams = {e: [] for e in ENGS}
        for op in ops:
            streams[op.eng].append(op)
        fence = self.fence_deps

        def emit_stream(eng_name, e):
            waited = self.waited[eng_name]

            def wait(sem, val):
                if waited.get(sem.name, 0) >= val:
                    return
                waited[sem.name] = val
                e.wait_ge(sem, val)

            first = True
            for op in streams[eng_name]:
                if first:
                    for d in fence:
                        if d.sem is not None:
                            wait(d.sem, d.val)
                    first = False
                for d in op.deps:
                    if d.sem is None:
                        continue
                    if (not d.is_dma) and d.eng == "pe" and eng_name == "pe" and not op.is_dma:
                        continue
                    wait(d.sem, d.val)
                if op.is_dma:
                    if op.lane[1] > 0:
                        wait(op.lane[0], op.lane[1])
                    ins = op.fn(e)
                    ins.then_inc(op.sem, 16)
                else:
                    ins = op.fn(e)
                    if op.signal:
                        ins.then_inc(op.sem, 1)
            if final and eng_name == "sp":
                for q in self.lanes:
                    for sem, tot in self.lanes[q]:
                        if tot > 0:
                            wait(sem, tot)

        with nc.Block() as block:
            @block.tensor
            def _(e):
                emit_stream("pe", e)

            @block.scalar
            def _(e):
                emit_stream("act", e)

            @block.vector
            def _(e):
                emit_stream("dve", e)

            @block.gpsimd
            def _(e):
                emit_stream("pool", e)

            @block.sync
            def _(e):
                emit_stream("sp", e)
        fd = [op for op in self.last_sig.values() if op is not None]
        latest = {}
        for op in self.all_dma:
            latest[op.sem.name] = op
        fd += list(latest.values())
        self.fence_deps = fd
        self.all_dma = list(latest.values())


def CALL(name, *a, **k):
    return lambda e: getattr(e, name)(*a, **k)


class TT:
    __slots__ = ("t", "b")

    def __init__(self, t, name, excl=False):
        self.t = t
        self.b = Buf(name, excl)


C_ID, C_UT, C_LT, C_LE, C_CN, C_G, C_PW, C_EPS, C_ONE = 0, 128, 256, 384, 512, 640, 768, 800, 801
NCST = 832


def make_consts():
    c = np.zeros((128, NCST), np.float32)
    p = np.arange(128)[:, None]
    j = np.arange(128)[None, :]
    c[:, C_ID:C_ID + 128] = (p == j)
    c[:, C_UT:C_UT + 128] = np.where(p <= j, -1.0 / 16, 0.0)
    c[:, C_LT:C_LT + 128] = np.where(p > j, -1.0 / 16, 0.0)
    c[:, C_LE:C_LE + 128] = (p <= j)
    c[:, C_CN:C_CN + 128] = np.where(j > p, -BIG, 0.0)
    c[:, C_G:C_G + 128] = ((p // 32 == j // 32) & (p % 8 == j % 8))
    c[:, C_PW:C_PW + 32] = 2.0 ** -(np.arange(32)[None, :] + 1.0)
    c[:, C_EPS] = EPS
    c[:, C_ONE] = 1.0
    return c


O_QA, O_KA, O_VA, O_QI, O_KI, O_WI, O_QB, O_KB, O_VB, O_GB, O_AB, O_GA = (
    0, 512, 640, 768, 1280, 1344, 1352, 1864, 2376, 3400, 4424, 4440)


def w_layout():
    fm, tm = [], []
    off = 0

    def add(lst, name, pieces):
        nonlocal off
        w = sum(b - a for a, b in pieces)
        lst.append((name, off, w, pieces))
        off += w

    for j in range(4):
        add(fm, "qa%d" % j, [(O_QA + 64 * j, O_QA + 64 * j + 64), (O_QA + 64 * (4 + j), O_QA + 64 * (4 + j) + 64)])
    add(fm, "ka", [(O_KA, O_KA + 128)])
    for j in range(4):
        add(fm, "qi%d" % j, [(O_QI + 128 * j, O_QI + 128 * j + 128)])
    add(fm, "ki", [(O_KI, O_KI + 64), (O_KI, O_KI + 64)])
    for j in range(4):
        add(fm, "qb%d" % j, [(O_QB + 128 * j, O_QB + 128 * j + 128)])
    for j in range(4):
        add(fm, "kb%d" % j, [(O_KB + 128 * j, O_KB + 128 * j + 128)])
    add(fm, "ab", [(O_AB, O_AB + 16)])
    for j in range(16):
        add(fm, "gt%d" % j, [(O_GA + 128 * j, O_GA + 128 * j + 128)])
    add(tm, "ta", [(O_KA, O_KA + 256), (O_KI, O_KI + 72)])
    add(tm, "tkb", [(O_KB, O_KB + 512)])
    add(tm, "tvb0", [(O_VB, O_VB + 512)])
    add(tm, "tvb1", [(O_VB + 512, O_VB + 1024)])
    add(tm, "tgb0", [(O_GB, O_GB + 512)])
    add(tm, "tgb1", [(O_GB + 512, O_GB + 1024)])
    return fm, tm, off


def build(NTP=4096, NPOOL=5120, phases="12345", dbg=False):
    nc = bass.Bass("TRN2", target_bir_lowering=False)
    P = Prog(nc)
    NT = NTP + NS
    NBLK = NTP // 128
    TOPK = min(256, NTP // 4)
    TOPK_S = min(256, (NPAGES * 128 + 8) // 4)

    def din(name, shape, dt=F32):
        return nc.dram_tensor(name, list(shape), dt, kind="ExternalInput")

    def dout(name, shape, dt=F32):
        return nc.dram_tensor(name, list(shape), dt, kind="ExternalOutput")

    def dscr(name, shape, dt):
        return nc.dram_tensor(name, list(shape), dt, kind="Internal")

    xT_d = din("xT", [D, NT])
    x_d = din("x", [NT, D])
    ck_d = din("cache_k", [NPOOL * 8, 2048])
    cv_d = din("cache_v", [NPOOL * 8, 2048])
    cki_d = din("cache_kidx", [NPOOL * 4, 2048])
    st_d = din("state_gla", [NB_S, 4, 128, 256])
    pt_d = din("page_table", [NB_S, NPAGES], I32)
    w_in_d = din("w_in", [D, D_IN])
    wal_d = din("w_alpha2", [16, 512])
    bal_d = din("b_alpha", [1, 512])
    gng_d = din("gla_norm_g", [1, 256])
    wao_d = din("w_attn_o", [512, D])
    wgo_d = din("w_gla_o", [D, D])
    wo_d = din("w_out", [D, D])
    ln_d = {n: din(n, [1, D]) for n in ("ln1_g", "ln1_b", "ln2_g", "ln2_b")}
    wf1_d = din("w_ff1", [D, 4096])
    wf2_d = din("w_ff2", [4096, D])
    cst_d = din("cst", [128, NCST])

    y_d = dout("y", [NT, D])
    ko_d = dout("ko", [NT, 128])
    vo_d = dout("vo", [NT, 128])
    kio_d = dout("kio", [NT, 64])
    glap_d = dout("gla_p", [4, 128, 256])
    glas_d = dout("gla_s", [NB_S, 4, 128, 256])

    qaT_s = dscr("qaT_s", [4, 128, NT], BF16)
    qiT_s = dscr("qiT_s", [4, 128, NT], BF16)
    wi_s = dscr("wi_s", [NT, 8], F32)
    qbT_s = dscr("qbT_s", [4, 128, NT], BF16)
    kbT_s = dscr("kbT_s", [4, 128, NT], BF16)
    abT_s = dscr("abT_s", [16, NT], BF16)
    gtT_s = dscr("gtT_s", [16, 128, NT], F32)
    kb_s = dscr("kb_s", [NT, 512], BF16)
    vb_s = dscr("vb_s", [NT, 1024], BF16)
    gb_s = dscr("gb_s", [NT, 1024], F32)
    attnT_s = dscr("attnT_s", [4, 128, NT], BF16)
    obT_s = dscr("obT_s", [8, 128, NT], BF16)
    hT_s = dscr("hT_s", [8, 128, NT], BF16)
    h_s = dscr("h_s", [NT, D], F32)

    DBD = {}
    NJ = {"qaT": 4, "qiT": 4, "wi": 1, "qbT": 4, "kbT": 4, "abT": 1, "gtT": 16, "kb": 1, "vb": 2, "gb": 2, "attnT": 4,
          "obT": 1, "hT": 1, "h": 1, "ko": 1, "vo": 1, "kio": 1, "y": 1}
    B_gla_out = Buf("gla_out")

    def dbs(name, tok0, n, j=None):
        js = range(NJ[name]) if j is None else [j]
        out = []
        for jj in js:
            for tl in range(tok0 // 128, (tok0 + n - 1) // 128 + 1):
                key = (name, jj, tl)
                if key not in DBD:
                    DBD[key] = Buf("%s_%d_%d" % key)
                out.append(DBD[key])
        return out

    g = ExitStack()

    def SB(stack, name, shape, dt):
        return TT(stack.enter_context(nc.sbuf_tensor("sb_" + name, list(shape), dt)), name)

    def PSB(stack, name, shape, dt=F32):
        return TT(stack.enter_context(nc.psum_tensor("pp_" + name, list(shape), dt)), name, True)

    with g:
        cst = SB(g, "cst", [128, NCST], F32)
        ident_bf = SB(g, "ident_bf", [128, 128], BF16)
        i4_bf = SB(g, "i4_bf", [128, 4, 128], BF16)
        mle_bf = SB(g, "mle_bf", [128, 128], BF16)
        ones_bf = SB(g, "ones_bf", [128, 128], BF16)
        P.dma("sp", CALL("dma_start", out=cst.t[:], in_=cst_d[:, :]), [], [cst.b])
        P.dve(CALL("tensor_copy", out=ident_bf.t[:], in_=cst.t[:, C_ID:C_ID + 128]), [cst.b], [ident_bf.b])
        for k in range(4):
            P.dve(CALL("tensor_copy", out=i4_bf.t[:, k, :], in_=cst.t[:, C_ID:C_ID + 128]), [cst.b], [i4_bf.b])
        P.dve(CALL("tensor_copy", out=mle_bf.t[:], in_=cst.t[:, C_LE:C_LE + 128]), [cst.b], [mle_bf.b])
        P.pool(CALL("memset", ones_bf.t[:], 1.0), [], [ones_bf.b])
        ident_f = cst.t[:, C_ID:C_ID + 128]
        utri_f = cst.t[:, C_UT:C_UT + 128]
        ltri_f = cst.t[:, C_LT:C_LT + 128]
        cneg_f = cst.t[:, C_CN:C_CN + 128]
        eps_c = cst.t[:, C_EPS:C_EPS + 1]
        one_c = cst.t[:, C_ONE:C_ONE + 1]

        lnb = {n: SB(g, "lnb_" + n, [128, D], F32) for n in ln_d}
        s12 = ExitStack()
        with s12:
            kA = SB(s12, "kA", [128, NT], BF16)
            kB = SB(s12, "kB", [128, NT], BF16)
            kiA = SB(s12, "kiA", [128, NT], BF16)
            kiB = SB(s12, "kiB", [128, NT], BF16)
            vaug = SB(s12, "vaug", [128, NBLK, 2, 65], BF16)
            for tt_ in (kA, kB, kiA, kiB):
                P.pool(CALL("memset", tt_.t[:], 0.0), [], [tt_.b])
            P.pool(CALL("memset", vaug.t[:], 1.0), [], [vaug.b])

            if "1" in phases:
                s1 = ExitStack()
                with s1:
                    fm, tm, ncol = w_layout()
                    w_sb = SB(s1, "w_sb", [128, 8, ncol], BF16)
                    w_src = w_in_d.rearrange("(kc p) n -> p kc n", p=128)
                    wb = {}
                    for lst in (fm, tm):
                        for (name, off, wd, pieces) in lst:
                            b = Buf("w_" + name)
                            wb[name] = b
                            o = off
                            for (a0, a1) in pieces:
                                P.dma("pool", CALL("dma_start",
                                    out=w_sb.t[:, :, o:o + (a1 - a0)], in_=w_src[:, :, a0:a1]), [], [b])
                                o += a1 - a0
                    import os as _os
                    NXS = int(_os.environ.get("XSLOTS", "2"))
                    xts = [SB(s1, "xT%d" % i, [128, 8, 512], BF16) for i in range(NXS)]
                    stg_b = [SB(s1, "stgb%d" % i, [128, 512], BF16) for i in range(4)]
                    stg_f = [SB(s1, "stgf%d" % i, [128, 512], F32) for i in range(4)]
                    pss = [PSB(s1, "ps1_%d" % i, [128, 512]) for i in range(6)]
                    rr = {"ps": 0, "sb": 0, "sf": 0, "ev": 0}
                    xT_src = xT_d.rearrange("(kc p) t -> p kc t", p=128)

                    def nxt(key, n):
                        v = rr[key]
                        rr[key] = (v + 1) % n
                        return v

                    def evac(out_ap, in_ap, reads, writes):
                        if nxt("ev", 2) == 0:
                            P.act(CALL("activation", out=out_ap, in_=in_ap, func=AF.Copy), reads, writes)
                        else:
                            P.dve(CALL("tensor_copy", out=out_ap, in_=in_ap), reads, writes)

                    STW = int(_os.environ.get("STW", "512"))
                    sts = [(t0, min(STW, NTP - t0)) for t0 in range(0, NTP, STW)] + [(NTP, NS)]

                    def load_x(si):
                        t0, W = sts[si]
                        xt = xts[si % NXS]
                        P.dma("pool", CALL("dma_start", out=xt.t[:, :, 0:W], in_=xT_src[:, :, t0:t0 + W]), [], [xt.b])

                    load_x(0)
                    for si, (t0, W) in enumerate(sts):
                        if si + 1 < len(sts):
                            load_x(si + 1)
                        xt = xts[si % NXS]
                        for (name, off, m, pieces) in fm:
                            ps = pss[nxt("ps", 6)]
                            for kc in range(8):
                                P.pe(CALL("matmul",
                                    ps.t[0:m, 0:W], lhsT=w_sb.t[:, kc, off:off + m], rhs=xt.t[:, kc, 0:W],
                                    start=(kc == 0), stop=(kc == 7)), [xt.b, wb[name]], [ps.b])
                            if name == "ka":
                                evac(kA.t[0:64, t0:t0 + W], ps.t[0:64, 0:W], [ps.b], [kA.b])
                                evac(kB.t[64:128, t0:t0 + W], ps.t[64:128, 0:W], [ps.b], [kB.b])
                            elif name == "ki":
                                evac(kiA.t[0:64, t0:t0 + W], ps.t[0:64, 0:W], [ps.b], [kiA.b])
                                evac(kiB.t[64:128, t0:t0 + W], ps.t[64:128, 0:W], [ps.b], [kiB.b])
                            else:
                                kind = name[:2]
                                j = int(name[2:]) if len(name) > 2 else 0
                                if kind == "gt":
                                    sg = stg_f[nxt("sf", 4)]
                                    dst = gtT_s[j, :, t0:t0 + W]
                                    dbn = "gtT"
                                else:
                                    sg = stg_b[nxt("sb", 4)]
                                    dst = {"qa": qaT_s, "qi": qiT_s, "qb": qbT_s, "kb": kbT_s}[kind][j, :, t0:t0 + W] \
                                        if kind != "ab" else abT_s[:, t0:t0 + W]
                                    dbn = {"qa": "qaT", "qi": "qiT", "qb": "qbT", "kb": "kbT", "ab": "abT"}[kind]
                                evac(sg.t[0:m, 0:W], ps.t[0:m, 0:W], [ps.b], [sg.b])
                                P.dma("sp", CALL("dma_start", out=dst, in_=sg.t[0:m, 0:W]),
                                      [sg.b], dbs(dbn, t0, W, j if NJ[dbn] > 1 else 0))
                        ntile = (W + 127) // 128
                        for tt_i in range(ntile):
                            T = min(128, W - tt_i * 128)
                            tk0 = t0 + tt_i * 128
                            for (name, off, n, pieces) in tm:
                                ps = pss[nxt("ps", 6)]
                                for kc in range(8):
                                    P.pe(CALL("matmul",
                                        ps.t[0:T, 0:n], lhsT=xt.t[:, kc, tt_i * 128:tt_i * 128 + T],
                                        rhs=w_sb.t[:, kc, off:off + n], start=(kc == 0), stop=(kc == 7)),
                                        [xt.b, wb[name]], [ps.b])
                                if name == "ta":
                                    sg = stg_f[nxt("sf", 4)]
                                    evac(sg.t[0:T, 0:n], ps.t[0:T, 0:n], [ps.b], [sg.b])
                                    for (dst, c0, c1, dbn) in ((ko_d, 0, 128, "ko"), (vo_d, 128, 256, "vo"),
                                                               (kio_d, 256, 320, "kio"), (wi_s, 320, 328, "wi")):
                                        P.dma("sp", CALL("dma_start",
                                            out=dst[tk0:tk0 + T, :], in_=sg.t[0:T, c0:c1]), [sg.b], dbs(dbn, tk0, T))
                                    if tk0 < NTP:
                                        blk = tk0 // 128
                                        P.dve(CALL("tensor_copy",
                                            out=vaug.t[:, blk, :, 1:65],
                                            in_=ps.t[:, 128:256].rearrange("p (h d) -> p h d", h=2)), [ps.b], [vaug.b])
                                else:
                                    if name.startswith("tgb"):
                                        sg = stg_f[nxt("sf", 4)]
                                        dst, dbn = gb_s, "gb"
                                    else:
                                        sg = stg_b[nxt("sb", 4)]
                                        dst, dbn = (kb_s, "kb") if name == "tkb" else (vb_s, "vb")
                                    c0 = 512 if name.endswith("1") else 0
                                    evac(sg.t[0:T, 0:n], ps.t[0:T, 0:n], [ps.b], [sg.b])
                                    P.dma("sp", CALL("dma_start",
                                        out=dst[tk0:tk0 + T, c0:c0 + n], in_=sg.t[0:T, 0:n]), [sg.b],
                                        dbs(dbn, tk0, T, (c0 // 512) if NJ[dbn] > 1 else 0))
                    P.flush()

            if "2" in phases:
                s2 = ExitStack()
                with s2:
                    NK = NTP
                    isc = [SB(s2, "isc%d" % i, [128, NK], F32) for i in range(2)]
                    mbs = [SB(s2, "mb%d" % i, [128, NK], BF16) for i in range(2)]
                    junk = SB(s2, "junk2", [128, NK], BF16)
                    rsl = [SB(s2, "rsl%d" % i, [128, 512], BF16) for i in range(4)]
                    ptl = [SB(s2, "ptl%d" % i, [128, 512], BF16) for i in range(4)]
                    qib = [SB(s2, "qib%d" % i, [128, 4, 128], BF16) for i in range(2)]
                    qab = [SB(s2, "qab%d" % i, [128, 4, 128], BF16) for i in range(2)]
                    wib = [SB(s2, "wib%d" % i, [128, 8], F32) for i in range(2)]
                    dgs = [SB(s2, "dg%d" % i, [128, 8, 128], BF16) for i in range(2)]
                    stt = [SB(s2, "st%d" % i, [128, 64], F32) for i in range(2)]
                    ost = [[SB(s2, "ost%d_%d" % (i, k), [128, 260], F32) for k in range(2)] for i in range(2)]
                    atk = [SB(s2, "atk%d" % i, [128, 512], BF16) for i in range(2)]
                    atT = SB(s2, "atT", [128, 4, 128], BF16)
                    rct = SB(s2, "rct", [128, 8], F32)
                    tmpd = SB(s2, "tmpd", [128, 128], F32)
                    psd = [PSB(s2, "psd%d" % i, [128, 512]) for i in range(2)]
                    psi = [PSB(s2, "psi%d" % i, [128, 512]) for i in range(2)]
                    pss_ = [PSB(s2, "pss%d" % i, [128, 512]) for i in range(2)]
                    pos_ = [PSB(s2, "pos%d" % i, [128, 512]) for i in range(2)]
                    rr2 = {"d": 0, "i": 0, "s": 0, "r": 0, "p": 0}

                    def nx2(k, n):
                        v = rr2[k]
                        rr2[k] = (v + 1) % n
                        return v

                    S_MIN1, S_MIN2, S_MAX, S_W0, S_MID, S_CNT, S_U, S_THR, S_WALL = 0, 1, 2, 3, 4, 5, 6, 7, 8
                    WSCALE = (8.0 ** -0.5) / 8.0

                    def stage_a(i):
                        sl = i % 2
                        q0 = i * 128
                        nk = (i + 1) * 128
                        qi_, qa_, wi_, dg_, I_, st_ = qib[sl], qab[sl], wib[sl], dgs[sl], isc[sl], stt[sl]
                        P.dma("sp", CALL("dma_start", out=qi_.t[:], in_=qiT_s[:, :, q0:q0 + 128].rearrange("j p t -> p j t")),
                              dbs("qiT", q0, 128), [qi_.b])
                        P.dma("sp", CALL("dma_start", out=qa_.t[:], in_=qaT_s[:, :, q0:q0 + 128].rearrange("j p t -> p j t")),
                              dbs("qaT", q0, 128), [qa_.b])
                        P.dma("sp", CALL("dma_start", out=wi_.t[:], in_=wi_s[q0:q0 + 128, :]), dbs("wi", q0, 128), [wi_.b])
                        for h in range(8):
                            P.pool(CALL("tensor_scalar", out=dg_.t[:, h, :], in0=ident_f, scalar1=wi_.t[:, h:h + 1],
                                                                 scalar2=WSCALE, op0=ALU.mult, op1=ALU.mult),
                                   [wi_.b, cst.b], [dg_.b])
                        nch = (nk + 511) // 512
                        steps = [(c, h) for c in range(nch) for h in range(8)]
                        pds = {}
                        pIs = {}

                        def dots(k):
                            c, h = steps[k]
                            k0 = c * 512
                            Wc = min(512, nk - k0)
                            pd = psd[nx2("d", 2)]
                            pds[k] = pd
                            kis = kiA if h % 2 == 0 else kiB
                            P.pe(CALL("matmul", pd.t[:, 0:Wc], lhsT=qi_.t[:, h // 2, :], rhs=kis.t[:, k0:k0 + Wc], start=True, stop=True),
                                 [qi_.b, kis.b], [pd.b])

                        dots(0)
                        for k, (c, h) in enumerate(steps):
                            k0 = c * 512
                            Wc = min(512, nk - k0)
                            if h == 0:
                                pIs[c] = psi[nx2("i", 2)]
                            pI = pIs[c]
                            pd = pds[k]
                            r_ = rsl[nx2("r", 4)]
                            P.act(CALL("activation", out=r_.t[:, 0:Wc], in_=pd.t[:, 0:Wc], func=AF.Relu), [pd.b], [r_.b])
                            if k + 1 < len(steps):
                                dots(k + 1)
                            P.pe(CALL("matmul", pI.t[:, 0:Wc], lhsT=dg_.t[:, h, :], rhs=r_.t[:, 0:Wc], start=(h == 0), stop=(h == 7)),
                                 [dg_.b, r_.b], [pI.b])
                            if h == 7:
                                P.act(CALL("activation", out=I_.t[:, k0:k0 + Wc], in_=pI.t[:, 0:Wc], func=AF.Copy), [pI.b], [I_.b])
                        d0 = i * 128
                        P.dve(CALL("tensor_tensor", out=tmpd.t[:], in0=I_.t[:, d0:d0 + 128], in1=cneg_f, op=ALU.subtract),
                              [I_.b, cst.b], [tmpd.b])
                        P.dve(CALL("tensor_reduce", out=st_.t[:, S_MIN2:S_MIN2 + 1], in_=tmpd.t[:], axis=AX.X, op=ALU.min),
                              [tmpd.b], [st_.b])
                        if i > 0:
                            P.dve(CALL("tensor_reduce", out=st_.t[:, S_MIN1:S_MIN1 + 1], in_=I_.t[:, 0:d0], axis=AX.X,
                                                            op=ALU.min), [I_.b], [st_.b])
                            P.dve(CALL("tensor_tensor", out=st_.t[:, S_MIN2:S_MIN2 + 1], in0=st_.t[:, S_MIN2:S_MIN2 + 1],
                                                            in1=st_.t[:, S_MIN1:S_MIN1 + 1], op=ALU.min), [st_.b], [st_.b])
                        P.dve(CALL("tensor_tensor", out=I_.t[:, d0:d0 + 128], in0=I_.t[:, d0:d0 + 128], in1=cneg_f, op=ALU.add),
                              [I_.b, cst.b], [I_.b])
                        P.dve(CALL("tensor_reduce", out=st_.t[:, S_MAX:S_MAX + 1], in_=I_.t[:, 0:nk], axis=AX.X, op=ALU.max),
                              [I_.b], [st_.b])

                    def stage_b(i):
                        sl = i % 2
                        nk = (i + 1) * 128
                        I_, st_, mb_ = isc[sl], stt[sl], mbs[sl]
                        c_ = lambda k: st_.t[:, k:k + 1]
                        P.dve(CALL("tensor_tensor", out=c_(S_W0), in0=c_(S_MAX), in1=c_(S_MIN2), op=ALU.subtract), [st_.b], [st_.b])
                        P.dve(CALL("tensor_scalar", out=c_(S_U), in0=c_(S_W0), scalar1=-1e-3, scalar2=-1e-4, op0=ALU.mult,
                                                        op1=ALU.add), [st_.b], [st_.b])
                        P.dve(CALL("tensor_tensor", out=c_(S_MIN2), in0=c_(S_MIN2), in1=c_(S_U), op=ALU.add), [st_.b], [st_.b])
                        P.dve(CALL("tensor_tensor", out=c_(S_W0), in0=c_(S_MAX), in1=c_(S_MIN2), op=ALU.subtract), [st_.b], [st_.b])
                        P.dve(CALL("tensor_scalar", out=c_(S_W0), in0=c_(S_W0), scalar1=1.0001, scalar2=1e-6, op0=ALU.mult,
                                                        op1=ALU.add), [st_.b], [st_.b])
                        P.dve(CALL("tensor_scalar", out=st_.t[:, S_WALL:S_WALL + 32], in0=cst.t[:, C_PW:C_PW + 32],
                                                        scalar1=c_(S_W0), scalar2=None, op0=ALU.mult), [st_.b, cst.b], [st_.b])
                        P.dve(CALL("tensor_tensor", out=c_(S_MID), in0=c_(S_MIN2), in1=c_(S_WALL), op=ALU.add), [st_.b], [st_.b])
                        for r in range(ROUNDS):
                            P.dve(CALL("tensor_scalar", out=junk.t[:, 0:nk], in0=I_.t[:, 0:nk], scalar1=c_(S_MID), scalar2=0.0,
                                                            op0=ALU.is_ge, op1=ALU.add, accum_out=c_(S_CNT)),
                                  [I_.b, st_.b], [junk.b, st_.b])
                            P.dve(CALL("tensor_scalar", out=c_(S_U), in0=c_(S_CNT), scalar1=TOPK - 0.5,
                                                                 scalar2=c_(S_WALL + r), op0=ALU.is_ge, op1=ALU.mult),
                                  [st_.b], [st_.b])
                            nxt_w = S_WALL + r + 1 if r + 1 < ROUNDS else S_WALL + r
                            dst = S_MID if r + 1 < ROUNDS else S_THR
                            P.dve(CALL("scalar_tensor_tensor",
                                out=c_(dst), in0=c_(S_U), scalar=c_(nxt_w), in1=c_(S_MID), op0=ALU.subtract, op1=ALU.add),
                                [st_.b], [st_.b])
                        P.dve(CALL("tensor_scalar", out=mb_.t[:, 0:nk], in0=I_.t[:, 0:nk], scalar1=c_(S_THR), scalar2=NEG,
                                                        op0=ALU.is_lt, op1=ALU.mult), [I_.b, st_.b], [mb_.b])

                    def stage_c(i):
                        sl = i % 2
                        qa_, mb_ = qab[sl], mbs[sl]
                        steps = [(kvh, c) for kvh in range(2) for c in range(i + 1)]
                        pls = {}

                        def logits(k):
                            kvh, c = steps[k]
                            kk = kA if kvh == 0 else kB
                            k0 = c * 128
                            ps_ = pss_[nx2("s", 2)]
                            pls[k] = ps_
                            P.pe(CALL("matmul", ps_.t[:, :], lhsT=kk.t[:, k0:k0 + 128], rhs=qa_.t[:, :, :], start=True, stop=False),
                                 [kk.b, qa_.b], [ps_.b])
                            P.pe(CALL("matmul", ps_.t[:, :], lhsT=mb_.t[:, k0:k0 + 128], rhs=i4_bf.t[:, :, :], start=False, stop=True),
                                 [mb_.b, i4_bf.b], [ps_.b])

                        logits(0)
                        for k, (kvh, c) in enumerate(steps):
                            po = pos_[kvh]
                            ps_ = pls[k]
                            pt_ = ptl[nx2("p", 4)]
                            P.act(CALL("activation", out=pt_.t[:, :], in_=ps_.t[:, :], func=AF.Exp, scale=0.125), [ps_.b], [pt_.b])
                            if k + 1 < len(steps):
                                logits(k + 1)
                            for j in range(4):
                                P.pe(CALL("matmul", po.t[:, j * 65:(j + 1) * 65], lhsT=pt_.t[:, j * 128:(j + 1) * 128],
                                          rhs=vaug.t[:, c, kvh, :], start=(c == 0 and j == 0), stop=(c == i), skip_group_check=True),
                                     [vaug.b, pt_.b], [po.b])
                            if c == i:
                                o_ = ost[sl][kvh]
                                P.act(CALL("activation", out=o_.t[:, :], in_=po.t[:, 0:260], func=AF.Copy), [po.b], [o_.b])

                    def stage_c_norm(i):
                        sl = i % 2
                        at_ = atk[sl]
                        for kvh in range(2):
                            o3 = ost[sl][kvh].t[:, :].rearrange("p (j e) -> p j e", e=65)
                            P.dve(CALL("reciprocal", out=rct.t[:, kvh * 4:kvh * 4 + 4].unsqueeze(2), in_=o3[:, :, 0:1]), [ost[sl][kvh].b], [rct.b])
                            P.dve(CALL("tensor_tensor", out=at_.t[:, kvh * 256:(kvh + 1) * 256].rearrange("p (j d) -> p j d", j=4),
                                       in0=o3[:, :, 1:65], in1=rct.t[:, kvh * 4:kvh * 4 + 4].unsqueeze(2).broadcast_to([128, 4, 64]),
                                       op=ALU.mult), [ost[sl][kvh].b, rct.b], [at_.b])

                    def stage_c_T(i):
                        sl = i % 2
                        q0 = i * 128
                        at_ = atk[sl]
                        po = pos_[0]
                        psx_ = po.t.bitcast(BF16)
                        for kc in range(4):
                            P.pe(CALL("transpose", psx_[:, kc * 128:(kc + 1) * 128], at_.t[:, kc * 128:(kc + 1) * 128], ident_bf.t[:, :]),
                                 [at_.b, ident_bf.b], [po.b])
                        P.act(CALL("activation", out=atT.t[:, :, :], in_=psx_[:, 0:512].rearrange("p (k t) -> p k t", k=4), func=AF.Copy),
                              [po.b], [atT.b])
                        P.dma("sp", CALL("dma_start", out=attnT_s[:, :, q0:q0 + 128].rearrange("k p t -> p k t"), in_=atT.t[:, :, :]),
                              [atT.b], dbs("attnT", q0, 128))

                    for i in range(NBLK):
                        stage_a(i)
                        if i >= 2:
                            stage_c_T(i - 2)
                        if i >= 1:
                            stage_c(i - 1)
                        stage_b(i)
                        if i >= 1:
                            stage_c_norm(i - 1)
                    if NBLK >= 2:
                        stage_c_T(NBLK - 2)
                    stage_c(NBLK - 1)
                    stage_c_norm(NBLK - 1)
                    stage_c_T(NBLK - 1)
                    P.flush()
            if "5" in phases:
                s2b = ExitStack()
                with s2b:
                    ROUNDS_S = 24
                    NKP = NPAGES * 128
                    SEGW = NKP // 4
                    IW = SEGW + 8
                    R = SB(s2b, "R2b", [128, 3 * NKP], BF16)
                    BR0, BR1, BR2 = Buf("BR0"), Buf("BR1"), Buf("BR2")
                    O1, O2 = NKP, 2 * NKP
                    Is = SB(s2b, "Is", [128, IW], F32)
                    MBs = SB(s2b, "MBs", [128, IW], BF16)
                    pti = SB(s2b, "pti", [128, 1], I32)
                    ptf = SB(s2b, "ptf", [128, 1], F32)
                    idf = SB(s2b, "idf", [128, 16], F32)
                    idxs = [SB(s2b, "idx%d" % i, [128, 16], I32) for i in range(NB_S)]
                    qis = SB(s2b, "qis", [64, 8, NS], BF16)
                    qs0 = SB(s2b, "qs0", [128, 4, NS], BF16)
                    qs1 = SB(s2b, "qs1", [128, 4, NS], BF16)
                    wis = SB(s2b, "wis", [8, NB_S, 8], F32)
                    dss = SB(s2b, "dss", [8, NB_S, 8, 8], BF16)
                    rs2 = [SB(s2b, "rs2_%d" % i, [64, 512], BF16) for i in range(4)]
                    qisb = SB(s2b, "qisb", [64, NB_S, 64], BF16)
                    wsel = SB(s2b, "wsel", [64, NB_S, 8], BF16)
                    qsb = SB(s2b, "qsb", [128, NB_S, 64], BF16)
                    sel2 = SB(s2b, "sel2", [128, 16, 64], BF16)
                    sg2 = [SB(s2b, "sg2_%d" % i, [8, 512], F32) for i in range(4)]
                    pt2 = [SB(s2b, "pt2_%d" % i, [128, 256], BF16) for i in range(4)]
                    sst = SB(s2b, "sst", [128, 64], F32)
                    vnew = SB(s2b, "vnew", [8, 128], BF16)
                    rcp2 = SB(s2b, "rcp2", [1, 64], F32)
                    bcs2 = SB(s2b, "bcs2", [64, 64], F32)
                    as2 = SB(s2b, "as2", [64, 2, 4, 8], BF16)
                    pst = [PSB(s2b, "pst%d" % i, [128, 1024], BF16) for i in range(2)]
                    psd2 = [PSB(s2b, "psd2_%d" % i, [128, 512]) for i in range(2)]
                    psi2s = [PSB(s2b, "psi2_%d" % i, [128, 512]) for i in range(2)]
                    psi2 = psi2s[0]
                    pso2s = [psi2s[1], PSB(s2b, "pso2b", [128, 512])]
                    psm2 = PSB(s2b, "psm2", [128, 512])
                    rr3 = {"t": 0, "d": 0, "r": 0, "g": 0, "p": 0, "e": 0, "i": 0}

                    def nx3(k, n):
                        v = rr3[k]
                        rr3[k] = (v + 1) % n
                        return v

                    def evac3(out_ap, in_ap, reads, writes):
                        if nx3("e", 2) == 0:
                            P.act(CALL("activation", out=out_ap, in_=in_ap, func=AF.Copy), reads, writes)
                        else:
                            P.dve(CALL("tensor_copy", out=out_ap, in_=in_ap), reads, writes)

                    S0 = NTP
                    WSC = (8.0 ** -0.5) / 8.0
                    G_f = cst.t[:, C_G:C_G + 128]
                    P.pool(CALL("memset", qs0.t[:, :, :], 0.0), [], [qs0.b])
                    P.pool(CALL("memset", qs1.t[:, :, :], 0.0), [], [qs1.b])
                    P.pool(CALL("memset", Is.t[:, SEGW:IW], -BIG), [], [Is.b])
                    P.dma("sp", CALL("dma_start", out=qs0.t[0:64, :, :], in_=qaT_s[:, 0:64, S0:S0 + NS].rearrange("j p t -> p j t")),
                          dbs("qaT", S0, NS), [qs0.b])
                    P.dma("sp", CALL("dma_start", out=qs1.t[64:128, :, :], in_=qaT_s[:, 64:128, S0:S0 + NS].rearrange("j p t -> p j t")),
                          dbs("qaT", S0, NS), [qs1.b])
                    for j in range(4):
                        for par in range(2):
                            P.dma("sp", CALL("dma_start", out=qis.t[:, 2 * j + par, :], in_=qiT_s[j, 64 * par:64 * par + 64, S0:S0 + NS]),
                                  dbs("qiT", S0, NS), [qis.b])
                    P.dma("sp", CALL("dma_start", out=wis.t[:, :, :], in_=wi_s[S0:S0 + NS, :].rearrange("(b q) h -> q b h", q=8)),
                          dbs("wi", S0, NS), [wis.b])
                    for b in range(NB_S):
                        for h in range(8):
                            P.dve(CALL("tensor_scalar", out=dss.t[:, b, h, :], in0=cst.t[0:8, C_ID:C_ID + 8], scalar1=wis.t[:, b, h:h + 1],
                                       scalar2=WSC, op0=ALU.mult, op1=ALU.mult), [wis.b, cst.b], [dss.b])
                    for bs in range(16):
                        P.dve(CALL("tensor_copy", out=sel2.t[:, bs, :].rearrange("p (a q) -> p a q", a=8),
                                   in_=cst.t[:, C_ID + bs * 8:C_ID + bs * 8 + 8].unsqueeze(1).broadcast_to([128, 8, 8])), [cst.b], [sel2.b])

                    def page_idx(b):
                        P.dma("sp", CALL("dma_start", out=pti.t[:, :], in_=pt_d[b:b + 1, :].rearrange("o p -> p o")), [], [pti.b])
                        P.dve(CALL("tensor_copy", out=ptf.t[:, :], in_=pti.t[:, :]), [pti.b], [ptf.b])
                        for o in range(4):
                            P.dve(CALL("tensor_scalar", out=idf.t[:, o:o + 1], in0=ptf.t[:, :], scalar1=4.0, scalar2=float(o),
                                       op0=ALU.mult, op1=ALU.add), [ptf.b], [idf.b])
                        for o in range(8):
                            P.dve(CALL("tensor_scalar", out=idf.t[:, 4 + o:5 + o], in0=ptf.t[:, :], scalar1=8.0, scalar2=float(o),
                                       op0=ALU.mult, op1=ALU.add), [ptf.b], [idf.b])
                        P.dve(CALL("tensor_copy", out=idxs[b].t[:, 0:12], in_=idf.t[:, 0:12]), [idf.b], [idxs[b].b])

                    def gather(b, src_d, n_o, icol0, dst0, bufR):
                        for o in range(n_o):
                            P.dma("pool", CALL("indirect_dma_start", out=R.t[:, dst0 + o * 2048:dst0 + (o + 1) * 2048], out_offset=None,
                                               in_=src_d[:, :],
                                               in_offset=bass.IndirectOffsetOnAxis(ap=idxs[b].t[:, icol0 + o:icol0 + o + 1], axis=0)),
                                  [idxs[b].b], [bufR])

                    for b in range(NB_S):
                        page_idx(b)

                    for b in range(NB_S):
                        P.dve(CALL("tensor_copy", out=qisb.t[:, b, :].rearrange("p (h q) -> p h q", h=8), in_=qis.t[:, :, 8 * b:8 * b + 8]),
                              [qis.b], [qisb.b])
                        ptw = pst[nx3("t", 2)]
                        P.pe(CALL("transpose", ptw.t[0:64, 0:8], dss.t[:, b, :, :].rearrange("p h q -> p (h q)"), ident_bf.t[0:8, 0:8]),
                             [dss.b, ident_bf.b], [ptw.b])
                        P.dve(CALL("tensor_copy", out=wsel.t[:, b, :], in_=ptw.t[0:64, 0:8]), [ptw.b], [wsel.b])
                    for b in range(NB_S):
                        gather(b, cki_d, 4, 0, 0, BR0)
                        if b == 0:
                            gather(0, cv_d, 8, 4, O2, BR2)
                        for g8 in range(16):
                            pt_ = pst[nx3("t", 2)]
                            for k in range(8):
                                t = g8 * 8 + k
                                P.pe(CALL("transpose", pt_.t[0:64, k * 128:(k + 1) * 128], R.t[:, t * 64:(t + 1) * 64], ident_bf.t[:, :]),
                                     [BR0, ident_bf.b], [pt_.b])
                            evac3(R.t[0:64, O1 + g8 * 1024:O1 + (g8 + 1) * 1024], pt_.t[0:64, :], [pt_.b], [BR1])
                        NCH = NKP // 512 + 1
                        pdd = {}

                        def idots(c, b=b):
                            new = (c == NKP // 512)
                            Wc = 8 if new else 512
                            pd = psd2[nx3("d", 2)]
                            pdd[c] = pd
                            rhs = kiA.t[0:64, S0 + 8 * b:S0 + 8 * b + 8] if new else R.t[0:64, O1 + c * 512:O1 + (c + 1) * 512]
                            P.pe(CALL("matmul", pd.t[0:64, 0:Wc], lhsT=qisb.t[:, b, :], rhs=rhs, start=True, stop=True),
                                 [qisb.b, kiA.b if new else BR1], [pd.b])

                        idots(0)
                        for c in range(NCH):
                            new = (c == NKP // 512)
                            Wc = 8 if new else 512
                            pd = pdd[c]
                            r_ = rs2[nx3("r", 4)]
                            P.act(CALL("activation", out=r_.t[:, 0:Wc], in_=pd.t[0:64, 0:Wc], func=AF.Relu), [pd.b], [r_.b])
                            if c + 1 < NCH:
                                idots(c + 1)
                            pi_ = psi2s[nx3("i", 2)]
                            P.pe(CALL("matmul", pi_.t[0:8, 0:Wc], lhsT=wsel.t[:, b, :], rhs=r_.t[:, 0:Wc], start=True, stop=True),
                                 [wsel.b, r_.b], [pi_.b])
                            sg_ = sg2[nx3("g", 4)]
                            if new:
                                P.dve(CALL("tensor_tensor", out=sg_.t[:, 0:8], in0=pi_.t[0:8, 0:8], in1=cst.t[0:8, C_CN:C_CN + 8], op=ALU.add),
                                      [pi_.b, cst.b], [sg_.b])
                                P.dma("sp", CALL("dma_start", out=Is.t[b * 32:b * 32 + 8, SEGW:IW], in_=sg_.t[:, 0:8]), [sg_.b], [Is.b])
                            else:
                                P.dve(CALL("tensor_copy", out=sg_.t[:, :], in_=pi_.t[0:8, :]), [pi_.b], [sg_.b])
                                seg, cc = c // (SEGW // 512), c % (SEGW // 512)
                                r0 = b * 32 + seg * 8
                                P.dma("sp", CALL("dma_start", out=Is.t[r0:r0 + 8, cc * 512:(cc + 1) * 512], in_=sg_.t[:, :]), [sg_.b], [Is.b])

                    gather(0, ck_d, 8, 4, 0, BR0)
                    c2 = lambda k: sst.t[:, k:k + 1]
                    Q_MAX, Q_MIN, Q_A, Q_HI, Q_LO, Q_W0, Q_MID, Q_CNT, Q_U, Q_THR, Q_WALL = 0, 1, 2, 3, 4, 5, 6, 7, 8, 9, 16
                    P.dve(CALL("tensor_reduce", out=c2(Q_MAX), in_=Is.t[:, 0:IW], axis=AX.X, op=ALU.max), [Is.b], [sst.b])
                    P.dve(CALL("tensor_reduce", out=c2(Q_MIN), in_=Is.t[:, 0:SEGW], axis=AX.X, op=ALU.min), [Is.b], [sst.b])
                    P.dve(CALL("tensor_scalar", out=c2(Q_MIN), in0=c2(Q_MIN), scalar1=-1.0, scalar2=None, op0=ALU.mult), [sst.b], [sst.b])
                    P.dve(CALL("tensor_tensor", out=c2(Q_A), in0=c2(Q_MAX), in1=c2(Q_MIN), op=ALU.max), [sst.b], [sst.b])
                    P.pe(CALL("matmul", psi2.t[:, 0:1], lhsT=G_f, rhs=c2(Q_A), start=True, stop=True), [sst.b, cst.b], [psi2.b])
                    P.dve(CALL("tensor_scalar", out=c2(Q_HI), in0=psi2.t[:, 0:1], scalar1=1.001, scalar2=1e-3, op0=ALU.mult, op1=ALU.add),
                          [psi2.b], [sst.b])
                    P.dve(CALL("tensor_scalar", out=c2(Q_LO), in0=c2(Q_HI), scalar1=-1.0, scalar2=None, op0=ALU.mult), [sst.b], [sst.b])
                    P.dve(CALL("tensor_scalar", out=c2(Q_W0), in0=c2(Q_HI), scalar1=2.0, scalar2=None, op0=ALU.mult), [sst.b], [sst.b])
                    P.dve(CALL("tensor_scalar", out=sst.t[:, Q_WALL:Q_WALL + 32], in0=cst.t[:, C_PW:C_PW + 32], scalar1=c2(Q_W0), scalar2=None,
                               op0=ALU.mult), [sst.b, cst.b], [sst.b])
                    P.dve(CALL("tensor_tensor", out=c2(Q_MID), in0=c2(Q_LO), in1=c2(Q_WALL), op=ALU.add), [sst.b], [sst.b])
                    for r in range(ROUNDS_S):
                        P.dve(CALL("tensor_scalar", out=MBs.t[:, :], in0=Is.t[:, :], scalar1=c2(Q_MID), scalar2=0.0, op0=ALU.is_ge,
                                   op1=ALU.add, accum_out=c2(Q_CNT)), [Is.b, sst.b], [MBs.b, sst.b])
                        P.pe(CALL("matmul", psi2.t[:, 0:1], lhsT=G_f, rhs=c2(Q_CNT), start=True, stop=True), [sst.b, cst.b], [psi2.b])
                        P.dve(CALL("tensor_scalar", out=c2(Q_U), in0=psi2.t[:, 0:1], scalar1=TOPK_S - 0.5, scalar2=c2(Q_WALL + r),
                                   op0=ALU.is_ge, op1=ALU.mult), [psi2.b, sst.b], [sst.b])
                        nxt_w = Q_WALL + r + 1 if r + 1 < ROUNDS_S else Q_WALL + r
                        dst = Q_MID if r + 1 < ROUNDS_S else Q_THR
                        P.dve(CALL("scalar_tensor_tensor", out=c2(dst), in0=c2(Q_U), scalar=c2(nxt_w), in1=c2(Q_MID), op0=ALU.subtract,
                                   op1=ALU.add), [sst.b], [sst.b])
                    P.dve(CALL("tensor_scalar", out=MBs.t[:, :], in0=Is.t[:, :], scalar1=c2(Q_THR), scalar2=NEG, op0=ALU.is_lt, op1=ALU.mult),
                          [Is.b, sst.b], [MBs.b])

                    for b in range(NB_S):
                        P.dve(CALL("tensor_copy", out=qsb.t[:, b, 0:32].rearrange("p (j q) -> p j q", j=4), in_=qs0.t[:, :, 8 * b:8 * b + 8]),
                              [qs0.b], [qsb.b])
                        P.dve(CALL("tensor_copy", out=qsb.t[:, b, 32:64].rearrange("p (j q) -> p j q", j=4), in_=qs1.t[:, :, 8 * b:8 * b + 8]),
                              [qs1.b], [qsb.b])
                    for b in range(NB_S):
                        if b >= 1:
                            gather(b, cv_d, 8, 4, O2, BR2)
                        P.dma("pool", CALL("dma_start", out=vnew.t[:, :], in_=vo_d[S0 + 8 * b:S0 + 8 * b + 8, :]), dbs("vo", S0, NS), [vnew.b])
                        for g8 in range(16):
                            pt_ = pst[nx3("t", 2)]
                            for k in range(8):
                                t = g8 * 8 + k
                                P.pe(CALL("transpose", pt_.t[:, k * 128:(k + 1) * 128], R.t[:, t * 128:(t + 1) * 128], ident_bf.t[:, :]),
                                     [BR0, ident_bf.b], [pt_.b])
                            evac3(R.t[:, O1 + g8 * 1024:O1 + (g8 + 1) * 1024], pt_.t[:, :], [pt_.b], [BR1])
                        if b + 1 < NB_S:
                            gather(b + 1, ck_d, 8, 4, 0, BR0)
                        ngrp = 128 // 4
                        pll = {}

                        def alogits(g4, b=b):
                            new = (g4 == ngrp)
                            nkk = 8 if new else 128
                            ps_ = psd2[nx3("d", 2)]
                            pll[g4] = ps_
                            ts_ = [128] if new else [g4 * 4 + k for k in range(4)]
                            for k, t in enumerate(ts_):
                                if new:
                                    l1a, l1b = kA.t[:, S0 + 8 * b:S0 + 8 * b + 8], kB.t[:, S0 + 8 * b:S0 + 8 * b + 8]
                                    l2 = MBs.t[:, SEGW:IW]
                                    sl_ = sel2.t[:, b * 4, :]
                                    P.pe(CALL("matmul", ps_.t[0:nkk, 0:64], lhsT=l1a, rhs=qsb.t[:, b, :], start=True, stop=False),
                                         [kA.b, qsb.b], [ps_.b])
                                    P.pe(CALL("matmul", ps_.t[0:nkk, 0:64], lhsT=l1b, rhs=qsb.t[:, b, :], start=False, stop=False),
                                         [kB.b, qsb.b], [ps_.b])
                                else:
                                    l1 = R.t[:, O1 + t * 128:O1 + (t + 1) * 128]
                                    seg, cc = t // 32, t % 32
                                    l2 = MBs.t[:, cc * 128:(cc + 1) * 128]
                                    sl_ = sel2.t[:, b * 4 + seg, :]
                                    P.pe(CALL("matmul", ps_.t[0:nkk, k * 64:(k + 1) * 64], lhsT=l1, rhs=qsb.t[:, b, :], start=True, stop=False),
                                         [BR1, qsb.b], [ps_.b])
                                P.pe(CALL("matmul", ps_.t[0:nkk, k * 64:(k + 1) * 64], lhsT=l2, rhs=sl_, start=False, stop=True),
                                     [MBs.b, sel2.b], [ps_.b])

                        alogits(0)
                        for g4 in range(ngrp + 1):
                            new = (g4 == ngrp)
                            nkk = 8 if new else 128
                            ncol = 64 if new else 256
                            ps_ = pll[g4]
                            ts_ = [128] if new else [g4 * 4 + k for k in range(4)]
                            p_ = pt2[nx3("p", 4)]
                            P.act(CALL("activation", out=p_.t[0:nkk, 0:ncol], in_=ps_.t[0:nkk, 0:ncol], func=AF.Exp, scale=0.125), [ps_.b], [p_.b])
                            if g4 + 1 <= ngrp:
                                alogits(g4 + 1)
                            for k, t in enumerate(ts_):
                                first = (g4 == 0 and k == 0)
                                for kvh in range(2):
                                    po_ = pso2s[kvh]
                                    if new:
                                        lv = vnew.t[0:8, kvh * 64:(kvh + 1) * 64]
                                        rdv = vnew.b
                                    else:
                                        lv = R.t[:, O2 + t * 128 + kvh * 64:O2 + t * 128 + kvh * 64 + 64]
                                        rdv = BR2
                                    P.pe(CALL("matmul", po_.t[0:64, 0:32], lhsT=lv, rhs=p_.t[0:nkk, k * 64 + kvh * 32:k * 64 + kvh * 32 + 32],
                                              start=first, stop=new), [rdv, p_.b], [po_.b])
                                P.pe(CALL("matmul", psm2.t[0:1, 0:64], lhsT=ones_bf.t[0:nkk, 0:1], rhs=p_.t[0:nkk, k * 64:(k + 1) * 64],
                                          start=first, stop=new), [ones_bf.b, p_.b], [psm2.b])
                        P.dve(CALL("reciprocal", out=rcp2.t[:, :], in_=psm2.t[0:1, 0:64]), [psm2.b], [rcp2.b])
                        pb_ = psd2[nx3("d", 2)]
                        P.pe(CALL("matmul", pb_.t[0:64, 0:64], lhsT=cst.t[0:1, C_LE:C_LE + 64], rhs=rcp2.t[:, :], start=True, stop=True),
                             [cst.b, rcp2.b], [pb_.b])
                        P.act(CALL("activation", out=bcs2.t[:, :], in_=pb_.t[0:64, 0:64], func=AF.Copy), [pb_.b], [bcs2.b])
                        for kvh in range(2):
                            P.dve(CALL("tensor_tensor", out=as2.t[:, kvh, :, :], in0=pso2s[kvh].t[0:64, 0:32].rearrange("p (j q) -> p j q", j=4),
                                       in1=bcs2.t[:, kvh * 32:(kvh + 1) * 32].rearrange("p (j q) -> p j q", j=4), op=ALU.mult),
                                  [pso2s[kvh].b, bcs2.b], [as2.b])
                        for kvh in range(2):
                            for j in range(4):
                                kc = 2 * kvh + j // 2
                                r0 = (j % 2) * 64
                                P.dma("sp", CALL("dma_start", out=attnT_s[kc, r0:r0 + 64, S0 + 8 * b:S0 + 8 * b + 8], in_=as2.t[:, kvh, j, :]),
                                      [as2.b], dbs("attnT", S0, NS, kc))
                    P.flush()
        wf1 = SB(g, "wf1", [128, 8, 4096], BF16)
        wf1b = [Buf("wf1_%d" % i) for i in range(8)]
        s3w = ExitStack()
        with s3w:
            wao = SB(s3w, "wao", [128, 4, D], BF16)
            wgo = SB(s3w, "wgo", [128, 8, D], BF16)
            wo = SB(s3w, "wo", [128, 8, D], BF16)
            if "4" in phases:
                for (wt, wd) in ((wao, wao_d), (wgo, wgo_d), (wo, wo_d)):
                    P.dma("pool", CALL("dma_start",
                        out=wt.t[:, :, :], in_=wd.rearrange("(kc p) n -> p kc n", p=128)), [], [wt.b])
                for n in ln_d:
                    P.dma("sp", CALL("dma_start", out=lnb[n].t[:, :], in_=ln_d[n][0:1, :].broadcast_to([128, D])),
                          [], [lnb[n].b])
                for kc in range(8):
                    for c0 in (0, 2048):
                        P.dma("pool", CALL("dma_start", out=wf1.t[:, kc, c0:c0 + 2048], in_=wf1_d[kc * 128:(kc + 1) * 128, c0:c0 + 2048]),
                              [], [wf1b[kc]])
            if "3" in phases:
                s3 = ExitStack()
                with s3:
                    aug = SB(s3, "aug", [17, 128], BF16)
                    wal = SB(s3, "wal", [17, 512], BF16)
                    gnb = SB(s3, "gnb", [128, 256], F32)
                    qbb = SB(s3, "qbb", [128, 4, 128], BF16)
                    kbb = SB(s3, "kbb", [128, 4, 128], BF16)
                    kbt = SB(s3, "kbt", [128, 512], BF16)
                    vbts = [SB(s3, "vbt%d" % i, [128, 1024], BF16) for i in range(2)]
                    gbt = SB(s3, "gbt", [128, 1024], F32)
                    ee = SB(s3, "ee", [128, 512], F32)
                    la = SB(s3, "la", [128, 512], F32)
                    Eqs = [SB(s3, "Eq%d" % i, [128, 4, 128], F32) for i in range(2)]
                    Ek = SB(s3, "Ek", [128, 4, 128], F32)
                    Er = SB(s3, "Er", [128, 512], F32)
                    qts = [SB(s3, "qt%d" % i, [128, 4, 128], BF16) for i in range(2)]
                    kts = [SB(s3, "kt%d" % i, [128, 4, 128], BF16) for i in range(2)]
                    kps = [SB(s3, "kp%d" % i, [128, 512], BF16) for i in range(2)]
                    attm = SB(s3, "attm", [128, 4, 128], BF16)
                    Sf = SB(s3, "Sf", [128, 4, 256], F32)
                    Sb = SB(s3, "Sb", [128, 4, 256], BF16)
                    onr = SB(s3, "onr", [128, 1024], F32)
                    osbs = [SB(s3, "osb%d" % i, [128, 1024], F32) for i in range(2)]
                    sgts = [SB(s3, "sgt%d" % i, [128, 1024], F32) for i in range(2)]
                    obb = SB(s3, "obb", [128, 1024], BF16)
                    obT = SB(s3, "obT", [128, 8, 128], BF16)
                    gst = SB(s3, "gst", [128, 16], F32)
                    jk3 = SB(s3, "jk3", [128, 256], BF16)
                    psA = PSB(s3, "psA", [128, 512])
                    psC = PSB(s3, "psC", [128, 512])
                    psT_ = PSB(s3, "psT", [128, 512])
                    psO = PSB(s3, "psO", [128, 1024])
                    psS = PSB(s3, "psS", [128, 1024])
                    psX = PSB(s3, "psX", [128, 1024], BF16)

                    P.pool(CALL("memset", aug.t[:, :], 1.0), [], [aug.b])
                    P.dma("pool", CALL("dma_start", out=wal.t[1:17, :], in_=wal_d[:, :]), [], [wal.b])
                    P.dma("pool", CALL("dma_start", out=wal.t[0:1, :], in_=bal_d[:, :]), [], [wal.b])
                    P.dma("sp", CALL("dma_start", out=gnb.t[:, :], in_=gng_d[0:1, :].broadcast_to([128, 256])), [], [gnb.b])

                    def gla_prep(ci, tok0, C):
                        sl = ci % 2
                        qt, kt, kp, vbt, sgt, Eq = qts[sl], kts[sl], kps[sl], vbts[sl], sgts[sl], Eqs[sl]
                        P.dma("sp", CALL("dma_start", out=aug.t[1:17, 0:C], in_=abT_s[:, tok0:tok0 + C]), dbs("abT", tok0, C), [aug.b])
                        P.dma("sp", CALL("dma_start", out=qbb.t[:, :, 0:C], in_=qbT_s[:, :, tok0:tok0 + C].rearrange("j p t -> p j t")),
                              dbs("qbT", tok0, C), [qbb.b])
                        P.dma("sp", CALL("dma_start", out=kbb.t[:, :, 0:C], in_=kbT_s[:, :, tok0:tok0 + C].rearrange("j p t -> p j t")),
                              dbs("kbT", tok0, C), [kbb.b])
                        P.dma("sp", CALL("dma_start", out=kbt.t[0:C, :], in_=kb_s[tok0:tok0 + C, :]), dbs("kb", tok0, C), [kbt.b])
                        P.dma("sp", CALL("dma_start", out=vbt.t[0:C, :], in_=vb_s[tok0:tok0 + C, :]), dbs("vb", tok0, C), [vbt.b])
                        P.pe(CALL("matmul", psA.t[0:C, :], lhsT=aug.t[:, 0:C], rhs=wal.t[:, :], start=True, stop=True), [aug.b, wal.b], [psA.b])
                        P.act(CALL("activation", out=ee.t[0:C, :], in_=psA.t[0:C, :], func=AF.Exp, scale=-1.0), [psA.b], [ee.b])
                        P.act(CALL("activation", out=la.t[0:C, :], in_=ee.t[0:C, :], func=AF.Ln, bias=one_c[0:C, :], scale=1.0),
                              [ee.b, cst.b], [la.b])
                        P.pe(CALL("matmul", psA.t[0:C, :], lhsT=ltri_f[0:C, 0:C], rhs=la.t[0:C, :], start=True, stop=True), [la.b, cst.b], [psA.b])
                        for h in range(4):
                            P.pe(CALL("matmul", psC.t[:, h * 128:h * 128 + C], lhsT=la.t[0:C, h * 128:(h + 1) * 128], rhs=utri_f[0:C, 0:C],
                                      start=True, stop=True), [la.b, cst.b], [psC.b])
                        psC3 = psC.t[:, :].rearrange("p (h t) -> p h t", h=4)
                        P.act(CALL("activation", out=Er.t[0:C, :], in_=psA.t[0:C, :], func=AF.Exp), [psA.b], [Er.b])
                        P.act(CALL("activation", out=Eq.t[:, :, 0:C], in_=psC3[:, :, 0:C], func=AF.Exp), [psC.b], [Eq.b])
                        P.act(CALL("activation", out=Ek.t[:, :, 0:C], in_=psC3[:, :, 0:C], func=AF.Exp, scale=-1.0), [psC.b], [Ek.b])
                        P.dve(CALL("scalar_tensor_tensor", out=qt.t[:, :, 0:C], in0=qbb.t[:, :, 0:C], scalar=128.0 ** -0.5, in1=Eq.t[:, :, 0:C],
                                   op0=ALU.mult, op1=ALU.mult), [qbb.b, Eq.b], [qt.b])
                        P.dve(CALL("tensor_tensor", out=kt.t[:, :, 0:C], in0=kbb.t[:, :, 0:C], in1=Ek.t[:, :, 0:C], op=ALU.mult),
                              [kbb.b, Ek.b], [kt.b])
                        P.dve(CALL("tensor_tensor", out=kp.t[0:C, :], in0=kbt.t[0:C, :], in1=Er.t[0:C, :], op=ALU.mult), [kbt.b, Er.b], [kp.b])

                    def gla_state(ci, tok0, C):
                        sl = ci % 2
                        qt, kt, kp, vbt, sgt, Eq = qts[sl], kts[sl], kps[sl], vbts[sl], sgts[sl], Eqs[sl]
                        for h in range(4):
                            P.pe(CALL("matmul", psT_.t[0:C, h * 128:h * 128 + C], lhsT=kt.t[:, h, 0:C], rhs=qt.t[:, h, 0:C], start=True, stop=True),
                                 [kt.b, qt.b], [psT_.b])
                        psT3 = psT_.t[:, :].rearrange("p (h t) -> p h t", h=4)
                        P.dve(CALL("tensor_tensor", out=attm.t[0:C, :, 0:C], in0=psT3[0:C, :, 0:C],
                                   in1=cst.t[0:C, C_LE:C_LE + C].unsqueeze(1).broadcast_to([C, 4, C]), op=ALU.mult), [psT_.b, cst.b], [attm.b])
                        for h in range(4):
                            P.pe(CALL("matmul", psO.t[0:C, h * 256:(h + 1) * 256], lhsT=attm.t[0:C, h, 0:C], rhs=vbt.t[0:C, h * 256:(h + 1) * 256],
                                      start=True, stop=False), [attm.b, vbt.b], [psO.b])
                            P.pe(CALL("matmul", psO.t[0:C, h * 256:(h + 1) * 256], lhsT=qt.t[:, h, 0:C], rhs=Sb.t[:, h, :], start=False, stop=True),
                                 [qt.b, Sb.b], [psO.b])
                        for h in range(4):
                            P.pe(CALL("matmul", psS.t[:, h * 256:(h + 1) * 256], lhsT=kp.t[0:C, h * 128:(h + 1) * 128],
                                      rhs=vbt.t[0:C, h * 256:(h + 1) * 256], start=True, stop=True), [kp.b, vbt.b], [psS.b])
                        for h in range(4):
                            P.dve(CALL("scalar_tensor_tensor", out=Sf.t[:, h, :], in0=Sf.t[:, h, :], scalar=Eq.t[:, h, C - 1:C],
                                       in1=psS.t[:, h * 256:(h + 1) * 256], op0=ALU.mult, op1=ALU.add), [Sf.b, Eq.b, psS.b], [Sf.b])
                        P.act(CALL("activation", out=Sb.t[:, :, :], in_=Sf.t[:, :, :], func=AF.Copy), [Sf.b], [Sb.b])
                        osb = osbs[sl]
                        P.act(CALL("activation", out=osb.t[0:C, :], in_=psO.t[0:C, :], func=AF.Copy), [psO.b], [osb.b])

                    def gla_out(ci, tok0, C):
                        sl = ci % 2
                        sgt = sgts[0]
                        osb = osbs[sl]
                        P.dma("sp", CALL("dma_start", out=gbt.t[0:C, :], in_=gb_s[tok0:tok0 + C, :]), dbs("gb", tok0, C), [gbt.b])
                        P.act(CALL("activation", out=sgt.t[0:C, :], in_=gbt.t[0:C, :], func=AF.Silu), [gbt.b], [sgt.b])
                        for h in range(4):
                            P.act(CALL("activation", out=jk3.t[0:C, :], in_=osb.t[0:C, h * 256:(h + 1) * 256], func=AF.Square,
                                       accum_out=gst.t[0:C, h:h + 1]), [osb.b], [jk3.b, gst.b])
                        P.act(CALL("activation", out=gst.t[0:C, 4:8], in_=gst.t[0:C, 0:4], func=AF.Ln, bias=eps_c[0:C, :], scale=1.0 / 256),
                              [gst.b, cst.b], [gst.b])
                        P.act(CALL("activation", out=gst.t[0:C, 8:12], in_=gst.t[0:C, 4:8], func=AF.Exp, scale=-0.5), [gst.b], [gst.b])
                        for h in range(4):
                            P.dve(CALL("scalar_tensor_tensor", out=onr.t[0:C, h * 256:(h + 1) * 256], in0=osb.t[0:C, h * 256:(h + 1) * 256],
                                       scalar=gst.t[0:C, 8 + h:9 + h], in1=gnb.t[0:C, :], op0=ALU.mult, op1=ALU.mult),
                                  [osb.b, gst.b, gnb.b], [onr.b])
                        P.dve(CALL("tensor_tensor", out=obb.t[0:C, :], in0=onr.t[0:C, :], in1=sgt.t[0:C, :], op=ALU.mult), [onr.b, sgt.b], [obb.b])
                        for kc in range(8):
                            P.pe(CALL("transpose", psX.t[:, kc * 128:kc * 128 + C], obb.t[0:C, kc * 128:(kc + 1) * 128], ident_bf.t[0:C, 0:C]),
                                 [obb.b, ident_bf.b], [psX.b])
                        P.dve(CALL("tensor_copy", out=obT.t[:, :, 0:C], in_=psX.t[:, :].rearrange("p (k t) -> p k t", k=8)[:, :, 0:C]),
                              [psX.b], [obT.b])
                        P.dma("sp", CALL("dma_start", out=obT_s[:, :, tok0:tok0 + C].rearrange("k p t -> p k t"), in_=obT.t[:, :, 0:C]),
                              [obT.b], dbs("obT", tok0, C))

                    chunks = [(i * 128, 128, None) for i in range(NBLK)] + [(NTP + 8 * b, 8, b) for b in range(NB_S)]
                    P.pool(CALL("memset", Sf.t[:, :, :], 0.0), [], [Sf.b])
                    P.pool(CALL("memset", Sb.t[:, :, :], 0.0), [], [Sb.b])
                    gla_prep(0, chunks[0][0], chunks[0][1])
                    for ci, (tok0, C, sb_) in enumerate(chunks):
                        if ci + 1 < len(chunks):
                            gla_prep(ci + 1, chunks[ci + 1][0], chunks[ci + 1][1])
                        if sb_ is not None:
                            if sb_ == 0:
                                P.dma("sp", CALL("dma_start", out=glap_d.rearrange("h d v -> d h v"), in_=Sf.t[:, :, :]), [Sf.b], [B_gla_out])
                            P.dma("sp", CALL("dma_start", out=Sf.t[:, :, :], in_=st_d[sb_].rearrange("h d v -> d h v")), [B_gla_out], [Sf.b])
                            P.act(CALL("activation", out=Sb.t[:, :, :], in_=Sf.t[:, :, :], func=AF.Copy), [Sf.b], [Sb.b])
                        gla_state(ci, tok0, C)
                        if sb_ is not None:
                            P.dma("sp", CALL("dma_start", out=glas_d[sb_].rearrange("h d v -> d h v"), in_=Sf.t[:, :, :]), [Sf.b], [B_gla_out])
                        if ci >= 1:
                            gla_out(ci - 1, chunks[ci - 1][0], chunks[ci - 1][1])
                    gla_out(len(chunks) - 1, chunks[-1][0], chunks[-1][1])
                    P.flush()

            def layer_norm(stack_tiles, y1, T, gname, bname, out_t, stt_):
                jk, = stack_tiles
                P.act(CALL("activation", out=jk.t[0:T, :], in_=y1.t[0:T, :], func=AF.Copy, accum_out=stt_.t[0:T, 0:1]),
                      [y1.b], [jk.b, stt_.b])
                P.act(CALL("activation", out=jk.t[0:T, :], in_=y1.t[0:T, :], func=AF.Square, accum_out=stt_.t[0:T, 1:2]),
                      [y1.b, jk.b], [jk.b, stt_.b])
                P.dve(CALL("tensor_scalar", out=stt_.t[0:T, 2:3], in0=stt_.t[0:T, 0:1], scalar1=1.0 / D, scalar2=None,
                                                op0=ALU.mult), [stt_.b], [stt_.b])
                P.dve(CALL("tensor_tensor", out=stt_.t[0:T, 3:4], in0=stt_.t[0:T, 2:3], in1=stt_.t[0:T, 2:3], op=ALU.mult),
                      [stt_.b], [stt_.b])
                P.dve(CALL("scalar_tensor_tensor", out=stt_.t[0:T, 4:5], in0=stt_.t[0:T, 1:2], scalar=1.0 / D,
                                                       in1=stt_.t[0:T, 3:4], op0=ALU.mult, op1=ALU.subtract), [stt_.b], [stt_.b])
                P.act(CALL("activation", out=stt_.t[0:T, 5:6], in_=stt_.t[0:T, 4:5], func=AF.Ln, bias=eps_c[0:T, :], scale=1.0),
                      [stt_.b, cst.b], [stt_.b])
                P.act(CALL("activation", out=stt_.t[0:T, 6:7], in_=stt_.t[0:T, 5:6], func=AF.Exp, scale=-0.5), [stt_.b], [stt_.b])
                P.dve(CALL("tensor_scalar", out=out_t.t[0:T, :], in0=y1.t[0:T, :], scalar1=stt_.t[0:T, 2:3],
                                                scalar2=stt_.t[0:T, 6:7], op0=ALU.subtract, op1=ALU.mult), [y1.b, stt_.b], [out_t.b])
                P.dve(CALL("tensor_tensor", out=out_t.t[0:T, :], in0=out_t.t[0:T, :], in1=lnb[gname].t[0:T, :], op=ALU.mult),
                      [out_t.b, lnb[gname].b], [out_t.b])
                P.dve(CALL("tensor_tensor", out=out_t.t[0:T, :], in0=out_t.t[0:T, :], in1=lnb[bname].t[0:T, :], op=ALU.add),
                      [out_t.b, lnb[bname].b], [out_t.b])

            tiles4 = [(t0, 128) for t0 in range(0, NTP, 128)] + ([(NTP, NS)] if "x" not in phases else [])
            if "4" in phases:
                s4 = ExitStack()
                with s4:
                    atb = [SB(s4, "atb%d" % i, [128, 4, 128], BF16) for i in range(2)]
                    obl = [SB(s4, "obl%d" % i, [128, 8, 128], BF16) for i in range(2)]
                    gtl = [SB(s4, "gtl%d" % i, [128, 16, 128], F32) for i in range(2)]
                    xbl = [SB(s4, "xbl%d" % i, [128, D], F32) for i in range(2)]
                    sig = SB(s4, "sig", [128, 16, 128], F32)
                    t1 = SB(s4, "t1", [128, 8, 128], F32)
                    mrgs = [SB(s4, "mrg%d" % i, [128, 8, 128], BF16) for i in range(2)]
                    y1 = SB(s4, "y1", [128, D], F32)
                    hh = SB(s4, "hh", [128, D], F32)
                    hb = SB(s4, "hb", [128, D], BF16)
                    hTt = SB(s4, "hTt", [128, 8, 128], BF16)
                    jk4 = SB(s4, "jk4", [128, D], BF16)
                    st4 = SB(s4, "st4", [128, 8], F32)
                    psa = PSB(s4, "psa", [128, 1024])
                    psb4 = PSB(s4, "psb4", [128, 1024])
                    psm = PSB(s4, "psm", [128, 1024])
                    psx = PSB(s4, "psx4", [128, 1024], BF16)

                    def load4(ti):
                        t0, T = tiles4[ti]
                        sl = ti % 2
                        P.dma("sp", CALL("dma_start", out=atb[sl].t[:, :, 0:T], in_=attnT_s[:, :, t0:t0 + T].rearrange("k p t -> p k t")),
                              dbs("attnT", t0, T), [atb[sl].b])
                        P.dma("sp", CALL("dma_start", out=obl[sl].t[:, :, 0:T], in_=obT_s[:, :, t0:t0 + T].rearrange("k p t -> p k t")),
                              dbs("obT", t0, T), [obl[sl].b])
                        P.dma("sp", CALL("dma_start", out=gtl[sl].t[:, :, 0:T], in_=gtT_s[:, :, t0:t0 + T].rearrange("k p t -> p k t")),
                              dbs("gtT", t0, T), [gtl[sl].b])
                        P.dma("sp", CALL("dma_start", out=xbl[sl].t[0:T, :], in_=x_d[t0:t0 + T, :]), [], [xbl[sl].b])

                    def s1_4a(ti):
                        t0, T = tiles4[ti]
                        sl = ti % 2
                        at_, ob_, gt_ = atb[sl], obl[sl], gtl[sl]
                        mrg = mrgs[sl]
                        P.act(CALL("activation", out=sig.t[:, :, 0:T], in_=gt_.t[:, :, 0:T], func=AF.Sigmoid), [gt_.b], [sig.b])
                        psa3 = psa.t[:, :].rearrange("p (c t) -> p c t", c=8)
                        psb3 = psb4.t[:, :].rearrange("p (c t) -> p c t", c=8)
                        for c in range(8):
                            for kc in range(4):
                                P.pe(CALL("matmul", psa.t[:, c * 128:c * 128 + T], lhsT=wao.t[:, kc, c * 128:(c + 1) * 128],
                                          rhs=at_.t[:, kc, 0:T], start=(kc == 0), stop=(kc == 3)), [wao.b, at_.b], [psa.b])
                        for c in range(8):
                            for kc in range(8):
                                P.pe(CALL("matmul", psb4.t[:, c * 128:c * 128 + T], lhsT=wgo.t[:, kc, c * 128:(c + 1) * 128],
                                          rhs=ob_.t[:, kc, 0:T], start=(kc == 0), stop=(kc == 7)), [wgo.b, ob_.b], [psb4.b])
                        P.dve(CALL("tensor_tensor", out=t1.t[:, :, 0:T], in0=psa3[:, :, 0:T], in1=sig.t[:, 0:8, 0:T], op=ALU.mult),
                              [psa.b, sig.b], [t1.b])
                        P.dve(CALL("tensor_tensor", out=sig.t[:, 8:16, 0:T], in0=psb3[:, :, 0:T], in1=sig.t[:, 8:16, 0:T], op=ALU.mult),
                              [psb4.b, sig.b], [sig.b])
                        P.dve(CALL("tensor_tensor", out=mrg.t[:, :, 0:T], in0=t1.t[:, :, 0:T], in1=sig.t[:, 8:16, 0:T], op=ALU.add),
                              [t1.b, sig.b], [mrg.b])

                    def s2_4a(ti):
                        t0, T = tiles4[ti]
                        sl = ti % 2
                        xb_ = xbl[sl]
                        mrg = mrgs[sl]
                        for n in range(2):
                            for kc in range(8):
                                P.pe(CALL("matmul", psm.t[0:T, n * 512:(n + 1) * 512], lhsT=mrg.t[:, kc, 0:T],
                                          rhs=wo.t[:, kc, n * 512:(n + 1) * 512], start=(kc == 0), stop=(kc == 7)), [mrg.b, wo.b], [psm.b])
                        for n in range(2):
                            P.dve(CALL("scalar_tensor_tensor", out=y1.t[0:T, n * 512:(n + 1) * 512], in0=xb_.t[0:T, n * 512:(n + 1) * 512],
                                       scalar=ALPHA, in1=psm.t[0:T, n * 512:(n + 1) * 512], op0=ALU.mult, op1=ALU.add),
                                  [xb_.b, psm.b], [y1.b])
                        layer_norm((jk4,), y1, T, "ln1_g", "ln1_b", hh, st4)
                        P.dma("sp", CALL("dma_start", out=h_s[t0:t0 + T, :], in_=hh.t[0:T, :]), [hh.b], dbs("h", t0, T))
                        P.act(CALL("activation", out=hb.t[0:T, :], in_=hh.t[0:T, :], func=AF.Copy), [hh.b], [hb.b])
                        for kc in range(8):
                            P.pe(CALL("transpose", psx.t[:, kc * 128:kc * 128 + T], hb.t[0:T, kc * 128:(kc + 1) * 128],
                                      ident_bf.t[0:T, 0:T]), [hb.b, ident_bf.b], [psx.b])
                        P.dve(CALL("tensor_copy", out=hTt.t[:, :, 0:T], in_=psx.t[:, :].rearrange("p (k t) -> p k t", k=8)[:, :, 0:T]),
                              [psx.b], [hTt.b])
                        P.dma("sp", CALL("dma_start", out=hT_s[:, :, t0:t0 + T].rearrange("k p t -> p k t"), in_=hTt.t[:, :, 0:T]),
                              [hTt.b], dbs("hT", t0, T))

                    load4(0)
                    s1_4a(0)
                    for ti in range(len(tiles4)):
                        if ti + 1 < len(tiles4):
                            load4(ti + 1)
                            s1_4a(ti + 1)
                        s2_4a(ti)
                    P.flush()

        if "4" in phases:
            s5 = ExitStack()
            with s5:
                wf2 = SB(s5, "wf2", [128, 32, D], BF16)
                wf2b = [Buf("wf2_%d" % i) for i in range(8)]
                for q in range(8):
                    P.dma("pool", CALL("dma_start", out=wf2.t[:, 4 * q:4 * q + 4, :],
                                                             in_=wf2_d[q * 512:(q + 1) * 512, :].rearrange("(kc p) n -> p kc n", p=128)),
                          [], [wf2b[q]])
                SW5 = 256
                hTl = [SB(s5, "hTl%d" % i, [128, 8, SW5], BF16) for i in range(2)]
                hl = [SB(s5, "hl%d" % i, [128, D], F32) for i in range(2)]
                rl = [SB(s5, "rl%d" % i, [128, 2, SW5], BF16) for i in range(2)]
                hid = SB(s5, "hid", [128, 32, SW5], BF16)
                y2 = SB(s5, "y2", [128, D], F32)
                yo = SB(s5, "yo", [128, D], F32)
                jk5 = SB(s5, "jk5", [128, D], BF16)
                st5 = SB(s5, "st5", [128, 8], F32)
                psf = [PSB(s5, "psf%d" % i, [128, 512]) for i in range(3)]
                psy = PSB(s5, "psy", [128, 1024])
                sup5 = [(t0, min(SW5, NTP - t0)) for t0 in range(0, NTP, SW5)] + (
                    [(NTP, NS)] if "x" not in phases else [])

                def load5T(ui):
                    t0, W = sup5[ui]
                    sl = ui % 2
                    P.dma("sp", CALL("dma_start", out=hTl[sl].t[:, :, 0:W], in_=hT_s[:, :, t0:t0 + W].rearrange("k p t -> p k t")),
                          dbs("hT", t0, W), [hTl[sl].b])

                fi_ = [0]
                hi_ = [0]

                def s1_4b(ui):
                    t0, W = sup5[ui]
                    hT_ = hTl[ui % 2]
                    for f2 in range(16):
                        pf = psf[fi_[0] % 3]
                        r_ = rl[fi_[0] % 2]
                        fi_[0] += 1
                        for cc in range(2):
                            f = f2 * 2 + cc
                            for kc in range(8):
                                P.pe(CALL("matmul", pf.t[:, cc * SW5:cc * SW5 + W], lhsT=wf1.t[:, kc, f * 128:(f + 1) * 128],
                                          rhs=hT_.t[:, kc, 0:W], start=(kc == 0), stop=(kc == 7)), [wf1b[kc], hT_.b], [pf.b])
                        pf3 = pf.t[:, :].rearrange("p (c t) -> p c t", c=2)
                        P.act(CALL("activation", out=r_.t[:, :, 0:W], in_=pf3[:, :, 0:W], func=AF.Relu), [pf.b], [r_.b])
                        P.dve(CALL("tensor_tensor", out=hid.t[:, f2 * 2:f2 * 2 + 2, 0:W], in0=r_.t[:, :, 0:W], in1=r_.t[:, :, 0:W],
                                   op=ALU.mult), [r_.b], [hid.b])

                def s2_4b(t0, T, c0):
                    h_ = hl[hi_[0] % 2]
                    hi_[0] += 1
                    P.dma("sp", CALL("dma_start", out=h_.t[0:T, :], in_=h_s[t0:t0 + T, :]), dbs("h", t0, T), [h_.b])
                    for n in range(2):
                        for kc in range(32):
                            P.pe(CALL("matmul", psy.t[0:T, n * 512:(n + 1) * 512], lhsT=hid.t[:, kc, c0:c0 + T],
                                      rhs=wf2.t[:, kc, n * 512:(n + 1) * 512], start=(kc == 0), stop=(kc == 31)),
                                 [hid.b, wf2b[kc // 4]], [psy.b])
                    for n in range(2):
                        P.dve(CALL("scalar_tensor_tensor", out=y2.t[0:T, n * 512:(n + 1) * 512], in0=h_.t[0:T, n * 512:(n + 1) * 512],
                                   scalar=ALPHA, in1=psy.t[0:T, n * 512:(n + 1) * 512], op0=ALU.mult, op1=ALU.add),
                              [h_.b, psy.b], [y2.b])
                    layer_norm((jk5,), y2, T, "ln2_g", "ln2_b", yo, st5)
                    P.dma("sp", CALL("dma_start", out=y_d[t0:t0 + T, :], in_=yo.t[0:T, :]), [yo.b], dbs("y", t0, T))

                load5T(0)
                for ui, (u0, W) in enumerate(sup5):
                    if ui + 1 < len(sup5):
                        load5T(ui + 1)
                    s1_4b(ui)
                    for c0 in range(0, W, 128):
                        s2_4b(u0 + c0, min(128, W - c0), c0)
                P.flush()
        P.flush(final=True)
    return nc


_NC_CACHE = {}


def make_in_maps(inp, n_cores, NTP, NPOOL):
    cst = make_consts()
    ck = np.ascontiguousarray(inp["cache_k"]).reshape(NPOOL * 8, 2048)
    cv = np.ascontiguousarray(inp["cache_v"]).reshape(NPOOL * 8, 2048)
    cki = np.ascontiguousarray(inp["cache_kidx"]).reshape(NPOOL * 4, 2048)
    maps = []
    for c in range(n_cores):
        xs = np.asarray(inp["x_sample"][NB_S * c:NB_S * (c + 1)]).reshape(NS, D)
        x = np.concatenate([np.asarray(inp["x_prompt"][c]), xs], axis=0).astype(np.float32)
        m = {
            "x": np.ascontiguousarray(x),
            "xT": np.ascontiguousarray(x.T),
            "cache_k": ck, "cache_v": cv, "cache_kidx": cki,
            "state_gla": np.ascontiguousarray(inp["state_gla"][0, NB_S * c:NB_S * (c + 1)]),
            "page_table": np.ascontiguousarray(inp["page_table"][NB_S * c:NB_S * (c + 1)]).astype(np.int32),
            "w_in": np.ascontiguousarray(inp["w_in"][0]),
            "w_alpha2": np.ascontiguousarray(inp["w_alpha2"][0]),
            "b_alpha": np.ascontiguousarray(inp["b_alpha"][0]).reshape(1, 512),
            "gla_norm_g": np.ascontiguousarray(inp["gla_norm_g"][0]).reshape(1, 256),
            "w_attn_o": np.ascontiguousarray(inp["w_attn_o"][0]),
            "w_gla_o": np.ascontiguousarray(inp["w_gla_o"][0]),
            "w_out": np.ascontiguousarray(inp["w_out"][0]),
            "ln1_g": np.ascontiguousarray(inp["ln1_g"][0]).reshape(1, D),
            "ln1_b": np.ascontiguousarray(inp["ln1_b"][0]).reshape(1, D),
            "ln2_g": np.ascontiguousarray(inp["ln2_g"][0]).reshape(1, D),
            "ln2_b": np.ascontiguousarray(inp["ln2_b"][0]).reshape(1, D),
            "w_ff1": np.ascontiguousarray(inp["w_ff1"][0]),
            "w_ff2": np.ascontiguousarray(inp["w_ff2"][0]),
            "cst": cst,
        }
        maps.append(m)
    return maps


def assemble(res, n_cores, NTP):
    f = np.float32
    y = np.stack([r["y"][:NTP] for r in res]).astype(f)
    ys = np.concatenate([r["y"][NTP:].reshape(NB_S, 8, D) for r in res]).astype(f)
    kp = np.stack([r["ko"][:NTP].reshape(NTP, 2, 64) for r in res])[None].astype(f)
    vp = np.stack([r["vo"][:NTP].reshape(NTP, 2, 64) for r in res])[None].astype(f)
    kip = np.stack([r["kio"][:NTP] for r in res])[None].astype(f)
    gp = np.stack([r["gla_p"] for r in res])[None].astype(f)
    ks = np.concatenate([r["ko"][NTP:].reshape(NB_S, 8, 2, 64) for r in res])[None].astype(f)
    vs = np.concatenate([r["vo"][NTP:].reshape(NB_S, 8, 2, 64) for r in res])[None].astype(f)
    kis = np.concatenate([r["kio"][NTP:].reshape(NB_S, 8, 64) for r in res])[None].astype(f)
    gs = np.concatenate([r["gla_s"] for r in res])[None].astype(f)
    return (y, ys, kp, vp, kip, gp, ks, vs, kis, gs)


def kernel(**inputs):
    n_cores = 8
    NTP = inputs["x_prompt"].shape[1]
    NPOOL = inputs["cache_k"].shape[1]
    nc = build(NTP=NTP, NPOOL=NPOOL)
    maps = make_in_maps(inputs, n_cores, NTP, NPOOL)
    out = run_bass_kernel_spmd(nc, maps, core_ids=list(range(n_cores)))
    return assemble(out.results, n_cores, NTP)
```

```python
from contextlib import ExitStack
import numpy as np
import concourse.bass as bass
import concourse.mybir as mybir
from concourse.bass_utils import run_bass_kernel_spmd

F32 = mybir.dt.float32
BF16 = mybir.dt.bfloat16
I32 = mybir.dt.int32
AF = mybir.ActivationFunctionType
ALU = mybir.AluOpType
AX = mybir.AxisListType

D = 1024
D_IN = 6488
NS = 32
NB_S = 4
NPAGES = 128
ROUNDS = 15
ALPHA = 2.0 ** 0.25
EPS = 1e-5
NEG = -30000.0
BIG = 1.0e30


class Buf:
    __slots__ = ("name", "writer", "readers", "excl")

    def __init__(self, name, excl=False):
        self.name = name
        self.writer = None
        self.readers = []
        self.excl = excl


class Op:
    __slots__ = ("eng", "fn", "deps", "signal", "sem", "val", "is_dma", "lane")

    def __init__(self, eng, fn, is_dma):
        self.eng = eng
        self.fn = fn
        self.deps = []
        self.signal = False
        self.sem = None
        self.val = 0
        self.is_dma = is_dma
        self.lane = None


ENGS = ("pe", "act", "dve", "pool", "sp")
EPOCH = 12000


class Prog:
    def __init__(self, nc, lanes=8):
        self.nc = nc
        self.ops = []
        self.sems = {}
        self.cnt = {e: 0 for e in ENGS}
        self.waited = {e: {} for e in ENGS}
        self.lanes = {}
        self.lane_rr = {}
        for q in ("sp", "pool", "act"):
            self.lanes[q] = [[nc.alloc_semaphore("ln_%s_%d" % (q, i)), 0] for i in range(lanes)]
            self.lane_rr[q] = 0
        self.last_sig = {e: None for e in ENGS}
        self.all_dma = []
        self.fence_deps = []
        self.n_ops = 0

    def _add(self, eng, fn, reads, writes, is_dma=False):
        op = Op(eng, fn, is_dma)
        deps = set()
        for b in reads:
            if b.writer is not None:
                deps.add(b.writer)
            if b.excl:
                for r in b.readers:
                    if r.eng != eng:
                        deps.add(r)
        for b in writes:
            if b.writer is not None:
                deps.add(b.writer)
            for r in b.readers:
                deps.add(r)
        op.deps = list(deps)
        for b in reads:
            b.readers.append(op)
        for b in writes:
            b.writer = op
            b.readers = []
        self.ops.append(op)
        return op

    def pe(self, fn, reads=(), writes=()):
        return self._add("pe", fn, reads, writes)

    def act(self, fn, reads=(), writes=()):
        return self._add("act", fn, reads, writes)

    def dve(self, fn, reads=(), writes=()):
        return self._add("dve", fn, reads, writes)

    def pool(self, fn, reads=(), writes=()):
        return self._add("pool", fn, reads, writes)

    def dma(self, q, fn, reads=(), writes=()):
        import os
        if q in os.environ.get("SKIPDMA", "").split(","):
            return None
        return self._add(q, fn, reads, writes, is_dma=True)

    def _sem_for(self, eng):
        ep = self.cnt[eng] // EPOCH
        key = (eng, ep)
        if key not in self.sems:
            self.sems[key] = self.nc.alloc_semaphore("s_%s_%d" % (eng, ep))
        return self.sems[key], ep

    def flush(self, final=False):
        nc = self.nc
        ops = self.ops
        self.ops = []
        if not ops and not final:
            return
        self.n_ops += len(ops)
        needed = set()
        for op in ops:
            for d in op.deps:
                if d.is_dma:
                    continue
                if d.eng == "pe" and op.eng == "pe" and not op.is_dma:
                    continue
                needed.add(d)
        last = {}
        for op in ops:
            if not op.is_dma:
                last[op.eng] = op
        for op in last.values():
            needed.add(op)
        for op in ops:
            if op.is_dma:
                lanes = self.lanes[op.eng]
                li = self.lane_rr[op.eng]
                self.lane_rr[op.eng] = (li + 1) % len(lanes)
                lane = lanes[li]
                op.lane = (lane[0], lane[1])
                lane[1] += 16
                op.sem = lane[0]
                op.val = lane[1]
                self.all_dma.append(op)
            elif op in needed and op.sem is None:
                sem, ep = self._sem_for(op.eng)
                self.cnt[op.eng] += 1
                op.sem = sem
                op.val = self.cnt[op.eng] - ep * EPOCH
                op.signal = True
                self.last_sig[op.eng] = op
        streams = {e: [] for e in ENGS}
        for op in ops:
            streams[op.eng].append(op)
        fence = self.fence_deps

        def emit_stream(eng_name, e):
            waited = self.waited[eng_name]

            def wait(sem, val):
                if waited.get(sem.name, 0) >= val:
                    return
                waited[sem.name] = val
                e.wait_ge(sem, val)

            first = True
            for op in streams[eng_name]:
                if first:
                    for d in fence:
                        if d.sem is not None:
                            wait(d.sem, d.val)
                    first = False
                for d in op.deps:
                    if d.sem is None:
                        continue
                    if (not d.is_dma) and d.eng == "pe" and eng_name == "pe" and not op.is_dma:
                        continue
                    wait(d.sem, d.val)
                if op.is_dma:
                    if op.lane[1] > 0:
                        wait(op.lane[0], op.lane[1])
                    ins = op.fn(e)
                    ins.then_inc(op.sem, 16)
                else:
                    ins = op.fn(e)
                    if op.signal:
                        ins.then_inc(op.sem, 1)
            if final and eng_name == "sp":
                for q in self.lanes:
                    for sem, tot in self.lanes[q]:
                        if tot > 0:
                            wait(sem, tot)

        with nc.Block() as block:
            @block.tensor
            def _(e):
                emit_stream("pe", e)

            @block.scalar
            def _(e):
                emit_stream("act", e)

            @block.vector
            def _(e):
                emit_stream("dve", e)

            @block.gpsimd
            def _(e):
                emit_stream("pool", e)

            @block.sync
            def _(e):
                emit_stream("sp", e)
        fd = [op for op in self.last_sig.values() if op is not None]
        latest = {}
        for op in self.all_dma:
            latest[op.sem.name] = op
        fd += list(latest.values())
        self.fence_deps = fd
        self.all_dma = list(latest.values())


def CALL(name, *a, **k):
    return lambda e: getattr(e, name)(*a, **k)


class TT:
    __slots__ = ("t", "b")

    def __init__(self, t, name, excl=False):
        self.t = t
        self.b = Buf(name, excl)


C_ID, C_UT, C_LT, C_LE, C_CN, C_G, C_PW, C_EPS, C_ONE = 0, 128, 256, 384, 512, 640, 768, 800, 801
NCST = 832


def make_consts():
    c = np.zeros((128, NCST), np.float32)
    p = np.arange(128)[:, None]
    j = np.arange(128)[None, :]
    c[:, C_ID:C_ID + 128] = (p == j)
    c[:, C_UT:C_UT + 128] = np.where(p <= j, -1.0 / 16, 0.0)
    c[:, C_LT:C_LT + 128] = np.where(p > j, -1.0 / 16, 0.0)
    c[:, C_LE:C_LE + 128] = (p <= j)
    c[:, C_CN:C_CN + 128] = np.where(j > p, -BIG, 0.0)
    c[:, C_G:C_G + 128] = ((p // 32 == j // 32) & (p % 8 == j % 8))
    c[:, C_PW:C_PW + 32] = 2.0 ** -(np.arange(32)[None, :] + 1.0)
    c[:, C_EPS] = EPS
    c[:, C_ONE] = 1.0
    return c


O_QA, O_KA, O_VA, O_QI, O_KI, O_WI, O_QB, O_KB, O_VB, O_GB, O_AB, O_GA = (
    0, 512, 640, 768, 1280, 1344, 1352, 1864, 2376, 3400, 4424, 4440)


def w_layout():
    fm, tm = [], []
    off = 0

    def add(lst, name, pieces):
        nonlocal off
        w = sum(b - a for a, b in pieces)
        lst.append((name, off, w, pieces))
        off += w

    for j in range(4):
        add(fm, "qa%d" % j, [(O_QA + 64 * j, O_QA + 64 * j + 64), (O_QA + 64 * (4 + j), O_QA + 64 * (4 + j) + 64)])
    add(fm, "ka", [(O_KA, O_KA + 128)])
    for j in range(4):
        add(fm, "qi%d" % j, [(O_QI + 128 * j, O_QI + 128 * j + 128)])
    add(fm, "ki", [(O_KI, O_KI + 64), (O_KI, O_KI + 64)])
    for j in range(4):
        add(fm, "qb%d" % j, [(O_QB + 128 * j, O_QB + 128 * j + 128)])
    for j in range(4):
        add(fm, "kb%d" % j, [(O_KB + 128 * j, O_KB + 128 * j + 128)])
    add(fm, "ab", [(O_AB, O_AB + 16)])
    for j in range(16):
        add(fm, "gt%d" % j, [(O_GA + 128 * j, O_GA + 128 * j + 128)])
    add(tm, "ta", [(O_KA, O_KA + 256), (O_KI, O_KI + 72)])
    add(tm, "tkb", [(O_KB, O_KB + 512)])
    add(tm, "tvb0", [(O_VB, O_VB + 512)])
    add(tm, "tvb1", [(O_VB + 512, O_VB + 1024)])
    add(tm, "tgb0", [(O_GB, O_GB + 512)])
    add(tm, "tgb1", [(O_GB + 512, O_GB + 1024)])
    return fm, tm, off


def build(NTP=4096, NPOOL=5120, phases="12345", dbg=False):
    nc = bass.Bass("TRN2", target_bir_lowering=False)
    P = Prog(nc)
    NT = NTP + NS
    NBLK = NTP // 128
    TOPK = min(256, NTP // 4)
    TOPK_S = min(256, (NPAGES * 128 + 8) // 4)

    def din(name, shape, dt=F32):
        return nc.dram_tensor(name, list(shape), dt, kind="ExternalInput")

    def dout(name, shape, dt=F32):
        return nc.dram_tensor(name, list(shape), dt, kind="ExternalOutput")

    def dscr(name, shape, dt):
        return nc.dram_tensor(name, list(shape), dt, kind="Internal")

    xT_d = din("xT", [D, NT])
    x_d = din("x", [NT, D])
    ck_d = din("cache_k", [NPOOL * 8, 2048])
    cv_d = din("cache_v", [NPOOL * 8, 2048])
    cki_d = din("cache_kidx", [NPOOL * 4, 2048])
    st_d = din("state_gla", [NB_S, 4, 128, 256])
    pt_d = din("page_table", [NB_S, NPAGES], I32)
    w_in_d = din("w_in", [D, D_IN])
    wal_d = din("w_alpha2", [16, 512])
    bal_d = din("b_alpha", [1, 512])
    gng_d = din("gla_norm_g", [1, 256])
    wao_d = din("w_attn_o", [512, D])
    wgo_d = din("w_gla_o", [D, D])
    wo_d = din("w_out", [D, D])
    ln_d = {n: din(n, [1, D]) for n in ("ln1_g", "ln1_b", "ln2_g", "ln2_b")}
    wf1_d = din("w_ff1", [D, 4096])
    wf2_d = din("w_ff2", [4096, D])
    cst_d = din("cst", [128, NCST])

    y_d = dout("y", [NT, D])
    ko_d = dout("ko", [NT, 128])
    vo_d = dout("vo", [NT, 128])
    kio_d = dout("kio", [NT, 64])
    glap_d = dout("gla_p", [4, 128, 256])
    glas_d = dout("gla_s", [NB_S, 4, 128, 256])

    qaT_s = dscr("qaT_s", [4, 128, NT], BF16)
    qiT_s = dscr("qiT_s", [4, 128, NT], BF16)
    wi_s = dscr("wi_s", [NT, 8], F32)
    qbT_s = dscr("qbT_s", [4, 128, NT], BF16)
    kbT_s = dscr("kbT_s", [4, 128, NT], BF16)
    abT_s = dscr("abT_s", [16, NT], BF16)
    gtT_s = dscr("gtT_s", [16, 128, NT], F32)
    kb_s = dscr("kb_s", [NT, 512], BF16)
    vb_s = dscr("vb_s", [NT, 1024], BF16)
    gb_s = dscr("gb_s", [NT, 1024], F32)
    attnT_s = dscr("attnT_s", [4, 128, NT], BF16)
    obT_s = dscr("obT_s", [8, 128, NT], BF16)
    hT_s = dscr("hT_s", [8, 128, NT], BF16)
    h_s = dscr("h_s", [NT, D], F32)

    DBD = {}
    NJ = {"qaT": 4, "qiT": 4, "wi": 1, "qbT": 4, "kbT": 4, "abT": 1, "gtT": 16, "kb": 1, "vb": 2, "gb": 2, "attnT": 4,
          "obT": 1, "hT": 1, "h": 1, "ko": 1, "vo": 1, "kio": 1, "y": 1}
    B_gla_out = Buf("gla_out")

    def dbs(name, tok0, n, j=None):
        js = range(NJ[name]) if j is None else [j]
        out = []
        for jj in js:
            for tl in range(tok0 // 128, (tok0 + n - 1) // 128 + 1):
                key = (name, jj, tl)
                if key not in DBD:
                    DBD[key] = Buf("%s_%d_%d" % key)
                out.append(DBD[key])
        return out

    g = ExitStack()

    def SB(stack, name, shape, dt):
        return TT(stack.enter_context(nc.sbuf_tensor("sb_" + name, list(shape), dt)), name)

    def PSB(stack, name, shape, dt=F32):
        return TT(stack.enter_context(nc.psum_tensor("pp_" + name, list(shape), dt)), name, True)

    with g:
        cst = SB(g, "cst", [128, NCST], F32)
        ident_bf = SB(g, "ident_bf", [128, 128], BF16)
        i4_bf = SB(g, "i4_bf", [128, 4, 128], BF16)
        mle_bf = SB(g, "mle_bf", [128, 128], BF16)
        ones_bf = SB(g, "ones_bf", [128, 128], BF16)
        P.dma("sp", CALL("dma_start", out=cst.t[:], in_=cst_d[:, :]), [], [cst.b])
        P.dve(CALL("tensor_copy", out=ident_bf.t[:], in_=cst.t[:, C_ID:C_ID + 128]), [cst.b], [ident_bf.b])
        for k in range(4):
            P.dve(CALL("tensor_copy", out=i4_bf.t[:, k, :], in_=cst.t[:, C_ID:C_ID + 128]), [cst.b], [i4_bf.b])
        P.dve(CALL("tensor_copy", out=mle_bf.t[:], in_=cst.t[:, C_LE:C_LE + 128]), [cst.b], [mle_bf.b])
        P.pool(CALL("memset", ones_bf.t[:], 1.0), [], [ones_bf.b])
        ident_f = cst.t[:, C_ID:C_ID + 128]
        utri_f = cst.t[:, C_UT:C_UT + 128]
        ltri_f = cst.t[:, C_LT:C_LT + 128]
        cneg_f = cst.t[:, C_CN:C_CN + 128]
        eps_c = cst.t[:, C_EPS:C_EPS + 1]
        one_c = cst.t[:, C_ONE:C_ONE + 1]

        lnb = {n: SB(g, "lnb_" + n, [128, D], F32) for n in ln_d}
        s12 = ExitStack()
        with s12:
            kA = SB(s12, "kA", [128, NT], BF16)
            kB = SB(s12, "kB", [128, NT], BF16)
            kiA = SB(s12, "kiA", [128, NT], BF16)
            kiB = SB(s12, "kiB", [128, NT], BF16)
            vaug = SB(s12, "vaug", [128, NBLK, 2, 65], BF16)
            for tt_ in (kA, kB, kiA, kiB):
                P.pool(CALL("memset", tt_.t[:], 0.0), [], [tt_.b])
            P.pool(CALL("memset", vaug.t[:], 1.0), [], [vaug.b])

            if "1" in phases:
                s1 = ExitStack()
                with s1:
                    fm, tm, ncol = w_layout()
                    w_sb = SB(s1, "w_sb", [128, 8, ncol], BF16)
                    w_src = w_in_d.rearrange("(kc p) n -> p kc n", p=128)
                    wb = {}
                    for lst in (fm, tm):
                        for (name, off, wd, pieces) in lst:
                            b = Buf("w_" + name)
                            wb[name] = b
                            o = off
                            for (a0, a1) in pieces:
                                P.dma("pool", CALL("dma_start",
                                    out=w_sb.t[:, :, o:o + (a1 - a0)], in_=w_src[:, :, a0:a1]), [], [b])
                                o += a1 - a0
                    import os as _os
                    NXS = int(_os.environ.get("XSLOTS", "2"))
                    xts = [SB(s1, "xT%d" % i, [128, 8, 512], BF16) for i in range(NXS)]
                    stg_b = [SB(s1, "stgb%d" % i, [128, 512], BF16) for i in range(4)]
                    stg_f = [SB(s1, "stgf%d" % i, [128, 512], F32) for i in range(4)]
                    pss = [PSB(s1, "ps1_%d" % i, [128, 512]) for i in range(6)]
                    rr = {"ps": 0, "sb": 0, "sf": 0, "ev": 0}
                    xT_src = xT_d.rearrange("(kc p) t -> p kc t", p=128)

                    def nxt(key, n):
                        v = rr[key]
                        rr[key] = (v + 1) % n
                        return v

                    def evac(out_ap, in_ap, reads, writes):
                        if nxt("ev", 2) == 0:
                            P.act(CALL("activation", out=out_ap, in_=in_ap, func=AF.Copy), reads, writes)
                        else:
                            P.dve(CALL("tensor_copy", out=out_ap, in_=in_ap), reads, writes)

                    STW = int(_os.environ.get("STW", "512"))
                    sts = [(t0, min(STW, NTP - t0)) for t0 in range(0, NTP, STW)] + [(NTP, NS)]

                    def load_x(si):
                        t0, W = sts[si]
                        xt = xts[si % NXS]
                        P.dma("pool", CALL("dma_start", out=xt.t[:, :, 0:W], in_=xT_src[:, :, t0:t0 + W]), [], [xt.b])

                    load_x(0)
                    for si, (t0, W) in enumerate(sts):
                        if si + 1 < len(sts):
                            load_x(si + 1)
                        xt = xts[si % NXS]
                        for (name, off, m, pieces) in fm:
                            ps = pss[nxt("ps", 6)]
                            for kc in range(8):
                                P.pe(CALL("matmul",
                                    ps.t[0:m, 0:W], lhsT=w_sb.t[:, kc, off:off + m], rhs=xt.t[:, kc, 0:W],
                                    start=(kc == 0), stop=(kc == 7)), [xt.b, wb[name]], [ps.b])
                            if name == "ka":
                                evac(kA.t[0:64, t0:t0 + W], ps.t[0:64, 0:W], [ps.b], [kA.b])
                                evac(kB.t[64:128, t0:t0 + W], ps.t[64:128, 0:W], [ps.b], [kB.b])
                            elif name == "ki":
                                evac(kiA.t[0:64, t0:t0 + W], ps.t[0:64, 0:W], [ps.b], [kiA.b])
                                evac(kiB.t[64:128, t0:t0 + W], ps.t[64:128, 0:W], [ps.b], [kiB.b])
                            else:
                                kind = name[:2]
                                j = int(name[2:]) if len(name) > 2 else 0
                                if kind == "gt":
                                    sg = stg_f[nxt("sf", 4)]
                                    dst = gtT_s[j, :, t0:t0 + W]
                                    dbn = "gtT"
                                else:
                                    sg = stg_b[nxt("sb", 4)]
                                    dst = {"qa": qaT_s, "qi": qiT_s, "qb": qbT_s, "kb": kbT_s}[kind][j, :, t0:t0 + W] \
                                        if kind != "ab" else abT_s[:, t0:t0 + W]
                                    dbn = {"qa": "qaT", "qi": "qiT", "qb": "qbT", "kb": "kbT", "ab": "abT"}[kind]
                                if kind == "gt":
                                    P.act(CALL("activation", out=sg.t[0:m, 0:W], in_=ps.t[0:m, 0:W], func=AF.Sigmoid), [ps.b], [sg.b])
                                else:
                                    evac(sg.t[0:m, 0:W], ps.t[0:m, 0:W], [ps.b], [sg.b])
                                P.dma("sp", CALL("dma_start", out=dst, in_=sg.t[0:m, 0:W]),
                                      [sg.b], dbs(dbn, t0, W, j if NJ[dbn] > 1 else 0))
                        ntile = (W + 127) // 128
                        for tt_i in range(ntile):
                            T = min(128, W - tt_i * 128)
                            tk0 = t0 + tt_i * 128
                            for (name, off, n, pieces) in tm:
                                ps = pss[nxt("ps", 6)]
                                for kc in range(8):
                                    P.pe(CALL("matmul",
                                        ps.t[0:T, 0:n], lhsT=xt.t[:, kc, tt_i * 128:tt_i * 128 + T],
                                        rhs=w_sb.t[:, kc, off:off + n], start=(kc == 0), stop=(kc == 7)),
                                        [xt.b, wb[name]], [ps.b])
                                if name == "ta":
                                    sg = stg_f[nxt("sf", 4)]
                                    evac(sg.t[0:T, 0:n], ps.t[0:T, 0:n], [ps.b], [sg.b])
                                    for (dst, c0, c1, dbn) in ((ko_d, 0, 128, "ko"), (vo_d, 128, 256, "vo"),
                                                               (kio_d, 256, 320, "kio"), (wi_s, 320, 328, "wi")):
                                        P.dma("sp", CALL("dma_start",
                                            out=dst[tk0:tk0 + T, :], in_=sg.t[0:T, c0:c1]), [sg.b], dbs(dbn, tk0, T))
                                    if tk0 < NTP:
                                        blk = tk0 // 128
                                        P.dve(CALL("tensor_copy",
                                            out=vaug.t[:, blk, :, 1:65],
                                            in_=ps.t[:, 128:256].rearrange("p (h d) -> p h d", h=2)), [ps.b], [vaug.b])
                                else:
                                    if name.startswith("tgb"):
                                        sg = stg_f[nxt("sf", 4)]
                                        dst, dbn = gb_s, "gb"
                                    else:
                                        sg = stg_b[nxt("sb", 4)]
                                        dst, dbn = (kb_s, "kb") if name == "tkb" else (vb_s, "vb")
                                    c0 = 512 if name.endswith("1") else 0
                                    if name.startswith("tgb"):
                                        P.act(CALL("activation", out=sg.t[0:T, 0:n], in_=ps.t[0:T, 0:n], func=AF.Sigmoid), [ps.b], [sg.b])
                                        P.dve(CALL("tensor_tensor", out=sg.t[0:T, 0:n], in0=sg.t[0:T, 0:n], in1=ps.t[0:T, 0:n], op=ALU.mult),
                                              [sg.b, ps.b], [sg.b])
                                    else:
                                        evac(sg.t[0:T, 0:n], ps.t[0:T, 0:n], [ps.b], [sg.b])
                                    P.dma("sp", CALL("dma_start",
                                        out=dst[tk0:tk0 + T, c0:c0 + n], in_=sg.t[0:T, 0:n]), [sg.b],
                                        dbs(dbn, tk0, T, (c0 // 512) if NJ[dbn] > 1 else 0))
                    P.flush()

            if "2" in phases:
                s2 = ExitStack()
                with s2:
                    NK = NTP
                    isc = [SB(s2, "isc%d" % i, [128, NK], F32) for i in range(2)]
                    mbs = [SB(s2, "mb%d" % i, [128, NK], BF16) for i in range(2)]
                    junk = SB(s2, "junk2", [128, NK], BF16)
                    rsl = [SB(s2, "rsl%d" % i, [128, 512], BF16) for i in range(4)]
                    ptl = [SB(s2, "ptl%d" % i, [128, 512], BF16) for i in range(4)]
                    qib = [SB(s2, "qib%d" % i, [128, 4, 128], BF16) for i in range(2)]
                    qab = [SB(s2, "qab%d" % i, [128, 4, 128], BF16) for i in range(2)]
                    wib = [SB(s2, "wib%d" % i, [128, 8], F32) for i in range(2)]
                    dgs = [SB(s2, "dg%d" % i, [128, 8, 128], BF16) for i in range(2)]
                    stt = [SB(s2, "st%d" % i, [128, 64], F32) for i in range(2)]
                    ost = [[SB(s2, "ost%d_%d" % (i, k), [128, 260], F32) for k in range(2)] for i in range(2)]
                    atk = [SB(s2, "atk%d" % i, [128, 512], BF16) for i in range(2)]
                    atT = SB(s2, "atT", [128, 4, 128], BF16)
                    rct = SB(s2, "rct", [128, 8], F32)
                    tmpd = SB(s2, "tmpd", [128, 128], F32)
                    psd = [PSB(s2, "psd%d" % i, [128, 512]) for i in range(2)]
                    psi = [PSB(s2, "psi%d" % i, [128, 512]) for i in range(2)]
                    pss_ = [PSB(s2, "pss%d" % i, [128, 512]) for i in range(2)]
                    pos_ = [PSB(s2, "pos%d" % i, [128, 512]) for i in range(2)]
                    rr2 = {"d": 0, "i": 0, "s": 0, "r": 0, "p": 0}

                    def nx2(k, n):
                        v = rr2[k]
                        rr2[k] = (v + 1) % n
                        return v

                    S_MIN1, S_MIN2, S_MAX, S_W0, S_MID, S_CNT, S_U, S_THR, S_WALL = 0, 1, 2, 3, 4, 5, 6, 7, 8
                    WSCALE = (8.0 ** -0.5) / 8.0

                    def stage_a(i):
                        sl = i % 2
                        q0 = i * 128
                        nk = (i + 1) * 128
                        qi_, qa_, wi_, dg_, I_, st_ = qib[sl], qab[sl], wib[sl], dgs[sl], isc[sl], stt[sl]
                        P.dma("sp", CALL("dma_start", out=qi_.t[:], in_=qiT_s[:, :, q0:q0 + 128].rearrange("j p t -> p j t")),
                              dbs("qiT", q0, 128), [qi_.b])
                        P.dma("sp", CALL("dma_start", out=qa_.t[:], in_=qaT_s[:, :, q0:q0 + 128].rearrange("j p t -> p j t")),
                              dbs("qaT", q0, 128), [qa_.b])
                        P.dma("sp", CALL("dma_start", out=wi_.t[:], in_=wi_s[q0:q0 + 128, :]), dbs("wi", q0, 128), [wi_.b])
                        for h in range(8):
                            P.pool(CALL("tensor_scalar", out=dg_.t[:, h, :], in0=ident_f, scalar1=wi_.t[:, h:h + 1],
                                                                 scalar2=WSCALE, op0=ALU.mult, op1=ALU.mult),
                                   [wi_.b, cst.b], [dg_.b])
                        nch = (nk + 511) // 512
                        steps = [(c, h) for c in range(nch) for h in range(8)]
                        pds = {}
                        pIs = {}

                        def dots(k):
                            c, h = steps[k]
                            k0 = c * 512
                            Wc = min(512, nk - k0)
                            pd = psd[nx2("d", 2)]
                            pds[k] = pd
                            kis = kiA if h % 2 == 0 else kiB
                            P.pe(CALL("matmul", pd.t[:, 0:Wc], lhsT=qi_.t[:, h // 2, :], rhs=kis.t[:, k0:k0 + Wc], start=True, stop=True),
                                 [qi_.b, kis.b], [pd.b])

                        dots(0)
                        for k, (c, h) in enumerate(steps):
                            k0 = c * 512
                            Wc = min(512, nk - k0)
                            if h == 0:
                                pIs[c] = psi[nx2("i", 2)]
                            pI = pIs[c]
                            pd = pds[k]
                            r_ = rsl[nx2("r", 4)]
                            P.act(CALL("activation", out=r_.t[:, 0:Wc], in_=pd.t[:, 0:Wc], func=AF.Relu), [pd.b], [r_.b])
                            if k + 1 < len(steps):
                                dots(k + 1)
                            P.pe(CALL("matmul", pI.t[:, 0:Wc], lhsT=dg_.t[:, h, :], rhs=r_.t[:, 0:Wc], start=(h == 0), stop=(h == 7)),
                                 [dg_.b, r_.b], [pI.b])
                            if h == 7:
                                P.act(CALL("activation", out=I_.t[:, k0:k0 + Wc], in_=pI.t[:, 0:Wc], func=AF.Copy), [pI.b], [I_.b])
                        d0 = i * 128
                        P.dve(CALL("tensor_tensor", out=tmpd.t[:], in0=I_.t[:, d0:d0 + 128], in1=cneg_f, op=ALU.subtract),
                              [I_.b, cst.b], [tmpd.b])
                        P.dve(CALL("tensor_reduce", out=st_.t[:, S_MIN2:S_MIN2 + 1], in_=tmpd.t[:], axis=AX.X, op=ALU.min),
                              [tmpd.b], [st_.b])
                        if i > 0:
                            P.dve(CALL("tensor_reduce", out=st_.t[:, S_MIN1:S_MIN1 + 1], in_=I_.t[:, 0:d0], axis=AX.X,
                                                            op=ALU.min), [I_.b], [st_.b])
                            P.dve(CALL("tensor_tensor", out=st_.t[:, S_MIN2:S_MIN2 + 1], in0=st_.t[:, S_MIN2:S_MIN2 + 1],
                                                            in1=st_.t[:, S_MIN1:S_MIN1 + 1], op=ALU.min), [st_.b], [st_.b])
                        P.dve(CALL("tensor_tensor", out=I_.t[:, d0:d0 + 128], in0=I_.t[:, d0:d0 + 128], in1=cneg_f, op=ALU.add),
                              [I_.b, cst.b], [I_.b])
                        P.dve(CALL("tensor_reduce", out=st_.t[:, S_MAX:S_MAX + 1], in_=I_.t[:, 0:nk], axis=AX.X, op=ALU.max),
                              [I_.b], [st_.b])

                    def stage_b(i):
                        sl = i % 2
                        nk = (i + 1) * 128
                        I_, st_, mb_ = isc[sl], stt[sl], mbs[sl]
                        c_ = lambda k: st_.t[:, k:k + 1]
                        P.dve(CALL("tensor_tensor", out=c_(S_W0), in0=c_(S_MAX), in1=c_(S_MIN2), op=ALU.subtract), [st_.b], [st_.b])
                        P.dve(CALL("tensor_scalar", out=c_(S_U), in0=c_(S_W0), scalar1=-1e-3, scalar2=-1e-4, op0=ALU.mult,
                                                        op1=ALU.add), [st_.b], [st_.b])
                        P.dve(CALL("tensor_tensor", out=c_(S_MIN2), in0=c_(S_MIN2), in1=c_(S_U), op=ALU.add), [st_.b], [st_.b])
                        P.dve(CALL("tensor_tensor", out=c_(S_W0), in0=c_(S_MAX), in1=c_(S_MIN2), op=ALU.subtract), [st_.b], [st_.b])
                        P.dve(CALL("tensor_scalar", out=c_(S_W0), in0=c_(S_W0), scalar1=1.0001, scalar2=1e-6, op0=ALU.mult,
                                                        op1=ALU.add), [st_.b], [st_.b])
                        P.dve(CALL("tensor_scalar", out=st_.t[:, S_WALL:S_WALL + 32], in0=cst.t[:, C_PW:C_PW + 32],
                                                        scalar1=c_(S_W0), scalar2=None, op0=ALU.mult), [st_.b, cst.b], [st_.b])
                        P.dve(CALL("tensor_tensor", out=c_(S_MID), in0=c_(S_MIN2), in1=c_(S_WALL), op=ALU.add), [st_.b], [st_.b])
                        for r in range(ROUNDS):
                            P.dve(CALL("tensor_scalar", out=junk.t[:, 0:nk], in0=I_.t[:, 0:nk], scalar1=c_(S_MID), scalar2=0.0,
                                                            op0=ALU.is_ge, op1=ALU.add, accum_out=c_(S_CNT)),
                                  [I_.b, st_.b], [junk.b, st_.b])
                            P.dve(CALL("tensor_scalar", out=c_(S_U), in0=c_(S_CNT), scalar1=TOPK - 0.5,
                                                                 scalar2=c_(S_WALL + r), op0=ALU.is_ge, op1=ALU.mult),
                                  [st_.b], [st_.b])
                            nxt_w = S_WALL + r + 1 if r + 1 < ROUNDS else S_WALL + r
                            dst = S_MID if r + 1 < ROUNDS else S_THR
                            P.dve(CALL("scalar_tensor_tensor",
                                out=c_(dst), in0=c_(S_U), scalar=c_(nxt_w), in1=c_(S_MID), op0=ALU.subtract, op1=ALU.add),
                                [st_.b], [st_.b])
                        P.dve(CALL("tensor_scalar", out=mb_.t[:, 0:nk], in0=I_.t[:, 0:nk], scalar1=c_(S_THR), scalar2=NEG,
                                                        op0=ALU.is_lt, op1=ALU.mult), [I_.b, st_.b], [mb_.b])

                    def stage_c(i):
                        sl = i % 2
                        qa_, mb_ = qab[sl], mbs[sl]
                        steps = [(kvh, c) for kvh in range(2) for c in range(i + 1)]
                        pls = {}

                        def logits(k):
                            kvh, c = steps[k]
                            kk = kA if kvh == 0 else kB
                            k0 = c * 128
                            ps_ = pss_[nx2("s", 2)]
                            pls[k] = ps_
                            P.pe(CALL("matmul", ps_.t[:, :], lhsT=kk.t[:, k0:k0 + 128], rhs=qa_.t[:, :, :], start=True, stop=False),
                                 [kk.b, qa_.b], [ps_.b])
                            P.pe(CALL("matmul", ps_.t[:, :], lhsT=mb_.t[:, k0:k0 + 128], rhs=i4_bf.t[:, :, :], start=False, stop=True),
                                 [mb_.b, i4_bf.b], [ps_.b])

                        logits(0)
                        for k, (kvh, c) in enumerate(steps):
                            po = pos_[kvh]
                            ps_ = pls[k]
                            pt_ = ptl[nx2("p", 4)]
                            P.act(CALL("activation", out=pt_.t[:, :], in_=ps_.t[:, :], func=AF.Exp, scale=0.125), [ps_.b], [pt_.b])
                            if k + 1 < len(steps):
                                logits(k + 1)
                            for j in range(4):
                                P.pe(CALL("matmul", po.t[:, j * 65:(j + 1) * 65], lhsT=pt_.t[:, j * 128:(j + 1) * 128],
                                          rhs=vaug.t[:, c, kvh, :], start=(c == 0 and j == 0), stop=(c == i), skip_group_check=True),
                                     [vaug.b, pt_.b], [po.b])
                            if c == i:
                                o_ = ost[sl][kvh]
                                P.act(CALL("activation", out=o_.t[:, :], in_=po.t[:, 0:260], func=AF.Copy), [po.b], [o_.b])

                    def stage_c_norm(i):
                        sl = i % 2
                        at_ = atk[sl]
                        for kvh in range(2):
                            o3 = ost[sl][kvh].t[:, :].rearrange("p (j e) -> p j e", e=65)
                            P.dve(CALL("reciprocal", out=rct.t[:, kvh * 4:kvh * 4 + 4].unsqueeze(2), in_=o3[:, :, 0:1]), [ost[sl][kvh].b], [rct.b])
                            P.dve(CALL("tensor_tensor", out=at_.t[:, kvh * 256:(kvh + 1) * 256].rearrange("p (j d) -> p j d", j=4),
                                       in0=o3[:, :, 1:65], in1=rct.t[:, kvh * 4:kvh * 4 + 4].unsqueeze(2).broadcast_to([128, 4, 64]),
                                       op=ALU.mult), [ost[sl][kvh].b, rct.b], [at_.b])

                    def stage_c_T(i):
                        sl = i % 2
                        q0 = i * 128
                        at_ = atk[sl]
                        po = pos_[0]
                        psx_ = po.t.bitcast(BF16)
                        for kc in range(4):
                            P.pe(CALL("transpose", psx_[:, kc * 128:(kc + 1) * 128], at_.t[:, kc * 128:(kc + 1) * 128], ident_bf.t[:, :]),
                                 [at_.b, ident_bf.b], [po.b])
                        P.act(CALL("activation", out=atT.t[:, :, :], in_=psx_[:, 0:512].rearrange("p (k t) -> p k t", k=4), func=AF.Copy),
                              [po.b], [atT.b])
                        P.dma("sp", CALL("dma_start", out=attnT_s[:, :, q0:q0 + 128].rearrange("k p t -> p k t"), in_=atT.t[:, :, :]),
                              [atT.b], dbs("attnT", q0, 128))

                    for i in range(NBLK):
                        stage_a(i)
                        if i >= 2:
                            stage_c_T(i - 2)
                        if i >= 1:
                            stage_c(i - 1)
                        stage_b(i)
                        if i >= 1:
                            stage_c_norm(i - 1)
                    if NBLK >= 2:
                        stage_c_T(NBLK - 2)
                    stage_c(NBLK - 1)
                    stage_c_norm(NBLK - 1)
                    stage_c_T(NBLK - 1)
                    P.flush()
            if "5" in phases:
                s2b = ExitStack()
                with s2b:
                    ROUNDS_S = 24
                    NKP = NPAGES * 128
                    SEGW = NKP // 4
                    IW = SEGW + 8
                    R = SB(s2b, "R2b", [128, 3 * NKP], BF16)
                    BR0, BR1, BR2 = Buf("BR0"), Buf("BR1"), Buf("BR2")
                    O1, O2 = NKP, 2 * NKP
                    Is = SB(s2b, "Is", [128, IW], F32)
                    MBs = SB(s2b, "MBs", [128, IW], BF16)
                    pti = SB(s2b, "pti", [128, 1], I32)
                    ptf = SB(s2b, "ptf", [128, 1], F32)
                    idf = SB(s2b, "idf", [128, 16], F32)
                    idxs = [SB(s2b, "idx%d" % i, [128, 16], I32) for i in range(NB_S)]
                    qis = SB(s2b, "qis", [64, 8, NS], BF16)
                    qs0 = SB(s2b, "qs0", [128, 4, NS], BF16)
                    qs1 = SB(s2b, "qs1", [128, 4, NS], BF16)
                    wis = SB(s2b, "wis", [8, NB_S, 8], F32)
                    dss = SB(s2b, "dss", [8, NB_S, 8, 8], BF16)
                    rs2 = [SB(s2b, "rs2_%d" % i, [64, 512], BF16) for i in range(4)]
                    qisb = SB(s2b, "qisb", [64, NB_S, 64], BF16)
                    wsel = SB(s2b, "wsel", [64, NB_S, 8], BF16)
                    qsb = SB(s2b, "qsb", [128, NB_S, 64], BF16)
                    sel2 = SB(s2b, "sel2", [128, 16, 64], BF16)
                    sg2 = [SB(s2b, "sg2_%d" % i, [8, 512], F32) for i in range(4)]
                    pt2 = [SB(s2b, "pt2_%d" % i, [128, 256], BF16) for i in range(4)]
                    sst = SB(s2b, "sst", [128, 64], F32)
                    vnew = SB(s2b, "vnew", [8, 128], BF16)
                    rcp2 = SB(s2b, "rcp2", [1, 64], F32)
                    bcs2 = SB(s2b, "bcs2", [64, 64], F32)
                    as2 = SB(s2b, "as2", [64, 2, 4, 8], BF16)
                    pst = [PSB(s2b, "pst%d" % i, [128, 1024], BF16) for i in range(2)]
                    psd2 = [PSB(s2b, "psd2_%d" % i, [128, 512]) for i in range(2)]
                    psi2s = [PSB(s2b, "psi2_%d" % i, [128, 512]) for i in range(2)]
                    psi2 = psi2s[0]
                    pso2s = [psi2s[1], PSB(s2b, "pso2b", [128, 512])]
                    psm2 = PSB(s2b, "psm2", [128, 512])
                    rr3 = {"t": 0, "d": 0, "r": 0, "g": 0, "p": 0, "e": 0, "i": 0}

                    def nx3(k, n):
                        v = rr3[k]
                        rr3[k] = (v + 1) % n
                        return v

                    def evac3(out_ap, in_ap, reads, writes):
                        if nx3("e", 2) == 0:
                            P.act(CALL("activation", out=out_ap, in_=in_ap, func=AF.Copy), reads, writes)
                        else:
                            P.dve(CALL("tensor_copy", out=out_ap, in_=in_ap), reads, writes)

                    S0 = NTP
                    WSC = (8.0 ** -0.5) / 8.0
                    G_f = cst.t[:, C_G:C_G + 128]
                    P.pool(CALL("memset", qs0.t[:, :, :], 0.0), [], [qs0.b])
                    P.pool(CALL("memset", qs1.t[:, :, :], 0.0), [], [qs1.b])
                    P.pool(CALL("memset", Is.t[:, SEGW:IW], -BIG), [], [Is.b])
                    P.dma("sp", CALL("dma_start", out=qs0.t[0:64, :, :], in_=qaT_s[:, 0:64, S0:S0 + NS].rearrange("j p t -> p j t")),
                          dbs("qaT", S0, NS), [qs0.b])
                    P.dma("sp", CALL("dma_start", out=qs1.t[64:128, :, :], in_=qaT_s[:, 64:128, S0:S0 + NS].rearrange("j p t -> p j t")),
                          dbs("qaT", S0, NS), [qs1.b])
                    for j in range(4):
                        for par in range(2):
                            P.dma("sp", CALL("dma_start", out=qis.t[:, 2 * j + par, :], in_=qiT_s[j, 64 * par:64 * par + 64, S0:S0 + NS]),
                                  dbs("qiT", S0, NS), [qis.b])
                    P.dma("sp", CALL("dma_start", out=wis.t[:, :, :], in_=wi_s[S0:S0 + NS, :].rearrange("(b q) h -> q b h", q=8)),
                          dbs("wi", S0, NS), [wis.b])
                    for b in range(NB_S):
                        for h in range(8):
                            P.dve(CALL("tensor_scalar", out=dss.t[:, b, h, :], in0=cst.t[0:8, C_ID:C_ID + 8], scalar1=wis.t[:, b, h:h + 1],
                                       scalar2=WSC, op0=ALU.mult, op1=ALU.mult), [wis.b, cst.b], [dss.b])
                    for bs in range(16):
                        P.dve(CALL("tensor_copy", out=sel2.t[:, bs, :].rearrange("p (a q) -> p a q", a=8),
                                   in_=cst.t[:, C_ID + bs * 8:C_ID + bs * 8 + 8].unsqueeze(1).broadcast_to([128, 8, 8])), [cst.b], [sel2.b])

                    def page_idx(b):
                        P.dma("sp", CALL("dma_start", out=pti.t[:, :], in_=pt_d[b:b + 1, :].rearrange("o p -> p o")), [], [pti.b])
                        P.dve(CALL("tensor_copy", out=ptf.t[:, :], in_=pti.t[:, :]), [pti.b], [ptf.b])
                        for o in range(4):
                            P.dve(CALL("tensor_scalar", out=idf.t[:, o:o + 1], in0=ptf.t[:, :], scalar1=4.0, scalar2=float(o),
                                       op0=ALU.mult, op1=ALU.add), [ptf.b], [idf.b])
                        for o in range(8):
                            P.dve(CALL("tensor_scalar", out=idf.t[:, 4 + o:5 + o], in0=ptf.t[:, :], scalar1=8.0, scalar2=float(o),
                                       op0=ALU.mult, op1=ALU.add), [ptf.b], [idf.b])
                        P.dve(CALL("tensor_copy", out=idxs[b].t[:, 0:12], in_=idf.t[:, 0:12]), [idf.b], [idxs[b].b])

                    def gather(b, src_d, n_o, icol0, dst0, bufR):
                        for o in range(n_o):
                            P.dma("pool", CALL("indirect_dma_start", out=R.t[:, dst0 + o * 2048:dst0 + (o + 1) * 2048], out_offset=None,
                                               in_=src_d[:, :],
                                               in_offset=bass.IndirectOffsetOnAxis(ap=idxs[b].t[:, icol0 + o:icol0 + o + 1], axis=0)),
                                  [idxs[b].b], [bufR])

                    for b in range(NB_S):
                        page_idx(b)

                    for b in range(NB_S):
                        P.dve(CALL("tensor_copy", out=qisb.t[:, b, :].rearrange("p (h q) -> p h q", h=8), in_=qis.t[:, :, 8 * b:8 * b + 8]),
                              [qis.b], [qisb.b])
                        ptw = pst[nx3("t", 2)]
                        P.pe(CALL("transpose", ptw.t[0:64, 0:8], dss.t[:, b, :, :].rearrange("p h q -> p (h q)"), ident_bf.t[0:8, 0:8]),
                             [dss.b, ident_bf.b], [ptw.b])
                        P.dve(CALL("tensor_copy", out=wsel.t[:, b, :], in_=ptw.t[0:64, 0:8]), [ptw.b], [wsel.b])
                    for b in range(NB_S):
                        gather(b, cki_d, 4, 0, 0, BR0)
                        if b == 0:
                            gather(0, cv_d, 8, 4, O2, BR2)
                        for g8 in range(16):
                            pt_ = pst[nx3("t", 2)]
                            for k in range(8):
                                t = g8 * 8 + k
                                P.pe(CALL("transpose", pt_.t[0:64, k * 128:(k + 1) * 128], R.t[:, t * 64:(t + 1) * 64], ident_bf.t[:, :]),
                                     [BR0, ident_bf.b], [pt_.b])
                            evac3(R.t[0:64, O1 + g8 * 1024:O1 + (g8 + 1) * 1024], pt_.t[0:64, :], [pt_.b], [BR1])
                        NCH = NKP // 512 + 1
                        pdd = {}

                        def idots(c, b=b):
                            new = (c == NKP // 512)
                            Wc = 8 if new else 512
                            pd = psd2[nx3("d", 2)]
                            pdd[c] = pd
                            rhs = kiA.t[0:64, S0 + 8 * b:S0 + 8 * b + 8] if new else R.t[0:64, O1 + c * 512:O1 + (c + 1) * 512]
                            P.pe(CALL("matmul", pd.t[0:64, 0:Wc], lhsT=qisb.t[:, b, :], rhs=rhs, start=True, stop=True),
                                 [qisb.b, kiA.b if new else BR1], [pd.b])

                        idots(0)
                        for c in range(NCH):
                            new = (c == NKP // 512)
                            Wc = 8 if new else 512
                            pd = pdd[c]
                            r_ = rs2[nx3("r", 4)]
                            P.act(CALL("activation", out=r_.t[:, 0:Wc], in_=pd.t[0:64, 0:Wc], func=AF.Relu), [pd.b], [r_.b])
                            if c + 1 < NCH:
                                idots(c + 1)
                            pi_ = psi2s[nx3("i", 2)]
                            P.pe(CALL("matmul", pi_.t[0:8, 0:Wc], lhsT=wsel.t[:, b, :], rhs=r_.t[:, 0:Wc], start=True, stop=True),
                                 [wsel.b, r_.b], [pi_.b])
                            sg_ = sg2[nx3("g", 4)]
                            if new:
                                P.dve(CALL("tensor_tensor", out=sg_.t[:, 0:8], in0=pi_.t[0:8, 0:8], in1=cst.t[0:8, C_CN:C_CN + 8], op=ALU.add),
                                      [pi_.b, cst.b], [sg_.b])
                                P.dma("sp", CALL("dma_start", out=Is.t[b * 32:b * 32 + 8, SEGW:IW], in_=sg_.t[:, 0:8]), [sg_.b], [Is.b])
                            else:
                                P.dve(CALL("tensor_copy", out=sg_.t[:, :], in_=pi_.t[0:8, :]), [pi_.b], [sg_.b])
                                seg, cc = c // (SEGW // 512), c % (SEGW // 512)
                                r0 = b * 32 + seg * 8
                                P.dma("sp", CALL("dma_start", out=Is.t[r0:r0 + 8, cc * 512:(cc + 1) * 512], in_=sg_.t[:, :]), [sg_.b], [Is.b])

                    gather(0, ck_d, 8, 4, 0, BR0)
                    c2 = lambda k: sst.t[:, k:k + 1]
                    Q_MAX, Q_MIN, Q_A, Q_HI, Q_LO, Q_W0, Q_MID, Q_CNT, Q_U, Q_THR, Q_WALL = 0, 1, 2, 3, 4, 5, 6, 7, 8, 9, 16
                    P.dve(CALL("tensor_reduce", out=c2(Q_MAX), in_=Is.t[:, 0:IW], axis=AX.X, op=ALU.max), [Is.b], [sst.b])
                    P.dve(CALL("tensor_reduce", out=c2(Q_MIN), in_=Is.t[:, 0:SEGW], axis=AX.X, op=ALU.min), [Is.b], [sst.b])
                    P.dve(CALL("tensor_scalar", out=c2(Q_MIN), in0=c2(Q_MIN), scalar1=-1.0, scalar2=None, op0=ALU.mult), [sst.b], [sst.b])
                    P.dve(CALL("tensor_tensor", out=c2(Q_A), in0=c2(Q_MAX), in1=c2(Q_MIN), op=ALU.max), [sst.b], [sst.b])
                    P.pe(CALL("matmul", psi2.t[:, 0:1], lhsT=G_f, rhs=c2(Q_A), start=True, stop=True), [sst.b, cst.b], [psi2.b])
                    P.dve(CALL("tensor_scalar", out=c2(Q_HI), in0=psi2.t[:, 0:1], scalar1=1.001, scalar2=1e-3, op0=ALU.mult, op1=ALU.add),
                          [psi2.b], [sst.b])
                    P.dve(CALL("tensor_scalar", out=c2(Q_LO), in0=c2(Q_HI), scalar1=-1.0, scalar2=None, op0=ALU.mult), [sst.b], [sst.b])
                    P.dve(CALL("tensor_scalar", out=c2(Q_W0), in0=c2(Q_HI), scalar1=2.0, scalar2=None, op0=ALU.mult), [sst.b], [sst.b])
                    P.dve(CALL("tensor_scalar", out=sst.t[:, Q_WALL:Q_WALL + 32], in0=cst.t[:, C_PW:C_PW + 32], scalar1=c2(Q_W0), scalar2=None,
                               op0=ALU.mult), [sst.b, cst.b], [sst.b])
                    P.dve(CALL("tensor_tensor", out=c2(Q_MID), in0=c2(Q_LO), in1=c2(Q_WALL), op=ALU.add), [sst.b], [sst.b])
                    for r in range(ROUNDS_S):
                        P.dve(CALL("tensor_scalar", out=MBs.t[:, :], in0=Is.t[:, :], scalar1=c2(Q_MID), scalar2=0.0, op0=ALU.is_ge,
                                   op1=ALU.add, accum_out=c2(Q_CNT)), [Is.b, sst.b], [MBs.b, sst.b])
                        P.pe(CALL("matmul", psi2.t[:, 0:1], lhsT=G_f, rhs=c2(Q_CNT), start=True, stop=True), [sst.b, cst.b], [psi2.b])
                        P.dve(CALL("tensor_scalar", out=c2(Q_U), in0=psi2.t[:, 0:1], scalar1=TOPK_S - 0.5, scalar2=c2(Q_WALL + r),
                                   op0=ALU.is_ge, op1=ALU.mult), [psi2.b, sst.b], [sst.b])
                        nxt_w = Q_WALL + r + 1 if r + 1 < ROUNDS_S else Q_WALL + r
                        dst = Q_MID if r + 1 < ROUNDS_S else Q_THR
                        P.dve(CALL("scalar_tensor_tensor", out=c2(dst), in0=c2(Q_U), scalar=c2(nxt_w), in1=c2(Q_MID), op0=ALU.subtract,
                                   op1=ALU.add), [sst.b], [sst.b])
                    P.dve(CALL("tensor_scalar", out=MBs.t[:, :], in0=Is.t[:, :], scalar1=c2(Q_THR), scalar2=NEG, op0=ALU.is_lt, op1=ALU.mult),
                          [Is.b, sst.b], [MBs.b])

                    for b in range(NB_S):
                        P.dve(CALL("tensor_copy", out=qsb.t[:, b, 0:32].rearrange("p (j q) -> p j q", j=4), in_=qs0.t[:, :, 8 * b:8 * b + 8]),
                              [qs0.b], [qsb.b])
                        P.dve(CALL("tensor_copy", out=qsb.t[:, b, 32:64].rearrange("p (j q) -> p j q", j=4), in_=qs1.t[:, :, 8 * b:8 * b + 8]),
                              [qs1.b], [qsb.b])
                    for b in range(NB_S):
                        if b >= 1:
                            gather(b, cv_d, 8, 4, O2, BR2)
                        P.dma("pool", CALL("dma_start", out=vnew.t[:, :], in_=vo_d[S0 + 8 * b:S0 + 8 * b + 8, :]), dbs("vo", S0, NS), [vnew.b])
                        for g8 in range(16):
                            pt_ = pst[nx3("t", 2)]
                            for k in range(8):
                                t = g8 * 8 + k
                                P.pe(CALL("transpose", pt_.t[:, k * 128:(k + 1) * 128], R.t[:, t * 128:(t + 1) * 128], ident_bf.t[:, :]),
                                     [BR0, ident_bf.b], [pt_.b])
                            evac3(R.t[:, O1 + g8 * 1024:O1 + (g8 + 1) * 1024], pt_.t[:, :], [pt_.b], [BR1])
                        if b + 1 < NB_S:
                            gather(b + 1, ck_d, 8, 4, 0, BR0)
                        ngrp = 128 // 4
                        pll = {}

                        def alogits(g4, b=b):
                            new = (g4 == ngrp)
                            nkk = 8 if new else 128
                            ps_ = psd2[nx3("d", 2)]
                            pll[g4] = ps_
                            ts_ = [128] if new else [g4 * 4 + k for k in range(4)]
                            for k, t in enumerate(ts_):
                                if new:
                                    l1a, l1b = kA.t[:, S0 + 8 * b:S0 + 8 * b + 8], kB.t[:, S0 + 8 * b:S0 + 8 * b + 8]
                                    l2 = MBs.t[:, SEGW:IW]
                                    sl_ = sel2.t[:, b * 4, :]
                                    P.pe(CALL("matmul", ps_.t[0:nkk, 0:64], lhsT=l1a, rhs=qsb.t[:, b, :], start=True, stop=False),
                                         [kA.b, qsb.b], [ps_.b])
                                    P.pe(CALL("matmul", ps_.t[0:nkk, 0:64], lhsT=l1b, rhs=qsb.t[:, b, :], start=False, stop=False),
                                         [kB.b, qsb.b], [ps_.b])
                                else:
                                    l1 = R.t[:, O1 + t * 128:O1 + (t + 1) * 128]
                                    seg, cc = t // 32, t % 32
                                    l2 = MBs.t[:, cc * 128:(cc + 1) * 128]
                                    sl_ = sel2.t[:, b * 4 + seg, :]
                                    P.pe(CALL("matmul", ps_.t[0:nkk, k * 64:(k + 1) * 64], lhsT=l1, rhs=qsb.t[:, b, :], start=True, stop=False),
                                         [BR1, qsb.b], [ps_.b])
                                P.pe(CALL("matmul", ps_.t[0:nkk, k * 64:(k + 1) * 64], lhsT=l2, rhs=sl_, start=False, stop=True),
                                     [MBs.b, sel2.b], [ps_.b])

                        alogits(0)
                        for g4 in range(ngrp + 1):
                            new = (g4 == ngrp)
                            nkk = 8 if new else 128
                            ncol = 64 if new else 256
                            ps_ = pll[g4]
                            ts_ = [128] if new else [g4 * 4 + k for k in range(4)]
                            p_ = pt2[nx3("p", 4)]
                            P.act(CALL("activation", out=p_.t[0:nkk, 0:ncol], in_=ps_.t[0:nkk, 0:ncol], func=AF.Exp, scale=0.125), [ps_.b], [p_.b])
                            if g4 + 1 <= ngrp:
                                alogits(g4 + 1)
                            for k, t in enumerate(ts_):
                                first = (g4 == 0 and k == 0)
                                for kvh in range(2):
                                    po_ = pso2s[kvh]
                                    if new:
                                        lv = vnew.t[0:8, kvh * 64:(kvh + 1) * 64]
                                        rdv = vnew.b
                                    else:
                                        lv = R.t[:, O2 + t * 128 + kvh * 64:O2 + t * 128 + kvh * 64 + 64]
                                        rdv = BR2
                                    P.pe(CALL("matmul", po_.t[0:64, 0:32], lhsT=lv, rhs=p_.t[0:nkk, k * 64 + kvh * 32:k * 64 + kvh * 32 + 32],
                                              start=first, stop=new), [rdv, p_.b], [po_.b])
                                P.pe(CALL("matmul", psm2.t[0:1, 0:64], lhsT=ones_bf.t[0:nkk, 0:1], rhs=p_.t[0:nkk, k * 64:(k + 1) * 64],
                                          start=first, stop=new), [ones_bf.b, p_.b], [psm2.b])
                        P.dve(CALL("reciprocal", out=rcp2.t[:, :], in_=psm2.t[0:1, 0:64]), [psm2.b], [rcp2.b])
                        pb_ = psd2[nx3("d", 2)]
                        P.pe(CALL("matmul", pb_.t[0:64, 0:64], lhsT=cst.t[0:1, C_LE:C_LE + 64], rhs=rcp2.t[:, :], start=True, stop=True),
                             [cst.b, rcp2.b], [pb_.b])
                        P.act(CALL("activation", out=bcs2.t[:, :], in_=pb_.t[0:64, 0:64], func=AF.Copy), [pb_.b], [bcs2.b])
                        for kvh in range(2):
                            P.dve(CALL("tensor_tensor", out=as2.t[:, kvh, :, :], in0=pso2s[kvh].t[0:64, 0:32].rearrange("p (j q) -> p j q", j=4),
                                       in1=bcs2.t[:, kvh * 32:(kvh + 1) * 32].rearrange("p (j q) -> p j q", j=4), op=ALU.mult),
                                  [pso2s[kvh].b, bcs2.b], [as2.b])
                        for kvh in range(2):
                            for j in range(4):
                                kc = 2 * kvh + j // 2
                                r0 = (j % 2) * 64
                                P.dma("sp", CALL("dma_start", out=attnT_s[kc, r0:r0 + 64, S0 + 8 * b:S0 + 8 * b + 8], in_=as2.t[:, kvh, j, :]),
                                      [as2.b], dbs("attnT", S0, NS, kc))
                    P.flush()
        wf1 = SB(g, "wf1", [128, 8, 4096], BF16)
        wf1b = [Buf("wf1_%d" % i) for i in range(8)]
        s3w = ExitStack()
        with s3w:
            wao = SB(s3w, "wao", [128, 4, D], BF16)
            wgo = SB(s3w, "wgo", [128, 8, D], BF16)
            wo = SB(s3w, "wo", [128, 8, D], BF16)
            if "4" in phases:
                for (wt, wd) in ((wao, wao_d), (wgo, wgo_d), (wo, wo_d)):
                    P.dma("pool", CALL("dma_start",
                        out=wt.t[:, :, :], in_=wd.rearrange("(kc p) n -> p kc n", p=128)), [], [wt.b])
                for n in ln_d:
                    P.dma("sp", CALL("dma_start", out=lnb[n].t[:, :], in_=ln_d[n][0:1, :].broadcast_to([128, D])),
                          [], [lnb[n].b])
                for kc in range(8):
                    for c0 in (0, 2048):
                        P.dma("pool", CALL("dma_start", out=wf1.t[:, kc, c0:c0 + 2048], in_=wf1_d[kc * 128:(kc + 1) * 128, c0:c0 + 2048]),
                              [], [wf1b[kc]])
            if "3" in phases:
                s3 = ExitStack()
                with s3:
                    aug = SB(s3, "aug", [17, 128], BF16)
                    wal = SB(s3, "wal", [17, 512], BF16)
                    gnb = SB(s3, "gnb", [128, 256], F32)
                    qbb = SB(s3, "qbb", [128, 4, 128], BF16)
                    kbb = SB(s3, "kbb", [128, 4, 128], BF16)
                    kbt = SB(s3, "kbt", [128, 512], BF16)
                    vbts = [SB(s3, "vbt%d" % i, [128, 1024], BF16) for i in range(2)]
                    gbt = SB(s3, "gbt", [128, 1024], F32)
                    ee = SB(s3, "ee", [128, 512], F32)
                    la = SB(s3, "la", [128, 512], F32)
                    Eqs = [SB(s3, "Eq%d" % i, [128, 4, 128], F32) for i in range(2)]
                    Ek = SB(s3, "Ek", [128, 4, 128], F32)
                    Er = SB(s3, "Er", [128, 512], F32)
                    qts = [SB(s3, "qt%d" % i, [128, 4, 128], BF16) for i in range(2)]
                    kts = [SB(s3, "kt%d" % i, [128, 4, 128], BF16) for i in range(2)]
                    kps = [SB(s3, "kp%d" % i, [128, 512], BF16) for i in range(2)]
                    attm = SB(s3, "attm", [128, 4, 128], BF16)
                    Sf = SB(s3, "Sf", [128, 4, 256], F32)
                    Sb = SB(s3, "Sb", [128, 4, 256], BF16)
                    onr = SB(s3, "onr", [128, 1024], F32)
                    osbs = [SB(s3, "osb%d" % i, [128, 1024], F32) for i in range(2)]
                    sgts = [SB(s3, "sgt%d" % i, [128, 1024], F32) for i in range(2)]
                    obb = SB(s3, "obb", [128, 1024], BF16)
                    obT = SB(s3, "obT", [128, 8, 128], BF16)
                    gst = SB(s3, "gst", [128, 16], F32)
                    jk3 = SB(s3, "jk3", [128, 256], BF16)
                    psA = PSB(s3, "psA", [128, 512])
                    psC = PSB(s3, "psC", [128, 512])
                    psT_ = PSB(s3, "psT", [128, 512])
                    psO = PSB(s3, "psO", [128, 1024])
                    psS = PSB(s3, "psS", [128, 1024])
                    psX = PSB(s3, "psX", [128, 1024], BF16)

                    P.pool(CALL("memset", aug.t[:, :], 1.0), [], [aug.b])
                    P.dma("pool", CALL("dma_start", out=wal.t[1:17, :], in_=wal_d[:, :]), [], [wal.b])
                    P.dma("pool", CALL("dma_start", out=wal.t[0:1, :], in_=bal_d[:, :]), [], [wal.b])
                    P.dma("sp", CALL("dma_start", out=gnb.t[:, :], in_=gng_d[0:1, :].broadcast_to([128, 256])), [], [gnb.b])

                    def gla_prep(ci, tok0, C):
                        sl = ci % 2
                        qt, kt, kp, vbt, sgt, Eq = qts[sl], kts[sl], kps[sl], vbts[sl], sgts[sl], Eqs[sl]
                        P.dma("sp", CALL("dma_start", out=aug.t[1:17, 0:C], in_=abT_s[:, tok0:tok0 + C]), dbs("abT", tok0, C), [aug.b])
                        P.dma("sp", CALL("dma_start", out=qbb.t[:, :, 0:C], in_=qbT_s[:, :, tok0:tok0 + C].rearrange("j p t -> p j t")),
                              dbs("qbT", tok0, C), [qbb.b])
                        P.dma("sp", CALL("dma_start", out=kbb.t[:, :, 0:C], in_=kbT_s[:, :, tok0:tok0 + C].rearrange("j p t -> p j t")),
                              dbs("kbT", tok0, C), [kbb.b])
                        P.dma("sp", CALL("dma_start", out=kbt.t[0:C, :], in_=kb_s[tok0:tok0 + C, :]), dbs("kb", tok0, C), [kbt.b])
                        P.dma("sp", CALL("dma_start", out=vbt.t[0:C, :], in_=vb_s[tok0:tok0 + C, :]), dbs("vb", tok0, C), [vbt.b])
                        P.pe(CALL("matmul", psA.t[0:C, :], lhsT=aug.t[:, 0:C], rhs=wal.t[:, :], start=True, stop=True), [aug.b, wal.b], [psA.b])
                        P.act(CALL("activation", out=ee.t[0:C, :], in_=psA.t[0:C, :], func=AF.Exp, scale=-1.0), [psA.b], [ee.b])
                        P.act(CALL("activation", out=la.t[0:C, :], in_=ee.t[0:C, :], func=AF.Ln, bias=one_c[0:C, :], scale=1.0),
                              [ee.b, cst.b], [la.b])
                        P.pe(CALL("matmul", psA.t[0:C, :], lhsT=ltri_f[0:C, 0:C], rhs=la.t[0:C, :], start=True, stop=True), [la.b, cst.b], [psA.b])
                        for h in range(4):
                            P.pe(CALL("matmul", psC.t[:, h * 128:h * 128 + C], lhsT=la.t[0:C, h * 128:(h + 1) * 128], rhs=utri_f[0:C, 0:C],
                                      start=True, stop=True), [la.b, cst.b], [psC.b])
                        psC3 = psC.t[:, :].rearrange("p (h t) -> p h t", h=4)
                        P.act(CALL("activation", out=Er.t[0:C, :], in_=psA.t[0:C, :], func=AF.Exp), [psA.b], [Er.b])
                        P.act(CALL("activation", out=Eq.t[:, :, 0:C], in_=psC3[:, :, 0:C], func=AF.Exp), [psC.b], [Eq.b])
                        P.act(CALL("activation", out=Ek.t[:, :, 0:C], in_=psC3[:, :, 0:C], func=AF.Exp, scale=-1.0), [psC.b], [Ek.b])
                        P.dve(CALL("scalar_tensor_tensor", out=qt.t[:, :, 0:C], in0=qbb.t[:, :, 0:C], scalar=128.0 ** -0.5, in1=Eq.t[:, :, 0:C],
                                   op0=ALU.mult, op1=ALU.mult), [qbb.b, Eq.b], [qt.b])
                        P.dve(CALL("tensor_tensor", out=kt.t[:, :, 0:C], in0=kbb.t[:, :, 0:C], in1=Ek.t[:, :, 0:C], op=ALU.mult),
                              [kbb.b, Ek.b], [kt.b])
                        P.dve(CALL("tensor_tensor", out=kp.t[0:C, :], in0=kbt.t[0:C, :], in1=Er.t[0:C, :], op=ALU.mult), [kbt.b, Er.b], [kp.b])

                    def gla_state(ci, tok0, C):
                        sl = ci % 2
                        qt, kt, kp, vbt, sgt, Eq = qts[sl], kts[sl], kps[sl], vbts[sl], sgts[sl], Eqs[sl]
                        for h in range(4):
                            P.pe(CALL("matmul", psT_.t[0:C, h * 128:h * 128 + C], lhsT=kt.t[:, h, 0:C], rhs=qt.t[:, h, 0:C], start=True, stop=True),
                                 [kt.b, qt.b], [psT_.b])
                        psT3 = psT_.t[:, :].rearrange("p (h t) -> p h t", h=4)
                        P.dve(CALL("tensor_tensor", out=attm.t[0:C, :, 0:C], in0=psT3[0:C, :, 0:C],
                                   in1=cst.t[0:C, C_LE:C_LE + C].unsqueeze(1).broadcast_to([C, 4, C]), op=ALU.mult), [psT_.b, cst.b], [attm.b])
                        for h in range(4):
                            P.pe(CALL("matmul", psO.t[0:C, h * 256:(h + 1) * 256], lhsT=attm.t[0:C, h, 0:C], rhs=vbt.t[0:C, h * 256:(h + 1) * 256],
                                      start=True, stop=False), [attm.b, vbt.b], [psO.b])
                            P.pe(CALL("matmul", psO.t[0:C, h * 256:(h + 1) * 256], lhsT=qt.t[:, h, 0:C], rhs=Sb.t[:, h, :], start=False, stop=True),
                                 [qt.b, Sb.b], [psO.b])
                        for h in range(4):
                            P.pe(CALL("matmul", psS.t[:, h * 256:(h + 1) * 256], lhsT=kp.t[0:C, h * 128:(h + 1) * 128],
                                      rhs=vbt.t[0:C, h * 256:(h + 1) * 256], start=True, stop=True), [kp.b, vbt.b], [psS.b])
                        for h in range(4):
                            P.dve(CALL("scalar_tensor_tensor", out=Sf.t[:, h, :], in0=Sf.t[:, h, :], scalar=Eq.t[:, h, C - 1:C],
                                       in1=psS.t[:, h * 256:(h + 1) * 256], op0=ALU.mult, op1=ALU.add), [Sf.b, Eq.b, psS.b], [Sf.b])
                        P.act(CALL("activation", out=Sb.t[:, :, :], in_=Sf.t[:, :, :], func=AF.Copy), [Sf.b], [Sb.b])
                        osb = osbs[sl]
                        P.act(CALL("activation", out=osb.t[0:C, :], in_=psO.t[0:C, :], func=AF.Copy), [psO.b], [osb.b])

                    def gla_out(ci, tok0, C):
                        sl = ci % 2
                        sgt = sgts[0]
                        osb = osbs[sl]
                        P.dma("sp", CALL("dma_start", out=gbt.t[0:C, :], in_=gb_s[tok0:tok0 + C, :]), dbs("gb", tok0, C), [gbt.b])
                        for h in range(4):
                            P.act(CALL("activation", out=jk3.t[0:C, :], in_=osb.t[0:C, h * 256:(h + 1) * 256], func=AF.Square,
                                       accum_out=gst.t[0:C, h:h + 1]), [osb.b], [jk3.b, gst.b])
                        P.act(CALL("activation", out=gst.t[0:C, 4:8], in_=gst.t[0:C, 0:4], func=AF.Ln, bias=eps_c[0:C, :], scale=1.0 / 256),
                              [gst.b, cst.b], [gst.b])
                        P.act(CALL("activation", out=gst.t[0:C, 8:12], in_=gst.t[0:C, 4:8], func=AF.Exp, scale=-0.5), [gst.b], [gst.b])
                        for h in range(4):
                            P.dve(CALL("scalar_tensor_tensor", out=onr.t[0:C, h * 256:(h + 1) * 256], in0=osb.t[0:C, h * 256:(h + 1) * 256],
                                       scalar=gst.t[0:C, 8 + h:9 + h], in1=gnb.t[0:C, :], op0=ALU.mult, op1=ALU.mult),
                                  [osb.b, gst.b, gnb.b], [onr.b])
                        P.dve(CALL("tensor_tensor", out=obb.t[0:C, :], in0=onr.t[0:C, :], in1=gbt.t[0:C, :], op=ALU.mult), [onr.b, gbt.b], [obb.b])
                        for kc in range(8):
                            P.pe(CALL("transpose", psX.t[:, kc * 128:kc * 128 + C], obb.t[0:C, kc * 128:(kc + 1) * 128], ident_bf.t[0:C, 0:C]),
                                 [obb.b, ident_bf.b], [psX.b])
                        P.dve(CALL("tensor_copy", out=obT.t[:, :, 0:C], in_=psX.t[:, :].rearrange("p (k t) -> p k t", k=8)[:, :, 0:C]),
                              [psX.b], [obT.b])
                        P.dma("sp", CALL("dma_start", out=obT_s[:, :, tok0:tok0 + C].rearrange("k p t -> p k t"), in_=obT.t[:, :, 0:C]),
                              [obT.b], dbs("obT", tok0, C))

                    chunks = [(i * 128, 128, None) for i in range(NBLK)] + [(NTP + 8 * b, 8, b) for b in range(NB_S)]
                    P.pool(CALL("memset", Sf.t[:, :, :], 0.0), [], [Sf.b])
                    P.pool(CALL("memset", Sb.t[:, :, :], 0.0), [], [Sb.b])
                    gla_prep(0, chunks[0][0], chunks[0][1])
                    for ci, (tok0, C, sb_) in enumerate(chunks):
                        if ci + 1 < len(chunks):
                            gla_prep(ci + 1, chunks[ci + 1][0], chunks[ci + 1][1])
                        if sb_ is not None:
                            if sb_ == 0:
                                P.dma("sp", CALL("dma_start", out=glap_d.rearrange("h d v -> d h v"), in_=Sf.t[:, :, :]), [Sf.b], [B_gla_out])
                            P.dma("sp", CALL("dma_start", out=Sf.t[:, :, :], in_=st_d[sb_].rearrange("h d v -> d h v")), [B_gla_out], [Sf.b])
                            P.act(CALL("activation", out=Sb.t[:, :, :], in_=Sf.t[:, :, :], func=AF.Copy), [Sf.b], [Sb.b])
                        gla_state(ci, tok0, C)
                        if sb_ is not None:
                            P.dma("sp", CALL("dma_start", out=glas_d[sb_].rearrange("h d v -> d h v"), in_=Sf.t[:, :, :]), [Sf.b], [B_gla_out])
                        if ci >= 1:
                            gla_out(ci - 1, chunks[ci - 1][0], chunks[ci - 1][1])
                    gla_out(len(chunks) - 1, chunks[-1][0], chunks[-1][1])
                    P.flush()

            def layer_norm(stack_tiles, y1, T, gname, bname, out_t, stt_):
                jk, = stack_tiles
                P.act(CALL("activation", out=jk.t[0:T, :], in_=y1.t[0:T, :], func=AF.Copy, accum_out=stt_.t[0:T, 0:1]),
                      [y1.b], [jk.b, stt_.b])
                P.act(CALL("activation", out=jk.t[0:T, :], in_=y1.t[0:T, :], func=AF.Square, accum_out=stt_.t[0:T, 1:2]),
                      [y1.b, jk.b], [jk.b, stt_.b])
                P.dve(CALL("tensor_scalar", out=stt_.t[0:T, 2:3], in0=stt_.t[0:T, 0:1], scalar1=1.0 / D, scalar2=None,
                                                op0=ALU.mult), [stt_.b], [stt_.b])
                P.dve(CALL("tensor_tensor", out=stt_.t[0:T, 3:4], in0=stt_.t[0:T, 2:3], in1=stt_.t[0:T, 2:3], op=ALU.mult),
                      [stt_.b], [stt_.b])
                P.dve(CALL("scalar_tensor_tensor", out=stt_.t[0:T, 4:5], in0=stt_.t[0:T, 1:2], scalar=1.0 / D,
                                                       in1=stt_.t[0:T, 3:4], op0=ALU.mult, op1=ALU.subtract), [stt_.b], [stt_.b])
                P.act(CALL("activation", out=stt_.t[0:T, 5:6], in_=stt_.t[0:T, 4:5], func=AF.Ln, bias=eps_c[0:T, :], scale=1.0),
                      [stt_.b, cst.b], [stt_.b])
                P.act(CALL("activation", out=stt_.t[0:T, 6:7], in_=stt_.t[0:T, 5:6], func=AF.Exp, scale=-0.5), [stt_.b], [stt_.b])
                P.dve(CALL("tensor_scalar", out=out_t.t[0:T, :], in0=y1.t[0:T, :], scalar1=stt_.t[0:T, 2:3],
                                                scalar2=stt_.t[0:T, 6:7], op0=ALU.subtract, op1=ALU.mult), [y1.b, stt_.b], [out_t.b])
                P.dve(CALL("tensor_tensor", out=out_t.t[0:T, :], in0=out_t.t[0:T, :], in1=lnb[gname].t[0:T, :], op=ALU.mult),
                      [out_t.b, lnb[gname].b], [out_t.b])
                P.dve(CALL("tensor_tensor", out=out_t.t[0:T, :], in0=out_t.t[0:T, :], in1=lnb[bname].t[0:T, :], op=ALU.add),
                      [out_t.b, lnb[bname].b], [out_t.b])

            tiles4 = [(t0, 128) for t0 in range(0, NTP, 128)] + ([(NTP, NS)] if "x" not in phases else [])
            if "4" in phases:
                s4 = ExitStack()
                with s4:
                    atb = [SB(s4, "atb%d" % i, [128, 4, 128], BF16) for i in range(2)]
                    obl = [SB(s4, "obl%d" % i, [128, 8, 128], BF16) for i in range(2)]
                    gtl = [SB(s4, "gtl%d" % i, [128, 16, 128], F32) for i in range(2)]
                    xbl = [SB(s4, "xbl%d" % i, [128, D], F32) for i in range(2)]
                    sig = SB(s4, "sig", [128, 16, 128], F32)
                    t1 = SB(s4, "t1", [128, 8, 128], F32)
                    mrgs = [SB(s4, "mrg%d" % i, [128, 8, 128], BF16) for i in range(2)]
                    y1 = SB(s4, "y1", [128, D], F32)
                    hh = SB(s4, "hh", [128, D], F32)
                    hb = SB(s4, "hb", [128, D], BF16)
                    hTt = SB(s4, "hTt", [128, 8, 128], BF16)
                    jk4 = SB(s4, "jk4", [128, D], BF16)
                    st4 = SB(s4, "st4", [128, 8], F32)
                    psa = PSB(s4, "psa", [128, 1024])
                    psb4 = PSB(s4, "psb4", [128, 1024])
                    psm = PSB(s4, "psm", [128, 1024])
                    psx = PSB(s4, "psx4", [128, 1024], BF16)

                    def load4(ti):
                        t0, T = tiles4[ti]
                        sl = ti % 2
                        P.dma("sp", CALL("dma_start", out=atb[sl].t[:, :, 0:T], in_=attnT_s[:, :, t0:t0 + T].rearrange("k p t -> p k t")),
                              dbs("attnT", t0, T), [atb[sl].b])
                        P.dma("sp", CALL("dma_start", out=obl[sl].t[:, :, 0:T], in_=obT_s[:, :, t0:t0 + T].rearrange("k p t -> p k t")),
                              dbs("obT", t0, T), [obl[sl].b])
                        P.dma("sp", CALL("dma_start", out=gtl[sl].t[:, :, 0:T], in_=gtT_s[:, :, t0:t0 + T].rearrange("k p t -> p k t")),
                              dbs("gtT", t0, T), [gtl[sl].b])
                        P.dma("sp", CALL("dma_start", out=xbl[sl].t[0:T, :], in_=x_d[t0:t0 + T, :]), [], [xbl[sl].b])

                    def s1_4a(ti):
                        t0, T = tiles4[ti]
                        sl = ti % 2
                        at_, ob_, gt_ = atb[sl], obl[sl], gtl[sl]
                        mrg = mrgs[sl]
                        psa3 = psa.t[:, :].rearrange("p (c t) -> p c t", c=8)
                        psb3 = psb4.t[:, :].rearrange("p (c t) -> p c t", c=8)
                        for c in range(8):
                            for kc in range(4):
                                P.pe(CALL("matmul", psa.t[:, c * 128:c * 128 + T], lhsT=wao.t[:, kc, c * 128:(c + 1) * 128],
                                          rhs=at_.t[:, kc, 0:T], start=(kc == 0), stop=(kc == 3)), [wao.b, at_.b], [psa.b])
                        for c in range(8):
                            for kc in range(8):
                                P.pe(CALL("matmul", psb4.t[:, c * 128:c * 128 + T], lhsT=wgo.t[:, kc, c * 128:(c + 1) * 128],
                                          rhs=ob_.t[:, kc, 0:T], start=(kc == 0), stop=(kc == 7)), [wgo.b, ob_.b], [psb4.b])
                        P.dve(CALL("tensor_tensor", out=t1.t[:, :, 0:T], in0=psa3[:, :, 0:T], in1=gt_.t[:, 0:8, 0:T], op=ALU.mult),
                              [psa.b, gt_.b], [t1.b])
                        P.dve(CALL("tensor_tensor", out=sig.t[:, 8:16, 0:T], in0=psb3[:, :, 0:T], in1=gt_.t[:, 8:16, 0:T], op=ALU.mult),
                              [psb4.b, gt_.b], [sig.b])
                        P.dve(CALL("tensor_tensor", out=mrg.t[:, :, 0:T], in0=t1.t[:, :, 0:T], in1=sig.t[:, 8:16, 0:T], op=ALU.add),
                              [t1.b, sig.b], [mrg.b])

                    def s2_4a(ti):
                        t0, T = tiles4[ti]
                        sl = ti % 2
                        xb_ = xbl[sl]
                        mrg = mrgs[sl]
                        for n in range(2):
                            for kc in range(8):
                                P.pe(CALL("matmul", psm.t[0:T, n * 512:(n + 1) * 512], lhsT=mrg.t[:, kc, 0:T],
                                          rhs=wo.t[:, kc, n * 512:(n + 1) * 512], start=(kc == 0), stop=(kc == 7)), [mrg.b, wo.b], [psm.b])
                        for n in range(2):
                            P.dve(CALL("scalar_tensor_tensor", out=y1.t[0:T, n * 512:(n + 1) * 512], in0=xb_.t[0:T, n * 512:(n + 1) * 512],
                                       scalar=ALPHA, in1=psm.t[0:T, n * 512:(n + 1) * 512], op0=ALU.mult, op1=ALU.add),
                                  [xb_.b, psm.b], [y1.b])
                        layer_norm((jk4,), y1, T, "ln1_g", "ln1_b", hh, st4)
                        P.dma("sp", CALL("dma_start", out=h_s[t0:t0 + T, :], in_=hh.t[0:T, :]), [hh.b], dbs("h", t0, T))
                        P.act(CALL("activation", out=hb.t[0:T, :], in_=hh.t[0:T, :], func=AF.Copy), [hh.b], [hb.b])
                        for kc in range(8):
                            P.pe(CALL("transpose", psx.t[:, kc * 128:kc * 128 + T], hb.t[0:T, kc * 128:(kc + 1) * 128],
                                      ident_bf.t[0:T, 0:T]), [hb.b, ident_bf.b], [psx.b])
                        P.dve(CALL("tensor_copy", out=hTt.t[:, :, 0:T], in_=psx.t[:, :].rearrange("p (k t) -> p k t", k=8)[:, :, 0:T]),
                              [psx.b], [hTt.b])
                        P.dma("sp", CALL("dma_start", out=hT_s[:, :, t0:t0 + T].rearrange("k p t -> p k t"), in_=hTt.t[:, :, 0:T]),
                              [hTt.b], dbs("hT", t0, T))

                    load4(0)
                    s1_4a(0)
                    for ti in range(len(tiles4)):
                        if ti + 1 < len(tiles4):
                            load4(ti + 1)
                            s1_4a(ti + 1)
                        s2_4a(ti)
                    P.flush()

        if "4" in phases:
            s5 = ExitStack()
            with s5:
                wf2 = SB(s5, "wf2", [128, 32, D], BF16)
                wf2b = [Buf("wf2_%d" % i) for i in range(8)]
                for q in range(8):
                    P.dma("pool", CALL("dma_start", out=wf2.t[:, 4 * q:4 * q + 4, :],
                                                             in_=wf2_d[q * 512:(q + 1) * 512, :].rearrange("(kc p) n -> p kc n", p=128)),
                          [], [wf2b[q]])
                SW5 = 256
                hTl = [SB(s5, "hTl%d" % i, [128, 8, SW5], BF16) for i in range(2)]
                hl = [SB(s5, "hl%d" % i, [128, D], F32) for i in range(2)]
                rl = [SB(s5, "rl%d" % i, [128, 2, SW5], BF16) for i in range(2)]
                hid = SB(s5, "hid", [128, 32, SW5], BF16)
                y2 = SB(s5, "y2", [128, D], F32)
                yo = SB(s5, "yo", [128, D], F32)
                jk5 = SB(s5, "jk5", [128, D], BF16)
                st5 = SB(s5, "st5", [128, 8], F32)
                psf = [PSB(s5, "psf%d" % i, [128, 512]) for i in range(3)]
                psy = PSB(s5, "psy", [128, 1024])
                sup5 = [(t0, min(SW5, NTP - t0)) for t0 in range(0, NTP, SW5)] + (
                    [(NTP, NS)] if "x" not in phases else [])

                def load5T(ui):
                    t0, W = sup5[ui]
                    sl = ui % 2
                    P.dma("sp", CALL("dma_start", out=hTl[sl].t[:, :, 0:W], in_=hT_s[:, :, t0:t0 + W].rearrange("k p t -> p k t")),
                          dbs("hT", t0, W), [hTl[sl].b])

                fi_ = [0]
                hi_ = [0]

                def s1_4b(ui):
                    t0, W = sup5[ui]
                    hT_ = hTl[ui % 2]
                    for f2 in range(16):
                        pf = psf[fi_[0] % 3]
                        r_ = rl[fi_[0] % 2]
                        fi_[0] += 1
                        for cc in range(2):
                            f = f2 * 2 + cc
                            for kc in range(8):
                                P.pe(CALL("matmul", pf.t[:, cc * SW5:cc * SW5 + W], lhsT=wf1.t[:, kc, f * 128:(f + 1) * 128],
                                          rhs=hT_.t[:, kc, 0:W], start=(kc == 0), stop=(kc == 7)), [wf1b[kc], hT_.b], [pf.b])
                        pf3 = pf.t[:, :].rearrange("p (c t) -> p c t", c=2)
                        P.act(CALL("activation", out=r_.t[:, :, 0:W], in_=pf3[:, :, 0:W], func=AF.Relu), [pf.b], [r_.b])
                        P.dve(CALL("tensor_tensor", out=hid.t[:, f2 * 2:f2 * 2 + 2, 0:W], in0=r_.t[:, :, 0:W], in1=r_.t[:, :, 0:W],
                                   op=ALU.mult), [r_.b], [hid.b])

                def s2_4b(t0, T, c0):
                    h_ = hl[hi_[0] % 2]
                    hi_[0] += 1
                    P.dma("sp", CALL("dma_start", out=h_.t[0:T, :], in_=h_s[t0:t0 + T, :]), dbs("h", t0, T), [h_.b])
                    for n in range(2):
                        for kc in range(32):
                            P.pe(CALL("matmul", psy.t[0:T, n * 512:(n + 1) * 512], lhsT=hid.t[:, kc, c0:c0 + T],
                                      rhs=wf2.t[:, kc, n * 512:(n + 1) * 512], start=(kc == 0), stop=(kc == 31)),
                                 [hid.b, wf2b[kc // 4]], [psy.b])
                    for n in range(2):
                        P.dve(CALL("scalar_tensor_tensor", out=y2.t[0:T, n * 512:(n + 1) * 512], in0=h_.t[0:T, n * 512:(n + 1) * 512],
                                   scalar=ALPHA, in1=psy.t[0:T, n * 512:(n + 1) * 512], op0=ALU.mult, op1=ALU.add),
                              [h_.b, psy.b], [y2.b])
                    layer_norm((jk5,), y2, T, "ln2_g", "ln2_b", yo, st5)
                    P.dma("sp", CALL("dma_start", out=y_d[t0:t0 + T, :], in_=yo.t[0:T, :]), [yo.b], dbs("y", t0, T))

                load5T(0)
                for ui, (u0, W) in enumerate(sup5):
                    if ui + 1 < len(sup5):
                        load5T(ui + 1)
                    s1_4b(ui)
                    for c0 in range(0, W, 128):
                        s2_4b(u0 + c0, min(128, W - c0), c0)
                P.flush()
        P.flush(final=True)
    return nc


_NC_CACHE = {}


def make_in_maps(inp, n_cores, NTP, NPOOL):
    cst = make_consts()
    ck = np.ascontiguousarray(inp["cache_k"]).reshape(NPOOL * 8, 2048)
    cv = np.ascontiguousarray(inp["cache_v"]).reshape(NPOOL * 8, 2048)
    cki = np.ascontiguousarray(inp["cache_kidx"]).reshape(NPOOL * 4, 2048)
    maps = []
    for c in range(n_cores):
        xs = np.asarray(inp["x_sample"][NB_S * c:NB_S * (c + 1)]).reshape(NS, D)
        x = np.concatenate([np.asarray(inp["x_prompt"][c]), xs], axis=0).astype(np.float32)
        m = {
            "x": np.ascontiguousarray(x),
            "xT": np.ascontiguousarray(x.T),
            "cache_k": ck, "cache_v": cv, "cache_kidx": cki,
            "state_gla": np.ascontiguousarray(inp["state_gla"][0, NB_S * c:NB_S * (c + 1)]),
            "page_table": np.ascontiguousarray(inp["page_table"][NB_S * c:NB_S * (c + 1)]).astype(np.int32),
            "w_in": np.ascontiguousarray(inp["w_in"][0]),
            "w_alpha2": np.ascontiguousarray(inp["w_alpha2"][0]),
            "b_alpha": np.ascontiguousarray(inp["b_alpha"][0]).reshape(1, 512),
            "gla_norm_g": np.ascontiguousarray(inp["gla_norm_g"][0]).reshape(1, 256),
            "w_attn_o": np.ascontiguousarray(inp["w_attn_o"][0]),
            "w_gla_o": np.ascontiguousarray(inp["w_gla_o"][0]),
            "w_out": np.ascontiguousarray(inp["w_out"][0]),
            "ln1_g": np.ascontiguousarray(inp["ln1_g"][0]).reshape(1, D),
            "ln1_b": np.ascontiguousarray(inp["ln1_b"][0]).reshape(1, D),
            "ln2_g": np.ascontiguousarray(inp["ln2_g"][0]).reshape(1, D),
            "ln2_b": np.ascontiguousarray(inp["ln2_b"][0]).reshape(1, D),
            "w_ff1": np.ascontiguousarray(inp["w_ff1"][0]),
            "w_ff2": np.ascontiguousarray(inp["w_ff2"][0]),
            "cst": cst,
        }
        maps.append(m)
    return maps


def assemble(res, n_cores, NTP):
    f = np.float32
    y = np.stack([r["y"][:NTP] for r in res]).astype(f)
    ys = np.concatenate([r["y"][NTP:].reshape(NB_S, 8, D) for r in res]).astype(f)
    kp = np.stack([r["ko"][:NTP].reshape(NTP, 2, 64) for r in res])[None].astype(f)
    vp = np.stack([r["vo"][:NTP].reshape(NTP, 2, 64) for r in res])[None].astype(f)
    kip = np.stack([r["kio"][:NTP] for r in res])[None].astype(f)
    gp = np.stack([r["gla_p"] for r in res])[None].astype(f)
    ks = np.concatenate([r["ko"][NTP:].reshape(NB_S, 8, 2, 64) for r in res])[None].astype(f)
    vs = np.concatenate([r["vo"][NTP:].reshape(NB_S, 8, 2, 64) for r in res])[None].astype(f)
    kis = np.concatenate([r["kio"][NTP:].reshape(NB_S, 8, 64) for r in res])[None].astype(f)
    gs = np.concatenate([r["gla_s"] for r in res])[None].astype(f)
    return (y, ys, kp, vp, kip, gp, ks, vs, kis, gs)


def kernel(**inputs):
    n_cores = 8
    NTP = inputs["x_prompt"].shape[1]
    NPOOL = inputs["cache_k"].shape[1]
    nc = build(NTP=NTP, NPOOL=NPOOL)
    maps = make_in_maps(inputs, n_cores, NTP, NPOOL)
    out = run_bass_kernel_spmd(nc, maps, core_ids=list(range(n_cores)))
    return assemble(out.results, n_cores, NTP)
```

```python
from contextlib import ExitStack
import numpy as np
import concourse.bass as bass
import concourse.mybir as mybir
from concourse.bass_utils import run_bass_kernel_spmd

F32 = mybir.dt.float32
BF16 = mybir.dt.bfloat16
I32 = mybir.dt.int32
AF = mybir.ActivationFunctionType
ALU = mybir.AluOpType
AX = mybir.AxisListType

D = 1024
D_IN = 6488
NS = 32
NB_S = 4
NPAGES = 128
ROUNDS = 15
ALPHA = 2.0 ** 0.25
EPS = 1e-5
NEG = -30000.0
BIG = 1.0e30


class Buf:
    __slots__ = ("name", "writer", "readers", "excl")

    def __init__(self, name, excl=False):
        self.name = name
        self.writer = None
        self.readers = []
        self.excl = excl


class Op:
    __slots__ = ("eng", "fn", "deps", "signal", "sem", "val", "is_dma", "lane")

    def __init__(self, eng, fn, is_dma):
        self.eng = eng
        self.fn = fn
        self.deps = []
        self.signal = False
        self.sem = None
        self.val = 0
        self.is_dma = is_dma
        self.lane = None


ENGS = ("pe", "act", "dve", "pool", "sp")
EPOCH = 12000


class Prog:
    def __init__(self, nc, lanes=8):
        self.nc = nc
        self.ops = []
        self.sems = {}
        self.cnt = {e: 0 for e in ENGS}
        self.waited = {e: {} for e in ENGS}
        self.lanes = {}
        self.lane_rr = {}
        for q in ("sp", "pool", "act"):
            self.lanes[q] = [[nc.alloc_semaphore("ln_%s_%d" % (q, i)), 0] for i in range(lanes)]
            self.lane_rr[q] = 0
        self.last_sig = {e: None for e in ENGS}
        self.all_dma = []
        self.fence_deps = []
        self.n_ops = 0
        self.capture = None

    def cap(self, f, *args):
        self.capture = []
        f(*args)
        lst = self.capture
        self.capture = None
        return lst

    def replay_rr(self, lists):
        idx = [0] * len(lists)
        left = sum(len(l) for l in lists)
        while left:
            for k, l in enumerate(lists):
                if idx[k] < len(l):
                    eng, fn, reads, writes, is_dma = l[idx[k]]
                    idx[k] += 1
                    left -= 1
                    self._add(eng, fn, reads, writes, is_dma)

    def _add(self, eng, fn, reads, writes, is_dma=False):
        if self.capture is not None:
            self.capture.append((eng, fn, reads, writes, is_dma))
            return None
        op = Op(eng, fn, is_dma)
        deps = set()
        for b in reads:
            if b.writer is not None:
                deps.add(b.writer)
            if b.excl:
                for r in b.readers:
                    if r.eng != eng:
                        deps.add(r)
        for b in writes:
            if b.writer is not None:
                deps.add(b.writer)
            for r in b.readers:
                deps.add(r)
        op.deps = list(deps)
        for b in reads:
            b.readers.append(op)
        for b in writes:
            b.writer = op
            b.readers = []
        self.ops.append(op)
        return op

    def pe(self, fn, reads=(), writes=()):
        return self._add("pe", fn, reads, writes)

    def act(self, fn, reads=(), writes=()):
        return self._add("act", fn, reads, writes)

    def dve(self, fn, reads=(), writes=()):
        return self._add("dve", fn, reads, writes)

    def pool(self, fn, reads=(), writes=()):
        return self._add("pool", fn, reads, writes)

    def dma(self, q, fn, reads=(), writes=()):
        import os
        if q in os.environ.get("SKIPDMA", "").split(","):
            return None
        return self._add(q, fn, reads, writes, is_dma=True)

    def _sem_for(self, eng):
        ep = self.cnt[eng] // EPOCH
        key = (eng, ep)
        if key not in self.sems:
            self.sems[key] = self.nc.alloc_semaphore("s_%s_%d" % (eng, ep))
        return self.sems[key], ep

    def flush(self, final=False):
        nc = self.nc
        ops = self.ops
        self.ops = []
        if not ops and not final:
            return
        self.n_ops += len(ops)
        needed = set()
        for op in ops:
            for d in op.deps:
                if d.is_dma:
                    continue
                if d.eng == "pe" and op.eng == "pe" and not op.is_dma:
                    continue
                needed.add(d)
        last = {}
        for op in ops:
            if not op.is_dma:
                last[op.eng] = op
        for op in last.values():
            needed.add(op)
        for op in ops:
            if op.is_dma:
                lanes = self.lanes[op.eng]
                li = self.lane_rr[op.eng]
                self.lane_rr[op.eng] = (li + 1) % len(lanes)
                lane = lanes[li]
                op.lane = (lane[0], lane[1])
                lane[1] += 16
                op.sem = lane[0]
                op.val = lane[1]
                self.all_dma.append(op)
            elif op in needed and op.sem is None:
                sem, ep = self._sem_for(op.eng)
                self.cnt[op.eng] += 1
                op.sem = sem
                op.val = self.cnt[op.eng] - ep * EPOCH
                op.signal = True
                self.last_sig[op.eng] = op
        streams = {e: [] for e in ENGS}
        for op in ops:
            streams[op.eng].append(op)
        fence = self.fence_deps

        def emit_stream(eng_name, e):
            waited = self.waited[eng_name]

            def wait(sem, val):
                if waited.get(sem.name, 0) >= val:
                    return
                waited[sem.name] = val
                e.wait_ge(sem, val)

            first = True
            for op in streams[eng_name]:
                if first:
                    for d in fence:
                        if d.sem is not None:
                            wait(d.sem, d.val)
                    first = False
                for d in op.deps:
                    if d.sem is None:
                        continue
                    if (not d.is_dma) and d.eng == "pe" and eng_name == "pe" and not op.is_dma:
                        continue
                    wait(d.sem, d.val)
                if op.is_dma:
                    if op.lane[1] > 0:
                        wait(op.lane[0], op.lane[1])
                    ins = op.fn(e)
                    ins.then_inc(op.sem, 16)
                else:
                    ins = op.fn(e)
                    if op.signal:
                        ins.then_inc(op.sem, 1)
            if final and eng_name == "sp":
                for q in self.lanes:
                    for sem, tot in self.lanes[q]:
                        if tot > 0:
                            wait(sem, tot)

        with nc.Block() as block:
            @block.tensor
            def _(e):
                emit_stream("pe", e)

            @block.scalar
            def _(e):
                emit_stream("act", e)

            @block.vector
            def _(e):
                emit_stream("dve", e)

            @block.gpsimd
            def _(e):
                emit_stream("pool", e)

            @block.sync
            def _(e):
                emit_stream("sp", e)
        fd = [op for op in self.last_sig.values() if op is not None]
        latest = {}
        for op in self.all_dma:
            latest[op.sem.name] = op
        fd += list(latest.values())
        self.fence_deps = fd
        self.all_dma = list(latest.values())


def CALL(name, *a, **k):
    return lambda e: getattr(e, name)(*a, **k)


class TT:
    __slots__ = ("t", "b")

    def __init__(self, t, name, excl=False):
        self.t = t
        self.b = Buf(name, excl)


C_ID, C_UT, C_LT, C_LE, C_CN, C_G, C_PW, C_EPS, C_ONE = 0, 128, 256, 384, 512, 640, 768, 800, 801
NCST = 832


def make_consts():
    c = np.zeros((128, NCST), np.float32)
    p = np.arange(128)[:, None]
    j = np.arange(128)[None, :]
    c[:, C_ID:C_ID + 128] = (p == j)
    c[:, C_UT:C_UT + 128] = np.where(p <= j, -1.0 / 16, 0.0)
    c[:, C_LT:C_LT + 128] = np.where(p > j, -1.0 / 16, 0.0)
    c[:, C_LE:C_LE + 128] = (p <= j)
    c[:, C_CN:C_CN + 128] = np.where(j > p, -BIG, 0.0)
    c[:, C_G:C_G + 128] = ((p // 32 == j // 32) & (p % 8 == j % 8))
    c[:, C_PW:C_PW + 32] = 2.0 ** -(np.arange(32)[None, :] + 1.0)
    c[:, C_EPS] = EPS
    c[:, C_ONE] = 1.0
    return c


O_QA, O_KA, O_VA, O_QI, O_KI, O_WI, O_QB, O_KB, O_VB, O_GB, O_AB, O_GA = (
    0, 512, 640, 768, 1280, 1344, 1352, 1864, 2376, 3400, 4424, 4440)


def w_layout():
    fm, tm = [], []
    off = 0

    def add(lst, name, pieces):
        nonlocal off
        w = sum(b - a for a, b in pieces)
        lst.append((name, off, w, pieces))
        off += w

    for j in range(4):
        add(fm, "qa%d" % j, [(O_QA + 64 * j, O_QA + 64 * j + 64), (O_QA + 64 * (4 + j), O_QA + 64 * (4 + j) + 64)])
    add(fm, "ka", [(O_KA, O_KA + 128)])
    for j in range(4):
        add(fm, "qi%d" % j, [(O_QI + 128 * j, O_QI + 128 * j + 128)])
    add(fm, "ki", [(O_KI, O_KI + 64), (O_KI, O_KI + 64)])
    for j in range(4):
        add(fm, "qb%d" % j, [(O_QB + 128 * j, O_QB + 128 * j + 128)])
    for j in range(4):
        add(fm, "kb%d" % j, [(O_KB + 128 * j, O_KB + 128 * j + 128)])
    add(fm, "ab", [(O_AB, O_AB + 16)])
    for j in range(16):
        add(fm, "gt%d" % j, [(O_GA + 128 * j, O_GA + 128 * j + 128)])
    add(tm, "ta", [(O_KA, O_KA + 256), (O_KI, O_KI + 72)])
    add(tm, "tkb", [(O_KB, O_KB + 512)])
    add(tm, "tvb0", [(O_VB, O_VB + 512)])
    add(tm, "tvb1", [(O_VB + 512, O_VB + 1024)])
    add(tm, "tgb0", [(O_GB, O_GB + 512)])
    add(tm, "tgb1", [(O_GB + 512, O_GB + 1024)])
    return fm, tm, off


def build(NTP=4096, NPOOL=5120, phases="12345", dbg=False):
    nc = bass.Bass("TRN2", target_bir_lowering=False)
    P = Prog(nc)
    NT = NTP + NS
    NBLK = NTP // 128
    TOPK = min(256, NTP // 4)
    TOPK_S = min(256, (NPAGES * 128 + 8) // 4)

    def din(name, shape, dt=F32):
        return nc.dram_tensor(name, list(shape), dt, kind="ExternalInput")

    def dout(name, shape, dt=F32):
        return nc.dram_tensor(name, list(shape), dt, kind="ExternalOutput")

    def dscr(name, shape, dt):
        return nc.dram_tensor(name, list(shape), dt, kind="Internal")

    xT_d = din("xT", [D, NT])
    x_d = din("x", [NT, D])
    ck_d = din("cache_k", [NPOOL * 8, 2048])
    cv_d = din("cache_v", [NPOOL * 8, 2048])
    cki_d = din("cache_kidx", [NPOOL * 4, 2048])
    st_d = din("state_gla", [NB_S, 4, 128, 256])
    pt_d = din("page_table", [NB_S, NPAGES], I32)
    w_in_d = din("w_in", [D, D_IN])
    wal_d = din("w_alpha2", [16, 512])
    bal_d = din("b_alpha", [1, 512])
    gng_d = din("gla_norm_g", [1, 256])
    wao_d = din("w_attn_o", [512, D])
    wgo_d = din("w_gla_o", [D, D])
    wo_d = din("w_out", [D, D])
    ln_d = {n: din(n, [1, D]) for n in ("ln1_g", "ln1_b", "ln2_g", "ln2_b")}
    wf1_d = din("w_ff1", [D, 4096])
    wf2_d = din("w_ff2", [4096, D])
    cst_d = din("cst", [128, NCST])

    y_d = dout("y", [NT, D])
    ko_d = dout("ko", [NT, 128])
    vo_d = dout("vo", [NT, 128])
    kio_d = dout("kio", [NT, 64])
    glap_d = dout("gla_p", [4, 128, 256])
    glas_d = dout("gla_s", [NB_S, 4, 128, 256])

    qaT_s = dscr("qaT_s", [4, 128, NT], BF16)
    qiT_s = dscr("qiT_s", [4, 128, NT], BF16)
    wi_s = dscr("wi_s", [NT, 8], F32)
    qbT_s = dscr("qbT_s", [4, 128, NT], BF16)
    kbT_s = dscr("kbT_s", [4, 128, NT], BF16)
    abT_s = dscr("abT_s", [16, NT], BF16)
    gtT_s = dscr("gtT_s", [16, 128, NT], F32)
    kb_s = dscr("kb_s", [NT, 512], BF16)
    vb_s = dscr("vb_s", [NT, 1024], BF16)
    gb_s = dscr("gb_s", [NT, 1024], F32)
    attnT_s = dscr("attnT_s", [4, 128, NT], BF16)
    obT_s = dscr("obT_s", [8, 128, NT], BF16)
    hT_s = dscr("hT_s", [8, 128, NT], BF16)
    h_s = dscr("h_s", [NT, D], F32)

    DBD = {}
    NJ = {"qaT": 4, "qiT": 4, "wi": 1, "qbT": 4, "kbT": 4, "abT": 1, "gtT": 16, "kb": 1, "vb": 2, "gb": 2, "attnT": 4,
          "obT": 1, "hT": 1, "h": 1, "ko": 1, "vo": 1, "kio": 1, "y": 1}
    B_gla_out = Buf("gla_out")

    def dbs(name, tok0, n, j=None):
        js = range(NJ[name]) if j is None else [j]
        out = []
        for jj in js:
            for tl in range(tok0 // 128, (tok0 + n - 1) // 128 + 1):
                key = (name, jj, tl)
                if key not in DBD:
                    DBD[key] = Buf("%s_%d_%d" % key)
                out.append(DBD[key])
        return out

    g = ExitStack()

    def SB(stack, name, shape, dt):
        return TT(stack.enter_context(nc.sbuf_tensor("sb_" + name, list(shape), dt)), name)

    def PSB(stack, name, shape, dt=F32):
        return TT(stack.enter_context(nc.psum_tensor("pp_" + name, list(shape), dt)), name, True)

    with g:
        cst = SB(g, "cst", [128, NCST], F32)
        ident_bf = SB(g, "ident_bf", [128, 128], BF16)
        i4_bf = SB(g, "i4_bf", [128, 4, 128], BF16)
        mle_bf = SB(g, "mle_bf", [128, 128], BF16)
        ones_bf = SB(g, "ones_bf", [128, 128], BF16)
        P.dma("sp", CALL("dma_start", out=cst.t[:], in_=cst_d[:, :]), [], [cst.b])
        P.dve(CALL("tensor_copy", out=ident_bf.t[:], in_=cst.t[:, C_ID:C_ID + 128]), [cst.b], [ident_bf.b])
        for k in range(4):
            P.dve(CALL("tensor_copy", out=i4_bf.t[:, k, :], in_=cst.t[:, C_ID:C_ID + 128]), [cst.b], [i4_bf.b])
        P.dve(CALL("tensor_copy", out=mle_bf.t[:], in_=cst.t[:, C_LE:C_LE + 128]), [cst.b], [mle_bf.b])
        P.pool(CALL("memset", ones_bf.t[:], 1.0), [], [ones_bf.b])
        ident_f = cst.t[:, C_ID:C_ID + 128]
        utri_f = cst.t[:, C_UT:C_UT + 128]
        ltri_f = cst.t[:, C_LT:C_LT + 128]
        cneg_f = cst.t[:, C_CN:C_CN + 128]
        eps_c = cst.t[:, C_EPS:C_EPS + 1]
        one_c = cst.t[:, C_ONE:C_ONE + 1]

        lnb = {n: SB(g, "lnb_" + n, [128, D], F32) for n in ln_d}
        s12 = ExitStack()
        with s12:
            kA = SB(s12, "kA", [128, NT], BF16)
            kB = SB(s12, "kB", [128, NT], BF16)
            kiA = SB(s12, "kiA", [128, NT], BF16)
            kiB = SB(s12, "kiB", [128, NT], BF16)
            vaug = SB(s12, "vaug", [128, NBLK, 2, 65], BF16)
            for tt_ in (kA, kB, kiA, kiB):
                P.pool(CALL("memset", tt_.t[:], 0.0), [], [tt_.b])
            P.pool(CALL("memset", vaug.t[:], 1.0), [], [vaug.b])

            if "1" in phases:
                s1 = ExitStack()
                with s1:
                    fm, tm, ncol = w_layout()
                    w_sb = SB(s1, "w_sb", [128, 8, ncol], BF16)
                    w_src = w_in_d.rearrange("(kc p) n -> p kc n", p=128)
                    wb = {}
                    for lst in (fm, tm):
                        for (name, off, wd, pieces) in lst:
                            b = Buf("w_" + name)
                            wb[name] = b
                            o = off
                            for (a0, a1) in pieces:
                                P.dma("pool", CALL("dma_start",
                                    out=w_sb.t[:, :, o:o + (a1 - a0)], in_=w_src[:, :, a0:a1]), [], [b])
                                o += a1 - a0
                    import os as _os
                    NXS = int(_os.environ.get("XSLOTS", "2"))
                    xts = [SB(s1, "xT%d" % i, [128, 8, 512], BF16) for i in range(NXS)]
                    stg_b = [SB(s1, "stgb%d" % i, [128, 512], BF16) for i in range(4)]
                    stg_f = [SB(s1, "stgf%d" % i, [128, 512], F32) for i in range(4)]
                    pss = [PSB(s1, "ps1_%d" % i, [128, 512]) for i in range(6)]
                    rr = {"ps": 0, "sb": 0, "sf": 0, "ev": 0}
                    xT_src = xT_d.rearrange("(kc p) t -> p kc t", p=128)

                    def nxt(key, n):
                        v = rr[key]
                        rr[key] = (v + 1) % n
                        return v

                    def evac(out_ap, in_ap, reads, writes):
                        if nxt("ev", 2) == 0:
                            P.act(CALL("activation", out=out_ap, in_=in_ap, func=AF.Copy), reads, writes)
                        else:
                            P.dve(CALL("tensor_copy", out=out_ap, in_=in_ap), reads, writes)

                    STW = int(_os.environ.get("STW", "512"))
                    sts = [(t0, min(STW, NTP - t0)) for t0 in range(0, NTP, STW)] + [(NTP, NS)]

                    def load_x(si):
                        t0, W = sts[si]
                        xt = xts[si % NXS]
                        P.dma("pool", CALL("dma_start", out=xt.t[:, :, 0:W], in_=xT_src[:, :, t0:t0 + W]), [], [xt.b])

                    load_x(0)
                    for si, (t0, W) in enumerate(sts):
                        if si + 1 < len(sts):
                            load_x(si + 1)
                        xt = xts[si % NXS]
                        for (name, off, m, pieces) in fm:
                            ps = pss[nxt("ps", 6)]
                            for kc in range(8):
                                P.pe(CALL("matmul",
                                    ps.t[0:m, 0:W], lhsT=w_sb.t[:, kc, off:off + m], rhs=xt.t[:, kc, 0:W],
                                    start=(kc == 0), stop=(kc == 7)), [xt.b, wb[name]], [ps.b])
                            if name == "ka":
                                evac(kA.t[0:64, t0:t0 + W], ps.t[0:64, 0:W], [ps.b], [kA.b])
                                evac(kB.t[64:128, t0:t0 + W], ps.t[64:128, 0:W], [ps.b], [kB.b])
                            elif name == "ki":
                                evac(kiA.t[0:64, t0:t0 + W], ps.t[0:64, 0:W], [ps.b], [kiA.b])
                                evac(kiB.t[64:128, t0:t0 + W], ps.t[64:128, 0:W], [ps.b], [kiB.b])
                            else:
                                kind = name[:2]
                                j = int(name[2:]) if len(name) > 2 else 0
                                if kind == "gt":
                                    sg = stg_f[nxt("sf", 4)]
                                    dst = gtT_s[j, :, t0:t0 + W]
                                    dbn = "gtT"
                                else:
                                    sg = stg_b[nxt("sb", 4)]
                                    dst = {"qa": qaT_s, "qi": qiT_s, "qb": qbT_s, "kb": kbT_s}[kind][j, :, t0:t0 + W] \
                                        if kind != "ab" else abT_s[:, t0:t0 + W]
                                    dbn = {"qa": "qaT", "qi": "qiT", "qb": "qbT", "kb": "kbT", "ab": "abT"}[kind]
                                if kind == "gt":
                                    P.act(CALL("activation", out=sg.t[0:m, 0:W], in_=ps.t[0:m, 0:W], func=AF.Sigmoid), [ps.b], [sg.b])
                                else:
                                    evac(sg.t[0:m, 0:W], ps.t[0:m, 0:W], [ps.b], [sg.b])
                                P.dma("sp", CALL("dma_start", out=dst, in_=sg.t[0:m, 0:W]),
                                      [sg.b], dbs(dbn, t0, W, j if NJ[dbn] > 1 else 0))
                        ntile = (W + 127) // 128
                        for tt_i in range(ntile):
                            T = min(128, W - tt_i * 128)
                            tk0 = t0 + tt_i * 128
                            for (name, off, n, pieces) in tm:
                                ps = pss[nxt("ps", 6)]
                                for kc in range(8):
                                    P.pe(CALL("matmul",
                                        ps.t[0:T, 0:n], lhsT=xt.t[:, kc, tt_i * 128:tt_i * 128 + T],
                                        rhs=w_sb.t[:, kc, off:off + n], start=(kc == 0), stop=(kc == 7)),
                                        [xt.b, wb[name]], [ps.b])
                                if name == "ta":
                                    sg = stg_f[nxt("sf", 4)]
                                    evac(sg.t[0:T, 0:n], ps.t[0:T, 0:n], [ps.b], [sg.b])
                                    for (dst, c0, c1, dbn) in ((ko_d, 0, 128, "ko"), (vo_d, 128, 256, "vo"),
                                                               (kio_d, 256, 320, "kio"), (wi_s, 320, 328, "wi")):
                                        P.dma("sp", CALL("dma_start",
                                            out=dst[tk0:tk0 + T, :], in_=sg.t[0:T, c0:c1]), [sg.b], dbs(dbn, tk0, T))
                                    if tk0 < NTP:
                                        blk = tk0 // 128
                                        P.dve(CALL("tensor_copy",
                                            out=vaug.t[:, blk, :, 1:65],
                                            in_=ps.t[:, 128:256].rearrange("p (h d) -> p h d", h=2)), [ps.b], [vaug.b])
                                else:
                                    if name.startswith("tgb"):
                                        sg = stg_f[nxt("sf", 4)]
                                        dst, dbn = gb_s, "gb"
                                    else:
                                        sg = stg_b[nxt("sb", 4)]
                                        dst, dbn = (kb_s, "kb") if name == "tkb" else (vb_s, "vb")
                                    c0 = 512 if name.endswith("1") else 0
                                    if name.startswith("tgb"):
                                        P.act(CALL("activation", out=sg.t[0:T, 0:n], in_=ps.t[0:T, 0:n], func=AF.Sigmoid), [ps.b], [sg.b])
                                        P.dve(CALL("tensor_tensor", out=sg.t[0:T, 0:n], in0=sg.t[0:T, 0:n], in1=ps.t[0:T, 0:n], op=ALU.mult),
                                              [sg.b, ps.b], [sg.b])
                                    else:
                                        evac(sg.t[0:T, 0:n], ps.t[0:T, 0:n], [ps.b], [sg.b])
                                    P.dma("sp", CALL("dma_start",
                                        out=dst[tk0:tk0 + T, c0:c0 + n], in_=sg.t[0:T, 0:n]), [sg.b],
                                        dbs(dbn, tk0, T, (c0 // 512) if NJ[dbn] > 1 else 0))
                    P.flush()

            if "2" in phases:
                s2 = ExitStack()
                with s2:
                    NK = NTP
                    isc = [SB(s2, "isc%d" % i, [128, NK], F32) for i in range(2)]
                    mbs = [SB(s2, "mb%d" % i, [128, NK], BF16) for i in range(2)]
                    junk = SB(s2, "junk2", [128, NK], BF16)
                    rsl = [SB(s2, "rsl%d" % i, [128, 512], BF16) for i in range(4)]
                    ptl = [SB(s2, "ptl%d" % i, [128, 512], BF16) for i in range(4)]
                    qib = [SB(s2, "qib%d" % i, [128, 4, 128], BF16) for i in range(2)]
                    qab = [SB(s2, "qab%d" % i, [128, 4, 128], BF16) for i in range(2)]
                    wib = [SB(s2, "wib%d" % i, [128, 8], F32) for i in range(2)]
                    dgs = [SB(s2, "dg%d" % i, [128, 8, 128], BF16) for i in range(2)]
                    stt = [SB(s2, "st%d" % i, [128, 64], F32) for i in range(2)]
                    ost = [[SB(s2, "ost%d_%d" % (i, k), [128, 260], F32) for k in range(2)] for i in range(2)]
                    atk = [SB(s2, "atk%d" % i, [128, 512], BF16) for i in range(2)]
                    atT = SB(s2, "atT", [128, 4, 128], BF16)
                    rct = SB(s2, "rct", [128, 8], F32)
                    tmpd = SB(s2, "tmpd", [128, 128], F32)
                    psd = [PSB(s2, "psd%d" % i, [128, 512]) for i in range(2)]
                    psi = [PSB(s2, "psi%d" % i, [128, 512]) for i in range(2)]
                    pss_ = [PSB(s2, "pss%d" % i, [128, 512]) for i in range(2)]
                    pos_ = [PSB(s2, "pos%d" % i, [128, 512]) for i in range(2)]
                    rr2 = {"d": 0, "i": 0, "s": 0, "r": 0, "p": 0}

                    def nx2(k, n):
                        v = rr2[k]
                        rr2[k] = (v + 1) % n
                        return v

                    S_MIN1, S_MIN2, S_MAX, S_W0, S_MID, S_CNT, S_U, S_THR, S_WALL = 0, 1, 2, 3, 4, 5, 6, 7, 8
                    WSCALE = (8.0 ** -0.5) / 8.0

                    def stage_a(i):
                        sl = i % 2
                        q0 = i * 128
                        nk = (i + 1) * 128
                        qi_, qa_, wi_, dg_, I_, st_ = qib[sl], qab[sl], wib[sl], dgs[sl], isc[sl], stt[sl]
                        P.dma("sp", CALL("dma_start", out=qi_.t[:], in_=qiT_s[:, :, q0:q0 + 128].rearrange("j p t -> p j t")),
                              dbs("qiT", q0, 128), [qi_.b])
                        P.dma("sp", CALL("dma_start", out=qa_.t[:], in_=qaT_s[:, :, q0:q0 + 128].rearrange("j p t -> p j t")),
                              dbs("qaT", q0, 128), [qa_.b])
                        P.dma("sp", CALL("dma_start", out=wi_.t[:], in_=wi_s[q0:q0 + 128, :]), dbs("wi", q0, 128), [wi_.b])
                        for h in range(8):
                            P.pool(CALL("tensor_scalar", out=dg_.t[:, h, :], in0=ident_f, scalar1=wi_.t[:, h:h + 1],
                                                                 scalar2=WSCALE, op0=ALU.mult, op1=ALU.mult),
                                   [wi_.b, cst.b], [dg_.b])
                        nch = (nk + 511) // 512
                        steps = [(c, h) for c in range(nch) for h in range(8)]
                        pds = {}
                        pIs = {}

                        def dots(k):
                            c, h = steps[k]
                            k0 = c * 512
                            Wc = min(512, nk - k0)
                            pd = psd[nx2("d", 2)]
                            pds[k] = pd
                            kis = kiA if h % 2 == 0 else kiB
                            P.pe(CALL("matmul", pd.t[:, 0:Wc], lhsT=qi_.t[:, h // 2, :], rhs=kis.t[:, k0:k0 + Wc], start=True, stop=True),
                                 [qi_.b, kis.b], [pd.b])

                        dots(0)
                        for k, (c, h) in enumerate(steps):
                            k0 = c * 512
                            Wc = min(512, nk - k0)
                            if h == 0:
                                pIs[c] = psi[nx2("i", 2)]
                            pI = pIs[c]
                            pd = pds[k]
                            r_ = rsl[nx2("r", 4)]
                            P.act(CALL("activation", out=r_.t[:, 0:Wc], in_=pd.t[:, 0:Wc], func=AF.Relu), [pd.b], [r_.b])
                            if k + 1 < len(steps):
                                dots(k + 1)
                            P.pe(CALL("matmul", pI.t[:, 0:Wc], lhsT=dg_.t[:, h, :], rhs=r_.t[:, 0:Wc], start=(h == 0), stop=(h == 7)),
                                 [dg_.b, r_.b], [pI.b])
                            if h == 7:
                                P.act(CALL("activation", out=I_.t[:, k0:k0 + Wc], in_=pI.t[:, 0:Wc], func=AF.Copy), [pI.b], [I_.b])
                        d0 = i * 128
                        P.dve(CALL("tensor_tensor", out=tmpd.t[:], in0=I_.t[:, d0:d0 + 128], in1=cneg_f, op=ALU.subtract),
                              [I_.b, cst.b], [tmpd.b])
                        P.dve(CALL("tensor_reduce", out=st_.t[:, S_MIN2:S_MIN2 + 1], in_=tmpd.t[:], axis=AX.X, op=ALU.min),
                              [tmpd.b], [st_.b])
                        if i > 0:
                            P.dve(CALL("tensor_reduce", out=st_.t[:, S_MIN1:S_MIN1 + 1], in_=I_.t[:, 0:d0], axis=AX.X,
                                                            op=ALU.min), [I_.b], [st_.b])
                            P.dve(CALL("tensor_tensor", out=st_.t[:, S_MIN2:S_MIN2 + 1], in0=st_.t[:, S_MIN2:S_MIN2 + 1],
                                                            in1=st_.t[:, S_MIN1:S_MIN1 + 1], op=ALU.min), [st_.b], [st_.b])
                        P.dve(CALL("tensor_tensor", out=I_.t[:, d0:d0 + 128], in0=I_.t[:, d0:d0 + 128], in1=cneg_f, op=ALU.add),
                              [I_.b, cst.b], [I_.b])
                        P.dve(CALL("tensor_reduce", out=st_.t[:, S_MAX:S_MAX + 1], in_=I_.t[:, 0:nk], axis=AX.X, op=ALU.max),
                              [I_.b], [st_.b])

                    def stage_b(i):
                        sl = i % 2
                        nk = (i + 1) * 128
                        I_, st_, mb_ = isc[sl], stt[sl], mbs[sl]
                        c_ = lambda k: st_.t[:, k:k + 1]
                        P.dve(CALL("tensor_tensor", out=c_(S_W0), in0=c_(S_MAX), in1=c_(S_MIN2), op=ALU.subtract), [st_.b], [st_.b])
                        P.dve(CALL("tensor_scalar", out=c_(S_U), in0=c_(S_W0), scalar1=-1e-3, scalar2=-1e-4, op0=ALU.mult,
                                                        op1=ALU.add), [st_.b], [st_.b])
                        P.dve(CALL("tensor_tensor", out=c_(S_MIN2), in0=c_(S_MIN2), in1=c_(S_U), op=ALU.add), [st_.b], [st_.b])
                        P.dve(CALL("tensor_tensor", out=c_(S_W0), in0=c_(S_MAX), in1=c_(S_MIN2), op=ALU.subtract), [st_.b], [st_.b])
                        P.dve(CALL("tensor_scalar", out=c_(S_W0), in0=c_(S_W0), scalar1=1.0001, scalar2=1e-6, op0=ALU.mult,
                                                        op1=ALU.add), [st_.b], [st_.b])
                        P.dve(CALL("tensor_scalar", out=st_.t[:, S_WALL:S_WALL + 32], in0=cst.t[:, C_PW:C_PW + 32],
                                                        scalar1=c_(S_W0), scalar2=None, op0=ALU.mult), [st_.b, cst.b], [st_.b])
                        P.dve(CALL("tensor_tensor", out=c_(S_MID), in0=c_(S_MIN2), in1=c_(S_WALL), op=ALU.add), [st_.b], [st_.b])
                        for r in range(ROUNDS):
                            P.dve(CALL("tensor_scalar", out=junk.t[:, 0:nk], in0=I_.t[:, 0:nk], scalar1=c_(S_MID), scalar2=0.0,
                                                            op0=ALU.is_ge, op1=ALU.add, accum_out=c_(S_CNT)),
                                  [I_.b, st_.b], [junk.b, st_.b])
                            P.dve(CALL("tensor_scalar", out=c_(S_U), in0=c_(S_CNT), scalar1=TOPK - 0.5,
                                                                 scalar2=c_(S_WALL + r), op0=ALU.is_ge, op1=ALU.mult),
                                  [st_.b], [st_.b])
                            nxt_w = S_WALL + r + 1 if r + 1 < ROUNDS else S_WALL + r
                            dst = S_MID if r + 1 < ROUNDS else S_THR
                            P.dve(CALL("scalar_tensor_tensor",
                                out=c_(dst), in0=c_(S_U), scalar=c_(nxt_w), in1=c_(S_MID), op0=ALU.subtract, op1=ALU.add),
                                [st_.b], [st_.b])
                        P.dve(CALL("tensor_scalar", out=mb_.t[:, 0:nk], in0=I_.t[:, 0:nk], scalar1=c_(S_THR), scalar2=NEG,
                                                        op0=ALU.is_lt, op1=ALU.mult), [I_.b, st_.b], [mb_.b])

                    def stage_c(i):
                        sl = i % 2
                        qa_, mb_ = qab[sl], mbs[sl]
                        steps = [(kvh, c) for kvh in range(2) for c in range(i + 1)]
                        pls = {}

                        def logits(k):
                            kvh, c = steps[k]
                            kk = kA if kvh == 0 else kB
                            k0 = c * 128
                            ps_ = pss_[nx2("s", 2)]
                            pls[k] = ps_
                            P.pe(CALL("matmul", ps_.t[:, :], lhsT=kk.t[:, k0:k0 + 128], rhs=qa_.t[:, :, :], start=True, stop=False),
                                 [kk.b, qa_.b], [ps_.b])
                            P.pe(CALL("matmul", ps_.t[:, :], lhsT=mb_.t[:, k0:k0 + 128], rhs=i4_bf.t[:, :, :], start=False, stop=True),
                                 [mb_.b, i4_bf.b], [ps_.b])

                        logits(0)
                        for k, (kvh, c) in enumerate(steps):
                            po = pos_[kvh]
                            ps_ = pls[k]
                            pt_ = ptl[nx2("p", 4)]
                            P.act(CALL("activation", out=pt_.t[:, :], in_=ps_.t[:, :], func=AF.Exp, scale=0.125), [ps_.b], [pt_.b])
                            if k + 1 < len(steps):
                                logits(k + 1)
                            for j in range(4):
                                P.pe(CALL("matmul", po.t[:, j * 65:(j + 1) * 65], lhsT=pt_.t[:, j * 128:(j + 1) * 128],
                                          rhs=vaug.t[:, c, kvh, :], start=(c == 0 and j == 0), stop=(c == i), skip_group_check=True),
                                     [vaug.b, pt_.b], [po.b])
                            if c == i:
                                o_ = ost[sl][kvh]
                                P.act(CALL("activation", out=o_.t[:, :], in_=po.t[:, 0:260], func=AF.Copy), [po.b], [o_.b])

                    def stage_c_norm(i):
                        sl = i % 2
                        at_ = atk[sl]
                        for kvh in range(2):
                            o3 = ost[sl][kvh].t[:, :].rearrange("p (j e) -> p j e", e=65)
                            P.dve(CALL("reciprocal", out=rct.t[:, kvh * 4:kvh * 4 + 4].unsqueeze(2), in_=o3[:, :, 0:1]), [ost[sl][kvh].b], [rct.b])
                            P.dve(CALL("tensor_tensor", out=at_.t[:, kvh * 256:(kvh + 1) * 256].rearrange("p (j d) -> p j d", j=4),
                                       in0=o3[:, :, 1:65], in1=rct.t[:, kvh * 4:kvh * 4 + 4].unsqueeze(2).broadcast_to([128, 4, 64]),
                                       op=ALU.mult), [ost[sl][kvh].b, rct.b], [at_.b])

                    def stage_c_T(i):
                        sl = i % 2
                        q0 = i * 128
                        at_ = atk[sl]
                        po = pos_[0]
                        psx_ = po.t.bitcast(BF16)
                        for kc in range(4):
                            P.pe(CALL("transpose", psx_[:, kc * 128:(kc + 1) * 128], at_.t[:, kc * 128:(kc + 1) * 128], ident_bf.t[:, :]),
                                 [at_.b, ident_bf.b], [po.b])
                        P.act(CALL("activation", out=atT.t[:, :, :], in_=psx_[:, 0:512].rearrange("p (k t) -> p k t", k=4), func=AF.Copy),
                              [po.b], [atT.b])
                        P.dma("sp", CALL("dma_start", out=attnT_s[:, :, q0:q0 + 128].rearrange("k p t -> p k t"), in_=atT.t[:, :, :]),
                              [atT.b], dbs("attnT", q0, 128))

                    for i in range(NBLK):
                        stage_a(i)
                        if i >= 2:
                            stage_c_T(i - 2)
                        if i >= 1:
                            stage_c(i - 1)
                        stage_b(i)
                        if i >= 1:
                            stage_c_norm(i - 1)
                    if NBLK >= 2:
                        stage_c_T(NBLK - 2)
                    stage_c(NBLK - 1)
                    stage_c_norm(NBLK - 1)
                    stage_c_T(NBLK - 1)
                    P.flush()
            if "5" in phases:
                s2b = ExitStack()
                with s2b:
                    ROUNDS_S = 24
                    NKP = NPAGES * 128
                    SEGW = NKP // 4
                    IW = SEGW + 8
                    R = SB(s2b, "R2b", [128, 3 * NKP], BF16)
                    BR0, BR1, BR2 = Buf("BR0"), Buf("BR1"), Buf("BR2")
                    O1, O2 = NKP, 2 * NKP
                    Is = SB(s2b, "Is", [128, IW], F32)
                    MBs = SB(s2b, "MBs", [128, IW], BF16)
                    pti = SB(s2b, "pti", [128, 1], I32)
                    ptf = SB(s2b, "ptf", [128, 1], F32)
                    idf = SB(s2b, "idf", [128, 16], F32)
                    idxs = [SB(s2b, "idx%d" % i, [128, 16], I32) for i in range(NB_S)]
                    qis = SB(s2b, "qis", [64, 8, NS], BF16)
                    qs0 = SB(s2b, "qs0", [128, 4, NS], BF16)
                    qs1 = SB(s2b, "qs1", [128, 4, NS], BF16)
                    wis = SB(s2b, "wis", [8, NB_S, 8], F32)
                    dss = SB(s2b, "dss", [8, NB_S, 8, 8], BF16)
                    rs2 = [SB(s2b, "rs2_%d" % i, [64, 512], BF16) for i in range(4)]
                    qisb = SB(s2b, "qisb", [64, NB_S, 64], BF16)
                    wsel = SB(s2b, "wsel", [64, NB_S, 8], BF16)
                    qsb = SB(s2b, "qsb", [128, NB_S, 64], BF16)
                    sel2 = SB(s2b, "sel2", [128, 16, 64], BF16)
                    sg2 = [SB(s2b, "sg2_%d" % i, [8, 512], F32) for i in range(4)]
                    pt2 = [SB(s2b, "pt2_%d" % i, [128, 256], BF16) for i in range(4)]
                    sst = SB(s2b, "sst", [128, 64], F32)
                    vnew = SB(s2b, "vnew", [8, 128], BF16)
                    rcp2 = SB(s2b, "rcp2", [1, 64], F32)
                    bcs2 = SB(s2b, "bcs2", [64, 64], F32)
                    as2 = SB(s2b, "as2", [64, 2, 4, 8], BF16)
                    pst = [PSB(s2b, "pst%d" % i, [128, 1024], BF16) for i in range(2)]
                    psd2 = [PSB(s2b, "psd2_%d" % i, [128, 512]) for i in range(2)]
                    psi2s = [PSB(s2b, "psi2_%d" % i, [128, 512]) for i in range(2)]
                    psi2 = psi2s[0]
                    pso2s = [psi2s[1], PSB(s2b, "pso2b", [128, 512])]
                    psm2 = PSB(s2b, "psm2", [128, 512])
                    rr3 = {"t": 0, "d": 0, "r": 0, "g": 0, "p": 0, "e": 0, "i": 0}

                    def nx3(k, n):
                        v = rr3[k]
                        rr3[k] = (v + 1) % n
                        return v

                    def evac3(out_ap, in_ap, reads, writes):
                        if nx3("e", 2) == 0:
                            P.act(CALL("activation", out=out_ap, in_=in_ap, func=AF.Copy), reads, writes)
                        else:
                            P.dve(CALL("tensor_copy", out=out_ap, in_=in_ap), reads, writes)

                    S0 = NTP
                    WSC = (8.0 ** -0.5) / 8.0
                    G_f = cst.t[:, C_G:C_G + 128]
                    P.pool(CALL("memset", qs0.t[:, :, :], 0.0), [], [qs0.b])
                    P.pool(CALL("memset", qs1.t[:, :, :], 0.0), [], [qs1.b])
                    P.pool(CALL("memset", Is.t[:, SEGW:IW], -BIG), [], [Is.b])
                    P.dma("sp", CALL("dma_start", out=qs0.t[0:64, :, :], in_=qaT_s[:, 0:64, S0:S0 + NS].rearrange("j p t -> p j t")),
                          dbs("qaT", S0, NS), [qs0.b])
                    P.dma("sp", CALL("dma_start", out=qs1.t[64:128, :, :], in_=qaT_s[:, 64:128, S0:S0 + NS].rearrange("j p t -> p j t")),
                          dbs("qaT", S0, NS), [qs1.b])
                    for j in range(4):
                        for par in range(2):
                            P.dma("sp", CALL("dma_start", out=qis.t[:, 2 * j + par, :], in_=qiT_s[j, 64 * par:64 * par + 64, S0:S0 + NS]),
                                  dbs("qiT", S0, NS), [qis.b])
                    P.dma("sp", CALL("dma_start", out=wis.t[:, :, :], in_=wi_s[S0:S0 + NS, :].rearrange("(b q) h -> q b h", q=8)),
                          dbs("wi", S0, NS), [wis.b])
                    for b in range(NB_S):
                        for h in range(8):
                            P.dve(CALL("tensor_scalar", out=dss.t[:, b, h, :], in0=cst.t[0:8, C_ID:C_ID + 8], scalar1=wis.t[:, b, h:h + 1],
                                       scalar2=WSC, op0=ALU.mult, op1=ALU.mult), [wis.b, cst.b], [dss.b])
                    for bs in range(16):
                        P.dve(CALL("tensor_copy", out=sel2.t[:, bs, :].rearrange("p (a q) -> p a q", a=8),
                                   in_=cst.t[:, C_ID + bs * 8:C_ID + bs * 8 + 8].unsqueeze(1).broadcast_to([128, 8, 8])), [cst.b], [sel2.b])

                    def page_idx(b):
                        P.dma("sp", CALL("dma_start", out=pti.t[:, :], in_=pt_d[b:b + 1, :].rearrange("o p -> p o")), [], [pti.b])
                        P.dve(CALL("tensor_copy", out=ptf.t[:, :], in_=pti.t[:, :]), [pti.b], [ptf.b])
                        for o in range(4):
                            P.dve(CALL("tensor_scalar", out=idf.t[:, o:o + 1], in0=ptf.t[:, :], scalar1=4.0, scalar2=float(o),
                                       op0=ALU.mult, op1=ALU.add), [ptf.b], [idf.b])
                        for o in range(8):
                            P.dve(CALL("tensor_scalar", out=idf.t[:, 4 + o:5 + o], in0=ptf.t[:, :], scalar1=8.0, scalar2=float(o),
                                       op0=ALU.mult, op1=ALU.add), [ptf.b], [idf.b])
                        P.dve(CALL("tensor_copy", out=idxs[b].t[:, 0:12], in_=idf.t[:, 0:12]), [idf.b], [idxs[b].b])

                    def gather(b, src_d, n_o, icol0, dst0, bufR):
                        for o in range(n_o):
                            P.dma("pool", CALL("indirect_dma_start", out=R.t[:, dst0 + o * 2048:dst0 + (o + 1) * 2048], out_offset=None,
                                               in_=src_d[:, :],
                                               in_offset=bass.IndirectOffsetOnAxis(ap=idxs[b].t[:, icol0 + o:icol0 + o + 1], axis=0)),
                                  [idxs[b].b], [bufR])

                    for b in range(NB_S):
                        page_idx(b)

                    for b in range(NB_S):
                        P.dve(CALL("tensor_copy", out=qisb.t[:, b, :].rearrange("p (h q) -> p h q", h=8), in_=qis.t[:, :, 8 * b:8 * b + 8]),
                              [qis.b], [qisb.b])
                        ptw = pst[nx3("t", 2)]
                        P.pe(CALL("transpose", ptw.t[0:64, 0:8], dss.t[:, b, :, :].rearrange("p h q -> p (h q)"), ident_bf.t[0:8, 0:8]),
                             [dss.b, ident_bf.b], [ptw.b])
                        P.dve(CALL("tensor_copy", out=wsel.t[:, b, :], in_=ptw.t[0:64, 0:8]), [ptw.b], [wsel.b])
                    for b in range(NB_S):
                        gather(b, cki_d, 4, 0, 0, BR0)
                        if b == 0:
                            gather(0, cv_d, 8, 4, O2, BR2)
                        for g8 in range(16):
                            pt_ = pst[nx3("t", 2)]
                            for k in range(8):
                                t = g8 * 8 + k
                                P.pe(CALL("transpose", pt_.t[0:64, k * 128:(k + 1) * 128], R.t[:, t * 64:(t + 1) * 64], ident_bf.t[:, :]),
                                     [BR0, ident_bf.b], [pt_.b])
                            evac3(R.t[0:64, O1 + g8 * 1024:O1 + (g8 + 1) * 1024], pt_.t[0:64, :], [pt_.b], [BR1])
                        NCH = NKP // 512 + 1
                        pdd = {}

                        def idots(c, b=b):
                            new = (c == NKP // 512)
                            Wc = 8 if new else 512
                            pd = psd2[nx3("d", 2)]
                            pdd[c] = pd
                            rhs = kiA.t[0:64, S0 + 8 * b:S0 + 8 * b + 8] if new else R.t[0:64, O1 + c * 512:O1 + (c + 1) * 512]
                            P.pe(CALL("matmul", pd.t[0:64, 0:Wc], lhsT=qisb.t[:, b, :], rhs=rhs, start=True, stop=True),
                                 [qisb.b, kiA.b if new else BR1], [pd.b])

                        idots(0)
                        for c in range(NCH):
                            new = (c == NKP // 512)
                            Wc = 8 if new else 512
                            pd = pdd[c]
                            r_ = rs2[nx3("r", 4)]
                            P.act(CALL("activation", out=r_.t[:, 0:Wc], in_=pd.t[0:64, 0:Wc], func=AF.Relu), [pd.b], [r_.b])
                            if c + 1 < NCH:
                                idots(c + 1)
                            pi_ = psi2s[nx3("i", 2)]
                            P.pe(CALL("matmul", pi_.t[0:8, 0:Wc], lhsT=wsel.t[:, b, :], rhs=r_.t[:, 0:Wc], start=True, stop=True),
                                 [wsel.b, r_.b], [pi_.b])
                            sg_ = sg2[nx3("g", 4)]
                            if new:
                                P.dve(CALL("tensor_tensor", out=sg_.t[:, 0:8], in0=pi_.t[0:8, 0:8], in1=cst.t[0:8, C_CN:C_CN + 8], op=ALU.add),
                                      [pi_.b, cst.b], [sg_.b])
                                P.dma("sp", CALL("dma_start", out=Is.t[b * 32:b * 32 + 8, SEGW:IW], in_=sg_.t[:, 0:8]), [sg_.b], [Is.b])
                            else:
                                P.dve(CALL("tensor_copy", out=sg_.t[:, :], in_=pi_.t[0:8, :]), [pi_.b], [sg_.b])
                                seg, cc = c // (SEGW // 512), c % (SEGW // 512)
                                r0 = b * 32 + seg * 8
                                P.dma("sp", CALL("dma_start", out=Is.t[r0:r0 + 8, cc * 512:(cc + 1) * 512], in_=sg_.t[:, :]), [sg_.b], [Is.b])

                    gather(0, ck_d, 8, 4, 0, BR0)
                    c2 = lambda k: sst.t[:, k:k + 1]
                    Q_MAX, Q_MIN, Q_A, Q_HI, Q_LO, Q_W0, Q_MID, Q_CNT, Q_U, Q_THR, Q_WALL = 0, 1, 2, 3, 4, 5, 6, 7, 8, 9, 16
                    P.dve(CALL("tensor_reduce", out=c2(Q_MAX), in_=Is.t[:, 0:IW], axis=AX.X, op=ALU.max), [Is.b], [sst.b])
                    P.dve(CALL("tensor_reduce", out=c2(Q_MIN), in_=Is.t[:, 0:SEGW], axis=AX.X, op=ALU.min), [Is.b], [sst.b])
                    P.dve(CALL("tensor_scalar", out=c2(Q_MIN), in0=c2(Q_MIN), scalar1=-1.0, scalar2=None, op0=ALU.mult), [sst.b], [sst.b])
                    P.dve(CALL("tensor_tensor", out=c2(Q_A), in0=c2(Q_MAX), in1=c2(Q_MIN), op=ALU.max), [sst.b], [sst.b])
                    P.pe(CALL("matmul", psi2.t[:, 0:1], lhsT=G_f, rhs=c2(Q_A), start=True, stop=True), [sst.b, cst.b], [psi2.b])
                    P.dve(CALL("tensor_scalar", out=c2(Q_HI), in0=psi2.t[:, 0:1], scalar1=1.001, scalar2=1e-3, op0=ALU.mult, op1=ALU.add),
                          [psi2.b], [sst.b])
                    P.dve(CALL("tensor_scalar", out=c2(Q_LO), in0=c2(Q_HI), scalar1=-1.0, scalar2=None, op0=ALU.mult), [sst.b], [sst.b])
                    P.dve(CALL("tensor_scalar", out=c2(Q_W0), in0=c2(Q_HI), scalar1=2.0, scalar2=None, op0=ALU.mult), [sst.b], [sst.b])
                    P.dve(CALL("tensor_scalar", out=sst.t[:, Q_WALL:Q_WALL + 32], in0=cst.t[:, C_PW:C_PW + 32], scalar1=c2(Q_W0), scalar2=None,
                               op0=ALU.mult), [sst.b, cst.b], [sst.b])
                    P.dve(CALL("tensor_tensor", out=c2(Q_MID), in0=c2(Q_LO), in1=c2(Q_WALL), op=ALU.add), [sst.b], [sst.b])
                    for r in range(ROUNDS_S):
                        P.dve(CALL("tensor_scalar", out=MBs.t[:, :], in0=Is.t[:, :], scalar1=c2(Q_MID), scalar2=0.0, op0=ALU.is_ge,
                                   op1=ALU.add, accum_out=c2(Q_CNT)), [Is.b, sst.b], [MBs.b, sst.b])
                        P.pe(CALL("matmul", psi2.t[:, 0:1], lhsT=G_f, rhs=c2(Q_CNT), start=True, stop=True), [sst.b, cst.b], [psi2.b])
                        P.dve(CALL("tensor_scalar", out=c2(Q_U), in0=psi2.t[:, 0:1], scalar1=TOPK_S - 0.5, scalar2=c2(Q_WALL + r),
                                   op0=ALU.is_ge, op1=ALU.mult), [psi2.b, sst.b], [sst.b])
                        nxt_w = Q_WALL + r + 1 if r + 1 < ROUNDS_S else Q_WALL + r
                        dst = Q_MID if r + 1 < ROUNDS_S else Q_THR
                        P.dve(CALL("scalar_tensor_tensor", out=c2(dst), in0=c2(Q_U), scalar=c2(nxt_w), in1=c2(Q_MID), op0=ALU.subtract,
                                   op1=ALU.add), [sst.b], [sst.b])
                    P.dve(CALL("tensor_scalar", out=MBs.t[:, :], in0=Is.t[:, :], scalar1=c2(Q_THR), scalar2=NEG, op0=ALU.is_lt, op1=ALU.mult),
                          [Is.b, sst.b], [MBs.b])

                    for b in range(NB_S):
                        P.dve(CALL("tensor_copy", out=qsb.t[:, b, 0:32].rearrange("p (j q) -> p j q", j=4), in_=qs0.t[:, :, 8 * b:8 * b + 8]),
                              [qs0.b], [qsb.b])
                        P.dve(CALL("tensor_copy", out=qsb.t[:, b, 32:64].rearrange("p (j q) -> p j q", j=4), in_=qs1.t[:, :, 8 * b:8 * b + 8]),
                              [qs1.b], [qsb.b])
                    for b in range(NB_S):
                        if b >= 1:
                            gather(b, cv_d, 8, 4, O2, BR2)
                        P.dma("pool", CALL("dma_start", out=vnew.t[:, :], in_=vo_d[S0 + 8 * b:S0 + 8 * b + 8, :]), dbs("vo", S0, NS), [vnew.b])
                        for g8 in range(16):
                            pt_ = pst[nx3("t", 2)]
                            for k in range(8):
                                t = g8 * 8 + k
                                P.pe(CALL("transpose", pt_.t[:, k * 128:(k + 1) * 128], R.t[:, t * 128:(t + 1) * 128], ident_bf.t[:, :]),
                                     [BR0, ident_bf.b], [pt_.b])
                            evac3(R.t[:, O1 + g8 * 1024:O1 + (g8 + 1) * 1024], pt_.t[:, :], [pt_.b], [BR1])
                        if b + 1 < NB_S:
                            gather(b + 1, ck_d, 8, 4, 0, BR0)
                        ngrp = 128 // 4
                        pll = {}

                        def alogits(g4, b=b):
                            new = (g4 == ngrp)
                            nkk = 8 if new else 128
                            ps_ = psd2[nx3("d", 2)]
                            pll[g4] = ps_
                            ts_ = [128] if new else [g4 * 4 + k for k in range(4)]
                            for k, t in enumerate(ts_):
                                if new:
                                    l1a, l1b = kA.t[:, S0 + 8 * b:S0 + 8 * b + 8], kB.t[:, S0 + 8 * b:S0 + 8 * b + 8]
                                    l2 = MBs.t[:, SEGW:IW]
                                    sl_ = sel2.t[:, b * 4, :]
                                    P.pe(CALL("matmul", ps_.t[0:nkk, 0:64], lhsT=l1a, rhs=qsb.t[:, b, :], start=True, stop=False),
                                         [kA.b, qsb.b], [ps_.b])
                                    P.pe(CALL("matmul", ps_.t[0:nkk, 0:64], lhsT=l1b, rhs=qsb.t[:, b, :], start=False, stop=False),
                                         [kB.b, qsb.b], [ps_.b])
                                else:
                                    l1 = R.t[:, O1 + t * 128:O1 + (t + 1) * 128]
                                    seg, cc = t // 32, t % 32
                                    l2 = MBs.t[:, cc * 128:(cc + 1) * 128]
                                    sl_ = sel2.t[:, b * 4 + seg, :]
                                    P.pe(CALL("matmul", ps_.t[0:nkk, k * 64:(k + 1) * 64], lhsT=l1, rhs=qsb.t[:, b, :], start=True, stop=False),
                                         [BR1, qsb.b], [ps_.b])
                                P.pe(CALL("matmul", ps_.t[0:nkk, k * 64:(k + 1) * 64], lhsT=l2, rhs=sl_, start=False, stop=True),
                                     [MBs.b, sel2.b], [ps_.b])

                        alogits(0)
                        for g4 in range(ngrp + 1):
                            new = (g4 == ngrp)
                            nkk = 8 if new else 128
                            ncol = 64 if new else 256
                            ps_ = pll[g4]
                            ts_ = [128] if new else [g4 * 4 + k for k in range(4)]
                            p_ = pt2[nx3("p", 4)]
                            P.act(CALL("activation", out=p_.t[0:nkk, 0:ncol], in_=ps_.t[0:nkk, 0:ncol], func=AF.Exp, scale=0.125), [ps_.b], [p_.b])
                            if g4 + 1 <= ngrp:
                                alogits(g4 + 1)
                            for k, t in enumerate(ts_):
                                first = (g4 == 0 and k == 0)
                                for kvh in range(2):
                                    po_ = pso2s[kvh]
                                    if new:
                                        lv = vnew.t[0:8, kvh * 64:(kvh + 1) * 64]
                                        rdv = vnew.b
                                    else:
                                        lv = R.t[:, O2 + t * 128 + kvh * 64:O2 + t * 128 + kvh * 64 + 64]
                                        rdv = BR2
                                    P.pe(CALL("matmul", po_.t[0:64, 0:32], lhsT=lv, rhs=p_.t[0:nkk, k * 64 + kvh * 32:k * 64 + kvh * 32 + 32],
                                              start=first, stop=new), [rdv, p_.b], [po_.b])
                                P.pe(CALL("matmul", psm2.t[0:1, 0:64], lhsT=ones_bf.t[0:nkk, 0:1], rhs=p_.t[0:nkk, k * 64:(k + 1) * 64],
                                          start=first, stop=new), [ones_bf.b, p_.b], [psm2.b])
                        P.dve(CALL("reciprocal", out=rcp2.t[:, :], in_=psm2.t[0:1, 0:64]), [psm2.b], [rcp2.b])
                        pb_ = psd2[nx3("d", 2)]
                        P.pe(CALL("matmul", pb_.t[0:64, 0:64], lhsT=cst.t[0:1, C_LE:C_LE + 64], rhs=rcp2.t[:, :], start=True, stop=True),
                             [cst.b, rcp2.b], [pb_.b])
                        P.act(CALL("activation", out=bcs2.t[:, :], in_=pb_.t[0:64, 0:64], func=AF.Copy), [pb_.b], [bcs2.b])
                        for kvh in range(2):
                            P.dve(CALL("tensor_tensor", out=as2.t[:, kvh, :, :], in0=pso2s[kvh].t[0:64, 0:32].rearrange("p (j q) -> p j q", j=4),
                                       in1=bcs2.t[:, kvh * 32:(kvh + 1) * 32].rearrange("p (j q) -> p j q", j=4), op=ALU.mult),
                                  [pso2s[kvh].b, bcs2.b], [as2.b])
                        for kvh in range(2):
                            for j in range(4):
                                kc = 2 * kvh + j // 2
                                r0 = (j % 2) * 64
                                P.dma("sp", CALL("dma_start", out=attnT_s[kc, r0:r0 + 64, S0 + 8 * b:S0 + 8 * b + 8], in_=as2.t[:, kvh, j, :]),
                                      [as2.b], dbs("attnT", S0, NS, kc))
                    P.flush()
        wf1 = SB(g, "wf1", [128, 8, 4096], BF16)
        wf1b = [Buf("wf1_%d" % i) for i in range(8)]
        s3w = ExitStack()
        with s3w:
            wao = SB(s3w, "wao", [128, 4, D], BF16)
            wgo = SB(s3w, "wgo", [128, 8, D], BF16)
            wo = SB(s3w, "wo", [128, 8, D], BF16)
            if "4" in phases:
                for (wt, wd) in ((wao, wao_d), (wgo, wgo_d), (wo, wo_d)):
                    P.dma("pool", CALL("dma_start",
                        out=wt.t[:, :, :], in_=wd.rearrange("(kc p) n -> p kc n", p=128)), [], [wt.b])
                for n in ln_d:
                    P.dma("sp", CALL("dma_start", out=lnb[n].t[:, :], in_=ln_d[n][0:1, :].broadcast_to([128, D])),
                          [], [lnb[n].b])
                for kc in range(8):
                    for c0 in (0, 2048):
                        P.dma("pool", CALL("dma_start", out=wf1.t[:, kc, c0:c0 + 2048], in_=wf1_d[kc * 128:(kc + 1) * 128, c0:c0 + 2048]),
                              [], [wf1b[kc]])
            if "3" in phases:
                s3 = ExitStack()
                with s3:
                    aug = SB(s3, "aug", [17, 128], BF16)
                    wal = SB(s3, "wal", [17, 512], BF16)
                    gnb = SB(s3, "gnb", [128, 256], F32)
                    qbb = SB(s3, "qbb", [128, 4, 128], BF16)
                    kbb = SB(s3, "kbb", [128, 4, 128], BF16)
                    kbt = SB(s3, "kbt", [128, 512], BF16)
                    vbts = [SB(s3, "vbt%d" % i, [128, 1024], BF16) for i in range(2)]
                    gbt = SB(s3, "gbt", [128, 1024], F32)
                    ee = SB(s3, "ee", [128, 512], F32)
                    la = SB(s3, "la", [128, 512], F32)
                    Eqs = [SB(s3, "Eq%d" % i, [128, 4, 128], F32) for i in range(2)]
                    Ek = SB(s3, "Ek", [128, 4, 128], F32)
                    Er = SB(s3, "Er", [128, 512], F32)
                    qts = [SB(s3, "qt%d" % i, [128, 4, 128], BF16) for i in range(2)]
                    kts = [SB(s3, "kt%d" % i, [128, 4, 128], BF16) for i in range(2)]
                    kps = [SB(s3, "kp%d" % i, [128, 512], BF16) for i in range(2)]
                    attm = SB(s3, "attm", [128, 4, 128], BF16)
                    Sf = SB(s3, "Sf", [128, 4, 256], F32)
                    Sb = SB(s3, "Sb", [128, 4, 256], BF16)
                    onr = SB(s3, "onr", [128, 1024], F32)
                    osbs = [SB(s3, "osb%d" % i, [128, 1024], F32) for i in range(2)]
                    sgts = [SB(s3, "sgt%d" % i, [128, 1024], F32) for i in range(2)]
                    obb = SB(s3, "obb", [128, 1024], BF16)
                    obT = SB(s3, "obT", [128, 8, 128], BF16)
                    gst = SB(s3, "gst", [128, 16], F32)
                    jk3 = SB(s3, "jk3", [128, 256], BF16)
                    psA = PSB(s3, "psA", [128, 512])
                    psC = PSB(s3, "psC", [128, 512])
                    psT_ = PSB(s3, "psT", [128, 512])
                    psO = PSB(s3, "psO", [128, 1024])
                    psS = PSB(s3, "psS", [128, 1024])
                    psX = PSB(s3, "psX", [128, 1024], BF16)

                    P.pool(CALL("memset", aug.t[:, :], 1.0), [], [aug.b])
                    P.dma("pool", CALL("dma_start", out=wal.t[1:17, :], in_=wal_d[:, :]), [], [wal.b])
                    P.dma("pool", CALL("dma_start", out=wal.t[0:1, :], in_=bal_d[:, :]), [], [wal.b])
                    P.dma("sp", CALL("dma_start", out=gnb.t[:, :], in_=gng_d[0:1, :].broadcast_to([128, 256])), [], [gnb.b])

                    def gla_prep(ci, tok0, C):
                        sl = ci % 2
                        qt, kt, kp, vbt, sgt, Eq = qts[sl], kts[sl], kps[sl], vbts[sl], sgts[sl], Eqs[sl]
                        P.dma("sp", CALL("dma_start", out=aug.t[1:17, 0:C], in_=abT_s[:, tok0:tok0 + C]), dbs("abT", tok0, C), [aug.b])
                        P.dma("sp", CALL("dma_start", out=qbb.t[:, :, 0:C], in_=qbT_s[:, :, tok0:tok0 + C].rearrange("j p t -> p j t")),
                              dbs("qbT", tok0, C), [qbb.b])
                        P.dma("sp", CALL("dma_start", out=kbb.t[:, :, 0:C], in_=kbT_s[:, :, tok0:tok0 + C].rearrange("j p t -> p j t")),
                              dbs("kbT", tok0, C), [kbb.b])
                        P.dma("sp", CALL("dma_start", out=kbt.t[0:C, :], in_=kb_s[tok0:tok0 + C, :]), dbs("kb", tok0, C), [kbt.b])
                        P.dma("sp", CALL("dma_start", out=vbt.t[0:C, :], in_=vb_s[tok0:tok0 + C, :]), dbs("vb", tok0, C), [vbt.b])
                        P.pe(CALL("matmul", psA.t[0:C, :], lhsT=aug.t[:, 0:C], rhs=wal.t[:, :], start=True, stop=True), [aug.b, wal.b], [psA.b])
                        P.act(CALL("activation", out=ee.t[0:C, :], in_=psA.t[0:C, :], func=AF.Exp, scale=-1.0), [psA.b], [ee.b])
                        P.act(CALL("activation", out=la.t[0:C, :], in_=ee.t[0:C, :], func=AF.Ln, bias=one_c[0:C, :], scale=1.0),
                              [ee.b, cst.b], [la.b])
                        P.pe(CALL("matmul", psA.t[0:C, :], lhsT=ltri_f[0:C, 0:C], rhs=la.t[0:C, :], start=True, stop=True), [la.b, cst.b], [psA.b])
                        for h in range(4):
                            P.pe(CALL("matmul", psC.t[:, h * 128:h * 128 + C], lhsT=la.t[0:C, h * 128:(h + 1) * 128], rhs=utri_f[0:C, 0:C],
                                      start=True, stop=True), [la.b, cst.b], [psC.b])
                        psC3 = psC.t[:, :].rearrange("p (h t) -> p h t", h=4)
                        P.act(CALL("activation", out=Er.t[0:C, :], in_=psA.t[0:C, :], func=AF.Exp), [psA.b], [Er.b])
                        P.act(CALL("activation", out=Eq.t[:, :, 0:C], in_=psC3[:, :, 0:C], func=AF.Exp), [psC.b], [Eq.b])
                        P.act(CALL("activation", out=Ek.t[:, :, 0:C], in_=psC3[:, :, 0:C], func=AF.Exp, scale=-1.0), [psC.b], [Ek.b])
                        P.dve(CALL("scalar_tensor_tensor", out=qt.t[:, :, 0:C], in0=qbb.t[:, :, 0:C], scalar=128.0 ** -0.5, in1=Eq.t[:, :, 0:C],
                                   op0=ALU.mult, op1=ALU.mult), [qbb.b, Eq.b], [qt.b])
                        P.dve(CALL("tensor_tensor", out=kt.t[:, :, 0:C], in0=kbb.t[:, :, 0:C], in1=Ek.t[:, :, 0:C], op=ALU.mult),
                              [kbb.b, Ek.b], [kt.b])
                        P.dve(CALL("tensor_tensor", out=kp.t[0:C, :], in0=kbt.t[0:C, :], in1=Er.t[0:C, :], op=ALU.mult), [kbt.b, Er.b], [kp.b])

                    def gla_state(ci, tok0, C):
                        sl = ci % 2
                        qt, kt, kp, vbt, sgt, Eq = qts[sl], kts[sl], kps[sl], vbts[sl], sgts[sl], Eqs[sl]
                        for h in range(4):
                            P.pe(CALL("matmul", psT_.t[0:C, h * 128:h * 128 + C], lhsT=kt.t[:, h, 0:C], rhs=qt.t[:, h, 0:C], start=True, stop=True),
                                 [kt.b, qt.b], [psT_.b])
                        psT3 = psT_.t[:, :].rearrange("p (h t) -> p h t", h=4)
                        P.dve(CALL("tensor_tensor", out=attm.t[0:C, :, 0:C], in0=psT3[0:C, :, 0:C],
                                   in1=cst.t[0:C, C_LE:C_LE + C].unsqueeze(1).broadcast_to([C, 4, C]), op=ALU.mult), [psT_.b, cst.b], [attm.b])
                        for h in range(4):
                            P.pe(CALL("matmul", psO.t[0:C, h * 256:(h + 1) * 256], lhsT=attm.t[0:C, h, 0:C], rhs=vbt.t[0:C, h * 256:(h + 1) * 256],
                                      start=True, stop=False), [attm.b, vbt.b], [psO.b])
                            P.pe(CALL("matmul", psO.t[0:C, h * 256:(h + 1) * 256], lhsT=qt.t[:, h, 0:C], rhs=Sb.t[:, h, :], start=False, stop=True),
                                 [qt.b, Sb.b], [psO.b])
                        for h in range(4):
                            P.pe(CALL("matmul", psS.t[:, h * 256:(h + 1) * 256], lhsT=kp.t[0:C, h * 128:(h + 1) * 128],
                                      rhs=vbt.t[0:C, h * 256:(h + 1) * 256], start=True, stop=True), [kp.b, vbt.b], [psS.b])
                        for h in range(4):
                            P.dve(CALL("scalar_tensor_tensor", out=Sf.t[:, h, :], in0=Sf.t[:, h, :], scalar=Eq.t[:, h, C - 1:C],
                                       in1=psS.t[:, h * 256:(h + 1) * 256], op0=ALU.mult, op1=ALU.add), [Sf.b, Eq.b, psS.b], [Sf.b])
                        P.act(CALL("activation", out=Sb.t[:, :, :], in_=Sf.t[:, :, :], func=AF.Copy), [Sf.b], [Sb.b])
                        osb = osbs[sl]
                        P.act(CALL("activation", out=osb.t[0:C, :], in_=psO.t[0:C, :], func=AF.Copy), [psO.b], [osb.b])

                    def gla_out(ci, tok0, C):
                        sl = ci % 2
                        sgt = sgts[0]
                        osb = osbs[sl]
                        P.dma("sp", CALL("dma_start", out=gbt.t[0:C, :], in_=gb_s[tok0:tok0 + C, :]), dbs("gb", tok0, C), [gbt.b])
                        for h in range(4):
                            P.act(CALL("activation", out=jk3.t[0:C, :], in_=osb.t[0:C, h * 256:(h + 1) * 256], func=AF.Square,
                                       accum_out=gst.t[0:C, h:h + 1]), [osb.b], [jk3.b, gst.b])
                        P.act(CALL("activation", out=gst.t[0:C, 4:8], in_=gst.t[0:C, 0:4], func=AF.Ln, bias=eps_c[0:C, :], scale=1.0 / 256),
                              [gst.b, cst.b], [gst.b])
                        P.act(CALL("activation", out=gst.t[0:C, 8:12], in_=gst.t[0:C, 4:8], func=AF.Exp, scale=-0.5), [gst.b], [gst.b])
                        for h in range(4):
                            P.dve(CALL("scalar_tensor_tensor", out=onr.t[0:C, h * 256:(h + 1) * 256], in0=osb.t[0:C, h * 256:(h + 1) * 256],
                                       scalar=gst.t[0:C, 8 + h:9 + h], in1=gnb.t[0:C, :], op0=ALU.mult, op1=ALU.mult),
                                  [osb.b, gst.b, gnb.b], [onr.b])
                        P.dve(CALL("tensor_tensor", out=obb.t[0:C, :], in0=onr.t[0:C, :], in1=gbt.t[0:C, :], op=ALU.mult), [onr.b, gbt.b], [obb.b])
                        for kc in range(8):
                            P.pe(CALL("transpose", psX.t[:, kc * 128:kc * 128 + C], obb.t[0:C, kc * 128:(kc + 1) * 128], ident_bf.t[0:C, 0:C]),
                                 [obb.b, ident_bf.b], [psX.b])
                        P.dve(CALL("tensor_copy", out=obT.t[:, :, 0:C], in_=psX.t[:, :].rearrange("p (k t) -> p k t", k=8)[:, :, 0:C]),
                              [psX.b], [obT.b])
                        P.dma("sp", CALL("dma_start", out=obT_s[:, :, tok0:tok0 + C].rearrange("k p t -> p k t"), in_=obT.t[:, :, 0:C]),
                              [obT.b], dbs("obT", tok0, C))

                    chunks = [(i * 128, 128, None) for i in range(NBLK)] + [(NTP + 8 * b, 8, b) for b in range(NB_S)]
                    P.pool(CALL("memset", Sf.t[:, :, :], 0.0), [], [Sf.b])
                    P.pool(CALL("memset", Sb.t[:, :, :], 0.0), [], [Sb.b])
                    gla_prep(0, chunks[0][0], chunks[0][1])
                    for ci, (tok0, C, sb_) in enumerate(chunks):
                        if sb_ is not None:
                            if sb_ == 0:
                                P.dma("sp", CALL("dma_start", out=glap_d.rearrange("h d v -> d h v"), in_=Sf.t[:, :, :]), [Sf.b], [B_gla_out])
                            P.dma("sp", CALL("dma_start", out=Sf.t[:, :, :], in_=st_d[sb_].rearrange("h d v -> d h v")), [B_gla_out], [Sf.b])
                            P.act(CALL("activation", out=Sb.t[:, :, :], in_=Sf.t[:, :, :], func=AF.Copy), [Sf.b], [Sb.b])
                        lists = []
                        if ci + 1 < len(chunks):
                            lists.append(P.cap(gla_prep, ci + 1, chunks[ci + 1][0], chunks[ci + 1][1]))
                        lists.append(P.cap(gla_state, ci, tok0, C))
                        if ci >= 1:
                            lists.append(P.cap(gla_out, ci - 1, chunks[ci - 1][0], chunks[ci - 1][1]))
                        P.replay_rr(lists)
                        if sb_ is not None:
                            P.dma("sp", CALL("dma_start", out=glas_d[sb_].rearrange("h d v -> d h v"), in_=Sf.t[:, :, :]), [Sf.b], [B_gla_out])
                    gla_out(len(chunks) - 1, chunks[-1][0], chunks[-1][1])
                    P.flush()

            def layer_norm(stack_tiles, y1, T, gname, bname, out_t, stt_):
                jk, = stack_tiles
                P.act(CALL("activation", out=jk.t[0:T, :], in_=y1.t[0:T, :], func=AF.Copy, accum_out=stt_.t[0:T, 0:1]),
                      [y1.b], [jk.b, stt_.b])
                P.act(CALL("activation", out=jk.t[0:T, :], in_=y1.t[0:T, :], func=AF.Square, accum_out=stt_.t[0:T, 1:2]),
                      [y1.b, jk.b], [jk.b, stt_.b])
                P.dve(CALL("tensor_scalar", out=stt_.t[0:T, 2:3], in0=stt_.t[0:T, 0:1], scalar1=1.0 / D, scalar2=None,
                                                op0=ALU.mult), [stt_.b], [stt_.b])
                P.dve(CALL("tensor_tensor", out=stt_.t[0:T, 3:4], in0=stt_.t[0:T, 2:3], in1=stt_.t[0:T, 2:3], op=ALU.mult),
                      [stt_.b], [stt_.b])
                P.dve(CALL("scalar_tensor_tensor", out=stt_.t[0:T, 4:5], in0=stt_.t[0:T, 1:2], scalar=1.0 / D,
                                                       in1=stt_.t[0:T, 3:4], op0=ALU.mult, op1=ALU.subtract), [stt_.b], [stt_.b])
                P.act(CALL("activation", out=stt_.t[0:T, 5:6], in_=stt_.t[0:T, 4:5], func=AF.Ln, bias=eps_c[0:T, :], scale=1.0),
                      [stt_.b, cst.b], [stt_.b])
                P.act(CALL("activation", out=stt_.t[0:T, 6:7], in_=stt_.t[0:T, 5:6], func=AF.Exp, scale=-0.5), [stt_.b], [stt_.b])
                P.dve(CALL("tensor_scalar", out=out_t.t[0:T, :], in0=y1.t[0:T, :], scalar1=stt_.t[0:T, 2:3],
                                                scalar2=stt_.t[0:T, 6:7], op0=ALU.subtract, op1=ALU.mult), [y1.b, stt_.b], [out_t.b])
                P.dve(CALL("tensor_tensor", out=out_t.t[0:T, :], in0=out_t.t[0:T, :], in1=lnb[gname].t[0:T, :], op=ALU.mult),
                      [out_t.b, lnb[gname].b], [out_t.b])
                P.dve(CALL("tensor_tensor", out=out_t.t[0:T, :], in0=out_t.t[0:T, :], in1=lnb[bname].t[0:T, :], op=ALU.add),
                      [out_t.b, lnb[bname].b], [out_t.b])

            tiles4 = [(t0, 128) for t0 in range(0, NTP, 128)] + ([(NTP, NS)] if "x" not in phases else [])
            if "4" in phases:
                s4 = ExitStack()
                with s4:
                    atb = [SB(s4, "atb%d" % i, [128, 4, 128], BF16) for i in range(2)]
                    obl = [SB(s4, "obl%d" % i, [128, 8, 128], BF16) for i in range(2)]
                    gtl = [SB(s4, "gtl%d" % i, [128, 16, 128], F32) for i in range(2)]
                    xbl = [SB(s4, "xbl%d" % i, [128, D], F32) for i in range(2)]
                    sig = SB(s4, "sig", [128, 16, 128], F32)
                    t1 = SB(s4, "t1", [128, 8, 128], F32)
                    mrgs = [SB(s4, "mrg%d" % i, [128, 8, 128], BF16) for i in range(2)]
                    y1 = SB(s4, "y1", [128, D], F32)
                    hh = SB(s4, "hh", [128, D], F32)
                    hb = SB(s4, "hb", [128, D], BF16)
                    hTt = SB(s4, "hTt", [128, 8, 128], BF16)
                    jk4 = SB(s4, "jk4", [128, D], BF16)
                    st4 = SB(s4, "st4", [128, 8], F32)
                    psa = PSB(s4, "psa", [128, 1024])
                    psb4 = PSB(s4, "psb4", [128, 1024])
                    psm = PSB(s4, "psm", [128, 1024])
                    psx = PSB(s4, "psx4", [128, 1024], BF16)

                    def load4(ti):
                        t0, T = tiles4[ti]
                        sl = ti % 2
                        P.dma("sp", CALL("dma_start", out=atb[sl].t[:, :, 0:T], in_=attnT_s[:, :, t0:t0 + T].rearrange("k p t -> p k t")),
                              dbs("attnT", t0, T), [atb[sl].b])
                        P.dma("sp", CALL("dma_start", out=obl[sl].t[:, :, 0:T], in_=obT_s[:, :, t0:t0 + T].rearrange("k p t -> p k t")),
                              dbs("obT", t0, T), [obl[sl].b])
                        P.dma("sp", CALL("dma_start", out=gtl[sl].t[:, :, 0:T], in_=gtT_s[:, :, t0:t0 + T].rearrange("k p t -> p k t")),
                              dbs("gtT", t0, T), [gtl[sl].b])
                        P.dma("sp", CALL("dma_start", out=xbl[sl].t[0:T, :], in_=x_d[t0:t0 + T, :]), [], [xbl[sl].b])

                    def s1_4a(ti):
                        t0, T = tiles4[ti]
                        sl = ti % 2
                        at_, ob_, gt_ = atb[sl], obl[sl], gtl[sl]
                        mrg = mrgs[sl]
                        psa3 = psa.t[:, :].rearrange("p (c t) -> p c t", c=8)
                        psb3 = psb4.t[:, :].rearrange("p (c t) -> p c t", c=8)
                        for c in range(8):
                            for kc in range(4):
                                P.pe(CALL("matmul", psa.t[:, c * 128:c * 128 + T], lhsT=wao.t[:, kc, c * 128:(c + 1) * 128],
                                          rhs=at_.t[:, kc, 0:T], start=(kc == 0), stop=(kc == 3)), [wao.b, at_.b], [psa.b])
                        for c in range(8):
                            for kc in range(8):
                                P.pe(CALL("matmul", psb4.t[:, c * 128:c * 128 + T], lhsT=wgo.t[:, kc, c * 128:(c + 1) * 128],
                                          rhs=ob_.t[:, kc, 0:T], start=(kc == 0), stop=(kc == 7)), [wgo.b, ob_.b], [psb4.b])
                        P.dve(CALL("tensor_tensor", out=t1.t[:, :, 0:T], in0=psa3[:, :, 0:T], in1=gt_.t[:, 0:8, 0:T], op=ALU.mult),
                              [psa.b, gt_.b], [t1.b])
                        P.dve(CALL("tensor_tensor", out=sig.t[:, 8:16, 0:T], in0=psb3[:, :, 0:T], in1=gt_.t[:, 8:16, 0:T], op=ALU.mult),
                              [psb4.b, gt_.b], [sig.b])
                        P.dve(CALL("tensor_tensor", out=mrg.t[:, :, 0:T], in0=t1.t[:, :, 0:T], in1=sig.t[:, 8:16, 0:T], op=ALU.add),
                              [t1.b, sig.b], [mrg.b])

                    def s2_4a(ti):
                        t0, T = tiles4[ti]
                        sl = ti % 2
                        xb_ = xbl[sl]
                        mrg = mrgs[sl]
                        for n in range(2):
                            for kc in range(8):
                                P.pe(CALL("matmul", psm.t[0:T, n * 512:(n + 1) * 512], lhsT=mrg.t[:, kc, 0:T],
                                          rhs=wo.t[:, kc, n * 512:(n + 1) * 512], start=(kc == 0), stop=(kc == 7)), [mrg.b, wo.b], [psm.b])
                        for n in range(2):
                            P.dve(CALL("scalar_tensor_tensor", out=y1.t[0:T, n * 512:(n + 1) * 512], in0=xb_.t[0:T, n * 512:(n + 1) * 512],
                                       scalar=ALPHA, in1=psm.t[0:T, n * 512:(n + 1) * 512], op0=ALU.mult, op1=ALU.add),
                                  [xb_.b, psm.b], [y1.b])
                        layer_norm((jk4,), y1, T, "ln1_g", "ln1_b", hh, st4)
                        P.dma("sp", CALL("dma_start", out=h_s[t0:t0 + T, :], in_=hh.t[0:T, :]), [hh.b], dbs("h", t0, T))
                        P.act(CALL("activation", out=hb.t[0:T, :], in_=hh.t[0:T, :], func=AF.Copy), [hh.b], [hb.b])
                        for kc in range(8):
                            P.pe(CALL("transpose", psx.t[:, kc * 128:kc * 128 + T], hb.t[0:T, kc * 128:(kc + 1) * 128],
                                      ident_bf.t[0:T, 0:T]), [hb.b, ident_bf.b], [psx.b])
                        P.dve(CALL("tensor_copy", out=hTt.t[:, :, 0:T], in_=psx.t[:, :].rearrange("p (k t) -> p k t", k=8)[:, :, 0:T]),
                              [psx.b], [hTt.b])
                        P.dma("sp", CALL("dma_start", out=hT_s[:, :, t0:t0 + T].rearrange("k p t -> p k t"), in_=hTt.t[:, :, 0:T]),
                              [hTt.b], dbs("hT", t0, T))

                    load4(0)
                    s1_4a(0)
                    for ti in range(len(tiles4)):
                        lists = []
                        if ti + 1 < len(tiles4):
                            load4(ti + 1)
                            lists.append(P.cap(s1_4a, ti + 1))
                        lists.append(P.cap(s2_4a, ti))
                        P.replay_rr(lists)
                    P.flush()

        if "4" in phases:
            s5 = ExitStack()
            with s5:
                wf2 = SB(s5, "wf2", [128, 32, D], BF16)
                wf2b = [Buf("wf2_%d" % i) for i in range(8)]
                for q in range(8):
                    P.dma("pool", CALL("dma_start", out=wf2.t[:, 4 * q:4 * q + 4, :],
                                                             in_=wf2_d[q * 512:(q + 1) * 512, :].rearrange("(kc p) n -> p kc n", p=128)),
                          [], [wf2b[q]])
                SW5 = 256
                hTl = [SB(s5, "hTl%d" % i, [128, 8, SW5], BF16) for i in range(2)]
                hl = [SB(s5, "hl%d" % i, [128, D], F32) for i in range(2)]
                rl = [SB(s5, "rl%d" % i, [128, 2, SW5], BF16) for i in range(2)]
                hid = SB(s5, "hid", [128, 32, SW5], BF16)
                y2 = SB(s5, "y2", [128, D], F32)
                yo = SB(s5, "yo", [128, D], F32)
                jk5 = SB(s5, "jk5", [128, D], BF16)
                st5 = SB(s5, "st5", [128, 8], F32)
                psf = [PSB(s5, "psf%d" % i, [128, 512]) for i in range(3)]
                psy = PSB(s5, "psy", [128, 1024])
                sup5 = [(t0, min(SW5, NTP - t0)) for t0 in range(0, NTP, SW5)] + (
                    [(NTP, NS)] if "x" not in phases else [])

                def load5T(ui):
                    t0, W = sup5[ui]
                    sl = ui % 2
                    P.dma("sp", CALL("dma_start", out=hTl[sl].t[:, :, 0:W], in_=hT_s[:, :, t0:t0 + W].rearrange("k p t -> p k t")),
                          dbs("hT", t0, W), [hTl[sl].b])

                fi_ = [0]
                hi_ = [0]

                def s1_4b(ui):
                    t0, W = sup5[ui]
                    hT_ = hTl[ui % 2]
                    for f2 in range(16):
                        pf = psf[fi_[0] % 3]
                        r_ = rl[fi_[0] % 2]
                        fi_[0] += 1
                        for cc in range(2):
                            f = f2 * 2 + cc
                            for kc in range(8):
                                P.pe(CALL("matmul", pf.t[:, cc * SW5:cc * SW5 + W], lhsT=wf1.t[:, kc, f * 128:(f + 1) * 128],
                                          rhs=hT_.t[:, kc, 0:W], start=(kc == 0), stop=(kc == 7)), [wf1b[kc], hT_.b], [pf.b])
                        pf3 = pf.t[:, :].rearrange("p (c t) -> p c t", c=2)
                        P.act(CALL("activation", out=r_.t[:, :, 0:W], in_=pf3[:, :, 0:W], func=AF.Relu), [pf.b], [r_.b])
                        P.dve(CALL("tensor_tensor", out=hid.t[:, f2 * 2:f2 * 2 + 2, 0:W], in0=r_.t[:, :, 0:W], in1=r_.t[:, :, 0:W],
                                   op=ALU.mult), [r_.b], [hid.b])

                def s2_4b(t0, T, c0):
                    h_ = hl[hi_[0] % 2]
                    hi_[0] += 1
                    P.dma("sp", CALL("dma_start", out=h_.t[0:T, :], in_=h_s[t0:t0 + T, :]), dbs("h", t0, T), [h_.b])
                    for n in range(2):
                        for kc in range(32):
                            P.pe(CALL("matmul", psy.t[0:T, n * 512:(n + 1) * 512], lhsT=hid.t[:, kc, c0:c0 + T],
                                      rhs=wf2.t[:, kc, n * 512:(n + 1) * 512], start=(kc == 0), stop=(kc == 31)),
                                 [hid.b, wf2b[kc // 4]], [psy.b])
                    for n in range(2):
                        P.dve(CALL("scalar_tensor_tensor", out=y2.t[0:T, n * 512:(n + 1) * 512], in0=h_.t[0:T, n * 512:(n + 1) * 512],
                                   scalar=ALPHA, in1=psy.t[0:T, n * 512:(n + 1) * 512], op0=ALU.mult, op1=ALU.add),
                              [h_.b, psy.b], [y2.b])
                    layer_norm((jk5,), y2, T, "ln2_g", "ln2_b", yo, st5)
                    P.dma("sp", CALL("dma_start", out=y_d[t0:t0 + T, :], in_=yo.t[0:T, :]), [yo.b], dbs("y", t0, T))

                load5T(0)
                for ui, (u0, W) in enumerate(sup5):
                    if ui + 1 < len(sup5):
                        load5T(ui + 1)
                    s1_4b(ui)
                    for c0 in range(0, W, 128):
                        s2_4b(u0 + c0, min(128, W - c0), c0)
                P.flush()
        P.flush(final=True)
    return nc


_NC_CACHE = {}


def make_in_maps(inp, n_cores, NTP, NPOOL):
    cst = make_consts()
    ck = np.ascontiguousarray(inp["cache_k"]).reshape(NPOOL * 8, 2048)
    cv = np.ascontiguousarray(inp["cache_v"]).reshape(NPOOL * 8, 2048)
    cki = np.ascontiguousarray(inp["cache_kidx"]).reshape(NPOOL * 4, 2048)
    maps = []
    for c in range(n_cores):
        xs = np.asarray(inp["x_sample"][NB_S * c:NB_S * (c + 1)]).reshape(NS, D)
        x = np.concatenate([np.asarray(inp["x_prompt"][c]), xs], axis=0).astype(np.float32)
        m = {
            "x": np.ascontiguousarray(x),
            "xT": np.ascontiguousarray(x.T),
            "cache_k": ck, "cache_v": cv, "cache_kidx": cki,
            "state_gla": np.ascontiguousarray(inp["state_gla"][0, NB_S * c:NB_S * (c + 1)]),
            "page_table": np.ascontiguousarray(inp["page_table"][NB_S * c:NB_S * (c + 1)]).astype(np.int32),
            "w_in": np.ascontiguousarray(inp["w_in"][0]),
            "w_alpha2": np.ascontiguousarray(inp["w_alpha2"][0]),
            "b_alpha": np.ascontiguousarray(inp["b_alpha"][0]).reshape(1, 512),
            "gla_norm_g": np.ascontiguousarray(inp["gla_norm_g"][0]).reshape(1, 256),
            "w_attn_o": np.ascontiguousarray(inp["w_attn_o"][0]),
            "w_gla_o": np.ascontiguousarray(inp["w_gla_o"][0]),
            "w_out": np.ascontiguousarray(inp["w_out"][0]),
            "ln1_g": np.ascontiguousarray(inp["ln1_g"][0]).reshape(1, D),
            "ln1_b": np.ascontiguousarray(inp["ln1_b"][0]).reshape(1, D),
            "ln2_g": np.ascontiguousarray(inp["ln2_g"][0]).reshape(1, D),
            "ln2_b": np.ascontiguousarray(inp["ln2_b"][0]).reshape(1, D),
            "w_ff1": np.ascontiguousarray(inp["w_ff1"][0]),
            "w_ff2": np.ascontiguousarray(inp["w_ff2"][0]),
            "cst": cst,
        }
        maps.append(m)
    return maps


def assemble(res, n_cores, NTP):
    f = np.float32
    y = np.stack([r["y"][:NTP] for r in res]).astype(f)
    ys = np.concatenate([r["y"][NTP:].reshape(NB_S, 8, D) for r in res]).astype(f)
    kp = np.stack([r["ko"][:NTP].reshape(NTP, 2, 64) for r in res])[None].astype(f)
    vp = np.stack([r["vo"][:NTP].reshape(NTP, 2, 64) for r in res])[None].astype(f)
    kip = np.stack([r["kio"][:NTP] for r in res])[None].astype(f)
    gp = np.stack([r["gla_p"] for r in res])[None].astype(f)
    ks = np.concatenate([r["ko"][NTP:].reshape(NB_S, 8, 2, 64) for r in res])[None].astype(f)
    vs = np.concatenate([r["vo"][NTP:].reshape(NB_S, 8, 2, 64) for r in res])[None].astype(f)
    kis = np.concatenate([r["kio"][NTP:].reshape(NB_S, 8, 64) for r in res])[None].astype(f)
    gs = np.concatenate([r["gla_s"] for r in res])[None].astype(f)
    return (y, ys, kp, vp, kip, gp, ks, vs, kis, gs)


def kernel(**inputs):
    n_cores = 8
    NTP = inputs["x_prompt"].shape[1]
    NPOOL = inputs["cache_k"].shape[1]
    nc = build(NTP=NTP, NPOOL=NPOOL)
    maps = make_in_maps(inputs, n_cores, NTP, NPOOL)
    out = run_bass_kernel_spmd(nc, maps, core_ids=list(range(n_cores)))
    return assemble(out.results, n_cores, NTP)
```
